# Optimizing a Trainium2 kernel written in Bass

```python
import math
import jax, jax.numpy as jnp
from jax import lax
import numpy as np

D_MODEL = 2048
BATCH = 4
SEQ = 2048
DEPTH = 4
DEC_BATCH = 8
DEC_SEQ = 1
PAST_LEN = 16384
PAGE_SIZE = 128

N_MIXERS = 2
N_ATTN_LAYERS = (DEPTH + 1) // 2
N_SSM_LAYERS = DEPTH // 2
HEAD_DIM = 128
HEADS_PER_GROUP = D_MODEL // (2 * HEAD_DIM)
DIL_GROUPS = ((128, 1), (512, 4), (2048, 16))
N_DIL = len(DIL_GROUPS)
ROT_DIM = HEAD_DIM // 4
ROPE_THETA = 500000.0
Q_BLOCK = 128
NEG_INF = -1e30
D_SSM = D_MODEL
SSM_CH = 16
SSM_GROUPS = D_SSM // SSM_CH
SSM_STATE = 64
DT_MIN = 0.001
DT_MAX = 0.1
D_FF = ((8 * D_MODEL // 3 + 127) // 128) * 128
CONV_WIDTH = 3
RMS_EPS = 1e-6

kernel_name = "dilated_swa_s5_convffn_hybrid_step"


def rms_norm(x, g):
    xf = x.astype(jnp.float32)
    y = xf * lax.rsqrt(jnp.mean(xf * xf, axis=-1, keepdims=True) + RMS_EPS)
    return (y * g.astype(jnp.float32)).astype(x.dtype)


def rope_partial(x, pos):
    half = ROT_DIM // 2
    inv = ROPE_THETA ** (-(jnp.arange(half, dtype=jnp.float32) * (2.0 / ROT_DIM)))
    ang = pos.astype(jnp.float32)[:, None] * inv[None, :]
    cos = jnp.cos(ang)[None, :, None, None, :]
    sin = jnp.sin(ang)[None, :, None, None, :]
    xf = x.astype(jnp.float32)
    x1, x2 = xf[..., :half], xf[..., half:ROT_DIM]
    out = jnp.concatenate([x1 * cos - x2 * sin, x2 * cos + x1 * sin, xf[..., ROT_DIM:]], axis=-1)
    return out.astype(x.dtype)


def qkv_project(h, w_qkv, pos):
    B, T, _ = h.shape
    qkv = (h @ w_qkv).reshape(B, T, 3, N_DIL, HEADS_PER_GROUP, HEAD_DIM)
    q = rope_partial(qkv[:, :, 0], pos)
    k = rope_partial(qkv[:, :, 1], pos)
    v = qkv[:, :, 2]
    return q, k, v


def dilated_attend(q, k_all, v_all, q_idx, dilation, n_back):
    key_idx = q_idx[:, None] - dilation * jnp.arange(n_back + 1)[None, :]
    valid = key_idx >= 0
    key_idx = jnp.maximum(key_idx, 0)
    kg = jnp.take(k_all, key_idx, axis=1)
    vg = jnp.take(v_all, key_idx, axis=1)
    s = jnp.einsum('bqhd,bqkhd->bqhk', q, kg, preferred_element_type=jnp.float32) * (HEAD_DIM ** -0.5)
    s = jnp.where(valid[None, :, None, :], s, NEG_INF)
    m = jnp.max(s, axis=-1, keepdims=True)
    p = jnp.exp(s - m)
    l = jnp.sum(p, axis=-1, keepdims=True)
    o = jnp.einsum('bqhk,bqkhd->bqhd', p, vg.astype(jnp.float32)) / jnp.swapaxes(l, -1, -2)[..., 0][..., None]
    lse = (m + jnp.log(l))[..., 0]
    return o, lse


def merge_by_denominator(outs, lses):
    w = jax.nn.softmax(jnp.stack(lses, axis=0), axis=0)
    return jnp.sum(w[..., None] * jnp.stack(outs, axis=0), axis=0)


def attn_prompt(h, w_qkv, w_o):
    B, T, _ = h.shape
    q, k, v = qkv_project(h, w_qkv, jnp.arange(T))
    qs = [q[:, :, g] for g in range(N_DIL)]
    ks = [k[:, :, g] for g in range(N_DIL)]
    vs = [v[:, :, g] for g in range(N_DIL)]

    def block(start):
        idx = start + jnp.arange(Q_BLOCK)
        outs, lses = [], []
        for g, (win, dil) in enumerate(DIL_GROUPS):
            qb = lax.dynamic_slice_in_dim(qs[g], start, Q_BLOCK, axis=1)
            o, lse = dilated_attend(qb, ks[g], vs[g], idx, dil, win // dil)
            outs.append(o)
            lses.append(lse)
        return merge_by_denominator(outs, lses)

    o = lax.map(block, jnp.arange(0, T, Q_BLOCK))
    o = jnp.moveaxis(o, 0, 1).reshape(B, T, HEADS_PER_GROUP * HEAD_DIM).astype(h.dtype)
    new_kv = []
    for g, (win, _) in enumerate(DIL_GROUPS):
        keep = min(win, T)
        new_kv.append(jnp.stack([ks[g][:, T - keep:], vs[g][:, T - keep:]], axis=2))
    return o @ w_o, new_kv


def attn_sample(h, kv_bufs, w_qkv, w_o):
    B, S, _ = h.shape
    q, k, v = qkv_project(h, w_qkv, PAST_LEN + jnp.arange(S))
    outs, lses, new_rows = [], [], []
    for g, (win, dil) in enumerate(DIL_GROUPS):
        buf = kv_bufs[g]
        L = buf.shape[1]
        k_all = jnp.concatenate([buf[:, :, 0].astype(k.dtype), k[:, :, g]], axis=1)
        v_all = jnp.concatenate([buf[:, :, 1].astype(v.dtype), v[:, :, g]], axis=1)
        o, lse = dilated_attend(q[:, :, g], k_all, v_all, L + jnp.arange(S), dil, win // dil)
        outs.append(o)
        lses.append(lse)
        new_rows.append(jnp.stack([k[:, :, g], v[:, :, g]], axis=2))
    o = merge_by_denominator(outs, lses).reshape(B, S, HEADS_PER_GROUP * HEAD_DIM).astype(h.dtype)
    return o @ w_o, new_rows


def _linear_recurrence_combine(e1, e2):
    a1, b1 = e1
    a2, b2 = e2
    return a1 * a2, a2 * b1 + b2


def s5_mixer(h, h0, w_in, lam_re, lam_im, log_dt, b_re, b_im, c_re, c_im, d_skip, w_glu):
    B, T, _ = h.shape
    f32 = jnp.float32
    u = (h @ w_in).astype(f32)
    ug = u.reshape(B, T, SSM_GROUPS, SSM_CH)
    lam = lax.complex(jnp.minimum(lam_re.astype(f32), -1e-4), lam_im.astype(f32))
    dt = jnp.exp(log_dt.astype(f32))[:, None]
    a_bar = jnp.exp(lam * dt)
    b = lax.complex(b_re.astype(f32), b_im.astype(f32))
    b_bar = ((a_bar - 1.0) / lam)[..., None] * b
    c = lax.complex(c_re.astype(f32), c_im.astype(f32))
    bu = jnp.einsum('gpc,btgc->btgp', b_bar, ug.astype(jnp.complex64))
    h0c = lax.complex(h0[..., 0].astype(f32), h0[..., 1].astype(f32))
    bu = bu.at[:, 0].add(a_bar[None] * h0c)
    a = jnp.broadcast_to(a_bar, bu.shape)
    _, xs = lax.associative_scan(_linear_recurrence_combine, (a, bu), axis=1)
    y = jnp.einsum('gcp,btgp->btgc', c, xs).real.reshape(B, T, D_SSM) + d_skip.astype(f32) * u
    z = jax.nn.gelu(y).astype(h.dtype)
    ga = z @ w_glu
    out = ga[..., :D_MODEL] * jax.nn.sigmoid(ga[..., D_MODEL:])
    x_last = xs[:, -1]
    new_state = jnp.stack([x_last.real, x_last.imag], axis=-1).astype(h0.dtype)
    return out, new_state


def conv_ffn(h, conv_state, w_up, conv_w, conv_b, w_down):
    T = h.shape[1]
    u = h @ w_up
    up = jnp.concatenate([conv_state.astype(u.dtype), u], axis=1)
    c = conv_b
    for j in range(CONV_WIDTH):
        c = c + conv_w[j] * up[:, j:j + T]
    gate, val = c[..., :D_FF], c[..., D_FF:]
    out = (jax.nn.silu(gate) * val) @ w_down
    return out, up[:, -(CONV_WIDTH - 1):]


def setup_inputs(seed: int = 0) -> dict:
    key = jax.random.key(seed)
    ks = iter(jax.random.split(key, 32))

    def nrm(shape, scale=1.0):
        return scale * jax.random.normal(next(ks), shape, jnp.float32)

    attn_w = HEADS_PER_GROUP * HEAD_DIM
    qkv_cols = 3 * N_DIL * attn_w
    inp = {}
    inp["x_prompt"] = nrm((BATCH, SEQ, D_MODEL))
    inp["x_sample"] = nrm((DEC_BATCH, DEC_SEQ, D_MODEL))
    for g, (win, _) in enumerate(DIL_GROUPS):
        inp["cache_kv_g%d" % g] = nrm((N_ATTN_LAYERS, DEC_BATCH, min(win, PAST_LEN), 2, HEADS_PER_GROUP, HEAD_DIM))
    inp["state_ssm"] = nrm((N_SSM_LAYERS, DEC_BATCH, SSM_GROUPS, SSM_STATE, 2))
    inp["state_conv"] = nrm((DEPTH, DEC_BATCH, CONV_WIDTH - 1, 2 * D_FF))
    inp["norm_g"] = 1.0 + nrm((DEPTH, 4, D_MODEL), 0.05)
    inp["w_qkv"] = nrm((N_ATTN_LAYERS, D_MODEL, qkv_cols), D_MODEL ** -0.5)
    inp["w_attn_o"] = nrm((N_ATTN_LAYERS, attn_w, D_MODEL), attn_w ** -0.5)
    inp["w_ssm_in"] = nrm((N_SSM_LAYERS, D_MODEL, D_SSM), D_MODEL ** -0.5)
    inp["lambda_re"] = -0.5 + nrm((N_SSM_LAYERS, SSM_GROUPS, SSM_STATE), 0.01)
    inp["lambda_im"] = jnp.pi * jnp.arange(SSM_STATE, dtype=jnp.float32) + nrm((N_SSM_LAYERS, SSM_GROUPS, SSM_STATE), 0.01)
    inp["log_dt"] = jax.random.uniform(next(ks), (N_SSM_LAYERS, SSM_GROUPS), jnp.float32, math.log(DT_MIN), math.log(DT_MAX))
    inp["b_re"] = nrm((N_SSM_LAYERS, SSM_GROUPS, SSM_STATE, SSM_CH), (2 * SSM_CH) ** -0.5)
    inp["b_im"] = nrm((N_SSM_LAYERS, SSM_GROUPS, SSM_STATE, SSM_CH), (2 * SSM_CH) ** -0.5)
    inp["c_re"] = nrm((N_SSM_LAYERS, SSM_GROUPS, SSM_CH, SSM_STATE), (2 * SSM_STATE) ** -0.5)
    inp["c_im"] = nrm((N_SSM_LAYERS, SSM_GROUPS, SSM_CH, SSM_STATE), (2 * SSM_STATE) ** -0.5)
    inp["d_skip"] = nrm((N_SSM_LAYERS, D_SSM))
    inp["w_glu"] = nrm((N_SSM_LAYERS, D_SSM, 2 * D_MODEL), D_SSM ** -0.5)
    inp["w_up"] = nrm((DEPTH, D_MODEL, 2 * D_FF), D_MODEL ** -0.5)
    inp["conv_w"] = nrm((DEPTH, CONV_WIDTH, 2 * D_FF), CONV_WIDTH ** -0.5)
    inp["conv_b"] = nrm((DEPTH, 2 * D_FF), 0.01)
    inp["w_down"] = nrm((DEPTH, D_FF, D_MODEL), D_FF ** -0.5)
    return inp


def reference(x_prompt, x_sample, cache_kv_g0, cache_kv_g1, cache_kv_g2, state_ssm, state_conv,
              norm_g, w_qkv, w_attn_o, w_ssm_in, lambda_re, lambda_im, log_dt, b_re, b_im, c_re, c_im,
              d_skip, w_glu, w_up, conv_w, conv_b, w_down):
    xp, xs = x_prompt, x_sample
    Bp = xp.shape[0]
    kv_p = [[] for _ in range(N_DIL)]
    kv_s = [[] for _ in range(N_DIL)]
    ssm_p, ssm_s, conv_p, conv_s = [], [], [], []
    for i in range(DEPTH):
        li = i // N_MIXERS
        hp = rms_norm(xp, norm_g[i, 0])
        hs = rms_norm(xs, norm_g[i, 0])
        if i % N_MIXERS == 0:
            mp, new_p = attn_prompt(hp, w_qkv[li], w_attn_o[li])
            ms, new_s = attn_sample(hs, (cache_kv_g0[li], cache_kv_g1[li], cache_kv_g2[li]), w_qkv[li], w_attn_o[li])
            for g in range(N_DIL):
                kv_p[g].append(new_p[g])
                kv_s[g].append(new_s[g])
        else:
            prm = (w_ssm_in[li], lambda_re[li], lambda_im[li], log_dt[li], b_re[li], b_im[li],
                   c_re[li], c_im[li], d_skip[li], w_glu[li])
            h0_p = jnp.zeros((Bp, SSM_GROUPS, SSM_STATE, 2), state_ssm.dtype)
            mp, sp = s5_mixer(hp, h0_p, *prm)
            ms, ss = s5_mixer(hs, state_ssm[li], *prm)
            ssm_p.append(sp)
            ssm_s.append(ss)
        xp = xp + rms_norm(mp, norm_g[i, 1])
        xs = xs + rms_norm(ms, norm_g[i, 1])
        hp = rms_norm(xp, norm_g[i, 2])
        hs = rms_norm(xs, norm_g[i, 2])
        c0_p = jnp.zeros((Bp, CONV_WIDTH - 1, 2 * D_FF), state_conv.dtype)
        fp, cp = conv_ffn(hp, c0_p, w_up[i], conv_w[i], conv_b[i], w_down[i])
        fs, cs = conv_ffn(hs, state_conv[i], w_up[i], conv_w[i], conv_b[i], w_down[i])
        conv_p.append(cp)
        conv_s.append(cs)
        xp = xp + rms_norm(fp, norm_g[i, 3])
        xs = xs + rms_norm(fs, norm_g[i, 3])
    y_prompt, y_sample = xp, xs
    return (y_prompt, y_sample,
            jnp.stack(kv_p[0]), jnp.stack(kv_s[0]),
            jnp.stack(kv_p[1]), jnp.stack(kv_s[1]),
            jnp.stack(kv_p[2]), jnp.stack(kv_s[2]),
            jnp.stack(ssm_p), jnp.stack(ssm_s),
            jnp.stack(conv_p), jnp.stack(conv_s))
```

```python
import contextlib
import math
import numpy as np
import concourse.bass as bass
import concourse.mybir as mybir
from concourse.bass_utils import run_bass_kernel_spmd

F32 = mybir.dt.float32
BF16 = mybir.dt.bfloat16
AF = mybir.ActivationFunctionType
ALU = mybir.AluOpType
AX = mybir.AxisListType

T = 2048
D = 2048
KC = 16
NS = 2
TC = T + NS
TW = 512
NT = T // TW
TWS = TW + NS
DFF = 5504
NFC = 43
NUC = 86
NCORES = 4
PAST = 16384
WIN = (128, 512, 2048)
DIL = (1, 4, 16)
EPS = 1e-6
SLOT = 5504
NSLOT = 4
SCALE = 128 ** -0.5
GELU_C = 2.0 * math.sqrt(2.0 / math.pi)


class KB:
    NDS = 12

    def __init__(self):
        self.nc = bass.Bass("TRN2", target_bir_lowering=False)
        nc = self.nc
        self.es = contextlib.ExitStack()
        self.engs = {"pe": nc.tensor, "dve": nc.vector, "act": nc.scalar, "pool": nc.gpsimd, "sp": nc.sync}
        self.semh = {}
        for k in self.engs:
            self.semh[("e", k)] = self.es.enter_context(nc.semaphore("se_" + k))
        self.cnt = {k: 0 for k in self.engs}
        self.waited = {k: {} for k in self.engs}
        self.pending = {k: ([], []) for k in self.engs}
        self.lastw = {}
        self.readers = {}
        self.dcnt = {}
        self.dnext = {}
        for q in ("sp", "pool", "act"):
            for i in range(self.NDS):
                self.semh[("d", q, i)] = self.es.enter_context(nc.semaphore("sd_%s_%d" % (q, i)))
                self.dcnt[("d", q, i)] = 0
            self.dnext[q] = 0
        self.nins = 0

    def sb(self, name, shape, dt, es=None):
        self._uid = getattr(self, "_uid", 0) + 1
        return (es or self.es).enter_context(self.nc.sbuf_tensor("%s_%d" % (name, self._uid), list(shape), dt))

    def ps(self, name, shape, dt, es=None):
        return (es or self.es).enter_context(self.nc.psum_tensor(name, list(shape), dt))

    def dram(self, name, shape, dt, kind):
        return self.nc.dram_tensor(name, list(shape), dt, kind=kind).ap()

    def _deps(self, reads, writes):
        deps = {}
        for b in reads:
            lw = self.lastw.get(b)
            if lw is not None:
                deps[lw[0]] = max(deps.get(lw[0], 0), lw[1])
        for b in writes:
            lw = self.lastw.get(b)
            if lw is not None:
                deps[lw[0]] = max(deps.get(lw[0], 0), lw[1])
            for sk, v in self.readers.get(b, {}).items():
                deps[sk] = max(deps.get(sk, 0), v)
        return deps

    def _wait(self, eng, deps):
        e = self.engs[eng]
        w = self.waited[eng]
        for sk, v in deps.items():
            if eng == "pe" and sk == ("e", "pe"):
                continue
            if w.get(sk, 0) < v:
                e.wait_ge(self.semh[sk], v)
                w[sk] = v
                self.nins += 1

    def op(self, eng, fn, reads=(), writes=(), signal=True):
        self._wait(eng, self._deps(reads, writes))
        ins = fn(self.engs[eng])
        self.nins += 1
        pr, pw = self.pending[eng]
        pr.extend(reads)
        pw.extend(writes)
        if signal:
            self.cnt[eng] += 1
            sk = ("e", eng)
            ins.then_inc(self.semh[sk], 1)
            v = self.cnt[eng]
            for b in pw:
                self.lastw[b] = (sk, v)
                self.readers[b] = {}
            for b in pr:
                if b not in pw:
                    self.readers.setdefault(b, {})[sk] = v
            self.pending[eng] = ([], [])
        return ins

    def dma(self, q, out, in_, reads=(), writes=(), **kw):
        self._wait(q, self._deps(reads, writes))
        i = self.dnext[q]
        self.dnext[q] = (i + 1) % self.NDS
        sk = ("d", q, i)
        prev = self.dcnt[sk]
        if prev > 0 and self.waited[q].get(sk, 0) < prev:
            self.engs[q].wait_ge(self.semh[sk], prev)
            self.waited[q][sk] = prev
        ins = self.engs[q].dma_start(out=out, in_=in_, **kw)
        self.nins += 1
        self.dcnt[sk] += 16
        v = self.dcnt[sk]
        ins.then_inc(self.semh[sk], 16)
        for b in writes:
            self.lastw[b] = (sk, v)
            self.readers[b] = {}
        for b in reads:
            if b not in writes:
                self.readers.setdefault(b, {})[sk] = v
        return ins

    def barrier(self, engines=None):
        cur = {}
        for k in self.engs:
            assert not self.pending[k][0] and not self.pending[k][1]
            cur[("e", k)] = self.cnt[k]
        for sk, v in self.dcnt.items():
            cur[sk] = v
        for eng in (engines or list(self.engs)):
            for sk, v in cur.items():
                if sk == ("e", eng):
                    continue
                if v > 0 and self.waited[eng].get(sk, 0) < v:
                    self.engs[eng].wait_ge(self.semh[sk], v)
                    self.waited[eng][sk] = v
                    self.nins += 1
        if engines is None:
            self.lastw = {}
            self.readers = {}


class WRing:
    def __init__(self, kb, tensor):
        self.kb = kb
        self.t = tensor
        self.plan = []
        self.emitted = 0
        self.consumed = 0

    def add(self, tag, pieces):
        self.plan.append((tag, pieces))

    def _emit(self, k):
        tag, pieces = self.plan[k]
        s = k % NSLOT
        for (off, shp, src) in pieces:
            n = int(np.prod(shp))
            dst = self.t[:, s, off:off + n]
            if len(shp) == 2:
                dst = dst.rearrange("p (a b) -> p a b", a=shp[0])
            self.kb.dma("pool", dst, src, writes=[("w", s)])

    def get(self, tag):
        k = self.consumed
        assert self.plan[k][0] == tag, (self.plan[k][0], tag)
        while self.emitted < min(len(self.plan), k + NSLOT):
            self._emit(self.emitted)
            self.emitted += 1
        self.consumed += 1
        s = k % NSLOT
        return self.t[:, s, :], ("w", s)


def build(nlayers=4, dbg=False):
    kb = KB()
    nc = kb.nc
    op, dma = kb.op, kb.dma

    def din(name, shape, dt=F32):
        return kb.dram(name, shape, dt, "ExternalInput")

    def dout(name, shape, dt=F32):
        return kb.dram(name, shape, dt, "ExternalOutput")

    x_p = din("x_p", [T, D])
    x_s = din("x_s", [NS, D])
    ckv = [din("ckv%d" % g, [2, NS, WIN[g], 2 * 1024]) for g in range(3)]
    st_ssm = din("st_ssm", [2, NS, 128 * 64 * 2])
    st_conv = din("st_conv", [4, NS, 2, 2 * DFF])
    norm_g = din("norm_g", [16, D])
    w_qkv = din("w_qkv", [2, D, 9216])
    w_o = din("w_o", [2, 1024, D])
    w_in = din("w_in", [2, D, D])
    lam_re = din("lam_re", [2, 128, 64])
    lam_im = din("lam_im", [2, 128, 64])
    log_dt = din("log_dt", [2, 128])
    b_re = din("b_re", [2, 128, 64, 16])
    b_im = din("b_im", [2, 128, 64, 16])
    c_re = din("c_re", [2, 128, 16, 64])
    c_im = din("c_im", [2, 128, 16, 64])
    d_skip = din("d_skip", [2, D])
    w_glu = din("w_glu", [2, D, 2 * D])
    w_up = din("w_up", [4, D, 2 * DFF])
    conv_w = din("conv_w", [4, 3, 2 * DFF])
    conv_b = din("conv_b", [4, 2 * DFF])
    w_down = din("w_down", [4, DFF, D])
    c_ident = din("c_ident", [128, 128])
    c_masks = din("c_masks", [128, 7, 128])
    c_rope = din("c_rope", [128, 17, 32])
    c_mbq = din("c_mbq", [128, 7, 128])
    c_gmask = din("c_gmask", [128, 8])

    y_p = dout("y_p", [T, D])
    y_s = dout("y_s", [NS, D])
    kvp = [dout("kvp%d" % g, [2, min(WIN[g], T), 2048]) for g in range(3)]
    kvs = [dout("kvs%d" % g, [2, NS, 2048]) for g in range(3)]
    ssm_p = dout("ssm_p", [2, 128 * 64 * 2])
    ssm_s = dout("ssm_s", [2, NS, 128 * 64 * 2])
    conv_p = dout("conv_p", [4, 2, 2 * DFF])
    conv_s = dout("conv_s", [4, NS, 2, 2 * DFF])

    IK = "ExternalOutput" if dbg else "Internal"
    xT_s = kb.dram("xT_s", [KC, 128, TC], F32, "Internal")
    qT_s = kb.dram("qT_s", [3, 8, 128, T], BF16, IK)
    kT_s = kb.dram("kT_s", [3, 8, 128, T], BF16, IK)
    v_s = kb.dram("v_s", [3, T, 1024], BF16, IK)
    mT_s = kb.dram("mT_s", [8, 128, TC], BF16, IK)
    dbg_x1 = kb.dram("dbg_x1", [KC, 128, TC], F32, "ExternalOutput") if dbg else None
    dbg_m = kb.dram("dbg_m", [KC, 128, TC], F32, "ExternalOutput") if dbg else None
    uT_s = kb.dram("uT_s", [KC, 128, TC], F32, "Internal")
    zT_s = kb.dram("zT_s", [KC, 128, TC], BF16, "Internal")

    ident_f = kb.sb("ident_f", [128, 128], F32)
    ident_b = kb.sb("ident_b", [128, 128], BF16)
    ones_b = kb.sb("ones_b", [128, 128], BF16)
    masks = kb.sb("masks", [128, 7, 128], BF16)
    rope = kb.sb("rope", [128, 17, 32], F32)
    gsc = kb.sb("gsc", [128, KC, 16], F32)
    wring_t = kb.sb("wring", [128, NSLOT, SLOT], BF16)
    wr = WRing(kb, wring_t)
    sqT = kb.sb("sqT", [128, 24, NS], BF16)
    skT = kb.sb("skT", [128, 24, NS], BF16)
    svT = kb.sb("svT", [128, 24, NS], F32)
    epsb = kb.sb("epsb", [128, 1], F32)

    pbank = [kb.ps("pb%d" % i, [128, 512], F32) for i in range(8)]
    prr = [0]

    def psum(lo=0, hi=6):
        i = lo + prr[0] % (hi - lo)
        prr[0] += 1
        return pbank[i], ("ps", i)

    dma("sp", ident_f[:], c_ident[:, :], writes=["ident_f"])
    dma("pool", ident_b[:], c_ident[:, :], writes=["ident_b"])
    dma("pool", masks[:], c_masks[:, :, :], writes=["masks"])
    dma("sp", rope[:], c_rope[:, :, :], writes=["rope"])
    op("dve", lambda e: e.memset(ones_b[:], 1.0), writes=["ones_b"])
    op("dve", lambda e: e.memset(epsb[:], EPS), writes=["epsb"])

    es0 = contextlib.ExitStack()
    gtmp = kb.sb("gtmp", [16, D], F32, es0)
    dma("sp", gtmp[:], norm_g[:, :], writes=["gtmp"])
    for kc in range(KC):
        pb, pk = psum()
        op("pe", lambda e: e.transpose(pb[:, 0:16], gtmp[0:16, kc * 128:(kc + 1) * 128], ident_f[0:16, 0:16]),
           reads=["gtmp", "ident_f"], writes=[pk])
        op("act", lambda e: e.copy(gsc[:, kc, :], pb[:, 0:16]), reads=[pk], writes=["gsc"])
    kb.barrier()
    es0.close()

    def gs(i, j, kc):
        return gsc[:, kc, 4 * i + j:4 * i + j + 1]

    def plan_qkv(li):
        for ct in range(36):
            src = w_qkv[li, :, ct * 256:(ct + 1) * 256].rearrange("(k p) c -> p k c", p=128)
            wr.add(("qkv", li, ct), [(0, [KC, 256], src)])

    def plan_win(li):
        for dc in range(KC):
            src = w_in[li, :, dc * 128:(dc + 1) * 128].rearrange("(k p) c -> p k c", p=128)
            wr.add(("win", li, dc), [(0, [KC, 128], src)])

    def plan_tile(i):
        li = i // 2
        if i % 2 == 0:
            for dc in range(KC):
                src = w_o[li, :, dc * 128:(dc + 1) * 128].rearrange("(k p) c -> p k c", p=128)
                wr.add(("wo", li, dc), [(0, [8, 128], src)])
        else:
            for dc in range(KC):
                sv = w_glu[li, :, dc * 128:(dc + 1) * 128].rearrange("(k p) c -> p k c", p=128)
                sg = w_glu[li, :, D + dc * 128:D + (dc + 1) * 128].rearrange("(k p) c -> p k c", p=128)
                wr.add(("glu", li, dc), [(0, [KC, 128], sv), (KC * 128, [KC, 128], sg)])
        for j in range(NFC):
            sg = w_up[i, :, j * 128:(j + 1) * 128].rearrange("(k p) c -> p k c", p=128)
            sv = w_up[i, :, DFF + j * 128:DFF + (j + 1) * 128].rearrange("(k p) c -> p k c", p=128)
            wr.add(("up", i, j), [(0, [KC, 128], sg), (KC * 128, [KC, 128], sv)])
        for dc in range(KC):
            src = w_down[i, :, dc * 128:(dc + 1) * 128].rearrange("(j p) c -> p j c", p=128)
            wr.add(("down", i, dc), [(0, [NFC, 128], src)])
        if i + 1 < nlayers:
            if (i + 1) % 2 == 0:
                plan_qkv((i + 1) // 2)
            else:
                plan_win((i + 1) // 2)

    for t in range(NT):
        plan_qkv(0)
    for i in range(nlayers):
        for t in range(NT):
            plan_tile(i)

    xT = kb.sb("xT", [128, KC, TWS], F32)
    mT = kb.sb("mT", [128, KC, TWS], F32)
    hT = kb.sb("hT", [128, KC, TWS], BF16)
    aT = kb.sb("aT", [128, NFC, TWS], BF16)
    sq = [kb.sb("sq%d" % i, [128, TW], BF16) for i in range(2)]
    rstd = kb.sb("rstd", [128, TWS], F32)
    rtmp = kb.sb("rtmp", [128, TWS], F32)
    cgt = [kb.sb("cg%d" % i, [128, TW], F32) for i in range(2)]
    cvt = [kb.sb("cv%d" % i, [128, TW], F32) for i in range(2)]
    sgt = [kb.sb("sg%d" % i, [128, TW], F32) for i in range(2)]
    cw = kb.sb("cw", [128, 4, NUC], F32)
    uprev = kb.sb("uprev", [128, NUC, 2], F32)
    cst = kb.sb("cst", [128, NUC, 2, NS], F32)
    csout = kb.sb("csout", [128, NUC, 2, NS], F32)
    vtmp = kb.sb("vtmp", [NUC, 128], F32)
    vtmp2 = kb.sb("vtmp2", [NUC, 128], F32)
    rr = {"sq": 0, "cg": 0, "stg": 0}

    mT_flat = mT[:].rearrange("p a b -> p (a b)")
    aT_flat = aT[:].rearrange("p a b -> p (a b)")

    def segs_of(t):
        s = [(0, TW, 1)]
        if t == NT - 1:
            s.append((TW, NS, NS))
        return s

    def fm_load(dst_ap, dst_key, dram_row):
        dma("sp", vtmp[:], dram_row.rearrange("(c p) -> c p", p=128), writes=["vtmp"])
        pb, pk = psum()
        op("pe", lambda e: e.transpose(pb[:, 0:NUC], vtmp[:, :], ident_f[0:NUC, 0:NUC]),
           reads=["vtmp", "ident_f"], writes=[pk])
        op("act", lambda e: e.copy(dst_ap, pb[:, 0:NUC]), reads=[pk], writes=[dst_key])

    def fm_store(src_ap, src_key, dram_row):
        op("act", lambda e: e.copy(rtmp[:, 0:NUC], src_ap), reads=[src_key], writes=["rtmp"])
        pb, pk = psum()
        op("pe", lambda e: e.transpose(pb[0:NUC, 0:128], rtmp[:, 0:NUC], ident_f[:, :]),
           reads=["rtmp", "ident_f"], writes=[pk])
        op("act", lambda e: e.copy(vtmp2[:, :], pb[0:NUC, 0:128]), reads=[pk], writes=["vtmp2"])
        dma("sp", dram_row.rearrange("(c p) -> c p", p=128), vtmp2[:], reads=["vtmp2"])

    def stats_begin():
        return psum(7, 8)

    def stats_add(pst, src_ap, src_key, c0, n, first, last):
        pb, pk = pst
        s = sq[rr["sq"] % 2]
        sk = "sq%d" % (rr["sq"] % 2)
        rr["sq"] += 1
        op("act", lambda e: e.activation(out=s[:, 0:n], in_=src_ap, func=AF.Square), reads=[src_key], writes=[sk])
        op("pe", lambda e: e.matmul(pb[:, 0:n], ones_b[:, :], s[:, 0:n], start=first, stop=last),
           reads=[sk, "ones_b"], writes=[pk], signal=True)

    def stats_end(pst, c0, n):
        pb, pk = pst
        op("act", lambda e: e.activation(out=rtmp[:, c0:c0 + n], in_=pb[:, 0:n], func=AF.Sqrt, bias=epsb[:, 0:1], scale=1.0 / D),
           reads=[pk, "epsb"], writes=["rtmp"])
        op("dve", lambda e: e.reciprocal(rstd[:, c0:c0 + n], rtmp[:, c0:c0 + n]), reads=["rtmp"], writes=["rstd"])

    def prenorm(i, j, segs):
        for (c0, n, S) in segs:
            pst = stats_begin()
            for kc in range(KC):
                stats_add(pst, xT[:, kc, c0:c0 + n], "xT", c0, n, kc == 0, kc == KC - 1)
            stats_end(pst, c0, n)
            for kc in range(KC):
                op("dve", lambda e: e.scalar_tensor_tensor(out=hT[:, kc, c0:c0 + n], in0=xT[:, kc, c0:c0 + n],
                                                           scalar=gs(i, j, kc), in1=rstd[:, c0:c0 + n],
                                                           op0=ALU.mult, op1=ALU.mult),
                   reads=["xT", "rstd", "gsc"], writes=["hT"])

    def postnorm_residual(i, j, segs):
        for (c0, n, S) in segs:
            for kc in range(KC):
                op("dve", lambda e: e.scalar_tensor_tensor(out=mT[:, kc, c0:c0 + n], in0=mT[:, kc, c0:c0 + n],
                                                           scalar=gs(i, j, kc), in1=rstd[:, c0:c0 + n],
                                                           op0=ALU.mult, op1=ALU.mult),
                   reads=["mT", "rstd", "gsc"], writes=["mT"])
                op("dve", lambda e: e.tensor_tensor(out=xT[:, kc, c0:c0 + n], in0=xT[:, kc, c0:c0 + n],
                                                    in1=mT[:, kc, c0:c0 + n], op=ALU.add),
                   reads=["mT", "xT"], writes=["xT"])

    def xT_store(t, segs):
        g0 = t * TW
        dma("sp", xT_s[:, :, g0:g0 + TW].rearrange("k p t -> p k t"), xT[:, :, 0:TW], reads=["xT"], writes=["xT_s%d" % t])
        if len(segs) > 1:
            dma("sp", xT_s[:, :, T:TC].rearrange("k p t -> p k t"), xT[:, :, TW:TWS], reads=["xT"], writes=["xT_ss"])

    def xT_load(t, segs):
        g0 = t * TW
        dma("sp", xT[:, :, 0:TW], xT_s[:, :, g0:g0 + TW].rearrange("k p t -> p k t"), reads=["xT_s%d" % t], writes=["xT"])
        if len(segs) > 1:
            dma("sp", xT[:, :, TW:TWS], xT_s[:, :, T:TC].rearrange("k p t -> p k t"), reads=["xT_ss"], writes=["xT"])

    stg_f = mT_flat
    stg_b = aT_flat
    rt = kb.sb("ropetmp", [128, 4, 2, 16], F32)

    def qkv_phase(li, t, segs):
        blocks = [(b, 128) for b in range(4)]
        if len(segs) > 1:
            blocks.append((4, NS))
        kb.barrier()
        for ct in range(36):
            wv, wk = wr.get(("qkv", li, ct))
            w3 = wv[:, 0:KC * 256].rearrange("p (k c) -> p k c", k=KC)
            s = ct // 12
            g = (ct % 12) // 4
            hp = ct % 4
            for (bl, m) in blocks:
                gb = t * 4 + bl if bl < 4 else 16
                pb, pk = psum()
                for kc in range(KC):
                    lhs = hT[:, kc, bl * 128:bl * 128 + m] if bl < 4 else hT[:, kc, TW:TWS]
                    op("pe", lambda e: e.matmul(pb[0:m, 0:256], lhs, w3[:, kc, :], start=(kc == 0), stop=(kc == KC - 1)),
                       reads=["hT", wk], writes=[pk], signal=(kc == KC - 1))
                slot = rr["stg"] % 4
                rr["stg"] += 1
                sf = stg_f[:, slot * 256:(slot + 1) * 256]
                sfk = ("sf", slot)
                op("act", lambda e: e.copy(sf[0:m, :], pb[0:m, 0:256]), reads=[pk], writes=[sfk])
                sf3 = sf.rearrange("p (h d) -> p h d", h=2)
                if s < 2:
                    cosb = rope[0:m, gb, 0:16].unsqueeze(1).broadcast_to([m, 2, 16])
                    sinb = rope[0:m, gb, 16:32].unsqueeze(1).broadcast_to([m, 2, 16])
                    x1 = sf3[0:m, :, 0:16]
                    x2 = sf3[0:m, :, 16:32]
                    t1, t2, t3, t4 = (rt[0:m, k, :, :] for k in range(4))
                    op("dve", lambda e: e.tensor_tensor(out=t1, in0=x1, in1=cosb, op=ALU.mult), reads=[sfk, "rope"], writes=["rt"])
                    op("dve", lambda e: e.tensor_tensor(out=t2, in0=x2, in1=sinb, op=ALU.mult), reads=[sfk, "rope"], writes=["rt"])
                    op("dve", lambda e: e.tensor_tensor(out=t3, in0=x2, in1=cosb, op=ALU.mult), reads=[sfk, "rope"], writes=["rt"])
                    op("dve", lambda e: e.tensor_tensor(out=t4, in0=x1, in1=sinb, op=ALU.mult), reads=[sfk, "rope"], writes=["rt"])
                    op("dve", lambda e: e.tensor_tensor(out=x1, in0=t1, in1=t2, op=ALU.subtract), reads=["rt"], writes=[sfk])
                    op("dve", lambda e: e.tensor_tensor(out=x2, in0=t3, in1=t4, op=ALU.add), reads=["rt"], writes=[sfk])
                if s >= 1:
                    half = (s - 1) * 1024 + hp * 256
                    if bl < 4:
                        keep = min(WIN[g], T)
                        row0 = gb * 128 - (T - keep)
                        if row0 >= 0:
                            dma("sp", kvp[g][li, row0:row0 + 128, half:half + 256], sf[:, :], reads=[sfk])
                    else:
                        dma("sp", kvs[g][li, :, half:half + 256], sf[0:m, :], reads=[sfk])
                sb_ = stg_b[:, slot * 256:(slot + 1) * 256]
                sbk = ("sb", slot)
                tbk = ("tb", slot)
                if not (bl == 4 and s == 2):
                    op("pool", lambda e: e.tensor_copy(out=sb_[0:m, :], in_=sf[0:m, :]), reads=[sfk], writes=[sbk])
                if s == 2 and bl < 4:
                    dma("sp", v_s[g, gb * 128:(gb + 1) * 128, hp * 256:(hp + 1) * 256], sb_[:, :], reads=[sbk], writes=["v_s"])
                    continue
                if s == 2:
                    pt, ptk = psum()
                    for hh in range(2):
                        op("pe", lambda e: e.transpose(pt[:, hh * NS:(hh + 1) * NS], sf[0:m, hh * 128:(hh + 1) * 128], ident_f[0:m, 0:m]),
                           reads=[sfk, "ident_f"], writes=[ptk], signal=(hh == 1))
                    op("dve", lambda e: e.tensor_copy(out=svT[:, g * 8 + 2 * hp:g * 8 + 2 * hp + 2, :],
                                                      in_=pt[:, 0:2 * NS].rearrange("p (h s) -> p h s", h=2)),
                       reads=[ptk], writes=["svT"])
                    continue
                pt, ptk = psum()
                ptb = pt[:, 0:256].bitcast(BF16)
                for hh in range(2):
                    op("pe", lambda e: e.transpose(ptb[:, hh * 128:hh * 128 + m], sb_[0:m, hh * 128:(hh + 1) * 128], ident_b[0:m, 0:m]),
                       reads=[sbk, "ident_b"], writes=[ptk], signal=(hh == 1))
                if bl < 4:
                    tb = stg_b[:, 1024 + slot * 256:1024 + (slot + 1) * 256]
                    op("dve", lambda e: e.tensor_copy(out=tb, in_=ptb[:, 0:256]), reads=[ptk], writes=[tbk])
                    dst = (qT_s if s == 0 else kT_s)[g, 2 * hp:2 * hp + 2, :, gb * 128:(gb + 1) * 128].rearrange("h p t -> p h t")
                    dma("sp", dst, tb.rearrange("p (h t) -> p h t", h=2), reads=[tbk], writes=["qT_s" if s == 0 else "kT_s"])
                else:
                    dstt = sqT if s == 0 else skT
                    op("dve", lambda e: e.tensor_copy(out=dstt[:, g * 8 + 2 * hp:g * 8 + 2 * hp + 2, :],
                                                      in_=ptb[:, 0:256].rearrange("p (h t) -> p h t", h=2)[:, :, 0:NS]),
                       reads=[ptk], writes=["sqT" if s == 0 else "skT"])

    _qkv_inner = qkv_phase

    def qkv_phase(li, t, segs):
        _qkv_inner(li, t, segs)
        kb.barrier()

    def attn_phase(li):
        kb.barrier()
        es = contextlib.ExitStack()
        aTf = aT[:].rearrange("p a b -> p (a b)")
        hTf = hT[:].rearrange("p a b -> p (a b)")
        xTb = xT[:].rearrange("p a b -> p (a b)").bitcast(BF16)
        mTf = mT[:].rearrange("p a b -> p (a b)")
        QN = 3 * T
        qTt = [aTf[:, i * QN:(i + 1) * QN].rearrange("p (g t) -> p g t", g=3) for i in range(2)]
        kTt = [aTf[:, 2 * QN:3 * QN].rearrange("p (g t) -> p g t", g=3), hTf[:, 0:QN].rearrange("p (g t) -> p g t", g=3)]
        vt = [xTb[:, i * QN:(i + 1) * QN].rearrange("p (g b d) -> p g b d", g=3, b=16) for i in range(2)]
        pT = [hTf[:, QN + i * 512:QN + (i + 1) * 512] for i in range(3)]
        mh = [xTb[:, 2 * QN + i * TC:2 * QN + (i + 1) * TC] for i in range(2)]
        mbq = aTf[:, 3 * QN:3 * QN + 1792].bitcast(F32).rearrange("p (m k) -> p m k", m=7)
        smk = mTf[:, 0:2048]
        junk = mTf[:, 2048:3072].bitcast(BF16)
        D3 = mTf[:, 3072:3456]
        cbc = mTf[:, 3456:3840]
        acc = mTf[:, 3840:4096]
        ones_f = kb.sb("ones_f", [128, 128], F32, es)
        l0 = kb.sb("l0", [128, 16, 3], F32, es)
        colm = kb.sb("colm", [128, 8, 3], F32, es)
        dma("sp", mbq, c_mbq[:, :, :], writes=["mbq"])
        op("dve", lambda e: e.memset(ones_f[:], 1.0), writes=["ones_f"])

        def load_head(h):
            b = h % 2
            for g in range(3):
                dma("sp", qTt[b][:, g, :], qT_s[g, h, :, :], reads=["qT_s"], writes=[("qTt", b, g)])
                dma("sp", kTt[b][:, g, :], kT_s[g, h, :, :], reads=["kT_s"], writes=[("kTt", b, g)])
                dma("sp", vt[b][:, g, :, :], v_s[g, :, h * 128:(h + 1) * 128].rearrange("(b p) d -> p b d", p=128),
                    reads=["v_s"], writes=[("vt", b, g)])

        load_head(0)
        pidx = 0
        it = 0
        for h in range(8):
            if h + 1 < 8:
                load_head(h + 1)
            b = h % 2
            for qb in range(16):
                groups = []
                for g in range(3):
                    nb = WIN[g] // 128
                    lst = []
                    for kbk in range(max(0, qb - nb), qb + 1):
                        db = qb - kbk
                        if g == 0:
                            mi = 0 if db == 0 else 1
                        elif g == 1:
                            mi = 2 if db == 0 else (4 if db == 4 else 3)
                        else:
                            mi = 5 if db == 0 else 6
                        lst.append((kbk, mi))
                    groups.append(lst)
                po, pok = psum(4, 5) if it % 2 == 0 else psum(6, 7)
                pc, pck = psum(5, 6) if it % 2 == 0 else psum(7, 8)
                it += 1
                qsl = qTt[b][:, :, qb * 128:(qb + 1) * 128]
                for g in range(3):
                    lst = groups[g]
                    nk = len(lst) * 128
                    for c4 in range(0, len(lst), 4):
                        ch = lst[c4:c4 + 4]
                        ps_, psk = psum(2, 4)
                        k0 = ch[0][0]
                        op("pe", lambda e: e.matmul(ps_[:, 0:len(ch) * 128], qsl[:, g, :], kTt[b][:, g, k0 * 128:(k0 + len(ch)) * 128],
                                                    start=True, stop=True),
                           reads=[("qTt", b, g), ("kTt", b, g)], writes=[psk])
                        for ci, (kbk, mi) in enumerate(ch):
                            op("dve", lambda e: e.tensor_tensor(out=smk[:, (c4 + ci) * 128:(c4 + ci + 1) * 128], in0=ps_[:, ci * 128:(ci + 1) * 128],
                                                                in1=mbq[:, mi, :], op=ALU.add),
                               reads=[psk, "mbq"], writes=["smk"])
                    op("dve", lambda e: e.tensor_reduce(out=colm[:, 0, g:g + 1], in_=smk[:, 0:nk], axis=AX.X, op=ALU.max),
                       reads=["smk"], writes=["colm"])
                    op("dve", lambda e: e.tensor_scalar(out=colm[:, 1, g:g + 1], in0=colm[:, 0, g:g + 1], scalar1=-SCALE, scalar2=None, op0=ALU.mult),
                       reads=["colm"], writes=["colm"])
                    op("act", lambda e: e.activation(out=junk[:, 0:nk], in_=smk[:, 0:nk], func=AF.Exp, bias=colm[:, 1, g:g + 1], scale=SCALE,
                                                     accum_out=colm[:, 2, g:g + 1]),
                       reads=["smk", "colm"], writes=["junk", "colm"])
                    done = 0
                    for c4 in range(0, len(lst), 4):
                        ch = lst[c4:c4 + 4]
                        pb, pk = psum(0, 2)
                        for ci, (kbk, mi) in enumerate(ch):
                            op("pe", lambda e: e.matmul(pb[:, ci * 128:(ci + 1) * 128], kTt[b][:, g, kbk * 128:(kbk + 1) * 128], qsl[:, g, :],
                                                        start=True, stop=True),
                               reads=[("kTt", b, g), ("qTt", b, g)], writes=[pk], signal=(ci == len(ch) - 1))
                        p_ = pT[pidx % 3]
                        pk_ = "pT%d" % (pidx % 3)
                        pidx += 1
                        nc_ = len(ch) * 128
                        op("act", lambda e: e.activation(out=p_[:, 0:nc_], in_=pb[:, 0:nc_], func=AF.Exp, scale=SCALE), reads=[pk], writes=[pk_])
                        for ci, (kbk, mi) in enumerate(ch):
                            op("pool", lambda e: e.tensor_tensor(out=p_[:, ci * 128:(ci + 1) * 128], in0=p_[:, ci * 128:(ci + 1) * 128],
                                                                 in1=masks[:, mi, :], op=ALU.mult),
                               reads=[pk_, "masks"], writes=[pk_])
                        for ci, (kbk, mi) in enumerate(ch):
                            op("pe", lambda e: e.matmul(po[:, g * 128:(g + 1) * 128], vt[b][:, g, kbk, :], p_[:, ci * 128:(ci + 1) * 128],
                                                        start=(done == 0), stop=(done == len(lst) - 1)),
                               reads=[("vt", b, g), pk_], writes=[pok], signal=True)
                            done += 1
                op("act", lambda e: e.activation(out=colm[:, 3, :], in_=colm[:, 1, :], func=AF.Exp, scale=-1.0), reads=["colm"], writes=["colm"])
                op("dve", lambda e: e.tensor_tensor(out=colm[:, 4, :], in0=colm[:, 2, :], in1=colm[:, 3, :], op=ALU.mult), reads=["colm"], writes=["colm"])
                op("dve", lambda e: e.tensor_reduce(out=colm[:, 7, 0:1], in_=colm[:, 4, :], axis=AX.X, op=ALU.add), reads=["colm"], writes=["colm"])
                if h == 0:
                    op("dve", lambda e: e.tensor_copy(out=l0[:, qb, :], in_=colm[:, 2, :]), reads=["colm"], writes=["l0"])
                op("dve", lambda e: e.tensor_scalar(out=colm[:, 5, :], in0=l0[:, qb, :], scalar1=colm[:, 7, 0:1], scalar2=None, op0=ALU.mult),
                   reads=["colm", "l0"], writes=["colm"])
                op("dve", lambda e: e.reciprocal(colm[:, 5, :], colm[:, 5, :]), reads=["colm"], writes=["colm"])
                op("dve", lambda e: e.tensor_tensor(out=colm[:, 6, :], in0=colm[:, 2, :], in1=colm[:, 5, :], op=ALU.mult), reads=["colm"], writes=["colm"])
                for g in range(3):
                    op("dve", lambda e: e.tensor_scalar(out=D3[:, g * 128:(g + 1) * 128], in0=ident_f[:, :], scalar1=colm[:, 6, g:g + 1], scalar2=None, op0=ALU.mult),
                       reads=["colm", "ident_f"], writes=["D3"])
                op("pe", lambda e: e.matmul(pc[:, 0:384], ones_f[:, :], D3[:, :], start=True, stop=True), reads=["ones_f", "D3"], writes=[pck])
                op("act", lambda e: e.copy(cbc[:, :], pc[:, 0:384]), reads=[pck], writes=["cbc"])
                op("dve", lambda e: e.tensor_tensor(out=acc[:, 0:128], in0=po[:, 0:128], in1=cbc[:, 0:128], op=ALU.mult), reads=[pok, "cbc"], writes=["acc"])
                op("dve", lambda e: e.tensor_tensor(out=acc[:, 128:256], in0=po[:, 128:256], in1=cbc[:, 128:256], op=ALU.mult), reads=[pok, "cbc"], writes=["acc"])
                op("dve", lambda e: e.tensor_tensor(out=acc[:, 0:128], in0=acc[:, 0:128], in1=acc[:, 128:256], op=ALU.add), reads=["acc"], writes=["acc"])
                op("dve", lambda e: e.tensor_tensor(out=acc[:, 128:256], in0=po[:, 256:384], in1=cbc[:, 256:384], op=ALU.mult), reads=[pok, "cbc"], writes=["acc"])
                op("dve", lambda e: e.tensor_tensor(out=mh[b][:, qb * 128:(qb + 1) * 128], in0=acc[:, 0:128], in1=acc[:, 128:256], op=ALU.add),
                   reads=["acc"], writes=["mh%d" % b])
            dma("sp", mT_s[h, :, 0:T], mh[b][:, 0:T], reads=["mh%d" % b], writes=["mT_s"])

        kb.barrier()
        cache = [mTf[:, i * 2048:(i + 1) * 2048] for i in range(2)]
        cb16 = [mTf[:, 4096 + i * 512:4096 + (i + 1) * 512].bitcast(BF16) for i in range(2)]
        vkeep = [mTf[:, 5120 + g * 512:5120 + (g + 1) * 512].bitcast(BF16) for g in range(3)]
        ckT = kb.sb("ckT", [128, 8, 128], BF16, es)
        qk = kb.sb("qk", [128, 24 * NS], F32, es)
        qkr = kb.sb("qkr", [1, 24 * NS], F32, es)
        srow = aTf[0:1, 0:2064].bitcast(F32).rearrange("p (h k) -> p h k", h=8)
        prow = aTf[0:1, 2064:2064 + 6192].bitcast(F32).rearrange("p (g h k) -> p g h k", g=3, h=8)
        rw = kb.sb("rw", [1, 12, 24], F32, es)
        one1 = kb.sb("one1", [1, 1], F32, es)
        pSk = kb.sb("pSk", [128, 3, 8], BF16, es)
        pnb = kb.sb("pnb", [128, 3, 8], F32, es)
        og = kb.sb("og", [128, 3, 8], F32, es)
        cbs = kb.sb("cbs", [128, 3, 8], F32, es)
        so = kb.sb("so", [128, 8], F32, es)
        sob = kb.sb("sob", [128, 8, NS], BF16, es)
        op("dve", lambda e: e.memset(one1[:], 1.0), writes=["one1"])
        op("dve", lambda e: e.tensor_tensor(out=qk[:, :], in0=sqT[:].rearrange("p a s -> p (a s)"), in1=skT[:].rearrange("p a s -> p (a s)"), op=ALU.mult),
           reads=["sqT", "skT"], writes=["qk"])
        pb, pk = psum(0, 2)
        op("pe", lambda e: e.matmul(pb[0:1, 0:24 * NS], ones_f[:, 0:1], qk[:, :], start=True, stop=True), reads=["ones_f", "qk"], writes=[pk])
        op("act", lambda e: e.copy(qkr[:, :], pb[0:1, 0:24 * NS]), reads=[pk], writes=["qkr"])
        qkr3 = qkr[:].rearrange("p (a s) -> p a s", s=NS)
        rwm, rwmm, rwl, rwe, rwt, rwc = (rw[:, k, :].rearrange("p (g h) -> p g h", g=3) for k in range(6))
        for s_ in range(NS):
            pov, povk = psum(4, 5)
            for g in range(3):
                cbuf = cache[g % 2]
                ck = "cache%d" % (g % 2)
                src = ckv[g][li, s_, :, :].rearrange("(j d) c -> j d c", d=DIL[g])[:, 0, :]
                dma("sp", cbuf[:, :], src, writes=[ck])
                c16 = cb16[g % 2]
                c16k = "cb16_%d" % (g % 2)
                op("pool", lambda e: e.tensor_copy(out=c16[:, 0:1024], in_=cbuf[:, 0:1024]), reads=[ck], writes=[c16k])
                op("act", lambda e: e.copy(vkeep[g][:, :], cbuf[:, 1024:2048]), reads=[ck], writes=["vk%d" % g])
                for h in range(8):
                    pt, ptk = psum(0, 2)
                    ptb = pt[:, 0:64].bitcast(BF16)
                    op("pe", lambda e: e.transpose(ptb[:, 0:128], c16[:, h * 128:(h + 1) * 128], ident_b[:, :]),
                       reads=[c16k, "ident_b"], writes=[ptk])
                    op("dve", lambda e: e.tensor_copy(out=ckT[:, h, :], in_=ptb[:, 0:128]), reads=[ptk], writes=["ckT"])
                for hq in range(2):
                    pr, prk = psum(2, 4)
                    for hh in range(4):
                        h = hq * 4 + hh
                        op("pe", lambda e: e.matmul(pr[0:1, hh * 128:(hh + 1) * 128], sqT[:, g * 8 + h, s_:s_ + 1], ckT[:, h, :], start=True, stop=True),
                           reads=["sqT", "ckT"], writes=[prk], signal=(hh == 3))
                    op("act", lambda e: e.copy(srow[0:1, hq * 4:(hq + 1) * 4, 0:128], pr[0:1, :].rearrange("p (h k) -> p h k", h=4)),
                       reads=[prk], writes=["srow"])
                op("dve", lambda e: e.tensor_copy(out=srow[0:1, :, 128:129], in_=qkr3[0:1, g * 8:(g + 1) * 8, s_:s_ + 1]), reads=["qkr"], writes=["srow"])
                op("dve", lambda e: e.tensor_reduce(out=rwm[0:1, g, :], in_=srow[0:1, :, :], axis=AX.X, op=ALU.max), reads=["srow"], writes=["rw"])
                op("dve", lambda e: e.tensor_tensor(out=srow[0:1, :, :], in0=srow[0:1, :, :],
                                                    in1=rwm[0:1, g, :].unsqueeze(2).broadcast_to([1, 8, 129]), op=ALU.subtract),
                   reads=["srow", "rw"], writes=["srow"])
                op("act", lambda e: e.activation(out=prow[0:1, g, :, :], in_=srow[0:1, :, :], func=AF.Exp, scale=SCALE), reads=["srow"], writes=["prow"])
                op("dve", lambda e: e.tensor_reduce(out=rwl[0:1, g, :], in_=prow[0:1, g, :, :], axis=AX.X, op=ALU.add), reads=["prow"], writes=["rw"])
                pp, ppk = psum(2, 4)
                for h in range(8):
                    op("pe", lambda e: e.matmul(pp[:, h:h + 1], prow[0:1, g, h, 0:128], one1[0:1, 0:1], start=True, stop=True),
                       reads=["prow", "one1"], writes=[ppk], signal=(h == 7))
                op("dve", lambda e: e.tensor_copy(out=pSk[:, g, :], in_=pp[:, 0:8]), reads=[ppk], writes=["pSk"])
                pn_, pnk = psum(2, 4)
                op("pe", lambda e: e.matmul(pn_[:, 0:8], ones_f[0:1, :], prow[0:1, g, :, 128], start=True, stop=True), reads=["ones_f", "prow"], writes=[pnk])
                op("dve", lambda e: e.tensor_copy(out=pnb[:, g, :], in_=pn_[:, 0:8]), reads=[pnk], writes=["pnb"])
                for h in range(8):
                    op("pe", lambda e: e.matmul(pov[:, g * 8 + h:g * 8 + h + 1], vkeep[g][:, h * 128:(h + 1) * 128], pSk[:, g, h:h + 1], start=True, stop=True),
                       reads=["vk%d" % g, "pSk"], writes=[povk], signal=(h == 7))
            op("dve", lambda e: e.tensor_tensor(out=og[:, :, :], in0=svT[:, :, s_].rearrange("p (g h) -> p g h", g=3), in1=pnb[:, :, :], op=ALU.mult),
               reads=["svT", "pnb"], writes=["og"])
            op("dve", lambda e: e.tensor_tensor(out=og[:, :, :], in0=og[:, :, :], in1=pov[:, 0:24].rearrange("p (g h) -> p g h", g=3), op=ALU.add),
               reads=["og", povk], writes=["og"])
            op("dve", lambda e: e.tensor_scalar(out=rwmm[0:1, :, :], in0=rwm[0:1, :, :], scalar1=SCALE, scalar2=None, op0=ALU.mult), reads=["rw"], writes=["rw"])
            op("act", lambda e: e.activation(out=rwe[0:1, :, :], in_=rwmm[0:1, :, :], func=AF.Exp), reads=["rw"], writes=["rw"])
            op("dve", lambda e: e.tensor_tensor(out=rwe[0:1, :, :], in0=rwe[0:1, :, :], in1=rwl[0:1, :, :], op=ALU.mult), reads=["rw"], writes=["rw"])
            op("dve", lambda e: e.tensor_tensor(out=rwt[0:1, 0, :], in0=rwe[0:1, 0, :], in1=rwe[0:1, 1, :], op=ALU.add), reads=["rw"], writes=["rw"])
            op("dve", lambda e: e.tensor_tensor(out=rwt[0:1, 0, :], in0=rwt[0:1, 0, :], in1=rwe[0:1, 2, :], op=ALU.add), reads=["rw"], writes=["rw"])
            op("dve", lambda e: e.tensor_tensor(out=rwc[0:1, :, :], in0=rwt[0:1, 0, :].unsqueeze(1).broadcast_to([1, 3, 8]),
                                                in1=rwl[0:1, :, 0:1].broadcast_to([1, 3, 8]), op=ALU.mult), reads=["rw"], writes=["rw"])
            op("dve", lambda e: e.reciprocal(rwc[0:1, :, :], rwc[0:1, :, :]), reads=["rw"], writes=["rw"])
            op("dve", lambda e: e.tensor_tensor(out=rwc[0:1, :, :], in0=rwc[0:1, :, :], in1=rwe[0:1, :, :], op=ALU.mult), reads=["rw"], writes=["rw"])
            pcb, pcbk = psum(2, 4)
            op("pe", lambda e: e.matmul(pcb[:, 0:24], ones_f[0:1, :], rw[0:1, 5, :], start=True, stop=True), reads=["ones_f", "rw"], writes=[pcbk])
            op("dve", lambda e: e.tensor_tensor(out=og[:, :, :], in0=og[:, :, :], in1=pcb[:, 0:24].rearrange("p (g h) -> p g h", g=3), op=ALU.mult),
               reads=["og", pcbk], writes=["og"])
            op("dve", lambda e: e.tensor_tensor(out=so[:, :], in0=og[:, 0, :], in1=og[:, 1, :], op=ALU.add), reads=["og"], writes=["so"])
            op("dve", lambda e: e.tensor_tensor(out=sob[:, :, s_], in0=so[:, :], in1=og[:, 2, :], op=ALU.add), reads=["so", "og"], writes=["sob"])
        with nc.allow_non_contiguous_dma(reason="tiny sample columns"):
            dma("sp", mT_s[:, :, T:TC].rearrange("h p s -> p h s"), sob[:, :, :], reads=["sob"], writes=["mT_s"])
        kb.barrier()
        es.close()

    LCH = 32
    NCH = T // LCH

    def win_phase(li, t, segs):
        g0 = t * TW
        for dc in range(KC):
            wv, wk = wr.get(("win", li, dc))
            w3 = wv[:, 0:KC * 128].rearrange("p (k c) -> p k c", k=KC)
            for (c0, n, S) in segs:
                pb, pk = psum()
                for kc in range(KC):
                    op("pe", lambda e: e.matmul(pb[:, 0:n], w3[:, kc, :], hT[:, kc, c0:c0 + n], start=(kc == 0), stop=(kc == KC - 1)),
                       reads=["hT", wk], writes=[pk], signal=(kc == KC - 1))
                r = rr["cg"] % 2
                rr["cg"] += 1
                op("act", lambda e: e.copy(cgt[r][:, 0:n], pb[:, 0:n]), reads=[pk], writes=["cg%d" % r])
                gc = g0 if S == 1 else T
                if S == 1:
                    dma("sp", uT_s[dc, :, gc:gc + n], cgt[r][:, 0:n], reads=["cg%d" % r], writes=["uT_s"])
                else:
                    with nc.allow_non_contiguous_dma(reason="tiny sample columns"):
                        dma("sp", uT_s[dc, :, gc:gc + n], cgt[r][:, 0:n], reads=["cg%d" % r], writes=["uT_s"])

    def mixer_out_ssm(li, t, segs):
        g0 = t * TW
        minT = aT[:, 0:KC, :]
        dma("sp", minT[:, :, 0:TW], zT_s[:, :, g0:g0 + TW].rearrange("k p t -> p k t"), reads=["zT_s"], writes=["aT"])
        if len(segs) > 1:
            with nc.allow_non_contiguous_dma(reason="tiny sample columns"):
                dma("sp", minT[:, :, TW:TWS], zT_s[:, :, T:TC].rearrange("k p t -> p k t"), reads=["zT_s"], writes=["aT"])
        psts = [stats_begin() if si == 0 else psum(6, 7) for si in range(len(segs))]
        for dc in range(KC):
            wv, wk = wr.get(("glu", li, dc))
            wval = wv[:, 0:KC * 128].rearrange("p (k c) -> p k c", k=KC)
            wgat = wv[:, KC * 128:2 * KC * 128].rearrange("p (k c) -> p k c", k=KC)
            for si, (c0, n, S) in enumerate(segs):
                pv, pvk = psum()
                pg, pgk = psum()
                for kc in range(KC):
                    op("pe", lambda e: e.matmul(pv[:, 0:n], wval[:, kc, :], minT[:, kc, c0:c0 + n], start=(kc == 0), stop=(kc == KC - 1)),
                       reads=["aT", wk], writes=[pvk], signal=(kc == KC - 1))
                for kc in range(KC):
                    op("pe", lambda e: e.matmul(pg[:, 0:n], wgat[:, kc, :], minT[:, kc, c0:c0 + n], start=(kc == 0), stop=(kc == KC - 1)),
                       reads=["aT", wk], writes=[pgk], signal=(kc == KC - 1))
                r = rr["cg"] % 2
                rr["cg"] += 1
                op("act", lambda e: e.activation(out=sgt[r][:, 0:n], in_=pg[:, 0:n], func=AF.Sigmoid), reads=[pgk], writes=["sg%d" % r])
                op("dve", lambda e: e.tensor_tensor(out=mT[:, dc, c0:c0 + n], in0=pv[:, 0:n], in1=sgt[r][:, 0:n], op=ALU.mult),
                   reads=[pvk, "sg%d" % r], writes=["mT"])
                stats_add(psts[si], mT[:, dc, c0:c0 + n], "mT", c0, n, dc == 0, dc == KC - 1)
        for si, (c0, n, S) in enumerate(segs):
            stats_end(psts[si], c0, n)

    def ssm_phase(li):
        kb.barrier()
        es = contextlib.ExitStack()
        aTf = aT[:].rearrange("p a b -> p (a b)")
        hTf = hT[:].rearrange("p a b -> p (a b)")
        xTf = xT[:].rearrange("p a b -> p (a b)")
        mTf = mT[:].rearrange("p a b -> p (a b)")
        Wb = aTf[:, 0:16384].rearrange("p (k g m) -> p k g m", k=KC, g=8)
        npi_t = aTf[:, 16384:16384 + 4096].bitcast(F32).rearrange("p (a i) -> p a i", i=LCH)
        Wc = xTf[:, 0:8192].bitcast(BF16).rearrange("p (k j r m) -> p k j r m", k=KC, j=4, r=2)
        pr_parts = [cgt[0], cgt[1], cvt[0], cvt[1]]
        pi_parts = [sgt[0], sgt[1], rstd, rtmp]

        def tab(parts, pair):
            return parts[pair // 16][:, (pair % 16) * LCH:(pair % 16 + 1) * LCH]
        small = hTf[:, 0:3840].bitcast(F32).rearrange("p (k w) -> p k w", w=64)
        smallB = hTf[0:64, 3840:3840 + 4096].bitcast(F32).rearrange("p (k w) -> p k w", w=128)
        dsk = kb.sb("dsk", [128, KC], F32, es)
        gmask = kb.sb("gmask", [128, 8], F32, es)
        nat = kb.sb("nat", [128, 128], F32, es)
        dma("sp", gmask[:], c_gmask[:, :], writes=["gmask"])
        dma("sp", nat[0:KC, :], d_skip[li, :].rearrange("(k p) -> k p", p=128), writes=["nat"])
        pb, pk = psum(5, 8)
        op("pe", lambda e: e.transpose(pb[:, 0:KC], nat[0:KC, :], ident_f[0:KC, 0:KC]), reads=["nat", "ident_f"], writes=[pk])
        op("act", lambda e: e.copy(dsk[:, :], pb[:, 0:KC]), reads=[pk], writes=["dsk"])

        def abar_chain(P, W, sm, lam_r_ap, lam_i_ap, ldt_ap, key):
            S_ = lambda k: sm[0:P, k, 0:W]
            o = lambda fn, **kw: op("dve", fn, reads=[key], writes=[key])
            o(lambda e: e.tensor_scalar(out=S_(0), in0=lam_r_ap, scalar1=-1e-4, scalar2=None, op0=ALU.min))
            o(lambda e: e.tensor_copy(out=S_(1), in_=lam_i_ap))
            op("act", lambda e: e.activation(out=S_(2), in_=ldt_ap, func=AF.Exp), reads=[key], writes=[key])
            o(lambda e: e.tensor_tensor(out=S_(3), in0=S_(1), in1=S_(2), op=ALU.mult))
            o(lambda e: e.tensor_scalar(out=S_(3), in0=S_(3), scalar1=1.0 / 16.0, scalar2=None, op0=ALU.mult))
            o(lambda e: e.tensor_tensor(out=S_(4), in0=S_(3), in1=S_(3), op=ALU.mult))
            o(lambda e: e.tensor_scalar(out=S_(5), in0=S_(4), scalar1=1.0 / 362880.0, scalar2=None, op0=ALU.mult))
            for cf in (-1.0 / 5040.0, 1.0 / 120.0, -1.0 / 6.0):
                o(lambda e: e.scalar_tensor_tensor(out=S_(5), in0=S_(5), scalar=cf, in1=S_(4), op0=ALU.add, op1=ALU.mult))
            o(lambda e: e.scalar_tensor_tensor(out=S_(5), in0=S_(5), scalar=1.0, in1=S_(3), op0=ALU.add, op1=ALU.mult))
            o(lambda e: e.tensor_scalar(out=S_(6), in0=S_(4), scalar1=-1.0 / 3628800.0, scalar2=None, op0=ALU.mult))
            for cf in (1.0 / 40320.0, -1.0 / 720.0, 1.0 / 24.0, -0.5):
                o(lambda e: e.scalar_tensor_tensor(out=S_(6), in0=S_(6), scalar=cf, in1=S_(4), op0=ALU.add, op1=ALU.mult))
            o(lambda e: e.tensor_scalar(out=S_(6), in0=S_(6), scalar1=1.0, scalar2=None, op0=ALU.add))
            for _ in range(4):
                o(lambda e: e.tensor_tensor(out=S_(7), in0=S_(5), in1=S_(6), op=ALU.mult))
                o(lambda e: e.tensor_tensor(out=S_(11), in0=S_(5), in1=S_(5), op=ALU.mult))
                o(lambda e: e.tensor_scalar(out=S_(6), in0=S_(11), scalar1=-2.0, scalar2=1.0, op0=ALU.mult, op1=ALU.add))
                o(lambda e: e.tensor_scalar(out=S_(5), in0=S_(7), scalar1=2.0, scalar2=None, op0=ALU.mult))
            o(lambda e: e.tensor_tensor(out=S_(11), in0=S_(0), in1=S_(2), op=ALU.mult))
            op("act", lambda e: e.activation(out=S_(8), in_=S_(11), func=AF.Exp), reads=[key], writes=[key])
            o(lambda e: e.tensor_tensor(out=S_(9), in0=S_(8), in1=S_(6), op=ALU.mult))
            o(lambda e: e.tensor_tensor(out=S_(10), in0=S_(8), in1=S_(5), op=ALU.mult))
            return S_(9), S_(10), S_(0), S_(1)

        def load_A(dst, src2d):
            dma("sp", nat[0:64, :], src2d.rearrange("(g s) p -> g (s p)", s=2), writes=["nat"])
            pb, pk = psum(5, 8)
            op("pe", lambda e: e.transpose(pb[:, 0:64], nat[0:64, :], ident_f[0:64, 0:64]), reads=["nat", "ident_f"], writes=[pk])
            op("act", lambda e: e.copy(dst, pb[:, 0:64]), reads=[pk], writes=["small"])
        A_ = lambda k: small[:, k, :]
        load_A(A_(12), lam_re[li, :, :])
        load_A(A_(13), lam_im[li, :, :])
        dma("sp", nat[0:64, 0:2], log_dt[li, :].rearrange("(g s) -> g s", s=2), writes=["nat"])
        op("dve", lambda e: e.tensor_copy(out=nat[0:64, 64:128].rearrange("p (s q) -> p s q", s=2)[:, :, :] if False else smallB[0:64, 15, :].rearrange("p (s q) -> p s q", s=2),
                                          in_=nat[0:64, 0:2].unsqueeze(2).broadcast_to([64, 2, 64])), reads=["nat"], writes=["smallB"])
        pb, pk = psum(5, 8)
        op("pe", lambda e: e.transpose(pb[:, 0:64], smallB[0:64, 15, :], ident_f[0:64, 0:64]), reads=["smallB", "ident_f"], writes=[pk])
        op("act", lambda e: e.copy(A_(14), pb[:, 0:64]), reads=[pk], writes=["small"])
        arA, aiA, _, _ = abar_chain(128, 64, small, A_(12), A_(13), A_(14), "small")
        for pair0 in range(0, 64, 16):
            pass
        prv = lambda i: [p_[:, :].rearrange("p (a i) -> p a i", i=LCH)[:, :, i] for p_ in pr_parts]
        piv = lambda i: [p_[:, 0:512].rearrange("p (a i) -> p a i", i=LCH)[:, :, i] for p_ in pi_parts]
        tkeys = ["cg0", "cg1", "cv0", "cv1", "sg0", "sg1", "rstd", "rtmp", "npi"]
        for q in range(4):
            op("dve", lambda e: e.tensor_copy(out=prv(0)[q], in_=arA[:, q * 16:(q + 1) * 16]), reads=["small"], writes=tkeys)
            op("dve", lambda e: e.tensor_copy(out=piv(0)[q], in_=aiA[:, q * 16:(q + 1) * 16]), reads=["small"], writes=tkeys)
        for i in range(1, LCH):
            for q in range(4):
                a_r = arA[:, q * 16:(q + 1) * 16]
                a_i = aiA[:, q * 16:(q + 1) * 16]
                t1, t2 = small[:, 15, 0:16], small[:, 16, 0:16]
                op("dve", lambda e: e.tensor_tensor(out=t1, in0=prv(i - 1)[q], in1=a_r, op=ALU.mult), reads=tkeys + ["small"], writes=["small"])
                op("dve", lambda e: e.tensor_tensor(out=t2, in0=piv(i - 1)[q], in1=a_i, op=ALU.mult), reads=tkeys + ["small"], writes=["small"])
                op("dve", lambda e: e.tensor_tensor(out=prv(i)[q], in0=t1, in1=t2, op=ALU.subtract), reads=["small"], writes=tkeys)
                op("dve", lambda e: e.tensor_tensor(out=t1, in0=prv(i - 1)[q], in1=a_i, op=ALU.mult), reads=tkeys + ["small"], writes=["small"])
                op("dve", lambda e: e.tensor_tensor(out=t2, in0=piv(i - 1)[q], in1=a_r, op=ALU.mult), reads=tkeys + ["small"], writes=["small"])
                op("dve", lambda e: e.tensor_tensor(out=piv(i)[q], in0=t1, in1=t2, op=ALU.add), reads=["small"], writes=tkeys)
        for q in range(4):
            op("dve", lambda e: e.tensor_scalar(out=npi_t[:, q * 16:(q + 1) * 16, :], in0=pi_parts[q][:, 0:512].rearrange("p (a i) -> p a i", i=LCH),
                                                scalar1=-1.0, scalar2=None, op0=ALU.mult), reads=tkeys, writes=tkeys)

        B_ = lambda k: smallB[0:64, k, :]
        for (dst, src) in ((B_(12), lam_re), (B_(13), lam_im)):
            dma("sp", nat[:, 0:64], src[li, :, :], writes=["nat"])
            pb, pk = psum(5, 8)
            op("pe", lambda e: e.transpose(pb[0:64, 0:128], nat[:, 0:64], ident_f[:, :]), reads=["nat", "ident_f"], writes=[pk])
            op("act", lambda e: e.copy(dst, pb[0:64, 0:128]), reads=[pk], writes=["smallB"])
        dma("sp", B_(14), log_dt[li:li + 1, :].partition_broadcast(64).rearrange("p a g -> p (a g)") if False else log_dt[li:li + 1, :].broadcast_to([64, 128]), writes=["smallB"])
        arB, aiB, lrB, liB = abar_chain(64, 128, smallB, B_(12), B_(13), B_(14), "smallB")
        ob = lambda fn: op("dve", fn, reads=["smallB"], writes=["smallB"])
        ob(lambda e: e.tensor_scalar(out=B_(11), in0=arB, scalar1=-1.0, scalar2=None, op0=ALU.add))
        ob(lambda e: e.tensor_tensor(out=B_(7), in0=lrB, in1=lrB, op=ALU.mult))
        ob(lambda e: e.tensor_tensor(out=B_(2), in0=liB, in1=liB, op=ALU.mult))
        ob(lambda e: e.tensor_tensor(out=B_(7), in0=B_(7), in1=B_(2), op=ALU.add))
        ob(lambda e: e.reciprocal(B_(7), B_(7)))
        ob(lambda e: e.tensor_tensor(out=B_(3), in0=B_(11), in1=lrB, op=ALU.mult))
        ob(lambda e: e.tensor_tensor(out=B_(2), in0=aiB, in1=liB, op=ALU.mult))
        ob(lambda e: e.tensor_tensor(out=B_(3), in0=B_(3), in1=B_(2), op=ALU.add))
        ob(lambda e: e.tensor_tensor(out=B_(3), in0=B_(3), in1=B_(7), op=ALU.mult))
        ob(lambda e: e.tensor_tensor(out=B_(4), in0=aiB, in1=lrB, op=ALU.mult))
        ob(lambda e: e.tensor_tensor(out=B_(2), in0=B_(11), in1=liB, op=ALU.mult))
        ob(lambda e: e.tensor_tensor(out=B_(4), in0=B_(4), in1=B_(2), op=ALU.subtract))
        ob(lambda e: e.tensor_tensor(out=B_(4), in0=B_(4), in1=B_(7), op=ALU.mult))
        bre = mTf[0:64, 0:2048].rearrange("p (g c) -> p g c", c=16)
        bim = mTf[0:64, 2048:4096].rearrange("p (g c) -> p g c", c=16)
        bbr = mTf[0:64, 4096:6144].rearrange("p (g c) -> p g c", c=16)
        bbi = mTf[0:64, 6144:8192].rearrange("p (g c) -> p g c", c=16)
        with nc.allow_non_contiguous_dma(reason="b tensors 64B runs"):
            dma("sp", bre, b_re[li, :, :, :].rearrange("g p c -> p g c"), writes=["bb"])
            dma("sp", bim, b_im[li, :, :, :].rearrange("g p c -> p g c"), writes=["bb"])
        cr = B_(3).unsqueeze(2).broadcast_to([64, 128, 16])
        ci = B_(4).unsqueeze(2).broadcast_to([64, 128, 16])
        o2 = lambda fn: op("dve", fn, reads=["bb", "smallB"], writes=["bb"])
        o2(lambda e: e.tensor_tensor(out=bbr, in0=bre, in1=cr, op=ALU.mult))
        o2(lambda e: e.tensor_tensor(out=bbi, in0=bim, in1=ci, op=ALU.mult))
        o2(lambda e: e.tensor_tensor(out=bbr, in0=bbr, in1=bbi, op=ALU.subtract))
        o2(lambda e: e.tensor_tensor(out=bbi, in0=bre, in1=ci, op=ALU.mult))
        o2(lambda e: e.tensor_tensor(out=bre, in0=bim, in1=cr, op=ALU.mult))
        o2(lambda e: e.tensor_tensor(out=bbi, in0=bbi, in1=bre, op=ALU.add))
        for kc in range(KC):
            for r_, src in ((0, bbr), (1, bbi)):
                pb, pk = psum(5, 8)
                op("pe", lambda e: e.transpose(pb[:, 0:64], src[:, kc * 8:(kc + 1) * 8, :].rearrange("p g c -> p (g c)"), ident_f[0:64, 0:64]),
                   reads=["bb", "ident_f"], writes=[pk])
                for gl in range(8):
                    op("act" if gl % 2 else "dve",
                       (lambda e: e.activation(out=Wb[:, kc, gl, r_ * 64:(r_ + 1) * 64], in_=pb[:, 0:64], func=AF.Identity, scale=gmask[:, gl:gl + 1])) if gl % 2 else
                       (lambda e: e.tensor_scalar(out=Wb[:, kc, gl, r_ * 64:(r_ + 1) * 64], in0=pb[:, 0:64], scalar1=gmask[:, gl:gl + 1], scalar2=None, op0=ALU.mult)),
                       reads=[pk, "gmask"], writes=["Wb"])
        kb.barrier()
        Cn = [hTf[0:64, r_ * 4096:(r_ + 1) * 4096].bitcast(F32).rearrange("p (s c q) -> p s c q", s=2, c=16) for r_ in range(2)]
        dma("sp", Cn[0], c_re[li, :, :, :].rearrange("(g s) c q -> g s c q", s=2), writes=["Cn"])
        dma("sp", Cn[1], c_im[li, :, :, :].rearrange("(g s) c q -> g s c q", s=2), writes=["Cn"])
        op("dve", lambda e: e.memset(xTf[:, 0:8192], 0.0), writes=["Wc"])
        for r_ in range(2):
            for c in range(16):
                pb, pk = psum(5, 8)
                op("dve", lambda e: e.tensor_copy(out=nat[0:64, :].rearrange("p (s q) -> p s q", s=2), in_=Cn[r_][:, :, c, :]), reads=["Cn"], writes=["nat"])
                op("pe", lambda e: e.transpose(pb[:, 0:64], nat[0:64, :], ident_f[0:64, 0:64]), reads=["nat", "ident_f"], writes=[pk])
                for s in range(2):
                    for j4 in range(4):
                        src = pb[s * 64:(s + 1) * 64, 0:64].rearrange("p (k j) -> p k j", j=4)[:, :, j4]
                        dst = Wc[s * 64:(s + 1) * 64, :, j4, r_, 32 * j4 + 16 * s + c]
                        sc = 1.0 if r_ == 0 else -1.0
                        if (s + j4) % 2:
                            op("act", lambda e: e.mul(dst, src, sc), reads=[pk], writes=["Wc"])
                        else:
                            op("dve", lambda e: e.tensor_scalar(out=dst, in0=src, scalar1=sc, scalar2=None, op0=ALU.mult), reads=[pk], writes=["Wc"])
        kb.barrier()
        xr = mTf[:, 0:TC]
        xi = mTf[:, TC:2 * TC]
        uch = mTf[:, 2 * TC:3 * TC]
        Xr = mTf[:, 6150:6214]
        Xi = mTf[:, 6214:6278]
        ytmp = [mTf[:, 6278 + k * 512:6278 + (k + 1) * 512] for k in range(3)]
        h0 = mTf[:, 7814:7814 + 256].rearrange("p (a s r) -> p a s r", s=NS, r=2)
        sto = kb.sb("sto", [128, 64, 2], F32, es)
        stos = kb.sb("stos", [128, NS, 64, 2], F32, es)
        ubf = hTf[:, 0:TC]
        xbr = hTf[:, TC:2 * TC]
        xbi = hTf[:, 2 * TC:3 * TC]
        zst = hTf[:, 3 * TC:4 * TC]
        for s_ in range(NS):
            with nc.allow_non_contiguous_dma(reason="state 8B runs"):
                dma("sp", h0[:, :, s_, :], st_ssm[li, s_, :].rearrange("(a p r) -> p a r", p=128, r=2), writes=["h0"])
        coltiles = [(tq * TW, TW) for tq in range(NT)] + [(T, NS)]
        xr3 = xr[:, 0:T].rearrange("p (n l) -> p n l", l=LCH)
        xi3 = xi[:, 0:T].rearrange("p (n l) -> p n l", l=LCH)
        for kc in range(KC):
            dma("sp", uch[:, :], uT_s[kc, :, :], reads=["uT_s"], writes=["uch"])
            op("act", lambda e: e.copy(ubf[:, :], uch[:, :]), reads=["uch"], writes=["ubf"])
            ybanks = [(pbank[k], ("ps", k)) for k in range(5)]
            for j4 in range(4):
                pair = kc * 4 + j4
                PR, PI, NPI = tab(pr_parts, pair), tab(pi_parts, pair), npi_t[:, pair, :]
                ar, ai, nai = PR[:, 0:1], PI[:, 0:1], NPI[:, 0:1]
                Ar, Ai, nAi = PR[:, LCH - 1:LCH], PI[:, LCH - 1:LCH], NPI[:, LCH - 1:LCH]
                for (c0, n) in coltiles:
                    for r_, dstx in ((0, xr), (1, xi)):
                        pb, pk = psum(5, 8)
                        for s in range(2):
                            op("pe", lambda e: e.matmul(pb[s * 64:(s + 1) * 64, 0:n], Wb[:, kc, 2 * j4 + s, r_ * 64:(r_ + 1) * 64], ubf[:, c0:c0 + n],
                                                        start=True, stop=True), reads=["Wb", "ubf"], writes=[pk], signal=(s == 1))
                        op("act", lambda e: e.copy(dstx[:, c0:c0 + n], pb[:, 0:n]), reads=[pk], writes=["xr" if r_ == 0 else "xi"])
                sc = lambda fn, rd, wrt: op("dve", fn, reads=rd + tkeys, writes=wrt)
                for i in range(1, LCH):
                    sc(lambda e: e.scalar_tensor_tensor(out=xr3[:, :, i], in0=xr3[:, :, i - 1], scalar=ar, in1=xr3[:, :, i], op0=ALU.mult, op1=ALU.add), ["xr"], ["xr"])
                    sc(lambda e: e.scalar_tensor_tensor(out=xr3[:, :, i], in0=xi3[:, :, i - 1], scalar=nai, in1=xr3[:, :, i], op0=ALU.mult, op1=ALU.add), ["xr", "xi"], ["xr"])
                    sc(lambda e: e.scalar_tensor_tensor(out=xi3[:, :, i], in0=xi3[:, :, i - 1], scalar=ar, in1=xi3[:, :, i], op0=ALU.mult, op1=ALU.add), ["xi"], ["xi"])
                    sc(lambda e: e.scalar_tensor_tensor(out=xi3[:, :, i], in0=xr3[:, :, i - 1], scalar=ai, in1=xi3[:, :, i], op0=ALU.mult, op1=ALU.add), ["xr", "xi"], ["xi"])
                sc(lambda e: e.tensor_copy(out=Xr, in_=xr3[:, :, LCH - 1]), ["xr"], ["X"])
                sc(lambda e: e.tensor_copy(out=Xi, in_=xi3[:, :, LCH - 1]), ["xi"], ["X"])
                for n_ in range(1, NCH):
                    a_, b_ = slice(n_ - 1, n_), slice(n_, n_ + 1)
                    sc(lambda e: e.scalar_tensor_tensor(out=Xr[:, b_], in0=Xr[:, a_], scalar=Ar, in1=Xr[:, b_], op0=ALU.mult, op1=ALU.add), ["X"], ["X"])
                    sc(lambda e: e.scalar_tensor_tensor(out=Xr[:, b_], in0=Xi[:, a_], scalar=nAi, in1=Xr[:, b_], op0=ALU.mult, op1=ALU.add), ["X"], ["X"])
                    sc(lambda e: e.scalar_tensor_tensor(out=Xi[:, b_], in0=Xi[:, a_], scalar=Ar, in1=Xi[:, b_], op0=ALU.mult, op1=ALU.add), ["X"], ["X"])
                    sc(lambda e: e.scalar_tensor_tensor(out=Xi[:, b_], in0=Xr[:, a_], scalar=Ai, in1=Xi[:, b_], op0=ALU.mult, op1=ALU.add), ["X"], ["X"])
                for i in range(LCH):
                    p_r, p_i, np_i = PR[:, i:i + 1], PI[:, i:i + 1], NPI[:, i:i + 1]
                    sc(lambda e: e.scalar_tensor_tensor(out=xr3[:, 1:NCH, i], in0=Xr[:, 0:NCH - 1], scalar=p_r, in1=xr3[:, 1:NCH, i], op0=ALU.mult, op1=ALU.add), ["X", "xr"], ["xr"])
                    sc(lambda e: e.scalar_tensor_tensor(out=xr3[:, 1:NCH, i], in0=Xi[:, 0:NCH - 1], scalar=np_i, in1=xr3[:, 1:NCH, i], op0=ALU.mult, op1=ALU.add), ["X", "xr"], ["xr"])
                    sc(lambda e: e.scalar_tensor_tensor(out=xi3[:, 1:NCH, i], in0=Xi[:, 0:NCH - 1], scalar=p_r, in1=xi3[:, 1:NCH, i], op0=ALU.mult, op1=ALU.add), ["X", "xi"], ["xi"])
                    sc(lambda e: e.scalar_tensor_tensor(out=xi3[:, 1:NCH, i], in0=Xr[:, 0:NCH - 1], scalar=p_i, in1=xi3[:, 1:NCH, i], op0=ALU.mult, op1=ALU.add), ["X", "xi"], ["xi"])
                xs_r, xs_i = xr[:, T:TC], xi[:, T:TC]
                sc(lambda e: e.scalar_tensor_tensor(out=xs_r, in0=h0[:, pair, :, 0], scalar=ar, in1=xs_r, op0=ALU.mult, op1=ALU.add), ["h0", "xr"], ["xr"])
                sc(lambda e: e.scalar_tensor_tensor(out=xs_r, in0=h0[:, pair, :, 1], scalar=nai, in1=xs_r, op0=ALU.mult, op1=ALU.add), ["h0", "xr"], ["xr"])
                sc(lambda e: e.scalar_tensor_tensor(out=xs_i, in0=h0[:, pair, :, 1], scalar=ar, in1=xs_i, op0=ALU.mult, op1=ALU.add), ["h0", "xi"], ["xi"])
                sc(lambda e: e.scalar_tensor_tensor(out=xs_i, in0=h0[:, pair, :, 0], scalar=ai, in1=xs_i, op0=ALU.mult, op1=ALU.add), ["h0", "xi"], ["xi"])
                op("act", lambda e: e.copy(sto[:, pair, 0:1], xr[:, T - 1:T]), reads=["xr"], writes=["sto"])
                op("act", lambda e: e.copy(sto[:, pair, 1:2], xi[:, T - 1:T]), reads=["xi"], writes=["sto"])
                op("act", lambda e: e.copy(stos[:, :, pair, 0], xr[:, T:TC]), reads=["xr"], writes=["stos"])
                op("act", lambda e: e.copy(stos[:, :, pair, 1], xi[:, T:TC]), reads=["xi"], writes=["stos"])
                op("act", lambda e: e.copy(xbr[:, :], xr[:, :]), reads=["xr"], writes=["xbr"])
                op("pool", lambda e: e.tensor_copy(out=xbi[:, :], in_=xi[:, :]), reads=["xi"], writes=["xbi"])
                for ti, (c0, n) in enumerate(coltiles):
                    yb, ybk = ybanks[ti]
                    op("pe", lambda e: e.matmul(yb[:, 0:n], Wc[:, kc, j4, 0, :], xbr[:, c0:c0 + n], start=(j4 == 0), stop=False),
                       reads=["Wc", "xbr"], writes=[ybk], signal=False)
                    op("pe", lambda e: e.matmul(yb[:, 0:n], Wc[:, kc, j4, 1, :], xbi[:, c0:c0 + n], start=False, stop=(j4 == 3)),
                       reads=["Wc", "xbi"], writes=[ybk], signal=True)
            for ti, (c0, n) in enumerate(coltiles):
                yb, ybk = ybanks[ti]
                y_, w_, s_t = ytmp[0][:, 0:n], ytmp[1][:, 0:n], ytmp[2][:, 0:n]
                op("dve", lambda e: e.scalar_tensor_tensor(out=y_, in0=uch[:, c0:c0 + n], scalar=dsk[:, kc:kc + 1], in1=yb[:, 0:n], op0=ALU.mult, op1=ALU.add),
                   reads=["uch", "dsk", ybk], writes=["ytmp"])
                op("dve", lambda e: e.tensor_tensor(out=w_, in0=y_, in1=y_, op=ALU.mult), reads=["ytmp"], writes=["ytmp"])
                op("dve", lambda e: e.tensor_scalar(out=w_, in0=w_, scalar1=0.044715, scalar2=1.0, op0=ALU.mult, op1=ALU.add), reads=["ytmp"], writes=["ytmp"])
                op("dve", lambda e: e.tensor_tensor(out=w_, in0=w_, in1=y_, op=ALU.mult), reads=["ytmp"], writes=["ytmp"])
                op("act", lambda e: e.activation(out=s_t, in_=w_, func=AF.Sigmoid, scale=GELU_C), reads=["ytmp"], writes=["ytmp"])
                op("dve", lambda e: e.tensor_tensor(out=zst[:, c0:c0 + n], in0=y_, in1=s_t, op=ALU.mult), reads=["ytmp"], writes=["zst"])
            dma("sp", zT_s[kc, :, :], zst[:, :], reads=["zst"], writes=["zT_s"])
        with nc.allow_non_contiguous_dma(reason="state 8B runs"):
            dma("sp", ssm_p[li, :].rearrange("(a p r) -> p a r", p=128, r=2), sto[:, :, :], reads=["sto"])
            for s_ in range(NS):
                dma("sp", ssm_s[li, s_, :].rearrange("(a p r) -> p a r", p=128, r=2), stos[:, s_, :, :], reads=["stos"])
        kb.barrier()
        es.close()

    def tile_tail(i_next, t, segs):
        prenorm(i_next, 0, segs)
        if i_next % 2 == 0:
            qkv_phase(i_next // 2, t, segs)
        else:
            win_phase(i_next // 2, t, segs)

    def mixer_out_attn(li, t, segs):
        g0 = t * TW
        minT = aT[:, 0:8, :]
        dma("sp", minT[:, :, 0:TW], mT_s[:, :, g0:g0 + TW].rearrange("h p t -> p h t"), reads=["mT_s"], writes=["aT"])
        if len(segs) > 1:
            with nc.allow_non_contiguous_dma(reason="tiny sample columns"):
                dma("sp", minT[:, :, TW:TWS], mT_s[:, :, T:TC].rearrange("h p t -> p h t"), reads=["mT_s"], writes=["aT"])
        psts = [stats_begin() if si == 0 else psum(6, 7) for si in range(len(segs))]
        for dc in range(KC):
            wv, wk = wr.get(("wo", li, dc))
            w3 = wv[:, 0:8 * 128].rearrange("p (k c) -> p k c", k=8)
            for si, (c0, n, S) in enumerate(segs):
                pb, pk = psum()
                for hh in range(8):
                    op("pe", lambda e: e.matmul(pb[:, 0:n], w3[:, hh, :], minT[:, hh, c0:c0 + n], start=(hh == 0), stop=(hh == 7)),
                       reads=["aT", wk], writes=[pk], signal=(hh == 7))
                op("act", lambda e: e.copy(mT[:, dc, c0:c0 + n], pb[:, 0:n]), reads=[pk], writes=["mT"])
                stats_add(psts[si], pb[:, 0:n], pk, c0, n, dc == 0, dc == KC - 1)
        for si, (c0, n, S) in enumerate(segs):
            stats_end(psts[si], c0, n)

    def ffn(i, t, segs):
        for j in range(NFC):
            wv, wk = wr.get(("up", i, j))
            wg = wv[:, 0:KC * 128].rearrange("p (k c) -> p k c", k=KC)
            wvv = wv[:, KC * 128:2 * KC * 128].rearrange("p (k c) -> p k c", k=KC)
            for (c0, n, S) in segs:
                pg, pgk = psum()
                pv, pvk = psum()
                for kc in range(KC):
                    op("pe", lambda e: e.matmul(pg[:, 0:n], wg[:, kc, :], hT[:, kc, c0:c0 + n], start=(kc == 0), stop=(kc == KC - 1)),
                       reads=["hT", wk], writes=[pgk], signal=(kc == KC - 1))
                for kc in range(KC):
                    op("pe", lambda e: e.matmul(pv[:, 0:n], wvv[:, kc, :], hT[:, kc, c0:c0 + n], start=(kc == 0), stop=(kc == KC - 1)),
                       reads=["hT", wk], writes=[pvk], signal=(kc == KC - 1))
                r = rr["cg"] % 2
                rr["cg"] += 1
                outs = []
                for (pp, ppk, jj, ct_, ck_) in ((pg, pgk, j, cgt[r], "cg%d" % r), (pv, pvk, NFC + j, cvt[r], "cv%d" % r)):
                    w0 = cw[:, 0, jj:jj + 1]
                    w1 = cw[:, 1, jj:jj + 1]
                    w2 = cw[:, 2, jj:jj + 1]
                    bb = cw[:, 3, jj:jj + 1]
                    if S == 1:
                        halo = uprev[:, jj, :]
                        hk = "uprev"
                    else:
                        halo = cst[:, jj, :, :].rearrange("p a b -> p (a b)")
                        hk = "cst"
                    op("act", lambda e: e.activation(out=ct_[:, 0:n], in_=pp[:, 0:n], func=AF.Identity, bias=bb, scale=w2),
                       reads=[ppk, "cw"], writes=[ck_])
                    if n > S:
                        op("dve", lambda e: e.scalar_tensor_tensor(out=ct_[:, S:n], in0=pp[:, 0:n - S], scalar=w1, in1=ct_[:, S:n],
                                                                   op0=ALU.mult, op1=ALU.add), reads=[ppk, "cw", ck_], writes=[ck_])
                    if n > 2 * S:
                        op("dve", lambda e: e.scalar_tensor_tensor(out=ct_[:, 2 * S:n], in0=pp[:, 0:n - 2 * S], scalar=w0, in1=ct_[:, 2 * S:n],
                                                                   op0=ALU.mult, op1=ALU.add), reads=[ppk, "cw", ck_], writes=[ck_])
                    op("dve", lambda e: e.scalar_tensor_tensor(out=ct_[:, 0:S], in0=halo[:, S:2 * S], scalar=w1, in1=ct_[:, 0:S],
                                                               op0=ALU.mult, op1=ALU.add), reads=[hk, "cw", ck_], writes=[ck_])
                    m2 = min(2 * S, n)
                    op("dve", lambda e: e.scalar_tensor_tensor(out=ct_[:, 0:m2], in0=halo[:, 0:m2], scalar=w0, in1=ct_[:, 0:m2],
                                                               op0=ALU.mult, op1=ALU.add), reads=[hk, "cw", ck_], writes=[ck_])
                    if S == 1:
                        op("dve", lambda e: e.tensor_copy(out=uprev[:, jj, :], in_=pp[:, n - 2:n]), reads=[ppk], writes=["uprev"])
                    else:
                        op("dve", lambda e: e.tensor_copy(out=csout[:, jj, 0, :], in_=cst[:, jj, 1, :]), reads=["cst"], writes=["csout"])
                        op("dve", lambda e: e.tensor_copy(out=csout[:, jj, 1, :], in_=pp[:, 0:NS]), reads=[ppk], writes=["csout"])
                sg_ = sgt[r]
                op("act", lambda e: e.activation(out=sg_[:, 0:n], in_=cgt[r][:, 0:n], func=AF.Silu), reads=["cg%d" % r], writes=["sg%d" % r])
                op("dve", lambda e: e.tensor_tensor(out=aT[:, j, c0:c0 + n], in0=sg_[:, 0:n], in1=cvt[r][:, 0:n], op=ALU.mult),
                   reads=["sg%d" % r, "cv%d" % r], writes=["aT"])
        psts = [stats_begin() if si == 0 else psum(6, 7) for si in range(len(segs))]
        for dc in range(KC):
            wv, wk = wr.get(("down", i, dc))
            w3 = wv[:, 0:NFC * 128].rearrange("p (k c) -> p k c", k=NFC)
            for si, (c0, n, S) in enumerate(segs):
                pb, pk = psum()
                for j in range(NFC):
                    op("pe", lambda e: e.matmul(pb[:, 0:n], w3[:, j, :], aT[:, j, c0:c0 + n], start=(j == 0), stop=(j == NFC - 1)),
                       reads=["aT", wk], writes=[pk], signal=(j == NFC - 1))
                op("act", lambda e: e.copy(mT[:, dc, c0:c0 + n], pb[:, 0:n]), reads=[pk], writes=["mT"])
                stats_add(psts[si], pb[:, 0:n], pk, c0, n, dc == 0, dc == KC - 1)
        for si, (c0, n, S) in enumerate(segs):
            stats_end(psts[si], c0, n)

    def out_y(t, segs):
        ytok = mT_flat
        for bl in range(4):
            for q4 in range(4):
                pb, pk = psum()
                for k4 in range(4):
                    kc = q4 * 4 + k4
                    op("pe", lambda e: e.transpose(pb[:, k4 * 128:(k4 + 1) * 128], xT[:, kc, bl * 128:(bl + 1) * 128], ident_f[:, :]),
                       reads=["xT", "ident_f"], writes=[pk], signal=(k4 == 3))
                op("act", lambda e: e.copy(ytok[:, (bl % 2) * D + q4 * 512:(bl % 2) * D + (q4 + 1) * 512], pb[:, :]), reads=[pk], writes=["mT"])
            dma("sp", y_p[t * TW + bl * 128:t * TW + (bl + 1) * 128, :], ytok[:, (bl % 2) * D:(bl % 2 + 1) * D], reads=["mT"])
        if len(segs) > 1:
            pb, pk = psum()
            for kc in range(KC):
                op("pe", lambda e: e.transpose(pb[0:NS, (kc % 4) * 128:(kc % 4 + 1) * 128], xT[:, kc, TW:TWS], ident_f[:, :]),
                   reads=["xT", "ident_f"], writes=[pk], signal=True)
                if kc % 4 == 3:
                    q4 = kc // 4
                    op("act", lambda e: e.copy(ytok[0:NS, q4 * 512:(q4 + 1) * 512], pb[0:NS, :]), reads=[pk], writes=["mT"])
            dma("sp", y_s[:, :], ytok[0:NS, 0:D], reads=["mT"])

    xtok = mT_flat
    for t in range(NT):
        segs = segs_of(t)
        for bl in range(4):
            xo = (bl % 2) * D
            dma("sp", xtok[:, xo:xo + D], x_p[t * TW + bl * 128:t * TW + (bl + 1) * 128, :], writes=["mT"])
            for q4 in range(4):
                pb, pk = psum()
                for k4 in range(4):
                    kc = q4 * 4 + k4
                    op("pe", lambda e: e.transpose(pb[:, k4 * 128:(k4 + 1) * 128], xtok[:, xo + kc * 128:xo + (kc + 1) * 128], ident_f[:, :]),
                       reads=["mT", "ident_f"], writes=[pk], signal=(k4 == 3))
                op("act", lambda e: e.copy(xT[:, q4 * 4:(q4 + 1) * 4, bl * 128:(bl + 1) * 128], pb[:, :].rearrange("p (k t) -> p k t", k=4)),
                   reads=[pk], writes=["xT"])
        if len(segs) > 1:
            dma("sp", xtok[0:NS, 0:D], x_s[:, :], writes=["mT"])
            pb, pk = psum()
            for kc in range(KC):
                op("pe", lambda e: e.transpose(pb[:, kc * NS:(kc + 1) * NS], xtok[0:NS, kc * 128:(kc + 1) * 128], ident_f[0:NS, 0:NS]),
                   reads=["mT", "ident_f"], writes=[pk], signal=(kc == KC - 1))
            op("act", lambda e: e.copy(xT[:, :, TW:TWS], pb[:, 0:KC * NS].rearrange("p (k s) -> p k s", k=KC)), reads=[pk], writes=["xT"])
        xT_store(t, segs)
        tile_tail(0, t, segs)

    for i in range(nlayers):
        li = i // 2
        if i % 2 == 0:
            attn_phase(li)
        else:
            ssm_phase(li)
        for k3 in range(3):
            fm_load(cw[:, k3, :], "cw", conv_w[i, k3, :])
        fm_load(cw[:, 3, :], "cw", conv_b[i, :])
        for s_ in range(NS):
            for r_ in range(2):
                fm_load(cst[:, :, r_, s_], "cst", st_conv[i, s_, r_, :])
        op("dve", lambda e: e.memset(uprev[:], 0.0), writes=["uprev"])
        for t in range(NT):
            segs = segs_of(t)
            xT_load(t, segs)
            if i % 2 == 0:
                mixer_out_attn(li, t, segs)
            else:
                mixer_out_ssm(li, t, segs)
            if dbg and i == 0:
                dma("sp", dbg_m[:, :, t * TW:(t + 1) * TW].rearrange("k p t -> p k t"), mT[:, :, 0:TW], reads=["mT"])
                if len(segs) > 1:
                    dma("sp", dbg_m[:, :, T:TC].rearrange("k p t -> p k t"), mT[:, :, TW:TWS], reads=["mT"])
            postnorm_residual(i, 1, segs)
            if dbg and i == 0:
                dma("sp", dbg_x1[:, :, t * TW:(t + 1) * TW].rearrange("k p t -> p k t"), xT[:, :, 0:TW], reads=["xT"])
                if len(segs) > 1:
                    dma("sp", dbg_x1[:, :, T:TC].rearrange("k p t -> p k t"), xT[:, :, TW:TWS], reads=["xT"])
            prenorm(i, 2, segs)
            ffn(i, t, segs)
            postnorm_residual(i, 3, segs)
            if i + 1 < nlayers:
                xT_store(t, segs)
                tile_tail(i + 1, t, segs)
            else:
                out_y(t, segs)
        for r_ in range(2):
            fm_store(uprev[:, :, r_], "uprev", conv_p[i, r_, :])
            for s_ in range(NS):
                fm_store(csout[:, :, r_, s_], "csout", conv_s[i, s_, r_, :])

    def _unused():
        pass

    assert wr.consumed == len(wr.plan), (wr.consumed, len(wr.plan))
    kb.barrier()
    kb.es.close()
    return nc


def _consts():
    ident = np.eye(128, dtype=np.float32)
    k = np.arange(128)[:, None]
    q = np.arange(128)[None, :]
    m = np.zeros((128, 7, 128), np.float32)
    m[:, 0] = (q >= k)
    m[:, 1] = (q <= k)
    r4 = ((q - k) % 4 == 0)
    m[:, 2] = r4 & (q >= k)
    m[:, 3] = r4
    m[:, 4] = r4 & (q <= k)
    r16 = ((q - k) % 16 == 0)
    m[:, 5] = r16 & (q >= k)
    m[:, 6] = r16
    half = 16
    inv = (np.float32(500000.0) ** (-(np.arange(half, dtype=np.float32) * np.float32(2.0 / 32)))).astype(np.float32)
    pos = np.zeros((128, 17), np.float32)
    for b in range(16):
        pos[:, b] = b * 128 + np.arange(128)
    pos[:, 16] = PAST
    ang = (pos[:, :, None] * inv[None, None, :]).astype(np.float32)
    rope = np.concatenate([np.cos(ang), np.sin(ang)], axis=-1).astype(np.float32)
    mbq = np.where(m.transpose(2, 1, 0) > 0.5, 0.0, -30000.0).astype(np.float32)
    gmask = (np.arange(128)[:, None] // 16 == np.arange(8)[None, :]).astype(np.float32)
    return ident, m, rope, np.ascontiguousarray(mbq), gmask


_NC_CACHE = {}


def kernel(**inp):
    f = lambda a: np.ascontiguousarray(np.asarray(a, dtype=np.float32))
    ident, masks, rope, mbq, gmask = _consts()
    if "nc" not in _NC_CACHE:
        _NC_CACHE["nc"] = build(4)
    nc = _NC_CACHE["nc"]
    shared = {
        "norm_g": f(inp["norm_g"]).reshape(16, D), "w_qkv": f(inp["w_qkv"]), "w_o": f(inp["w_attn_o"]),
        "w_in": f(inp["w_ssm_in"]), "lam_re": f(inp["lambda_re"]), "lam_im": f(inp["lambda_im"]),
        "log_dt": f(inp["log_dt"]), "b_re": f(inp["b_re"]), "b_im": f(inp["b_im"]), "c_re": f(inp["c_re"]),
        "c_im": f(inp["c_im"]), "d_skip": f(inp["d_skip"]), "w_glu": f(inp["w_glu"]), "w_up": f(inp["w_up"]),
        "conv_w": f(inp["conv_w"]), "conv_b": f(inp["conv_b"]), "w_down": f(inp["w_down"]),
        "c_ident": ident, "c_masks": masks, "c_rope": rope, "c_mbq": mbq, "c_gmask": gmask,
    }
    xp = f(inp["x_prompt"])
    xs = f(inp["x_sample"]).reshape(8, D)
    ck = [f(inp["cache_kv_g0"]), f(inp["cache_kv_g1"]), f(inp["cache_kv_g2"])]
    ss = f(inp["state_ssm"])
    sc = f(inp["state_conv"])
    in_maps = []
    for c in range(NCORES):
        m = dict(shared)
        m["x_p"] = xp[c]
        sl = slice(2 * c, 2 * c + 2)
        m["x_s"] = np.ascontiguousarray(xs[sl])
        for g in range(3):
            m["ckv%d" % g] = np.ascontiguousarray(ck[g][:, sl].reshape(2, NS, WIN[g], 2048))
        m["st_ssm"] = np.ascontiguousarray(ss[:, sl].reshape(2, NS, 128 * 64 * 2))
        m["st_conv"] = np.ascontiguousarray(sc[:, sl])
        in_maps.append(m)
    res = run_bass_kernel_spmd(nc, in_maps, core_ids=list(range(NCORES)))
    R = res.results
    y_prompt = np.stack([R[c]["y_p"] for c in range(NCORES)], 0)
    y_sample = np.concatenate([R[c]["y_s"] for c in range(NCORES)], 0).reshape(8, 1, D)
    outs = [y_prompt, y_sample]
    for g in range(3):
        keep = min(WIN[g], T)
        outs.append(np.stack([R[c]["kvp%d" % g] for c in range(NCORES)], 1).reshape(2, 4, keep, 2, 8, 128))
        outs.append(np.concatenate([R[c]["kvs%d" % g] for c in range(NCORES)], 1).reshape(2, 8, 1, 2, 8, 128))
    outs.append(np.stack([R[c]["ssm_p"] for c in range(NCORES)], 1).reshape(2, 4, 128, 64, 2))
    outs.append(np.concatenate([R[c]["ssm_s"] for c in range(NCORES)], 1).reshape(2, 8, 128, 64, 2))
    outs.append(np.stack([R[c]["conv_p"] for c in range(NCORES)], 1).reshape(4, 4, 2, 2 * DFF))
    outs.append(np.concatenate([R[c]["conv_s"] for c in range(NCORES)], 1).reshape(4, 8, 2, 2 * DFF))
    return tuple(np.ascontiguousarray(o.astype(np.float32)) for o in outs)
```

```python
import contextlib
import math
import numpy as np
import concourse.bass as bass
import concourse.mybir as mybir
from concourse.bass_utils import run_bass_kernel_spmd

F32 = mybir.dt.float32
BF16 = mybir.dt.bfloat16
AF = mybir.ActivationFunctionType
ALU = mybir.AluOpType
AX = mybir.AxisListType

T = 2048
D = 2048
KC = 16
NS = 2
TC = T + NS
TW = 512
NT = T // TW
TWS = TW + NS
DFF = 5504
NFC = 43
NUC = 86
NCORES = 4
PAST = 16384
WIN = (128, 512, 2048)
DIL = (1, 4, 16)
EPS = 1e-6
SLOT = 5504
NSLOT = 4
SCALE = 128 ** -0.5
GELU_C = 2.0 * math.sqrt(2.0 / math.pi)


class KB:
    NDS = 12

    def __init__(self):
        self.nc = bass.Bass("TRN2", target_bir_lowering=False)
        nc = self.nc
        self.es = contextlib.ExitStack()
        self.engs = {"pe": nc.tensor, "dve": nc.vector, "act": nc.scalar, "pool": nc.gpsimd, "sp": nc.sync}
        self.semh = {}
        for k in self.engs:
            self.semh[("e", k)] = self.es.enter_context(nc.semaphore("se_" + k))
        self.cnt = {k: 0 for k in self.engs}
        self.waited = {k: {} for k in self.engs}
        self.pending = {k: ([], []) for k in self.engs}
        self.lastw = {}
        self.readers = {}
        self.dcnt = {}
        self.dnext = {}
        for q in ("sp", "pool", "act"):
            for i in range(self.NDS):
                self.semh[("d", q, i)] = self.es.enter_context(nc.semaphore("sd_%s_%d" % (q, i)))
                self.dcnt[("d", q, i)] = 0
            self.dnext[q] = 0
        self.nins = 0

    def sb(self, name, shape, dt, es=None):
        self._uid = getattr(self, "_uid", 0) + 1
        return (es or self.es).enter_context(self.nc.sbuf_tensor("%s_%d" % (name, self._uid), list(shape), dt))

    def ps(self, name, shape, dt, es=None):
        return (es or self.es).enter_context(self.nc.psum_tensor(name, list(shape), dt))

    def dram(self, name, shape, dt, kind):
        return self.nc.dram_tensor(name, list(shape), dt, kind=kind).ap()

    def _deps(self, reads, writes):
        deps = {}
        for b in reads:
            lw = self.lastw.get(b)
            if lw is not None:
                deps[lw[0]] = max(deps.get(lw[0], 0), lw[1])
        for b in writes:
            lw = self.lastw.get(b)
            if lw is not None:
                deps[lw[0]] = max(deps.get(lw[0], 0), lw[1])
            for sk, v in self.readers.get(b, {}).items():
                deps[sk] = max(deps.get(sk, 0), v)
        return deps

    def _wait(self, eng, deps):
        e = self.engs[eng]
        w = self.waited[eng]
        for sk, v in deps.items():
            if eng == "pe" and sk == ("e", "pe"):
                continue
            if w.get(sk, 0) < v:
                e.wait_ge(self.semh[sk], v)
                w[sk] = v
                self.nins += 1

    def fence(self, eng, reads=(), writes=()):
        self._wait(eng, self._deps(reads, writes))

    def mark(self, eng, writes=(), reads=()):
        sk = ("e", eng)
        v = self.cnt[eng]
        for b in writes:
            self.lastw[b] = (sk, v)
            self.readers[b] = {}
        for b in reads:
            self.readers.setdefault(b, {})[sk] = v

    def opx(self, eng, fn, after=()):
        sk = ("e", eng)
        v = max(after) if after else 0
        if v > self.waited[eng].get(sk, 0):
            self.engs[eng].wait_ge(self.semh[sk], v)
            self.waited[eng][sk] = v
            self.nins += 1
        ins = fn(self.engs[eng])
        self.nins += 1
        self.cnt[eng] += 1
        ins.then_inc(self.semh[sk], 1)
        return self.cnt[eng]

    def op(self, eng, fn, reads=(), writes=(), signal=True):
        self._wait(eng, self._deps(reads, writes))
        ins = fn(self.engs[eng])
        self.nins += 1
        pr, pw = self.pending[eng]
        pr.extend(reads)
        pw.extend(writes)
        if signal:
            self.cnt[eng] += 1
            sk = ("e", eng)
            ins.then_inc(self.semh[sk], 1)
            v = self.cnt[eng]
            for b in pw:
                self.lastw[b] = (sk, v)
                self.readers[b] = {}
            for b in pr:
                if b not in pw:
                    self.readers.setdefault(b, {})[sk] = v
            self.pending[eng] = ([], [])
        return ins

    def dma(self, q, out, in_, reads=(), writes=(), **kw):
        self._wait(q, self._deps(reads, writes))
        i = self.dnext[q]
        self.dnext[q] = (i + 1) % self.NDS
        sk = ("d", q, i)
        prev = self.dcnt[sk]
        if prev > 0 and self.waited[q].get(sk, 0) < prev:
            self.engs[q].wait_ge(self.semh[sk], prev)
            self.waited[q][sk] = prev
        ins = self.engs[q].dma_start(out=out, in_=in_, **kw)
        self.nins += 1
        self.dcnt[sk] += 16
        v = self.dcnt[sk]
        ins.then_inc(self.semh[sk], 16)
        for b in writes:
            self.lastw[b] = (sk, v)
            self.readers[b] = {}
        for b in reads:
            if b not in writes:
                self.readers.setdefault(b, {})[sk] = v
        return ins

    def barrier(self, engines=None):
        cur = {}
        for k in self.engs:
            assert not self.pending[k][0] and not self.pending[k][1]
            cur[("e", k)] = self.cnt[k]
        for sk, v in self.dcnt.items():
            cur[sk] = v
        for eng in (engines or list(self.engs)):
            for sk, v in cur.items():
                if sk == ("e", eng):
                    continue
                if v > 0 and self.waited[eng].get(sk, 0) < v:
                    self.engs[eng].wait_ge(self.semh[sk], v)
                    self.waited[eng][sk] = v
                    self.nins += 1
        if engines is None:
            self.lastw = {}
            self.readers = {}


class WRing:
    def __init__(self, kb, tensor):
        self.kb = kb
        self.t = tensor
        self.plan = []
        self.emitted = 0
        self.consumed = 0

    def add(self, tag, pieces):
        self.plan.append((tag, pieces))

    def _emit(self, k):
        tag, pieces = self.plan[k]
        s = k % NSLOT
        for (off, shp, src) in pieces:
            n = int(np.prod(shp))
            dst = self.t[:, s, off:off + n]
            if len(shp) == 2:
                dst = dst.rearrange("p (a b) -> p a b", a=shp[0])
            self.kb.dma("pool", dst, src, writes=[("w", s)])

    def get(self, tag):
        k = self.consumed
        assert self.plan[k][0] == tag, (self.plan[k][0], tag)
        while self.emitted < min(len(self.plan), k + NSLOT):
            self._emit(self.emitted)
            self.emitted += 1
        self.consumed += 1
        s = k % NSLOT
        return self.t[:, s, :], ("w", s)


def build(nlayers=4, dbg=False):
    kb = KB()
    nc = kb.nc
    op, dma = kb.op, kb.dma

    def din(name, shape, dt=F32):
        return kb.dram(name, shape, dt, "ExternalInput")

    def dout(name, shape, dt=F32):
        return kb.dram(name, shape, dt, "ExternalOutput")

    x_p = din("x_p", [T, D])
    x_s = din("x_s", [NS, D])
    ckv = [din("ckv%d" % g, [2, NS, WIN[g], 2 * 1024]) for g in range(3)]
    st_ssm = din("st_ssm", [2, NS, 128 * 64 * 2])
    st_conv = din("st_conv", [4, NS, 2, 2 * DFF])
    norm_g = din("norm_g", [16, D])
    w_qkv = din("w_qkv", [2, D, 9216])
    w_o = din("w_o", [2, 1024, D])
    w_in = din("w_in", [2, D, D])
    lam_re = din("lam_re", [2, 128, 64])
    lam_im = din("lam_im", [2, 128, 64])
    log_dt = din("log_dt", [2, 128])
    b_re = din("b_re", [2, 128, 64, 16])
    b_im = din("b_im", [2, 128, 64, 16])
    c_re = din("c_re", [2, 128, 16, 64])
    c_im = din("c_im", [2, 128, 16, 64])
    d_skip = din("d_skip", [2, D])
    w_glu = din("w_glu", [2, D, 2 * D])
    w_up = din("w_up", [4, D, 2 * DFF])
    conv_w = din("conv_w", [4, 3, 2 * DFF])
    conv_b = din("conv_b", [4, 2 * DFF])
    w_down = din("w_down", [4, DFF, D])
    c_ident = din("c_ident", [128, 128])
    c_masks = din("c_masks", [128, 7, 128])
    c_rope = din("c_rope", [128, 17, 32])
    c_mbq = din("c_mbq", [128, 7, 128])
    c_gmask = din("c_gmask", [128, 8])

    y_p = dout("y_p", [T, D])
    y_s = dout("y_s", [NS, D])
    kvp = [dout("kvp%d" % g, [2, min(WIN[g], T), 2048]) for g in range(3)]
    kvs = [dout("kvs%d" % g, [2, NS, 2048]) for g in range(3)]
    ssm_p = dout("ssm_p", [2, 128 * 64 * 2])
    ssm_s = dout("ssm_s", [2, NS, 128 * 64 * 2])
    conv_p = dout("conv_p", [4, 2, 2 * DFF])
    conv_s = dout("conv_s", [4, NS, 2, 2 * DFF])

    IK = "ExternalOutput" if dbg else "Internal"
    xT_s = kb.dram("xT_s", [KC, 128, TC], F32, "Internal")
    qT_s = kb.dram("qT_s", [3, 8, 128, T], BF16, IK)
    kT_s = kb.dram("kT_s", [3, 8, 128, T], BF16, IK)
    v_s = kb.dram("v_s", [3, T, 1024], BF16, IK)
    mT_s = kb.dram("mT_s", [8, 128, TC], BF16, IK)
    dbg_x1 = kb.dram("dbg_x1", [KC, 128, TC], F32, "ExternalOutput") if dbg else None
    dbg_m = kb.dram("dbg_m", [KC, 128, TC], F32, "ExternalOutput") if dbg else None
    uT_s = kb.dram("uT_s", [KC, 128, TC], F32, "Internal")
    zT_s = kb.dram("zT_s", [KC, 128, TC], BF16, "Internal")

    ident_f = kb.sb("ident_f", [128, 128], F32)
    ident_b = kb.sb("ident_b", [128, 128], BF16)
    ones_b = kb.sb("ones_b", [128, 128], BF16)
    masks = kb.sb("masks", [128, 7, 128], BF16)
    rope = kb.sb("rope", [128, 17, 32], F32)
    gsc = kb.sb("gsc", [128, KC, 16], F32)
    wring_t = kb.sb("wring", [128, NSLOT, SLOT], BF16)
    wr = WRing(kb, wring_t)
    sqT = kb.sb("sqT", [128, 24, NS], BF16)
    skT = kb.sb("skT", [128, 24, NS], BF16)
    svT = kb.sb("svT", [128, 24, NS], F32)
    epsb = kb.sb("epsb", [128, 1], F32)

    pbank = [kb.ps("pb%d" % i, [128, 512], F32) for i in range(8)]
    prr = [0]

    def psum(lo=0, hi=6):
        i = lo + prr[0] % (hi - lo)
        prr[0] += 1
        return pbank[i], ("ps", i)

    dma("sp", ident_f[:], c_ident[:, :], writes=["ident_f"])
    dma("pool", ident_b[:], c_ident[:, :], writes=["ident_b"])
    dma("pool", masks[:], c_masks[:, :, :], writes=["masks"])
    dma("sp", rope[:], c_rope[:, :, :], writes=["rope"])
    op("dve", lambda e: e.memset(ones_b[:], 1.0), writes=["ones_b"])
    op("dve", lambda e: e.memset(epsb[:], EPS), writes=["epsb"])

    es0 = contextlib.ExitStack()
    gtmp = kb.sb("gtmp", [16, D], F32, es0)
    dma("sp", gtmp[:], norm_g[:, :], writes=["gtmp"])
    for kc in range(KC):
        pb, pk = psum()
        op("pe", lambda e: e.transpose(pb[:, 0:16], gtmp[0:16, kc * 128:(kc + 1) * 128], ident_f[0:16, 0:16]),
           reads=["gtmp", "ident_f"], writes=[pk])
        op("act", lambda e: e.copy(gsc[:, kc, :], pb[:, 0:16]), reads=[pk], writes=["gsc"])
    kb.barrier()
    es0.close()

    def gs(i, j, kc):
        return gsc[:, kc, 4 * i + j:4 * i + j + 1]

    def plan_qkv(li):
        for ct in range(36):
            src = w_qkv[li, :, ct * 256:(ct + 1) * 256].rearrange("(k p) c -> p k c", p=128)
            wr.add(("qkv", li, ct), [(0, [KC, 256], src)])

    def plan_win(li):
        for dc in range(KC):
            src = w_in[li, :, dc * 128:(dc + 1) * 128].rearrange("(k p) c -> p k c", p=128)
            wr.add(("win", li, dc), [(0, [KC, 128], src)])

    def plan_tile(i):
        li = i // 2
        if i % 2 == 0:
            for dc in range(KC):
                src = w_o[li, :, dc * 128:(dc + 1) * 128].rearrange("(k p) c -> p k c", p=128)
                wr.add(("wo", li, dc), [(0, [8, 128], src)])
        else:
            for dc in range(KC):
                sv = w_glu[li, :, dc * 128:(dc + 1) * 128].rearrange("(k p) c -> p k c", p=128)
                sg = w_glu[li, :, D + dc * 128:D + (dc + 1) * 128].rearrange("(k p) c -> p k c", p=128)
                wr.add(("glu", li, dc), [(0, [KC, 128], sv), (KC * 128, [KC, 128], sg)])
        for j in range(NFC):
            sg = w_up[i, :, j * 128:(j + 1) * 128].rearrange("(k p) c -> p k c", p=128)
            sv = w_up[i, :, DFF + j * 128:DFF + (j + 1) * 128].rearrange("(k p) c -> p k c", p=128)
            wr.add(("up", i, j), [(0, [KC, 128], sg), (KC * 128, [KC, 128], sv)])
        for dc in range(KC):
            src = w_down[i, :, dc * 128:(dc + 1) * 128].rearrange("(j p) c -> p j c", p=128)
            wr.add(("down", i, dc), [(0, [NFC, 128], src)])
        if i + 1 < nlayers:
            if (i + 1) % 2 == 0:
                plan_qkv((i + 1) // 2)
            else:
                plan_win((i + 1) // 2)

    for t in range(NT):
        plan_qkv(0)
    for i in range(nlayers):
        for t in range(NT):
            plan_tile(i)

    xT = kb.sb("xT", [128, KC, TWS], F32)
    mT = kb.sb("mT", [128, KC, TWS], F32)
    hT = kb.sb("hT", [128, KC, TWS], BF16)
    aT = kb.sb("aT", [128, NFC, TWS], BF16)
    sq = [kb.sb("sq%d" % i, [128, TW], BF16) for i in range(2)]
    rstd = kb.sb("rstd", [128, TWS], F32)
    rtmp = kb.sb("rtmp", [128, TWS], F32)
    cgt = [kb.sb("cg%d" % i, [128, TW], F32) for i in range(2)]
    cvt = [kb.sb("cv%d" % i, [128, TW], F32) for i in range(2)]
    sgt = [kb.sb("sg%d" % i, [128, TW], F32) for i in range(2)]
    cw = kb.sb("cw", [128, 4, NUC], F32)
    uprev = kb.sb("uprev", [128, NUC, 2], F32)
    cst = kb.sb("cst", [128, NUC, 2, NS], F32)
    csout = kb.sb("csout", [128, NUC, 2, NS], F32)
    vtmp = kb.sb("vtmp", [NUC, 128], F32)
    vtmp2 = kb.sb("vtmp2", [NUC, 128], F32)
    rr = {"sq": 0, "cg": 0, "stg": 0}

    mT_flat = mT[:].rearrange("p a b -> p (a b)")
    aT_flat = aT[:].rearrange("p a b -> p (a b)")

    def segs_of(t):
        s = [(0, TW, 1)]
        if t == NT - 1:
            s.append((TW, NS, NS))
        return s

    def fm_load(dst_ap, dst_key, dram_row):
        dma("sp", vtmp[:], dram_row.rearrange("(c p) -> c p", p=128), writes=["vtmp"])
        pb, pk = psum()
        op("pe", lambda e: e.transpose(pb[:, 0:NUC], vtmp[:, :], ident_f[0:NUC, 0:NUC]),
           reads=["vtmp", "ident_f"], writes=[pk])
        op("act", lambda e: e.copy(dst_ap, pb[:, 0:NUC]), reads=[pk], writes=[dst_key])

    def fm_store(src_ap, src_key, dram_row):
        op("act", lambda e: e.copy(rtmp[:, 0:NUC], src_ap), reads=(src_key if isinstance(src_key, list) else [src_key]), writes=["rtmp"])
        pb, pk = psum()
        op("pe", lambda e: e.transpose(pb[0:NUC, 0:128], rtmp[:, 0:NUC], ident_f[:, :]),
           reads=["rtmp", "ident_f"], writes=[pk])
        op("act", lambda e: e.copy(vtmp2[:, :], pb[0:NUC, 0:128]), reads=[pk], writes=["vtmp2"])
        dma("sp", dram_row.rearrange("(c p) -> c p", p=128), vtmp2[:], reads=["vtmp2"])

    def stats_begin():
        return psum(7, 8)

    def stats_add(pst, src_ap, src_key, c0, n, first, last):
        pb, pk = pst
        s = sq[rr["sq"] % 2]
        sk = "sq%d" % (rr["sq"] % 2)
        rr["sq"] += 1
        op("act", lambda e: e.activation(out=s[:, 0:n], in_=src_ap, func=AF.Square), reads=[src_key], writes=[sk])
        op("pe", lambda e: e.matmul(pb[:, 0:n], ones_b[:, :], s[:, 0:n], start=first, stop=last),
           reads=[sk, "ones_b"], writes=[pk], signal=True)

    def stats_end(pst, c0, n):
        pb, pk = pst
        op("act", lambda e: e.activation(out=rtmp[:, c0:c0 + n], in_=pb[:, 0:n], func=AF.Sqrt, bias=epsb[:, 0:1], scale=1.0 / D),
           reads=[pk, "epsb"], writes=["rtmp"])
        op("dve", lambda e: e.reciprocal(rstd[:, c0:c0 + n], rtmp[:, c0:c0 + n]), reads=["rtmp"], writes=["rstd"])

    def prenorm(i, j, segs):
        for (c0, n, S) in segs:
            pst = stats_begin()
            for kc in range(KC):
                stats_add(pst, xT[:, kc, c0:c0 + n], "xT", c0, n, kc == 0, kc == KC - 1)
            stats_end(pst, c0, n)
            for kc in range(KC):
                op("dve", lambda e: e.scalar_tensor_tensor(out=hT[:, kc, c0:c0 + n], in0=xT[:, kc, c0:c0 + n],
                                                           scalar=gs(i, j, kc), in1=rstd[:, c0:c0 + n],
                                                           op0=ALU.mult, op1=ALU.mult),
                   reads=["xT", "rstd", "gsc"], writes=["hT"])

    def postnorm_residual(i, j, segs):
        for (c0, n, S) in segs:
            for kc in range(KC):
                op("dve", lambda e: e.scalar_tensor_tensor(out=mT[:, kc, c0:c0 + n], in0=mT[:, kc, c0:c0 + n],
                                                           scalar=gs(i, j, kc), in1=rstd[:, c0:c0 + n],
                                                           op0=ALU.mult, op1=ALU.mult),
                   reads=["mT", "rstd", "gsc"], writes=["mT"])
                op("dve", lambda e: e.tensor_tensor(out=xT[:, kc, c0:c0 + n], in0=xT[:, kc, c0:c0 + n],
                                                    in1=mT[:, kc, c0:c0 + n], op=ALU.add),
                   reads=["mT", "xT"], writes=["xT"])

    def xT_store(t, segs):
        g0 = t * TW
        dma("sp", xT_s[:, :, g0:g0 + TW].rearrange("k p t -> p k t"), xT[:, :, 0:TW], reads=["xT"], writes=["xT_s%d" % t])
        if len(segs) > 1:
            dma("sp", xT_s[:, :, T:TC].rearrange("k p t -> p k t"), xT[:, :, TW:TWS], reads=["xT"], writes=["xT_ss"])

    def xT_load(t, segs):
        g0 = t * TW
        dma("sp", xT[:, :, 0:TW], xT_s[:, :, g0:g0 + TW].rearrange("k p t -> p k t"), reads=["xT_s%d" % t], writes=["xT"])
        if len(segs) > 1:
            dma("sp", xT[:, :, TW:TWS], xT_s[:, :, T:TC].rearrange("k p t -> p k t"), reads=["xT_ss"], writes=["xT"])

    stg_f = mT_flat
    stg_b = aT_flat
    rt = kb.sb("ropetmp", [128, 4, 2, 16], F32)

    def qkv_phase(li, t, segs):
        blocks = [(b, 128) for b in range(4)]
        if len(segs) > 1:
            blocks.append((4, NS))
        kb.barrier()
        for ct in range(36):
            wv, wk = wr.get(("qkv", li, ct))
            w3 = wv[:, 0:KC * 256].rearrange("p (k c) -> p k c", k=KC)
            s = ct // 12
            g = (ct % 12) // 4
            hp = ct % 4
            for (bl, m) in blocks:
                gb = t * 4 + bl if bl < 4 else 16
                pb, pk = psum()
                for kc in range(KC):
                    lhs = hT[:, kc, bl * 128:bl * 128 + m] if bl < 4 else hT[:, kc, TW:TWS]
                    op("pe", lambda e: e.matmul(pb[0:m, 0:256], lhs, w3[:, kc, :], start=(kc == 0), stop=(kc == KC - 1)),
                       reads=["hT", wk], writes=[pk], signal=(kc == KC - 1))
                slot = rr["stg"] % 4
                rr["stg"] += 1
                sf = stg_f[:, slot * 256:(slot + 1) * 256]
                sfk = ("sf", slot)
                op("act", lambda e: e.copy(sf[0:m, :], pb[0:m, 0:256]), reads=[pk], writes=[sfk])
                sf3 = sf.rearrange("p (h d) -> p h d", h=2)
                if s < 2:
                    cosb = rope[0:m, gb, 0:16].unsqueeze(1).broadcast_to([m, 2, 16])
                    sinb = rope[0:m, gb, 16:32].unsqueeze(1).broadcast_to([m, 2, 16])
                    x1 = sf3[0:m, :, 0:16]
                    x2 = sf3[0:m, :, 16:32]
                    t1, t2, t3, t4 = (rt[0:m, k, :, :] for k in range(4))
                    op("dve", lambda e: e.tensor_tensor(out=t1, in0=x1, in1=cosb, op=ALU.mult), reads=[sfk, "rope"], writes=["rt1"])
                    op("dve", lambda e: e.tensor_tensor(out=t2, in0=x2, in1=sinb, op=ALU.mult), reads=[sfk, "rope"], writes=["rt2"])
                    op("dve", lambda e: e.tensor_tensor(out=t3, in0=x2, in1=cosb, op=ALU.mult), reads=[sfk, "rope"], writes=["rt3"])
                    op("dve", lambda e: e.tensor_tensor(out=t4, in0=x1, in1=sinb, op=ALU.mult), reads=[sfk, "rope"], writes=["rt4"])
                    op("dve", lambda e: e.tensor_tensor(out=x1, in0=t1, in1=t2, op=ALU.subtract), reads=["rt1", "rt2", "rt3", "rt4"], writes=[sfk])
                    op("dve", lambda e: e.tensor_tensor(out=x2, in0=t3, in1=t4, op=ALU.add), reads=["rt3", "rt4"], writes=[sfk])
                if s >= 1:
                    half = (s - 1) * 1024 + hp * 256
                    if bl < 4:
                        keep = min(WIN[g], T)
                        row0 = gb * 128 - (T - keep)
                        if row0 >= 0:
                            dma("sp", kvp[g][li, row0:row0 + 128, half:half + 256], sf[:, :], reads=[sfk])
                    else:
                        dma("sp", kvs[g][li, :, half:half + 256], sf[0:m, :], reads=[sfk])
                sb_ = stg_b[:, slot * 256:(slot + 1) * 256]
                sbk = ("sb", slot)
                tbk = ("tb", slot)
                if not (bl == 4 and s == 2):
                    op("pool", lambda e: e.tensor_copy(out=sb_[0:m, :], in_=sf[0:m, :]), reads=[sfk], writes=[sbk])
                if s == 2 and bl < 4:
                    dma("sp", v_s[g, gb * 128:(gb + 1) * 128, hp * 256:(hp + 1) * 256], sb_[:, :], reads=[sbk], writes=["v_s"])
                    continue
                if s == 2:
                    pt, ptk = psum()
                    for hh in range(2):
                        op("pe", lambda e: e.transpose(pt[:, hh * NS:(hh + 1) * NS], sf[0:m, hh * 128:(hh + 1) * 128], ident_f[0:m, 0:m]),
                           reads=[sfk, "ident_f"], writes=[ptk], signal=(hh == 1))
                    op("dve", lambda e: e.tensor_copy(out=svT[:, g * 8 + 2 * hp:g * 8 + 2 * hp + 2, :],
                                                      in_=pt[:, 0:2 * NS].rearrange("p (h s) -> p h s", h=2)),
                       reads=[ptk], writes=["svT"])
                    continue
                pt, ptk = psum()
                ptb = pt[:, 0:256].bitcast(BF16)
                for hh in range(2):
                    op("pe", lambda e: e.transpose(ptb[:, hh * 128:hh * 128 + m], sb_[0:m, hh * 128:(hh + 1) * 128], ident_b[0:m, 0:m]),
                       reads=[sbk, "ident_b"], writes=[ptk], signal=(hh == 1))
                if bl < 4:
                    tb = stg_b[:, 1024 + slot * 256:1024 + (slot + 1) * 256]
                    op("dve", lambda e: e.tensor_copy(out=tb, in_=ptb[:, 0:256]), reads=[ptk], writes=[tbk])
                    dst = (qT_s if s == 0 else kT_s)[g, 2 * hp:2 * hp + 2, :, gb * 128:(gb + 1) * 128].rearrange("h p t -> p h t")
                    dma("sp", dst, tb.rearrange("p (h t) -> p h t", h=2), reads=[tbk], writes=["qT_s" if s == 0 else "kT_s"])
                else:
                    dstt = sqT if s == 0 else skT
                    op("dve", lambda e: e.tensor_copy(out=dstt[:, g * 8 + 2 * hp:g * 8 + 2 * hp + 2, :],
                                                      in_=ptb[:, 0:256].rearrange("p (h t) -> p h t", h=2)[:, :, 0:NS]),
                       reads=[ptk], writes=["sqT" if s == 0 else "skT"])

    _qkv_inner = qkv_phase

    def qkv_phase(li, t, segs):
        _qkv_inner(li, t, segs)
        kb.barrier()

    def attn_phase(li):
        kb.barrier()
        es = contextlib.ExitStack()
        aTf = aT[:].rearrange("p a b -> p (a b)")
        hTf = hT[:].rearrange("p a b -> p (a b)")
        xTb = xT[:].rearrange("p a b -> p (a b)").bitcast(BF16)
        mTf = mT[:].rearrange("p a b -> p (a b)")
        QN = 3 * T
        qTt = [aTf[:, i * QN:(i + 1) * QN].rearrange("p (g t) -> p g t", g=3) for i in range(2)]
        kTt = [aTf[:, 2 * QN:3 * QN].rearrange("p (g t) -> p g t", g=3), hTf[:, 0:QN].rearrange("p (g t) -> p g t", g=3)]
        vt = [xTb[:, i * QN:(i + 1) * QN].rearrange("p (g b d) -> p g b d", g=3, b=16) for i in range(2)]
        pT = [hTf[:, QN + i * 512:QN + (i + 1) * 512] for i in range(3)]
        mh = [xTb[:, 2 * QN + i * TC:2 * QN + (i + 1) * TC] for i in range(2)]
        mbq = aTf[:, 3 * QN:3 * QN + 1792].bitcast(F32).rearrange("p (m k) -> p m k", m=7)
        smk = mTf[:, 0:2048]
        junk = mTf[:, 2048:3072].bitcast(BF16)
        acc2 = mTf[:, 4096:4224]
        D3 = mTf[:, 3072:3456]
        cbc = mTf[:, 3456:3840]
        acc = mTf[:, 3840:4096]
        ones_f = kb.sb("ones_f", [128, 128], F32, es)
        l0 = kb.sb("l0", [128, 16, 3], F32, es)
        colm = kb.sb("colm", [128, 8, 3], F32, es)
        dma("sp", mbq, c_mbq[:, :, :], writes=["mbq"])
        op("dve", lambda e: e.memset(ones_f[:], 1.0), writes=["ones_f"])

        def load_head(h):
            b = h % 2
            for g in range(3):
                dma("sp", qTt[b][:, g, :], qT_s[g, h, :, :], reads=["qT_s"], writes=[("qTt", b, g)])
                dma("sp", kTt[b][:, g, :], kT_s[g, h, :, :], reads=["kT_s"], writes=[("kTt", b, g)])
                dma("sp", vt[b][:, g, :, :], v_s[g, :, h * 128:(h + 1) * 128].rearrange("(b p) d -> p b d", p=128),
                    reads=["v_s"], writes=[("vt", b, g)])

        load_head(0)
        pidx = 0
        it = 0
        for h in range(8):
            if h + 1 < 8:
                load_head(h + 1)
            b = h % 2
            for qb in range(16):
                groups = []
                for g in range(3):
                    nb = WIN[g] // 128
                    lst = []
                    for kbk in range(max(0, qb - nb), qb + 1):
                        db = qb - kbk
                        if g == 0:
                            mi = 0 if db == 0 else 1
                        elif g == 1:
                            mi = 2 if db == 0 else (4 if db == 4 else 3)
                        else:
                            mi = 5 if db == 0 else 6
                        lst.append((kbk, mi))
                    groups.append(lst)
                po, pok = psum(4, 5) if it % 2 == 0 else psum(6, 7)
                pc, pck = psum(5, 6) if it % 2 == 0 else psum(7, 8)
                it += 1
                qsl = qTt[b][:, :, qb * 128:(qb + 1) * 128]
                for g in range(3):
                    lst = groups[g]
                    nk = len(lst) * 128
                    for c4 in range(0, len(lst), 4):
                        ch = lst[c4:c4 + 4]
                        ps_, psk = psum(2, 4)
                        k0 = ch[0][0]
                        op("pe", lambda e: e.matmul(ps_[:, 0:len(ch) * 128], qsl[:, g, :], kTt[b][:, g, k0 * 128:(k0 + len(ch)) * 128],
                                                    start=True, stop=True),
                           reads=[("qTt", b, g), ("kTt", b, g)], writes=[psk])
                        for ci, (kbk, mi) in enumerate(ch):
                            op("dve", lambda e: e.tensor_tensor(out=smk[:, (c4 + ci) * 128:(c4 + ci + 1) * 128], in0=ps_[:, ci * 128:(ci + 1) * 128],
                                                                in1=mbq[:, mi, :], op=ALU.add),
                               reads=[psk, "mbq"], writes=[("smk", c4 + ci)])
                    smks = [("smk", k_) for k_ in range(len(lst))]
                    op("dve", lambda e: e.tensor_reduce(out=colm[:, 0, g:g + 1], in_=smk[:, 0:nk], axis=AX.X, op=ALU.max),
                       reads=smks, writes=[("c0", g)])
                    op("dve", lambda e: e.tensor_scalar(out=colm[:, 1, g:g + 1], in0=colm[:, 0, g:g + 1], scalar1=-SCALE, scalar2=None, op0=ALU.mult),
                       reads=[("c0", g)], writes=[("c1", g)])
                    op("act", lambda e: e.activation(out=junk[:, 0:nk], in_=smk[:, 0:nk], func=AF.Exp, bias=colm[:, 1, g:g + 1], scale=SCALE,
                                                     accum_out=colm[:, 2, g:g + 1]),
                       reads=smks + [("c1", g)], writes=[("c2", g)])
                    done = 0
                    for c4 in range(0, len(lst), 4):
                        ch = lst[c4:c4 + 4]
                        pb, pk = psum(0, 2)
                        for ci, (kbk, mi) in enumerate(ch):
                            op("pe", lambda e: e.matmul(pb[:, ci * 128:(ci + 1) * 128], kTt[b][:, g, kbk * 128:(kbk + 1) * 128], qsl[:, g, :],
                                                        start=True, stop=True),
                               reads=[("kTt", b, g), ("qTt", b, g)], writes=[pk], signal=(ci == len(ch) - 1))
                        p_ = pT[pidx % 3]
                        pk_ = "pT%d" % (pidx % 3)
                        pidx += 1
                        nc_ = len(ch) * 128
                        op("act", lambda e: e.activation(out=p_[:, 0:nc_], in_=pb[:, 0:nc_], func=AF.Exp, scale=SCALE), reads=[pk], writes=[pk_])
                        for ci, (kbk, mi) in enumerate(ch):
                            op("pool", lambda e: e.tensor_tensor(out=p_[:, ci * 128:(ci + 1) * 128], in0=p_[:, ci * 128:(ci + 1) * 128],
                                                                 in1=masks[:, mi, :], op=ALU.mult),
                               reads=[pk_, "masks"], writes=[pk_])
                        for ci, (kbk, mi) in enumerate(ch):
                            op("pe", lambda e: e.matmul(po[:, g * 128:(g + 1) * 128], vt[b][:, g, kbk, :], p_[:, ci * 128:(ci + 1) * 128],
                                                        start=(done == 0), stop=(done == len(lst) - 1)),
                               reads=[("vt", b, g), pk_], writes=[pok], signal=True)
                            done += 1
                c1s = [("c1", g_) for g_ in range(3)]
                c2s = [("c2", g_) for g_ in range(3)]
                op("act", lambda e: e.activation(out=colm[:, 3, :], in_=colm[:, 1, :], func=AF.Exp, scale=-1.0), reads=c1s, writes=["c3"])
                op("dve", lambda e: e.tensor_tensor(out=colm[:, 4, :], in0=colm[:, 2, :], in1=colm[:, 3, :], op=ALU.mult), reads=c2s + ["c3"], writes=["c4"])
                op("dve", lambda e: e.tensor_reduce(out=colm[:, 7, 0:1], in_=colm[:, 4, :], axis=AX.X, op=ALU.add), reads=["c4"], writes=["c7"])
                if h == 0:
                    op("dve", lambda e: e.tensor_copy(out=l0[:, qb, :], in_=colm[:, 2, :]), reads=c2s, writes=["l0"])
                op("dve", lambda e: e.tensor_scalar(out=colm[:, 5, :], in0=l0[:, qb, :], scalar1=colm[:, 7, 0:1], scalar2=None, op0=ALU.mult),
                   reads=["c7", "l0"], writes=["c5"])
                op("dve", lambda e: e.reciprocal(colm[:, 5, :], colm[:, 5, :]), reads=["c5"], writes=["c5"])
                op("dve", lambda e: e.tensor_tensor(out=colm[:, 6, :], in0=colm[:, 2, :], in1=colm[:, 5, :], op=ALU.mult), reads=c2s + ["c5"], writes=["c6"])
                for g in range(3):
                    op("dve", lambda e: e.tensor_scalar(out=D3[:, g * 128:(g + 1) * 128], in0=ident_f[:, :], scalar1=colm[:, 6, g:g + 1], scalar2=None, op0=ALU.mult),
                       reads=["c6", "ident_f"], writes=[("D3", g)])
                op("pe", lambda e: e.matmul(pc[:, 0:384], ones_f[:, :], D3[:, :], start=True, stop=True), reads=["ones_f"] + [("D3", g_) for g_ in range(3)], writes=[pck])
                op("act", lambda e: e.copy(cbc[:, :], pc[:, 0:384]), reads=[pck], writes=["cbc"])
                op("dve", lambda e: e.tensor_tensor(out=acc[:, 0:128], in0=po[:, 0:128], in1=cbc[:, 0:128], op=ALU.mult), reads=[pok, "cbc"], writes=["acc0"])
                op("dve", lambda e: e.tensor_tensor(out=acc[:, 128:256], in0=po[:, 128:256], in1=cbc[:, 128:256], op=ALU.mult), reads=[pok, "cbc"], writes=["acc1"])
                op("dve", lambda e: e.tensor_tensor(out=acc2, in0=po[:, 256:384], in1=cbc[:, 256:384], op=ALU.mult), reads=[pok, "cbc"], writes=["acc2"])
                op("dve", lambda e: e.tensor_tensor(out=acc[:, 0:128], in0=acc[:, 0:128], in1=acc[:, 128:256], op=ALU.add), reads=["acc0", "acc1"], writes=["acc0"])
                op("dve", lambda e: e.tensor_tensor(out=mh[b][:, qb * 128:(qb + 1) * 128], in0=acc[:, 0:128], in1=acc2, op=ALU.add),
                   reads=["acc0", "acc2"], writes=["mh%d" % b])
            dma("sp", mT_s[h, :, 0:T], mh[b][:, 0:T], reads=["mh%d" % b], writes=["mT_s"])

        kb.barrier()
        cache = [mTf[:, i * 2048:(i + 1) * 2048] for i in range(2)]
        cb16 = [mTf[:, 4096 + i * 512:4096 + (i + 1) * 512].bitcast(BF16) for i in range(2)]
        vkeep = [mTf[:, 5120 + g * 512:5120 + (g + 1) * 512].bitcast(BF16) for g in range(3)]
        ckT = kb.sb("ckT", [128, 8, 128], BF16, es)
        qk = kb.sb("qk", [128, 24 * NS], F32, es)
        qkr = kb.sb("qkr", [1, 24 * NS], F32, es)
        srow = aTf[0:1, 0:2064].bitcast(F32).rearrange("p (h k) -> p h k", h=8)
        prow = aTf[0:1, 2064:2064 + 6192].bitcast(F32).rearrange("p (g h k) -> p g h k", g=3, h=8)
        rw = kb.sb("rw", [1, 12, 24], F32, es)
        one1 = kb.sb("one1", [1, 1], F32, es)
        pSk = kb.sb("pSk", [128, 3, 8], BF16, es)
        pnb = kb.sb("pnb", [128, 3, 8], F32, es)
        og = kb.sb("og", [128, 3, 8], F32, es)
        cbs = kb.sb("cbs", [128, 3, 8], F32, es)
        so = kb.sb("so", [128, 8], F32, es)
        sob = kb.sb("sob", [128, 8, NS], BF16, es)
        op("dve", lambda e: e.memset(one1[:], 1.0), writes=["one1"])
        op("dve", lambda e: e.tensor_tensor(out=qk[:, :], in0=sqT[:].rearrange("p a s -> p (a s)"), in1=skT[:].rearrange("p a s -> p (a s)"), op=ALU.mult),
           reads=["sqT", "skT"], writes=["qk"])
        pb, pk = psum(0, 2)
        op("pe", lambda e: e.matmul(pb[0:1, 0:24 * NS], ones_f[:, 0:1], qk[:, :], start=True, stop=True), reads=["ones_f", "qk"], writes=[pk])
        op("act", lambda e: e.copy(qkr[:, :], pb[0:1, 0:24 * NS]), reads=[pk], writes=["qkr"])
        qkr3 = qkr[:].rearrange("p (a s) -> p a s", s=NS)
        rwm, rwmm, rwl, rwe, rwt, rwc = (rw[:, k, :].rearrange("p (g h) -> p g h", g=3) for k in range(6))
        for s_ in range(NS):
            pov, povk = psum(4, 5)
            for g in range(3):
                cbuf = cache[g % 2]
                ck = "cache%d" % (g % 2)
                src = ckv[g][li, s_, :, :].rearrange("(j d) c -> j d c", d=DIL[g])[:, 0, :]
                dma("sp", cbuf[:, :], src, writes=[ck])
                c16 = cb16[g % 2]
                c16k = "cb16_%d" % (g % 2)
                op("pool", lambda e: e.tensor_copy(out=c16[:, 0:1024], in_=cbuf[:, 0:1024]), reads=[ck], writes=[c16k])
                op("act", lambda e: e.copy(vkeep[g][:, :], cbuf[:, 1024:2048]), reads=[ck], writes=["vk%d" % g])
                for h in range(8):
                    pt, ptk = psum(0, 2)
                    ptb = pt[:, 0:64].bitcast(BF16)
                    op("pe", lambda e: e.transpose(ptb[:, 0:128], c16[:, h * 128:(h + 1) * 128], ident_b[:, :]),
                       reads=[c16k, "ident_b"], writes=[ptk])
                    op("dve", lambda e: e.tensor_copy(out=ckT[:, h, :], in_=ptb[:, 0:128]), reads=[ptk], writes=["ckT"])
                for hq in range(2):
                    pr, prk = psum(2, 4)
                    for hh in range(4):
                        h = hq * 4 + hh
                        op("pe", lambda e: e.matmul(pr[0:1, hh * 128:(hh + 1) * 128], sqT[:, g * 8 + h, s_:s_ + 1], ckT[:, h, :], start=True, stop=True),
                           reads=["sqT", "ckT"], writes=[prk], signal=(hh == 3))
                    op("act", lambda e: e.copy(srow[0:1, hq * 4:(hq + 1) * 4, 0:128], pr[0:1, :].rearrange("p (h k) -> p h k", h=4)),
                       reads=[prk], writes=["srow"])
                op("dve", lambda e: e.tensor_copy(out=srow[0:1, :, 128:129], in_=qkr3[0:1, g * 8:(g + 1) * 8, s_:s_ + 1]), reads=["qkr"], writes=["srow"])
                op("dve", lambda e: e.tensor_reduce(out=rwm[0:1, g, :], in_=srow[0:1, :, :], axis=AX.X, op=ALU.max), reads=["srow"], writes=["rw"])
                op("dve", lambda e: e.tensor_tensor(out=srow[0:1, :, :], in0=srow[0:1, :, :],
                                                    in1=rwm[0:1, g, :].unsqueeze(2).broadcast_to([1, 8, 129]), op=ALU.subtract),
                   reads=["srow", "rw"], writes=["srow"])
                op("act", lambda e: e.activation(out=prow[0:1, g, :, :], in_=srow[0:1, :, :], func=AF.Exp, scale=SCALE), reads=["srow"], writes=["prow"])
                op("dve", lambda e: e.tensor_reduce(out=rwl[0:1, g, :], in_=prow[0:1, g, :, :], axis=AX.X, op=ALU.add), reads=["prow"], writes=["rw"])
                pp, ppk = psum(2, 4)
                for h in range(8):
                    op("pe", lambda e: e.matmul(pp[:, h:h + 1], prow[0:1, g, h, 0:128], one1[0:1, 0:1], start=True, stop=True),
                       reads=["prow", "one1"], writes=[ppk], signal=(h == 7))
                op("dve", lambda e: e.tensor_copy(out=pSk[:, g, :], in_=pp[:, 0:8]), reads=[ppk], writes=["pSk"])
                pn_, pnk = psum(2, 4)
                op("pe", lambda e: e.matmul(pn_[:, 0:8], ones_f[0:1, :], prow[0:1, g, :, 128], start=True, stop=True), reads=["ones_f", "prow"], writes=[pnk])
                op("dve", lambda e: e.tensor_copy(out=pnb[:, g, :], in_=pn_[:, 0:8]), reads=[pnk], writes=["pnb"])
                for h in range(8):
                    op("pe", lambda e: e.matmul(pov[:, g * 8 + h:g * 8 + h + 1], vkeep[g][:, h * 128:(h + 1) * 128], pSk[:, g, h:h + 1], start=True, stop=True),
                       reads=["vk%d" % g, "pSk"], writes=[povk], signal=(h == 7))
            op("dve", lambda e: e.tensor_tensor(out=og[:, :, :], in0=svT[:, :, s_].rearrange("p (g h) -> p g h", g=3), in1=pnb[:, :, :], op=ALU.mult),
               reads=["svT", "pnb"], writes=["og"])
            op("dve", lambda e: e.tensor_tensor(out=og[:, :, :], in0=og[:, :, :], in1=pov[:, 0:24].rearrange("p (g h) -> p g h", g=3), op=ALU.add),
               reads=["og", povk], writes=["og"])
            op("dve", lambda e: e.tensor_scalar(out=rwmm[0:1, :, :], in0=rwm[0:1, :, :], scalar1=SCALE, scalar2=None, op0=ALU.mult), reads=["rw"], writes=["rw"])
            op("act", lambda e: e.activation(out=rwe[0:1, :, :], in_=rwmm[0:1, :, :], func=AF.Exp), reads=["rw"], writes=["rw"])
            op("dve", lambda e: e.tensor_tensor(out=rwe[0:1, :, :], in0=rwe[0:1, :, :], in1=rwl[0:1, :, :], op=ALU.mult), reads=["rw"], writes=["rw"])
            op("dve", lambda e: e.tensor_tensor(out=rwt[0:1, 0, :], in0=rwe[0:1, 0, :], in1=rwe[0:1, 1, :], op=ALU.add), reads=["rw"], writes=["rw"])
            op("dve", lambda e: e.tensor_tensor(out=rwt[0:1, 0, :], in0=rwt[0:1, 0, :], in1=rwe[0:1, 2, :], op=ALU.add), reads=["rw"], writes=["rw"])
            op("dve", lambda e: e.tensor_tensor(out=rwc[0:1, :, :], in0=rwt[0:1, 0, :].unsqueeze(1).broadcast_to([1, 3, 8]),
                                                in1=rwl[0:1, :, 0:1].broadcast_to([1, 3, 8]), op=ALU.mult), reads=["rw"], writes=["rw"])
            op("dve", lambda e: e.reciprocal(rwc[0:1, :, :], rwc[0:1, :, :]), reads=["rw"], writes=["rw"])
            op("dve", lambda e: e.tensor_tensor(out=rwc[0:1, :, :], in0=rwc[0:1, :, :], in1=rwe[0:1, :, :], op=ALU.mult), reads=["rw"], writes=["rw"])
            pcb, pcbk = psum(2, 4)
            op("pe", lambda e: e.matmul(pcb[:, 0:24], ones_f[0:1, :], rw[0:1, 5, :], start=True, stop=True), reads=["ones_f", "rw"], writes=[pcbk])
            op("dve", lambda e: e.tensor_tensor(out=og[:, :, :], in0=og[:, :, :], in1=pcb[:, 0:24].rearrange("p (g h) -> p g h", g=3), op=ALU.mult),
               reads=["og", pcbk], writes=["og"])
            op("dve", lambda e: e.tensor_tensor(out=so[:, :], in0=og[:, 0, :], in1=og[:, 1, :], op=ALU.add), reads=["og"], writes=["so"])
            op("dve", lambda e: e.tensor_tensor(out=sob[:, :, s_], in0=so[:, :], in1=og[:, 2, :], op=ALU.add), reads=["so", "og"], writes=["sob"])
        with nc.allow_non_contiguous_dma(reason="tiny sample columns"):
            dma("sp", mT_s[:, :, T:TC].rearrange("h p s -> p h s"), sob[:, :, :], reads=["sob"], writes=["mT_s"])
        kb.barrier()
        es.close()

    LCH = 32
    NCH = T // LCH

    def win_phase(li, t, segs):
        g0 = t * TW
        for dc in range(KC):
            wv, wk = wr.get(("win", li, dc))
            w3 = wv[:, 0:KC * 128].rearrange("p (k c) -> p k c", k=KC)
            for (c0, n, S) in segs:
                pb, pk = psum()
                for kc in range(KC):
                    op("pe", lambda e: e.matmul(pb[:, 0:n], w3[:, kc, :], hT[:, kc, c0:c0 + n], start=(kc == 0), stop=(kc == KC - 1)),
                       reads=["hT", wk], writes=[pk], signal=(kc == KC - 1))
                r = rr["cg"] % 2
                rr["cg"] += 1
                op("act", lambda e: e.copy(cgt[r][:, 0:n], pb[:, 0:n]), reads=[pk], writes=["cg%d" % r])
                gc = g0 if S == 1 else T
                if S == 1:
                    dma("sp", uT_s[dc, :, gc:gc + n], cgt[r][:, 0:n], reads=["cg%d" % r], writes=["uT_s"])
                else:
                    with nc.allow_non_contiguous_dma(reason="tiny sample columns"):
                        dma("sp", uT_s[dc, :, gc:gc + n], cgt[r][:, 0:n], reads=["cg%d" % r], writes=["uT_s"])

    def mixer_out_ssm(li, t, segs):
        g0 = t * TW
        minT = aT[:, 0:KC, :]
        dma("sp", minT[:, :, 0:TW], zT_s[:, :, g0:g0 + TW].rearrange("k p t -> p k t"), reads=["zT_s"], writes=["aT"])
        if len(segs) > 1:
            with nc.allow_non_contiguous_dma(reason="tiny sample columns"):
                dma("sp", minT[:, :, TW:TWS], zT_s[:, :, T:TC].rearrange("k p t -> p k t"), reads=["zT_s"], writes=["aT"])
        psts = [stats_begin() if si == 0 else psum(6, 7) for si in range(len(segs))]
        for dc in range(KC):
            wv, wk = wr.get(("glu", li, dc))
            wval = wv[:, 0:KC * 128].rearrange("p (k c) -> p k c", k=KC)
            wgat = wv[:, KC * 128:2 * KC * 128].rearrange("p (k c) -> p k c", k=KC)
            for si, (c0, n, S) in enumerate(segs):
                pv, pvk = psum()
                pg, pgk = psum()
                for kc in range(KC):
                    op("pe", lambda e: e.matmul(pv[:, 0:n], wval[:, kc, :], minT[:, kc, c0:c0 + n], start=(kc == 0), stop=(kc == KC - 1)),
                       reads=["aT", wk], writes=[pvk], signal=(kc == KC - 1))
                for kc in range(KC):
                    op("pe", lambda e: e.matmul(pg[:, 0:n], wgat[:, kc, :], minT[:, kc, c0:c0 + n], start=(kc == 0), stop=(kc == KC - 1)),
                       reads=["aT", wk], writes=[pgk], signal=(kc == KC - 1))
                r = rr["cg"] % 2
                rr["cg"] += 1
                op("act", lambda e: e.activation(out=sgt[r][:, 0:n], in_=pg[:, 0:n], func=AF.Sigmoid), reads=[pgk], writes=["sg%d" % r])
                op("dve", lambda e: e.tensor_tensor(out=mT[:, dc, c0:c0 + n], in0=pv[:, 0:n], in1=sgt[r][:, 0:n], op=ALU.mult),
                   reads=[pvk, "sg%d" % r], writes=["mT"])
                stats_add(psts[si], mT[:, dc, c0:c0 + n], "mT", c0, n, dc == 0, dc == KC - 1)
        for si, (c0, n, S) in enumerate(segs):
            stats_end(psts[si], c0, n)

    def ssm_phase(li):
        kb.barrier()
        es = contextlib.ExitStack()
        aTf = aT[:].rearrange("p a b -> p (a b)")
        hTf = hT[:].rearrange("p a b -> p (a b)")
        xTf = xT[:].rearrange("p a b -> p (a b)")
        mTf = mT[:].rearrange("p a b -> p (a b)")
        Wb = aTf[:, 0:16384].rearrange("p (k g m) -> p k g m", k=KC, g=8)
        npi_t = aTf[:, 16384:16384 + 4096].bitcast(F32).rearrange("p (a i) -> p a i", i=LCH)
        Wc = xTf[:, 0:8192].bitcast(BF16).rearrange("p (k j r m) -> p k j r m", k=KC, j=4, r=2)
        pr_parts = [cgt[0], cgt[1], cvt[0], cvt[1]]
        pi_parts = [sgt[0], sgt[1], rstd, rtmp]

        def tab(parts, pair):
            return parts[pair // 16][:, (pair % 16) * LCH:(pair % 16 + 1) * LCH]
        small = hTf[:, 0:3840].bitcast(F32).rearrange("p (k w) -> p k w", w=64)
        smallB = hTf[0:64, 3840:3840 + 4096].bitcast(F32).rearrange("p (k w) -> p k w", w=128)
        dsk = kb.sb("dsk", [128, KC], F32, es)
        gmask = kb.sb("gmask", [128, 8], F32, es)
        nat = kb.sb("nat", [128, 128], F32, es)
        dma("sp", gmask[:], c_gmask[:, :], writes=["gmask"])
        dma("sp", nat[0:KC, :], d_skip[li, :].rearrange("(k p) -> k p", p=128), writes=["nat"])
        pb, pk = psum(5, 8)
        op("pe", lambda e: e.transpose(pb[:, 0:KC], nat[0:KC, :], ident_f[0:KC, 0:KC]), reads=["nat", "ident_f"], writes=[pk])
        op("act", lambda e: e.copy(dsk[:, :], pb[:, 0:KC]), reads=[pk], writes=["dsk"])

        def abar_chain(P, W, sm, lam_r_ap, lam_i_ap, ldt_ap, key):
            S_ = lambda k: sm[0:P, k, 0:W]
            o = lambda fn, **kw: op("dve", fn, reads=[key], writes=[key])
            o(lambda e: e.tensor_scalar(out=S_(0), in0=lam_r_ap, scalar1=-1e-4, scalar2=None, op0=ALU.min))
            o(lambda e: e.tensor_copy(out=S_(1), in_=lam_i_ap))
            op("act", lambda e: e.activation(out=S_(2), in_=ldt_ap, func=AF.Exp), reads=[key], writes=[key])
            o(lambda e: e.tensor_tensor(out=S_(3), in0=S_(1), in1=S_(2), op=ALU.mult))
            o(lambda e: e.tensor_scalar(out=S_(3), in0=S_(3), scalar1=1.0 / 16.0, scalar2=None, op0=ALU.mult))
            o(lambda e: e.tensor_tensor(out=S_(4), in0=S_(3), in1=S_(3), op=ALU.mult))
            o(lambda e: e.tensor_scalar(out=S_(5), in0=S_(4), scalar1=1.0 / 362880.0, scalar2=None, op0=ALU.mult))
            for cf in (-1.0 / 5040.0, 1.0 / 120.0, -1.0 / 6.0):
                o(lambda e: e.scalar_tensor_tensor(out=S_(5), in0=S_(5), scalar=cf, in1=S_(4), op0=ALU.add, op1=ALU.mult))
            o(lambda e: e.scalar_tensor_tensor(out=S_(5), in0=S_(5), scalar=1.0, in1=S_(3), op0=ALU.add, op1=ALU.mult))
            o(lambda e: e.tensor_scalar(out=S_(6), in0=S_(4), scalar1=-1.0 / 3628800.0, scalar2=None, op0=ALU.mult))
            for cf in (1.0 / 40320.0, -1.0 / 720.0, 1.0 / 24.0, -0.5):
                o(lambda e: e.scalar_tensor_tensor(out=S_(6), in0=S_(6), scalar=cf, in1=S_(4), op0=ALU.add, op1=ALU.mult))
            o(lambda e: e.tensor_scalar(out=S_(6), in0=S_(6), scalar1=1.0, scalar2=None, op0=ALU.add))
            for _ in range(4):
                o(lambda e: e.tensor_tensor(out=S_(7), in0=S_(5), in1=S_(6), op=ALU.mult))
                o(lambda e: e.tensor_tensor(out=S_(11), in0=S_(5), in1=S_(5), op=ALU.mult))
                o(lambda e: e.tensor_scalar(out=S_(6), in0=S_(11), scalar1=-2.0, scalar2=1.0, op0=ALU.mult, op1=ALU.add))
                o(lambda e: e.tensor_scalar(out=S_(5), in0=S_(7), scalar1=2.0, scalar2=None, op0=ALU.mult))
            o(lambda e: e.tensor_tensor(out=S_(11), in0=S_(0), in1=S_(2), op=ALU.mult))
            op("act", lambda e: e.activation(out=S_(8), in_=S_(11), func=AF.Exp), reads=[key], writes=[key])
            o(lambda e: e.tensor_tensor(out=S_(9), in0=S_(8), in1=S_(6), op=ALU.mult))
            o(lambda e: e.tensor_tensor(out=S_(10), in0=S_(8), in1=S_(5), op=ALU.mult))
            return S_(9), S_(10), S_(0), S_(1)

        def load_A(dst, src2d):
            dma("sp", nat[0:64, :], src2d.rearrange("(g s) p -> g (s p)", s=2), writes=["nat"])
            pb, pk = psum(5, 8)
            op("pe", lambda e: e.transpose(pb[:, 0:64], nat[0:64, :], ident_f[0:64, 0:64]), reads=["nat", "ident_f"], writes=[pk])
            op("act", lambda e: e.copy(dst, pb[:, 0:64]), reads=[pk], writes=["small"])
        A_ = lambda k: small[:, k, :]
        load_A(A_(12), lam_re[li, :, :])
        load_A(A_(13), lam_im[li, :, :])
        dma("sp", nat[0:64, 0:2], log_dt[li, :].rearrange("(g s) -> g s", s=2), writes=["nat"])
        op("dve", lambda e: e.tensor_copy(out=nat[0:64, 64:128].rearrange("p (s q) -> p s q", s=2)[:, :, :] if False else smallB[0:64, 15, :].rearrange("p (s q) -> p s q", s=2),
                                          in_=nat[0:64, 0:2].unsqueeze(2).broadcast_to([64, 2, 64])), reads=["nat"], writes=["smallB"])
        pb, pk = psum(5, 8)
        op("pe", lambda e: e.transpose(pb[:, 0:64], smallB[0:64, 15, :], ident_f[0:64, 0:64]), reads=["smallB", "ident_f"], writes=[pk])
        op("act", lambda e: e.copy(A_(14), pb[:, 0:64]), reads=[pk], writes=["small"])
        arA, aiA, _, _ = abar_chain(128, 64, small, A_(12), A_(13), A_(14), "small")
        for pair0 in range(0, 64, 16):
            pass
        prv = lambda i: [p_[:, :].rearrange("p (a i) -> p a i", i=LCH)[:, :, i] for p_ in pr_parts]
        piv = lambda i: [p_[:, 0:512].rearrange("p (a i) -> p a i", i=LCH)[:, :, i] for p_ in pi_parts]
        tkeys = ["cg0", "cg1", "cv0", "cv1", "sg0", "sg1", "rstd", "rtmp", "npi"]
        for q in range(4):
            op("dve", lambda e: e.tensor_copy(out=prv(0)[q], in_=arA[:, q * 16:(q + 1) * 16]), reads=["small"], writes=tkeys)
            op("dve", lambda e: e.tensor_copy(out=piv(0)[q], in_=aiA[:, q * 16:(q + 1) * 16]), reads=["small"], writes=tkeys)
        for i in range(1, LCH):
            for q in range(4):
                a_r = arA[:, q * 16:(q + 1) * 16]
                a_i = aiA[:, q * 16:(q + 1) * 16]
                t1, t2 = small[:, 15, 0:16], small[:, 16, 0:16]
                op("dve", lambda e: e.tensor_tensor(out=t1, in0=prv(i - 1)[q], in1=a_r, op=ALU.mult), reads=tkeys + ["small"], writes=["small"])
                op("dve", lambda e: e.tensor_tensor(out=t2, in0=piv(i - 1)[q], in1=a_i, op=ALU.mult), reads=tkeys + ["small"], writes=["small"])
                op("dve", lambda e: e.tensor_tensor(out=prv(i)[q], in0=t1, in1=t2, op=ALU.subtract), reads=["small"], writes=tkeys)
                op("dve", lambda e: e.tensor_tensor(out=t1, in0=prv(i - 1)[q], in1=a_i, op=ALU.mult), reads=tkeys + ["small"], writes=["small"])
                op("dve", lambda e: e.tensor_tensor(out=t2, in0=piv(i - 1)[q], in1=a_r, op=ALU.mult), reads=tkeys + ["small"], writes=["small"])
                op("dve", lambda e: e.tensor_tensor(out=piv(i)[q], in0=t1, in1=t2, op=ALU.add), reads=["small"], writes=tkeys)
        for q in range(4):
            op("dve", lambda e: e.tensor_scalar(out=npi_t[:, q * 16:(q + 1) * 16, :], in0=pi_parts[q][:, 0:512].rearrange("p (a i) -> p a i", i=LCH),
                                                scalar1=-1.0, scalar2=None, op0=ALU.mult), reads=tkeys, writes=tkeys)

        B_ = lambda k: smallB[0:64, k, :]
        for (dst, src) in ((B_(12), lam_re), (B_(13), lam_im)):
            dma("sp", nat[:, 0:64], src[li, :, :], writes=["nat"])
            pb, pk = psum(5, 8)
            op("pe", lambda e: e.transpose(pb[0:64, 0:128], nat[:, 0:64], ident_f[:, :]), reads=["nat", "ident_f"], writes=[pk])
            op("act", lambda e: e.copy(dst, pb[0:64, 0:128]), reads=[pk], writes=["smallB"])
        dma("sp", B_(14), log_dt[li:li + 1, :].partition_broadcast(64).rearrange("p a g -> p (a g)") if False else log_dt[li:li + 1, :].broadcast_to([64, 128]), writes=["smallB"])
        arB, aiB, lrB, liB = abar_chain(64, 128, smallB, B_(12), B_(13), B_(14), "smallB")
        ob = lambda fn: op("dve", fn, reads=["smallB"], writes=["smallB"])
        ob(lambda e: e.tensor_scalar(out=B_(11), in0=arB, scalar1=-1.0, scalar2=None, op0=ALU.add))
        ob(lambda e: e.tensor_tensor(out=B_(7), in0=lrB, in1=lrB, op=ALU.mult))
        ob(lambda e: e.tensor_tensor(out=B_(2), in0=liB, in1=liB, op=ALU.mult))
        ob(lambda e: e.tensor_tensor(out=B_(7), in0=B_(7), in1=B_(2), op=ALU.add))
        ob(lambda e: e.reciprocal(B_(7), B_(7)))
        ob(lambda e: e.tensor_tensor(out=B_(3), in0=B_(11), in1=lrB, op=ALU.mult))
        ob(lambda e: e.tensor_tensor(out=B_(2), in0=aiB, in1=liB, op=ALU.mult))
        ob(lambda e: e.tensor_tensor(out=B_(3), in0=B_(3), in1=B_(2), op=ALU.add))
        ob(lambda e: e.tensor_tensor(out=B_(3), in0=B_(3), in1=B_(7), op=ALU.mult))
        ob(lambda e: e.tensor_tensor(out=B_(4), in0=aiB, in1=lrB, op=ALU.mult))
        ob(lambda e: e.tensor_tensor(out=B_(2), in0=B_(11), in1=liB, op=ALU.mult))
        ob(lambda e: e.tensor_tensor(out=B_(4), in0=B_(4), in1=B_(2), op=ALU.subtract))
        ob(lambda e: e.tensor_tensor(out=B_(4), in0=B_(4), in1=B_(7), op=ALU.mult))
        bre = mTf[0:64, 0:2048].rearrange("p (g c) -> p g c", c=16)
        bim = mTf[0:64, 2048:4096].rearrange("p (g c) -> p g c", c=16)
        bbr = mTf[0:64, 4096:6144].rearrange("p (g c) -> p g c", c=16)
        bbi = mTf[0:64, 6144:8192].rearrange("p (g c) -> p g c", c=16)
        with nc.allow_non_contiguous_dma(reason="b tensors 64B runs"):
            dma("sp", bre, b_re[li, :, :, :].rearrange("g p c -> p g c"), writes=["bb"])
            dma("sp", bim, b_im[li, :, :, :].rearrange("g p c -> p g c"), writes=["bb"])
        cr = B_(3).unsqueeze(2).broadcast_to([64, 128, 16])
        ci = B_(4).unsqueeze(2).broadcast_to([64, 128, 16])
        o2 = lambda fn: op("dve", fn, reads=["bb", "smallB"], writes=["bb"])
        o2(lambda e: e.tensor_tensor(out=bbr, in0=bre, in1=cr, op=ALU.mult))
        o2(lambda e: e.tensor_tensor(out=bbi, in0=bim, in1=ci, op=ALU.mult))
        o2(lambda e: e.tensor_tensor(out=bbr, in0=bbr, in1=bbi, op=ALU.subtract))
        o2(lambda e: e.tensor_tensor(out=bbi, in0=bre, in1=ci, op=ALU.mult))
        o2(lambda e: e.tensor_tensor(out=bre, in0=bim, in1=cr, op=ALU.mult))
        o2(lambda e: e.tensor_tensor(out=bbi, in0=bbi, in1=bre, op=ALU.add))
        for kc in range(KC):
            for r_, src in ((0, bbr), (1, bbi)):
                pb, pk = psum(5, 8)
                op("pe", lambda e: e.transpose(pb[:, 0:64], src[:, kc * 8:(kc + 1) * 8, :].rearrange("p g c -> p (g c)"), ident_f[0:64, 0:64]),
                   reads=["bb", "ident_f"], writes=[pk])
                for gl in range(8):
                    op("act" if gl % 2 else "dve",
                       (lambda e: e.activation(out=Wb[:, kc, gl, r_ * 64:(r_ + 1) * 64], in_=pb[:, 0:64], func=AF.Identity, scale=gmask[:, gl:gl + 1])) if gl % 2 else
                       (lambda e: e.tensor_scalar(out=Wb[:, kc, gl, r_ * 64:(r_ + 1) * 64], in0=pb[:, 0:64], scalar1=gmask[:, gl:gl + 1], scalar2=None, op0=ALU.mult)),
                       reads=[pk, "gmask"], writes=["Wb"])
        kb.barrier()
        Cn = [hTf[0:64, r_ * 4096:(r_ + 1) * 4096].bitcast(F32).rearrange("p (s c q) -> p s c q", s=2, c=16) for r_ in range(2)]
        dma("sp", Cn[0], c_re[li, :, :, :].rearrange("(g s) c q -> g s c q", s=2), writes=["Cn"])
        dma("sp", Cn[1], c_im[li, :, :, :].rearrange("(g s) c q -> g s c q", s=2), writes=["Cn"])
        op("dve", lambda e: e.memset(xTf[:, 0:8192], 0.0), writes=["Wc"])
        for r_ in range(2):
            for c in range(16):
                pb, pk = psum(5, 8)
                op("dve", lambda e: e.tensor_copy(out=nat[0:64, :].rearrange("p (s q) -> p s q", s=2), in_=Cn[r_][:, :, c, :]), reads=["Cn"], writes=["nat"])
                op("pe", lambda e: e.transpose(pb[:, 0:64], nat[0:64, :], ident_f[0:64, 0:64]), reads=["nat", "ident_f"], writes=[pk])
                for s in range(2):
                    for j4 in range(4):
                        src = pb[s * 64:(s + 1) * 64, 0:64].rearrange("p (k j) -> p k j", j=4)[:, :, j4]
                        dst = Wc[s * 64:(s + 1) * 64, :, j4, r_, 32 * j4 + 16 * s + c]
                        sc = 1.0 if r_ == 0 else -1.0
                        if (s + j4) % 2:
                            op("act", lambda e: e.mul(dst, src, sc), reads=[pk], writes=["Wc"])
                        else:
                            op("dve", lambda e: e.tensor_scalar(out=dst, in0=src, scalar1=sc, scalar2=None, op0=ALU.mult), reads=[pk], writes=["Wc"])
        kb.barrier()
        XR = [mTf[:, (2 * q) * TC:(2 * q + 1) * TC] for q in range(2)]
        XI = [mTf[:, (2 * q + 1) * TC:(2 * q + 2) * TC] for q in range(2)]
        ya = kb.sb("ya", [128, TW], F32, es)
        yb2 = kb.sb("yb2", [128, TW], F32, es)
        Xs = kb.sb("Xs", [128, 4, NCH], F32, es)
        h0t = kb.sb("h0t", [128, 64, NS, 2], F32, es)
        h0 = h0t[:]
        sto = kb.sb("sto", [128, 64, 2], F32, es)
        stos = kb.sb("stos", [128, NS, 64, 2], F32, es)
        ubf = hTf[:, 0:TC]
        xbr = hTf[:, TC:2 * TC]
        xbi = hTf[:, 2 * TC:3 * TC]
        zst = hTf[:, 3 * TC:4 * TC]
        for s_ in range(NS):
            with nc.allow_non_contiguous_dma(reason="state 8B runs"):
                dma("sp", h0[:, :, s_, :], st_ssm[li, s_, :].rearrange("(a p r) -> p a r", p=128, r=2), writes=["h0"])
        coltiles = [(tq * TW, TW) for tq in range(NT)] + [(T, NS)]
        X3R = [x[:, 0:T].rearrange("p (n l) -> p n l", l=LCH) for x in XR]
        X3I = [x[:, 0:T].rearrange("p (n l) -> p n l", l=LCH) for x in XI]
        STT = lambda o_, a_, sc_, b_: (lambda e: e.scalar_tensor_tensor(out=o_, in0=a_, scalar=sc_, in1=b_, op0=ALU.mult, op1=ALU.add))
        for kc in range(KC):
            dma("pool", ubf[:, :], uT_s[kc, :, :], reads=["uT_s"], writes=["ubf"])
            ybanks = [(pbank[k], ("ps", k)) for k in range(5)]
            for jp in range(2):
                pairs = [kc * 4 + 2 * jp + q for q in range(2)]
                tabs = []
                for q in range(2):
                    pair = pairs[q]
                    j4 = 2 * jp + q
                    PR, PI, NPI = tab(pr_parts, pair), tab(pi_parts, pair), npi_t[:, pair, :]
                    tabs.append((PR, PI, NPI))
                    for (c0, n) in coltiles:
                        for r_, dstx, dk in ((0, XR[q], ("xr", q)), (1, XI[q], ("xi", q))):
                            pb, pk = psum(5, 8)
                            for s in range(2):
                                op("pe", lambda e: e.matmul(pb[s * 64:(s + 1) * 64, 0:n], Wb[:, kc, 2 * j4 + s, r_ * 64:(r_ + 1) * 64], ubf[:, c0:c0 + n],
                                                            start=True, stop=True), reads=["Wb", "ubf"], writes=[pk], signal=(s == 1))
                            op("act", lambda e: e.copy(dstx[:, c0:c0 + n], pb[:, 0:n]), reads=[pk], writes=[dk])
                allk = [("xr", 0), ("xi", 0), ("xr", 1), ("xi", 1), "Xs"] + tkeys
                kb.fence("dve", reads=allk, writes=allk)
                ox = kb.opx
                l2 = [0, 0]
                l4 = [0, 0]
                for i in range(1, LCH):
                    c1 = [0, 0]
                    c3 = [0, 0]
                    for q in range(2):
                        PR, PI, NPI = tabs[q]
                        c1[q] = ox("dve", STT(X3R[q][:, :, i], X3R[q][:, :, i - 1], PR[:, 0:1], X3R[q][:, :, i]), after=[l2[q]])
                    for q in range(2):
                        PR, PI, NPI = tabs[q]
                        c3[q] = ox("dve", STT(X3I[q][:, :, i], X3I[q][:, :, i - 1], PR[:, 0:1], X3I[q][:, :, i]), after=[l4[q]])
                    for q in range(2):
                        PR, PI, NPI = tabs[q]
                        l2n = ox("dve", STT(X3R[q][:, :, i], X3I[q][:, :, i - 1], NPI[:, 0:1], X3R[q][:, :, i]), after=[c1[q], l4[q]])
                        l2[q] = l2n
                    for q in range(2):
                        PR, PI, NPI = tabs[q]
                        l4[q] = ox("dve", STT(X3I[q][:, :, i], X3R[q][:, :, i - 1], PI[:, 0:1], X3I[q][:, :, i]), after=[c3[q], l2[q] if False else c1[q]])
                XRs = [Xs[:, 2 * q, :] for q in range(2)]
                XIs = [Xs[:, 2 * q + 1, :] for q in range(2)]
                e2 = [0, 0]
                e4 = [0, 0]
                for q in range(2):
                    e2[q] = ox("dve", lambda e: e.tensor_copy(out=XRs[q], in_=X3R[q][:, :, LCH - 1]), after=[l2[q], l4[q]])
                    e4[q] = ox("dve", lambda e: e.tensor_copy(out=XIs[q], in_=X3I[q][:, :, LCH - 1]), after=[l2[q], l4[q]])
                for n_ in range(1, NCH):
                    a_, b_ = slice(n_ - 1, n_), slice(n_, n_ + 1)
                    c1 = [0, 0]
                    c3 = [0, 0]
                    for q in range(2):
                        PR, PI, NPI = tabs[q]
                        c1[q] = ox("dve", STT(XRs[q][:, b_], XRs[q][:, a_], PR[:, LCH - 1:LCH], XRs[q][:, b_]), after=[e2[q]])
                    for q in range(2):
                        PR, PI, NPI = tabs[q]
                        c3[q] = ox("dve", STT(XIs[q][:, b_], XIs[q][:, a_], PR[:, LCH - 1:LCH], XIs[q][:, b_]), after=[e4[q]])
                    for q in range(2):
                        PR, PI, NPI = tabs[q]
                        e2n = ox("dve", STT(XRs[q][:, b_], XIs[q][:, a_], NPI[:, LCH - 1:LCH], XRs[q][:, b_]), after=[c1[q], e4[q]])
                        e2[q] = e2n
                    for q in range(2):
                        PR, PI, NPI = tabs[q]
                        e4[q] = ox("dve", STT(XIs[q][:, b_], XRs[q][:, a_], PI[:, LCH - 1:LCH], XIs[q][:, b_]), after=[c3[q], c1[q]])
                for i in range(LCH):
                    ca = [0, 0]
                    cc = [0, 0]
                    for q in range(2):
                        PR, PI, NPI = tabs[q]
                        ca[q] = ox("dve", STT(X3R[q][:, 1:NCH, i], XRs[q][:, 0:NCH - 1], PR[:, i:i + 1], X3R[q][:, 1:NCH, i]), after=[e2[q], e4[q]])
                    for q in range(2):
                        PR, PI, NPI = tabs[q]
                        cc[q] = ox("dve", STT(X3I[q][:, 1:NCH, i], XIs[q][:, 0:NCH - 1], PR[:, i:i + 1], X3I[q][:, 1:NCH, i]), after=[e2[q], e4[q]])
                    for q in range(2):
                        PR, PI, NPI = tabs[q]
                        ox("dve", STT(X3R[q][:, 1:NCH, i], XIs[q][:, 0:NCH - 1], NPI[:, i:i + 1], X3R[q][:, 1:NCH, i]), after=[ca[q]])
                    for q in range(2):
                        PR, PI, NPI = tabs[q]
                        ox("dve", STT(X3I[q][:, 1:NCH, i], XRs[q][:, 0:NCH - 1], PI[:, i:i + 1], X3I[q][:, 1:NCH, i]), after=[cc[q]])
                kb.mark("dve", writes=[("xr", 0), ("xi", 0), ("xr", 1), ("xi", 1), "Xs"], reads=tkeys)
                for q in range(2):
                    pair = pairs[q]
                    j4 = 2 * jp + q
                    PR, PI, NPI = tabs[q]
                    ar, ai, nai = PR[:, 0:1], PI[:, 0:1], NPI[:, 0:1]
                    xr, xi = XR[q], XI[q]
                    xk, ik = ("xr", q), ("xi", q)
                    sc = lambda fn, rd, wrt: op("dve", fn, reads=rd + tkeys, writes=wrt)
                    xs_r, xs_i = xr[:, T:TC], xi[:, T:TC]
                    sc(STT(xs_r, h0[:, pair, :, 0], ar, xs_r), ["h0", xk], [xk])
                    sc(STT(xs_i, h0[:, pair, :, 1], ar, xs_i), ["h0", ik], [ik])
                    sc(STT(xs_r, h0[:, pair, :, 1], nai, xs_r), ["h0", xk], [xk])
                    sc(STT(xs_i, h0[:, pair, :, 0], ai, xs_i), ["h0", ik], [ik])
                    op("act", lambda e: e.copy(sto[:, pair, 0:1], xr[:, T - 1:T]), reads=[xk], writes=["sto"])
                    op("act", lambda e: e.copy(sto[:, pair, 1:2], xi[:, T - 1:T]), reads=[ik], writes=["sto"])
                    op("act", lambda e: e.copy(stos[:, :, pair, 0], xr[:, T:TC]), reads=[xk], writes=["stos"])
                    op("act", lambda e: e.copy(stos[:, :, pair, 1], xi[:, T:TC]), reads=[ik], writes=["stos"])
                    op("act", lambda e: e.copy(xbr[:, :], xr[:, :]), reads=[xk], writes=["xbr"])
                    op("pool", lambda e: e.tensor_copy(out=xbi[:, :], in_=xi[:, :]), reads=[ik], writes=["xbi"])
                    for ti, (c0, n) in enumerate(coltiles):
                        yb, ybk = ybanks[ti]
                        op("pe", lambda e: e.matmul(yb[:, 0:n], Wc[:, kc, j4, 0, :], xbr[:, c0:c0 + n], start=(j4 == 0), stop=False),
                           reads=["Wc", "xbr"], writes=[ybk], signal=False)
                        op("pe", lambda e: e.matmul(yb[:, 0:n], Wc[:, kc, j4, 1, :], xbi[:, c0:c0 + n], start=False, stop=(j4 == 3)),
                           reads=["Wc", "xbi"], writes=[ybk], signal=True)
            for ti, (c0, n) in enumerate(coltiles):
                yb, ybk = ybanks[ti]
                y_, w_ = ya[:, 0:n], yb2[:, 0:n]
                if n == TW:
                    dma("sp", y_, uT_s[kc, :, c0:c0 + n], reads=["uT_s"], writes=["ya"])
                else:
                    with nc.allow_non_contiguous_dma(reason="tiny sample columns"):
                        dma("sp", y_, uT_s[kc, :, c0:c0 + n], reads=["uT_s"], writes=["ya"])
                op("dve", lambda e: e.scalar_tensor_tensor(out=y_, in0=y_, scalar=dsk[:, kc:kc + 1], in1=yb[:, 0:n], op0=ALU.mult, op1=ALU.add),
                   reads=["ya", "dsk", ybk], writes=["ya"])
                op("dve", lambda e: e.tensor_tensor(out=w_, in0=y_, in1=y_, op=ALU.mult), reads=["ya"], writes=["yb2"])
                op("dve", lambda e: e.tensor_scalar(out=w_, in0=w_, scalar1=0.044715, scalar2=1.0, op0=ALU.mult, op1=ALU.add), reads=["yb2"], writes=["yb2"])
                op("dve", lambda e: e.tensor_tensor(out=w_, in0=w_, in1=y_, op=ALU.mult), reads=["yb2", "ya"], writes=["yb2"])
                op("act", lambda e: e.activation(out=w_, in_=w_, func=AF.Sigmoid, scale=GELU_C), reads=["yb2"], writes=["yb2"])
                op("dve", lambda e: e.tensor_tensor(out=zst[:, c0:c0 + n], in0=y_, in1=w_, op=ALU.mult), reads=["ya", "yb2"], writes=["zst"])
            dma("sp", zT_s[kc, :, :], zst[:, :], reads=["zst"], writes=["zT_s"])
        with nc.allow_non_contiguous_dma(reason="state 8B runs"):
            dma("sp", ssm_p[li, :].rearrange("(a p r) -> p a r", p=128, r=2), sto[:, :, :], reads=["sto"])
            for s_ in range(NS):
                dma("sp", ssm_s[li, s_, :].rearrange("(a p r) -> p a r", p=128, r=2), stos[:, s_, :, :], reads=["stos"])
        kb.barrier()
        es.close()

    def tile_tail(i_next, t, segs):
        prenorm(i_next, 0, segs)
        if i_next % 2 == 0:
            qkv_phase(i_next // 2, t, segs)
        else:
            win_phase(i_next // 2, t, segs)

    def mixer_out_attn(li, t, segs):
        g0 = t * TW
        minT = aT[:, 0:8, :]
        dma("sp", minT[:, :, 0:TW], mT_s[:, :, g0:g0 + TW].rearrange("h p t -> p h t"), reads=["mT_s"], writes=["aT"])
        if len(segs) > 1:
            with nc.allow_non_contiguous_dma(reason="tiny sample columns"):
                dma("sp", minT[:, :, TW:TWS], mT_s[:, :, T:TC].rearrange("h p t -> p h t"), reads=["mT_s"], writes=["aT"])
        psts = [stats_begin() if si == 0 else psum(6, 7) for si in range(len(segs))]
        for dc in range(KC):
            wv, wk = wr.get(("wo", li, dc))
            w3 = wv[:, 0:8 * 128].rearrange("p (k c) -> p k c", k=8)
            for si, (c0, n, S) in enumerate(segs):
                pb, pk = psum()
                for hh in range(8):
                    op("pe", lambda e: e.matmul(pb[:, 0:n], w3[:, hh, :], minT[:, hh, c0:c0 + n], start=(hh == 0), stop=(hh == 7)),
                       reads=["aT", wk], writes=[pk], signal=(hh == 7))
                op("act", lambda e: e.copy(mT[:, dc, c0:c0 + n], pb[:, 0:n]), reads=[pk], writes=["mT"])
                stats_add(psts[si], pb[:, 0:n], pk, c0, n, dc == 0, dc == KC - 1)
        for si, (c0, n, S) in enumerate(segs):
            stats_end(psts[si], c0, n)

    def ffn(i, t, segs):
        for j in range(NFC):
            wv, wk = wr.get(("up", i, j))
            wg = wv[:, 0:KC * 128].rearrange("p (k c) -> p k c", k=KC)
            wvv = wv[:, KC * 128:2 * KC * 128].rearrange("p (k c) -> p k c", k=KC)
            for (c0, n, S) in segs:
                pg, pgk = psum()
                pv, pvk = psum()
                for kc in range(KC):
                    op("pe", lambda e: e.matmul(pg[:, 0:n], wg[:, kc, :], hT[:, kc, c0:c0 + n], start=(kc == 0), stop=(kc == KC - 1)),
                       reads=["hT", wk], writes=[pgk], signal=(kc == KC - 1))
                for kc in range(KC):
                    op("pe", lambda e: e.matmul(pv[:, 0:n], wvv[:, kc, :], hT[:, kc, c0:c0 + n], start=(kc == 0), stop=(kc == KC - 1)),
                       reads=["hT", wk], writes=[pvk], signal=(kc == KC - 1))
                r = rr["cg"] % 2
                rr["cg"] += 1
                halves = []
                for (pp, ppk, jj, ct_, ck_) in ((pg, pgk, j, cgt[r], "cg%d" % r), (pv, pvk, NFC + j, cvt[r], "cv%d" % r)):
                    if S == 1:
                        halo, hk = uprev[:, jj, :], ("uprev", jj)
                    else:
                        halo, hk = cst[:, jj, :, :].rearrange("p a b -> p (a b)"), "cst"
                    halves.append((pp, ppk, jj, ct_, ck_, halo, hk, cw[:, 0, jj:jj + 1], cw[:, 1, jj:jj + 1], cw[:, 2, jj:jj + 1], cw[:, 3, jj:jj + 1]))
                m2 = min(2 * S, n)
                for (pp, ppk, jj, ct_, ck_, halo, hk, w0, w1, w2, bb) in halves:
                    op("act", lambda e: e.activation(out=ct_[:, 0:n], in_=pp[:, 0:n], func=AF.Identity, bias=bb, scale=w2),
                       reads=[ppk, "cw"], writes=[ck_])
                if n > S:
                    for (pp, ppk, jj, ct_, ck_, halo, hk, w0, w1, w2, bb) in halves:
                        op("dve", lambda e: e.scalar_tensor_tensor(out=ct_[:, S:n], in0=pp[:, 0:n - S], scalar=w1, in1=ct_[:, S:n],
                                                                   op0=ALU.mult, op1=ALU.add), reads=[ppk, "cw", ck_], writes=[ck_])
                if n > 2 * S:
                    for (pp, ppk, jj, ct_, ck_, halo, hk, w0, w1, w2, bb) in halves:
                        op("dve", lambda e: e.scalar_tensor_tensor(out=ct_[:, 2 * S:n], in0=pp[:, 0:n - 2 * S], scalar=w0, in1=ct_[:, 2 * S:n],
                                                                   op0=ALU.mult, op1=ALU.add), reads=[ppk, "cw", ck_], writes=[ck_])
                for (pp, ppk, jj, ct_, ck_, halo, hk, w0, w1, w2, bb) in halves:
                    op("dve", lambda e: e.scalar_tensor_tensor(out=ct_[:, 0:S], in0=halo[:, S:2 * S], scalar=w1, in1=ct_[:, 0:S],
                                                               op0=ALU.mult, op1=ALU.add), reads=[hk, "cw", ck_], writes=[ck_])
                for (pp, ppk, jj, ct_, ck_, halo, hk, w0, w1, w2, bb) in halves:
                    op("dve", lambda e: e.scalar_tensor_tensor(out=ct_[:, 0:m2], in0=halo[:, 0:m2], scalar=w0, in1=ct_[:, 0:m2],
                                                               op0=ALU.mult, op1=ALU.add), reads=[hk, "cw", ck_], writes=[ck_])
                for (pp, ppk, jj, ct_, ck_, halo, hk, w0, w1, w2, bb) in halves:
                    if S == 1:
                        op("dve", lambda e: e.tensor_copy(out=uprev[:, jj, :], in_=pp[:, n - 2:n]), reads=[ppk], writes=[hk])
                    else:
                        op("dve", lambda e: e.tensor_copy(out=csout[:, jj, 0, :], in_=cst[:, jj, 1, :]), reads=["cst"], writes=["csout"])
                        op("dve", lambda e: e.tensor_copy(out=csout[:, jj, 1, :], in_=pp[:, 0:NS]), reads=[ppk], writes=["csout"])
                sg_ = sgt[r]
                op("act", lambda e: e.activation(out=sg_[:, 0:n], in_=cgt[r][:, 0:n], func=AF.Silu), reads=["cg%d" % r], writes=["sg%d" % r])
                op("dve", lambda e: e.tensor_tensor(out=aT[:, j, c0:c0 + n], in0=sg_[:, 0:n], in1=cvt[r][:, 0:n], op=ALU.mult),
                   reads=["sg%d" % r, "cv%d" % r], writes=["aT"])
        psts = [stats_begin() if si == 0 else psum(6, 7) for si in range(len(segs))]
        for dc in range(KC):
            wv, wk = wr.get(("down", i, dc))
            w3 = wv[:, 0:NFC * 128].rearrange("p (k c) -> p k c", k=NFC)
            for si, (c0, n, S) in enumerate(segs):
                pb, pk = psum()
                for j in range(NFC):
                    op("pe", lambda e: e.matmul(pb[:, 0:n], w3[:, j, :], aT[:, j, c0:c0 + n], start=(j == 0), stop=(j == NFC - 1)),
                       reads=["aT", wk], writes=[pk], signal=(j == NFC - 1))
                op("act", lambda e: e.copy(mT[:, dc, c0:c0 + n], pb[:, 0:n]), reads=[pk], writes=["mT"])
                stats_add(psts[si], pb[:, 0:n], pk, c0, n, dc == 0, dc == KC - 1)
        for si, (c0, n, S) in enumerate(segs):
            stats_end(psts[si], c0, n)

    def out_y(t, segs):
        ytok = mT_flat
        for bl in range(4):
            for q4 in range(4):
                pb, pk = psum()
                for k4 in range(4):
                    kc = q4 * 4 + k4
                    op("pe", lambda e: e.transpose(pb[:, k4 * 128:(k4 + 1) * 128], xT[:, kc, bl * 128:(bl + 1) * 128], ident_f[:, :]),
                       reads=["xT", "ident_f"], writes=[pk], signal=(k4 == 3))
                op("act", lambda e: e.copy(ytok[:, (bl % 2) * D + q4 * 512:(bl % 2) * D + (q4 + 1) * 512], pb[:, :]), reads=[pk], writes=["mT"])
            dma("sp", y_p[t * TW + bl * 128:t * TW + (bl + 1) * 128, :], ytok[:, (bl % 2) * D:(bl % 2 + 1) * D], reads=["mT"])
        if len(segs) > 1:
            pb, pk = psum()
            for kc in range(KC):
                op("pe", lambda e: e.transpose(pb[0:NS, (kc % 4) * 128:(kc % 4 + 1) * 128], xT[:, kc, TW:TWS], ident_f[:, :]),
                   reads=["xT", "ident_f"], writes=[pk], signal=True)
                if kc % 4 == 3:
                    q4 = kc // 4
                    op("act", lambda e: e.copy(ytok[0:NS, q4 * 512:(q4 + 1) * 512], pb[0:NS, :]), reads=[pk], writes=["mT"])
            dma("sp", y_s[:, :], ytok[0:NS, 0:D], reads=["mT"])

    xtok = mT_flat
    for t in range(NT):
        segs = segs_of(t)
        for bl in range(4):
            xo = (bl % 2) * D
            dma("sp", xtok[:, xo:xo + D], x_p[t * TW + bl * 128:t * TW + (bl + 1) * 128, :], writes=["mT"])
            for q4 in range(4):
                pb, pk = psum()
                for k4 in range(4):
                    kc = q4 * 4 + k4
                    op("pe", lambda e: e.transpose(pb[:, k4 * 128:(k4 + 1) * 128], xtok[:, xo + kc * 128:xo + (kc + 1) * 128], ident_f[:, :]),
                       reads=["mT", "ident_f"], writes=[pk], signal=(k4 == 3))
                op("act", lambda e: e.copy(xT[:, q4 * 4:(q4 + 1) * 4, bl * 128:(bl + 1) * 128], pb[:, :].rearrange("p (k t) -> p k t", k=4)),
                   reads=[pk], writes=["xT"])
        if len(segs) > 1:
            dma("sp", xtok[0:NS, 0:D], x_s[:, :], writes=["mT"])
            pb, pk = psum()
            for kc in range(KC):
                op("pe", lambda e: e.transpose(pb[:, kc * NS:(kc + 1) * NS], xtok[0:NS, kc * 128:(kc + 1) * 128], ident_f[0:NS, 0:NS]),
                   reads=["mT", "ident_f"], writes=[pk], signal=(kc == KC - 1))
            op("act", lambda e: e.copy(xT[:, :, TW:TWS], pb[:, 0:KC * NS].rearrange("p (k s) -> p k s", k=KC)), reads=[pk], writes=["xT"])
        xT_store(t, segs)
        tile_tail(0, t, segs)

    for i in range(nlayers):
        li = i // 2
        if i % 2 == 0:
            attn_phase(li)
        else:
            ssm_phase(li)
        for k3 in range(3):
            fm_load(cw[:, k3, :], "cw", conv_w[i, k3, :])
        fm_load(cw[:, 3, :], "cw", conv_b[i, :])
        for s_ in range(NS):
            for r_ in range(2):
                fm_load(cst[:, :, r_, s_], "cst", st_conv[i, s_, r_, :])
        op("dve", lambda e: e.memset(uprev[:], 0.0), writes=[("uprev", jj_) for jj_ in range(NUC)])
        for t in range(NT):
            segs = segs_of(t)
            xT_load(t, segs)
            if i % 2 == 0:
                mixer_out_attn(li, t, segs)
            else:
                mixer_out_ssm(li, t, segs)
            if dbg and i == 0:
                dma("sp", dbg_m[:, :, t * TW:(t + 1) * TW].rearrange("k p t -> p k t"), mT[:, :, 0:TW], reads=["mT"])
                if len(segs) > 1:
                    dma("sp", dbg_m[:, :, T:TC].rearrange("k p t -> p k t"), mT[:, :, TW:TWS], reads=["mT"])
            postnorm_residual(i, 1, segs)
            if dbg and i == 0:
                dma("sp", dbg_x1[:, :, t * TW:(t + 1) * TW].rearrange("k p t -> p k t"), xT[:, :, 0:TW], reads=["xT"])
                if len(segs) > 1:
                    dma("sp", dbg_x1[:, :, T:TC].rearrange("k p t -> p k t"), xT[:, :, TW:TWS], reads=["xT"])
            prenorm(i, 2, segs)
            ffn(i, t, segs)
            postnorm_residual(i, 3, segs)
            if i + 1 < nlayers:
                xT_store(t, segs)
                tile_tail(i + 1, t, segs)
            else:
                out_y(t, segs)
        for r_ in range(2):
            fm_store(uprev[:, :, r_], [("uprev", jj_) for jj_ in range(NUC)], conv_p[i, r_, :])
            for s_ in range(NS):
                fm_store(csout[:, :, r_, s_], "csout", conv_s[i, s_, r_, :])

    def _unused():
        pass

    assert wr.consumed == len(wr.plan), (wr.consumed, len(wr.plan))
    kb.barrier()
    kb.es.close()
    return nc


def _consts():
    ident = np.eye(128, dtype=np.float32)
    k = np.arange(128)[:, None]
    q = np.arange(128)[None, :]
    m = np.zeros((128, 7, 128), np.float32)
    m[:, 0] = (q >= k)
    m[:, 1] = (q <= k)
    r4 = ((q - k) % 4 == 0)
    m[:, 2] = r4 & (q >= k)
    m[:, 3] = r4
    m[:, 4] = r4 & (q <= k)
    r16 = ((q - k) % 16 == 0)
    m[:, 5] = r16 & (q >= k)
    m[:, 6] = r16
    half = 16
    inv = (np.float32(500000.0) ** (-(np.arange(half, dtype=np.float32) * np.float32(2.0 / 32)))).astype(np.float32)
    pos = np.zeros((128, 17), np.float32)
    for b in range(16):
        pos[:, b] = b * 128 + np.arange(128)
    pos[:, 16] = PAST
    ang = (pos[:, :, None] * inv[None, None, :]).astype(np.float32)
    rope = np.concatenate([np.cos(ang), np.sin(ang)], axis=-1).astype(np.float32)
    mbq = np.where(m.transpose(2, 1, 0) > 0.5, 0.0, -30000.0).astype(np.float32)
    gmask = (np.arange(128)[:, None] // 16 == np.arange(8)[None, :]).astype(np.float32)
    return ident, m, rope, np.ascontiguousarray(mbq), gmask


_NC_CACHE = {}


def kernel(**inp):
    f = lambda a: np.ascontiguousarray(np.asarray(a, dtype=np.float32))
    ident, masks, rope, mbq, gmask = _consts()
    if "nc" not in _NC_CACHE:
        _NC_CACHE["nc"] = build(4)
    nc = _NC_CACHE["nc"]
    shared = {
        "norm_g": f(inp["norm_g"]).reshape(16, D), "w_qkv": f(inp["w_qkv"]), "w_o": f(inp["w_attn_o"]),
        "w_in": f(inp["w_ssm_in"]), "lam_re": f(inp["lambda_re"]), "lam_im": f(inp["lambda_im"]),
        "log_dt": f(inp["log_dt"]), "b_re": f(inp["b_re"]), "b_im": f(inp["b_im"]), "c_re": f(inp["c_re"]),
        "c_im": f(inp["c_im"]), "d_skip": f(inp["d_skip"]), "w_glu": f(inp["w_glu"]), "w_up": f(inp["w_up"]),
        "conv_w": f(inp["conv_w"]), "conv_b": f(inp["conv_b"]), "w_down": f(inp["w_down"]),
        "c_ident": ident, "c_masks": masks, "c_rope": rope, "c_mbq": mbq, "c_gmask": gmask,
    }
    xp = f(inp["x_prompt"])
    xs = f(inp["x_sample"]).reshape(8, D)
    ck = [f(inp["cache_kv_g0"]), f(inp["cache_kv_g1"]), f(inp["cache_kv_g2"])]
    ss = f(inp["state_ssm"])
    sc = f(inp["state_conv"])
    in_maps = []
    for c in range(NCORES):
        m = dict(shared)
        m["x_p"] = xp[c]
        sl = slice(2 * c, 2 * c + 2)
        m["x_s"] = np.ascontiguousarray(xs[sl])
        for g in range(3):
            m["ckv%d" % g] = np.ascontiguousarray(ck[g][:, sl].reshape(2, NS, WIN[g], 2048))
        m["st_ssm"] = np.ascontiguousarray(ss[:, sl].reshape(2, NS, 128 * 64 * 2))
        m["st_conv"] = np.ascontiguousarray(sc[:, sl])
        in_maps.append(m)
    res = run_bass_kernel_spmd(nc, in_maps, core_ids=list(range(NCORES)))
    R = res.results
    y_prompt = np.stack([R[c]["y_p"] for c in range(NCORES)], 0)
    y_sample = np.concatenate([R[c]["y_s"] for c in range(NCORES)], 0).reshape(8, 1, D)
    outs = [y_prompt, y_sample]
    for g in range(3):
        keep = min(WIN[g], T)
        outs.append(np.stack([R[c]["kvp%d" % g] for c in range(NCORES)], 1).reshape(2, 4, keep, 2, 8, 128))
        outs.append(np.concatenate([R[c]["kvs%d" % g] for c in range(NCORES)], 1).reshape(2, 8, 1, 2, 8, 128))
    outs.append(np.stack([R[c]["ssm_p"] for c in range(NCORES)], 1).reshape(2, 4, 128, 64, 2))
    outs.append(np.concatenate([R[c]["ssm_s"] for c in range(NCORES)], 1).reshape(2, 8, 128, 64, 2))
    outs.append(np.stack([R[c]["conv_p"] for c in range(NCORES)], 1).reshape(4, 4, 2, 2 * DFF))
    outs.append(np.concatenate([R[c]["conv_s"] for c in range(NCORES)], 1).reshape(4, 8, 2, 2 * DFF))
    return tuple(np.ascontiguousarray(o.astype(np.float32)) for o in outs)
```

```python
import contextlib
import math
import numpy as np
import concourse.bass as bass
import concourse.mybir as mybir
from concourse.bass_utils import run_bass_kernel_spmd

F32 = mybir.dt.float32
BF16 = mybir.dt.bfloat16
AF = mybir.ActivationFunctionType
ALU = mybir.AluOpType
AX = mybir.AxisListType

T = 2048
D = 2048
KC = 16
NS = 2
TC = T + NS
TW = 512
NT = T // TW
TWS = TW + NS
DFF = 5504
NFC = 43
NUC = 86
NCORES = 4
PAST = 16384
WIN = (128, 512, 2048)
DIL = (1, 4, 16)
EPS = 1e-6
SLOT = 5504
NSLOT = 4
SCALE = 128 ** -0.5
GELU_C = 2.0 * math.sqrt(2.0 / math.pi)


class KB:
    NDS = 12

    def __init__(self):
        self.nc = bass.Bass("TRN2", target_bir_lowering=False)
        nc = self.nc
        self.es = contextlib.ExitStack()
        self.engs = {"pe": nc.tensor, "dve": nc.vector, "act": nc.scalar, "pool": nc.gpsimd, "sp": nc.sync}
        self.semh = {}
        for k in self.engs:
            self.semh[("e", k)] = self.es.enter_context(nc.semaphore("se_" + k))
        self.cnt = {k: 0 for k in self.engs}
        self.waited = {k: {} for k in self.engs}
        self.pending = {k: ([], []) for k in self.engs}
        self.lastw = {}
        self.readers = {}
        self.dcnt = {}
        self.dnext = {}
        for q in ("sp", "pool", "act"):
            for i in range(self.NDS):
                self.semh[("d", q, i)] = self.es.enter_context(nc.semaphore("sd_%s_%d" % (q, i)))
                self.dcnt[("d", q, i)] = 0
            self.dnext[q] = 0
        self.nins = 0

    def sb(self, name, shape, dt, es=None):
        self._uid = getattr(self, "_uid", 0) + 1
        return (es or self.es).enter_context(self.nc.sbuf_tensor("%s_%d" % (name, self._uid), list(shape), dt))

    def ps(self, name, shape, dt, es=None):
        return (es or self.es).enter_context(self.nc.psum_tensor(name, list(shape), dt))

    def dram(self, name, shape, dt, kind):
        return self.nc.dram_tensor(name, list(shape), dt, kind=kind).ap()

    def _deps(self, reads, writes):
        deps = {}
        for b in reads:
            lw = self.lastw.get(b)
            if lw is not None:
                deps[lw[0]] = max(deps.get(lw[0], 0), lw[1])
        for b in writes:
            lw = self.lastw.get(b)
            if lw is not None:
                deps[lw[0]] = max(deps.get(lw[0], 0), lw[1])
            for sk, v in self.readers.get(b, {}).items():
                deps[sk] = max(deps.get(sk, 0), v)
        return deps

    def _wait(self, eng, deps):
        e = self.engs[eng]
        w = self.waited[eng]
        for sk, v in deps.items():
            if eng == "pe" and sk == ("e", "pe"):
                continue
            if w.get(sk, 0) < v:
                e.wait_ge(self.semh[sk], v)
                w[sk] = v
                self.nins += 1

    def fence(self, eng, reads=(), writes=()):
        self._wait(eng, self._deps(reads, writes))

    def mark(self, eng, writes=(), reads=()):
        sk = ("e", eng)
        v = self.cnt[eng]
        for b in writes:
            self.lastw[b] = (sk, v)
            self.readers[b] = {}
        for b in reads:
            self.readers.setdefault(b, {})[sk] = v

    def poison(self, keys):
        allr = {("e", k): v for k, v in self.cnt.items() if v > 0}
        for sk, v in self.dcnt.items():
            if v > 0:
                allr[sk] = v
        for b in keys:
            self.readers[b] = dict(allr)

    def opx(self, eng, fn, after=()):
        sk = ("e", eng)
        v = max(after) if after else 0
        if v > self.waited[eng].get(sk, 0):
            self.engs[eng].wait_ge(self.semh[sk], v)
            self.waited[eng][sk] = v
            self.nins += 1
        ins = fn(self.engs[eng])
        self.nins += 1
        self.cnt[eng] += 1
        ins.then_inc(self.semh[sk], 1)
        return self.cnt[eng]

    def op(self, eng, fn, reads=(), writes=(), signal=True):
        self._wait(eng, self._deps(reads, writes))
        ins = fn(self.engs[eng])
        self.nins += 1
        pr, pw = self.pending[eng]
        pr.extend(reads)
        pw.extend(writes)
        if signal:
            self.cnt[eng] += 1
            sk = ("e", eng)
            ins.then_inc(self.semh[sk], 1)
            v = self.cnt[eng]
            for b in pw:
                self.lastw[b] = (sk, v)
                self.readers[b] = {}
            for b in pr:
                if b not in pw:
                    self.readers.setdefault(b, {})[sk] = v
            self.pending[eng] = ([], [])
        return ins

    def dma(self, q, out, in_, reads=(), writes=(), **kw):
        self._wait(q, self._deps(reads, writes))
        i = self.dnext[q]
        self.dnext[q] = (i + 1) % self.NDS
        sk = ("d", q, i)
        prev = self.dcnt[sk]
        if prev > 0 and self.waited[q].get(sk, 0) < prev:
            self.engs[q].wait_ge(self.semh[sk], prev)
            self.waited[q][sk] = prev
        ins = self.engs[q].dma_start(out=out, in_=in_, **kw)
        self.nins += 1
        self.dcnt[sk] += 16
        v = self.dcnt[sk]
        ins.then_inc(self.semh[sk], 16)
        for b in writes:
            self.lastw[b] = (sk, v)
            self.readers[b] = {}
        for b in reads:
            if b not in writes:
                self.readers.setdefault(b, {})[sk] = v
        return ins

    def barrier(self, engines=None):
        cur = {}
        for k in self.engs:
            assert not self.pending[k][0] and not self.pending[k][1]
            cur[("e", k)] = self.cnt[k]
        for sk, v in self.dcnt.items():
            cur[sk] = v
        for eng in (engines or list(self.engs)):
            for sk, v in cur.items():
                if sk == ("e", eng):
                    continue
                if v > 0 and self.waited[eng].get(sk, 0) < v:
                    self.engs[eng].wait_ge(self.semh[sk], v)
                    self.waited[eng][sk] = v
                    self.nins += 1
        if engines is None:
            self.lastw = {}
            self.readers = {}


class WRing:
    def __init__(self, kb, tensor):
        self.kb = kb
        self.t = tensor
        self.plan = []
        self.emitted = 0
        self.consumed = 0

    def add(self, tag, pieces):
        self.plan.append((tag, pieces))

    def _emit(self, k):
        tag, pieces = self.plan[k]
        s = k % NSLOT
        for (off, shp, src) in pieces:
            n = int(np.prod(shp))
            dst = self.t[:, s, off:off + n]
            if len(shp) == 2:
                dst = dst.rearrange("p (a b) -> p a b", a=shp[0])
            self.kb.dma("pool", dst, src, writes=[("w", s)])

    def get(self, tag):
        k = self.consumed
        assert self.plan[k][0] == tag, (self.plan[k][0], tag)
        while self.emitted < min(len(self.plan), k + NSLOT):
            self._emit(self.emitted)
            self.emitted += 1
        self.consumed += 1
        s = k % NSLOT
        return self.t[:, s, :], ("w", s)


def build(nlayers=4, dbg=False):
    kb = KB()
    nc = kb.nc
    op, dma = kb.op, kb.dma

    def din(name, shape, dt=F32):
        return kb.dram(name, shape, dt, "ExternalInput")

    def dout(name, shape, dt=F32):
        return kb.dram(name, shape, dt, "ExternalOutput")

    x_p = din("x_p", [T, D])
    x_s = din("x_s", [NS, D])
    ckv = [din("ckv%d" % g, [2, NS, WIN[g], 2 * 1024]) for g in range(3)]
    st_ssm = din("st_ssm", [2, NS, 128 * 64 * 2])
    st_conv = din("st_conv", [4, NS, 2, 2 * DFF])
    norm_g = din("norm_g", [16, D])
    w_qkv = din("w_qkv", [2, D, 9216])
    w_o = din("w_o", [2, 1024, D])
    w_in = din("w_in", [2, D, D])
    lam_re = din("lam_re", [2, 128, 64])
    lam_im = din("lam_im", [2, 128, 64])
    log_dt = din("log_dt", [2, 128])
    b_re = din("b_re", [2, 128, 64, 16])
    b_im = din("b_im", [2, 128, 64, 16])
    c_re = din("c_re", [2, 128, 16, 64])
    c_im = din("c_im", [2, 128, 16, 64])
    d_skip = din("d_skip", [2, D])
    w_glu = din("w_glu", [2, D, 2 * D])
    w_up = din("w_up", [4, D, 2 * DFF])
    conv_w = din("conv_w", [4, 3, 2 * DFF])
    conv_b = din("conv_b", [4, 2 * DFF])
    w_down = din("w_down", [4, DFF, D])
    c_ident = din("c_ident", [128, 128])
    c_masks = din("c_masks", [128, 7, 128])
    c_rope = din("c_rope", [128, 17, 32])
    c_mbq = din("c_mbq", [128, 7, 128])
    c_gmask = din("c_gmask", [128, 8])

    y_p = dout("y_p", [T, D])
    y_s = dout("y_s", [NS, D])
    kvp = [dout("kvp%d" % g, [2, min(WIN[g], T), 2048]) for g in range(3)]
    kvs = [dout("kvs%d" % g, [2, NS, 2048]) for g in range(3)]
    ssm_p = dout("ssm_p", [2, 128 * 64 * 2])
    ssm_s = dout("ssm_s", [2, NS, 128 * 64 * 2])
    conv_p = dout("conv_p", [4, 2, 2 * DFF])
    conv_s = dout("conv_s", [4, NS, 2, 2 * DFF])

    IK = "ExternalOutput" if dbg else "Internal"
    xT_s = kb.dram("xT_s", [KC, 128, TC], F32, "Internal")
    qT_s = kb.dram("qT_s", [3, 8, 128, T], BF16, IK)
    kT_s = kb.dram("kT_s", [3, 8, 128, T], BF16, IK)
    v_s = kb.dram("v_s", [3, T, 1024], BF16, IK)
    mT_s = kb.dram("mT_s", [8, 128, TC], BF16, IK)
    dbg_x1 = kb.dram("dbg_x1", [KC, 128, TC], F32, "ExternalOutput") if dbg else None
    dbg_m = kb.dram("dbg_m", [KC, 128, TC], F32, "ExternalOutput") if dbg else None
    uT_s = kb.dram("uT_s", [KC, 128, TC], F32, "Internal")
    zT_s = kb.dram("zT_s", [KC, 128, TC], BF16, "Internal")

    ident_f = kb.sb("ident_f", [128, 128], F32)
    ident_b = kb.sb("ident_b", [128, 128], BF16)
    ones_b = kb.sb("ones_b", [128, 128], BF16)
    rope = kb.sb("rope", [128, 17, 32], F32)
    gsc = kb.sb("gsc", [128, KC, 16], F32)
    wring_t = kb.sb("wring", [128, NSLOT, SLOT], BF16)
    wr = WRing(kb, wring_t)
    sqT = kb.sb("sqT", [128, 24, NS], BF16)
    skT = kb.sb("skT", [128, 24, NS], BF16)
    svT = kb.sb("svT", [128, 24, NS], F32)
    epsb = kb.sb("epsb", [128, 1], F32)

    pbank = [kb.ps("pb%d" % i, [128, 512], F32) for i in range(8)]
    prr = [0]

    def psum(lo=0, hi=6):
        i = lo + prr[0] % (hi - lo)
        prr[0] += 1
        return pbank[i], ("ps", i)

    dma("sp", ident_f[:], c_ident[:, :], writes=["ident_f"])
    dma("pool", ident_b[:], c_ident[:, :], writes=["ident_b"])
    dma("sp", rope[:], c_rope[:, :, :], writes=["rope"])
    op("dve", lambda e: e.memset(ones_b[:], 1.0), writes=["ones_b"])
    op("dve", lambda e: e.memset(epsb[:], EPS), writes=["epsb"])

    es0 = contextlib.ExitStack()
    gtmp = kb.sb("gtmp", [16, D], F32, es0)
    dma("sp", gtmp[:], norm_g[:, :], writes=["gtmp"])
    for kc in range(KC):
        pb, pk = psum()
        op("pe", lambda e: e.transpose(pb[:, 0:16], gtmp[0:16, kc * 128:(kc + 1) * 128], ident_f[0:16, 0:16]),
           reads=["gtmp", "ident_f"], writes=[pk])
        op("act", lambda e: e.copy(gsc[:, kc, :], pb[:, 0:16]), reads=[pk], writes=["gsc"])
    kb.barrier()
    es0.close()

    def gs(i, j, kc):
        return gsc[:, kc, 4 * i + j:4 * i + j + 1]

    def plan_qkv(li):
        for ct in range(36):
            src = w_qkv[li, :, ct * 256:(ct + 1) * 256].rearrange("(k p) c -> p k c", p=128)
            wr.add(("qkv", li, ct), [(0, [KC, 256], src)])

    def plan_win(li):
        for dc in range(KC):
            src = w_in[li, :, dc * 128:(dc + 1) * 128].rearrange("(k p) c -> p k c", p=128)
            wr.add(("win", li, dc), [(0, [KC, 128], src)])

    def plan_tile(i):
        li = i // 2
        if i % 2 == 0:
            for dc in range(KC):
                src = w_o[li, :, dc * 128:(dc + 1) * 128].rearrange("(k p) c -> p k c", p=128)
                wr.add(("wo", li, dc), [(0, [8, 128], src)])
        else:
            for dc in range(KC):
                sv = w_glu[li, :, dc * 128:(dc + 1) * 128].rearrange("(k p) c -> p k c", p=128)
                sg = w_glu[li, :, D + dc * 128:D + (dc + 1) * 128].rearrange("(k p) c -> p k c", p=128)
                wr.add(("glu", li, dc), [(0, [KC, 128], sv), (KC * 128, [KC, 128], sg)])
        for j in range(NFC):
            sg = w_up[i, :, j * 128:(j + 1) * 128].rearrange("(k p) c -> p k c", p=128)
            sv = w_up[i, :, DFF + j * 128:DFF + (j + 1) * 128].rearrange("(k p) c -> p k c", p=128)
            wr.add(("up", i, j), [(0, [KC, 128], sg), (KC * 128, [KC, 128], sv)])
        for dc in range(KC):
            src = w_down[i, :, dc * 128:(dc + 1) * 128].rearrange("(j p) c -> p j c", p=128)
            wr.add(("down", i, dc), [(0, [NFC, 128], src)])
        if i + 1 < nlayers:
            if (i + 1) % 2 == 0:
                plan_qkv((i + 1) // 2)
            else:
                plan_win((i + 1) // 2)

    for t in range(NT):
        plan_qkv(0)
    for i in range(nlayers):
        for t in range(NT):
            plan_tile(i)

    xT = kb.sb("xT", [128, KC, TWS], F32)
    mT = kb.sb("mT", [128, KC, TWS], F32)
    hT = kb.sb("hT", [128, KC, TWS], BF16)
    aT = kb.sb("aT", [128, NFC, TWS], BF16)
    sq = [kb.sb("sq%d" % i, [128, TW], BF16) for i in range(2)]
    rstd = kb.sb("rstd", [128, TWS], F32)
    rtmp = kb.sb("rtmp", [128, TWS], F32)
    cgt = [kb.sb("cg%d" % i, [128, TW], F32) for i in range(2)]
    cvt = [kb.sb("cv%d" % i, [128, TW], F32) for i in range(2)]
    sgt = [kb.sb("sg%d" % i, [128, TW], F32) for i in range(2)]
    cw = kb.sb("cw", [128, 4, NUC], F32)
    uprev = kb.sb("uprev", [128, NUC, 2], F32)
    cst = kb.sb("cst", [128, NUC, 2, NS], F32)
    csout = kb.sb("csout", [128, NUC, 2, NS], F32)
    vtmp = kb.sb("vtmp", [NUC, 128], F32)
    vtmp2 = kb.sb("vtmp2", [NUC, 128], F32)
    rr = {"sq": 0, "cg": 0, "stg": 0}

    mT_flat = mT[:].rearrange("p a b -> p (a b)")
    aT_flat = aT[:].rearrange("p a b -> p (a b)")

    def segs_of(t):
        s = [(0, TW, 1)]
        if t == NT - 1:
            s.append((TW, NS, NS))
        return s

    def fm_load(dst_ap, dst_key, dram_row):
        dma("sp", vtmp[:], dram_row.rearrange("(c p) -> c p", p=128), writes=["vtmp"])
        pb, pk = psum()
        op("pe", lambda e: e.transpose(pb[:, 0:NUC], vtmp[:, :], ident_f[0:NUC, 0:NUC]),
           reads=["vtmp", "ident_f"], writes=[pk])
        op("act", lambda e: e.copy(dst_ap, pb[:, 0:NUC]), reads=[pk], writes=[dst_key])

    def fm_store(src_ap, src_key, dram_row):
        op("act", lambda e: e.copy(rtmp[:, 0:NUC], src_ap), reads=(src_key if isinstance(src_key, list) else [src_key]), writes=["rtmp"])
        pb, pk = psum()
        op("pe", lambda e: e.transpose(pb[0:NUC, 0:128], rtmp[:, 0:NUC], ident_f[:, :]),
           reads=["rtmp", "ident_f"], writes=[pk])
        op("act", lambda e: e.copy(vtmp2[:, :], pb[0:NUC, 0:128]), reads=[pk], writes=["vtmp2"])
        dma("sp", dram_row.rearrange("(c p) -> c p", p=128), vtmp2[:], reads=["vtmp2"])

    def stats_begin():
        return psum(7, 8)

    def stats_add(pst, src_ap, src_key, c0, n, first, last):
        pb, pk = pst
        s = sq[rr["sq"] % 2]
        sk = "sq%d" % (rr["sq"] % 2)
        rr["sq"] += 1
        op("act", lambda e: e.activation(out=s[:, 0:n], in_=src_ap, func=AF.Square), reads=[src_key], writes=[sk])
        op("pe", lambda e: e.matmul(pb[:, 0:n], ones_b[:, :], s[:, 0:n], start=first, stop=last),
           reads=[sk, "ones_b"], writes=[pk], signal=True)

    def stats_end(pst, c0, n):
        pb, pk = pst
        op("act", lambda e: e.activation(out=rtmp[:, c0:c0 + n], in_=pb[:, 0:n], func=AF.Sqrt, bias=epsb[:, 0:1], scale=1.0 / D),
           reads=[pk, "epsb"], writes=["rtmp"])
        op("dve", lambda e: e.reciprocal(rstd[:, c0:c0 + n], rtmp[:, c0:c0 + n]), reads=["rtmp"], writes=["rstd"])

    def prenorm(i, j, segs):
        for (c0, n, S) in segs:
            pst = stats_begin()
            for kc in range(KC):
                stats_add(pst, xT[:, kc, c0:c0 + n], "xT", c0, n, kc == 0, kc == KC - 1)
            stats_end(pst, c0, n)
            for kc in range(KC):
                op("dve", lambda e: e.scalar_tensor_tensor(out=hT[:, kc, c0:c0 + n], in0=xT[:, kc, c0:c0 + n],
                                                           scalar=gs(i, j, kc), in1=rstd[:, c0:c0 + n],
                                                           op0=ALU.mult, op1=ALU.mult),
                   reads=["xT", "rstd", "gsc"], writes=["hT"])

    def postnorm_residual(i, j, segs):
        for (c0, n, S) in segs:
            for kc in range(KC):
                op("dve", lambda e: e.scalar_tensor_tensor(out=mT[:, kc, c0:c0 + n], in0=mT[:, kc, c0:c0 + n],
                                                           scalar=gs(i, j, kc), in1=rstd[:, c0:c0 + n],
                                                           op0=ALU.mult, op1=ALU.mult),
                   reads=["mT", "rstd", "gsc"], writes=["mT"])
                op("dve", lambda e: e.tensor_tensor(out=xT[:, kc, c0:c0 + n], in0=xT[:, kc, c0:c0 + n],
                                                    in1=mT[:, kc, c0:c0 + n], op=ALU.add),
                   reads=["mT", "xT"], writes=["xT"])

    def xT_store(t, segs):
        g0 = t * TW
        dma("sp", xT_s[:, :, g0:g0 + TW].rearrange("k p t -> p k t"), xT[:, :, 0:TW], reads=["xT"], writes=["xT_s%d" % t])
        if len(segs) > 1:
            dma("sp", xT_s[:, :, T:TC].rearrange("k p t -> p k t"), xT[:, :, TW:TWS], reads=["xT"], writes=["xT_ss"])

    def xT_load(t, segs):
        g0 = t * TW
        dma("sp", xT[:, :, 0:TW], xT_s[:, :, g0:g0 + TW].rearrange("k p t -> p k t"), reads=["xT_s%d" % t], writes=["xT"])
        if len(segs) > 1:
            dma("sp", xT[:, :, TW:TWS], xT_s[:, :, T:TC].rearrange("k p t -> p k t"), reads=["xT_ss"], writes=["xT"])

    stg_f = mT_flat
    stg_b = aT_flat
    rt = kb.sb("ropetmp", [128, 4, 2, 16], F32)

    def qkv_phase(li, t, segs):
        blocks = [(b, 128) for b in range(4)]
        if len(segs) > 1:
            blocks.append((4, NS))
        ffk = ["cg0", "cg1", "cv0", "cv1", "sg0", "sg1"]
        for eng_ in ("act", "dve", "pe", "sp"):
            kb.fence(eng_, writes=ffk)
        stf_t = [cgt[0], cgt[1], cvt[0], cvt[1]]
        stb_t = [sgt[0][:, :].bitcast(BF16), sgt[1][:, :].bitcast(BF16)]
        for ct in range(36):
            wv, wk = wr.get(("qkv", li, ct))
            w3 = wv[:, 0:KC * 256].rearrange("p (k c) -> p k c", k=KC)
            s = ct // 12
            g = (ct % 12) // 4
            hp = ct % 4
            for (bl, m) in blocks:
                gb = t * 4 + bl if bl < 4 else 16
                pb, pk = psum()
                for kc in range(KC):
                    lhs = hT[:, kc, bl * 128:bl * 128 + m] if bl < 4 else hT[:, kc, TW:TWS]
                    op("pe", lambda e: e.matmul(pb[0:m, 0:256], lhs, w3[:, kc, :], start=(kc == 0), stop=(kc == KC - 1)),
                       reads=["hT", wk], writes=[pk], signal=(kc == KC - 1))
                slot = rr["stg"] % 4
                rr["stg"] += 1
                sf = stf_t[slot][:, 0:256]
                sfk = ("sf", slot)
                op("act", lambda e: e.copy(sf[0:m, :], pb[0:m, 0:256]), reads=[pk], writes=[sfk])
                sf3 = sf.rearrange("p (h d) -> p h d", h=2)
                if s < 2:
                    cosb = rope[0:m, gb, 0:16].unsqueeze(1).broadcast_to([m, 2, 16])
                    sinb = rope[0:m, gb, 16:32].unsqueeze(1).broadcast_to([m, 2, 16])
                    x1 = sf3[0:m, :, 0:16]
                    x2 = sf3[0:m, :, 16:32]
                    t1, t2, t3, t4 = (rt[0:m, k, :, :] for k in range(4))
                    op("dve", lambda e: e.tensor_tensor(out=t1, in0=x1, in1=cosb, op=ALU.mult), reads=[sfk, "rope"], writes=["rt1"])
                    op("dve", lambda e: e.tensor_tensor(out=t2, in0=x2, in1=sinb, op=ALU.mult), reads=[sfk, "rope"], writes=["rt2"])
                    op("dve", lambda e: e.tensor_tensor(out=t3, in0=x2, in1=cosb, op=ALU.mult), reads=[sfk, "rope"], writes=["rt3"])
                    op("dve", lambda e: e.tensor_tensor(out=t4, in0=x1, in1=sinb, op=ALU.mult), reads=[sfk, "rope"], writes=["rt4"])
                    op("dve", lambda e: e.tensor_tensor(out=x1, in0=t1, in1=t2, op=ALU.subtract), reads=["rt1", "rt2", "rt3", "rt4"], writes=[sfk])
                    op("dve", lambda e: e.tensor_tensor(out=x2, in0=t3, in1=t4, op=ALU.add), reads=["rt3", "rt4"], writes=[sfk])
                if s >= 1:
                    half = (s - 1) * 1024 + hp * 256
                    if bl < 4:
                        keep = min(WIN[g], T)
                        row0 = gb * 128 - (T - keep)
                        if row0 >= 0:
                            dma("sp", kvp[g][li, row0:row0 + 128, half:half + 256], sf[:, :], reads=[sfk])
                    else:
                        dma("sp", kvs[g][li, :, half:half + 256], sf[0:m, :], reads=[sfk])
                sb_ = stb_t[0][:, slot * 256:(slot + 1) * 256]
                sbk = ("sb", slot)
                tbk = ("tb", slot)
                if not (bl == 4 and s == 2):
                    op("act", lambda e: e.copy(sb_[0:m, :], sf[0:m, :]), reads=[sfk], writes=[sbk])
                if s == 2 and bl < 4:
                    dma("sp", v_s[g, gb * 128:(gb + 1) * 128, hp * 256:(hp + 1) * 256], sb_[:, :], reads=[sbk], writes=["v_s"])
                    continue
                if s == 2:
                    pt, ptk = psum()
                    for hh in range(2):
                        op("pe", lambda e: e.transpose(pt[:, hh * NS:(hh + 1) * NS], sf[0:m, hh * 128:(hh + 1) * 128], ident_f[0:m, 0:m]),
                           reads=[sfk, "ident_f"], writes=[ptk], signal=(hh == 1))
                    op("dve", lambda e: e.tensor_copy(out=svT[:, g * 8 + 2 * hp:g * 8 + 2 * hp + 2, :],
                                                      in_=pt[:, 0:2 * NS].rearrange("p (h s) -> p h s", h=2)),
                       reads=[ptk], writes=["svT"])
                    continue
                pt, ptk = psum()
                ptb = pt[:, 0:256].bitcast(BF16)
                for hh in range(2):
                    op("pe", lambda e: e.transpose(ptb[:, hh * 128:hh * 128 + m], sb_[0:m, hh * 128:(hh + 1) * 128], ident_b[0:m, 0:m]),
                       reads=[sbk, "ident_b"], writes=[ptk], signal=(hh == 1))
                if bl < 4:
                    tb = stb_t[1][:, slot * 256:(slot + 1) * 256]
                    op("dve", lambda e: e.tensor_copy(out=tb, in_=ptb[:, 0:256]), reads=[ptk], writes=[tbk])
                    dst = (qT_s if s == 0 else kT_s)[g, 2 * hp:2 * hp + 2, :, gb * 128:(gb + 1) * 128].rearrange("h p t -> p h t")
                    dma("sp", dst, tb.rearrange("p (h t) -> p h t", h=2), reads=[tbk], writes=["qT_s" if s == 0 else "kT_s"])
                else:
                    dstt = sqT if s == 0 else skT
                    op("dve", lambda e: e.tensor_copy(out=dstt[:, g * 8 + 2 * hp:g * 8 + 2 * hp + 2, :],
                                                      in_=ptb[:, 0:256].rearrange("p (h t) -> p h t", h=2)[:, :, 0:NS]),
                       reads=[ptk], writes=["sqT" if s == 0 else "skT"])

    _qkv_inner = qkv_phase

    def qkv_phase(li, t, segs):
        _qkv_inner(li, t, segs)
        kb.poison(["cg0", "cg1", "cv0", "cv1", "sg0", "sg1"])

    def attn_phase(li):
        kb.barrier()
        es = contextlib.ExitStack()
        aTf = aT[:].rearrange("p a b -> p (a b)")
        hTf = hT[:].rearrange("p a b -> p (a b)")
        xTb = xT[:].rearrange("p a b -> p (a b)").bitcast(BF16)
        mTf = mT[:].rearrange("p a b -> p (a b)")
        QN = 3 * T
        qTt = [aTf[:, i * QN:(i + 1) * QN].rearrange("p (g t) -> p g t", g=3) for i in range(2)]
        kTt = [aTf[:, 2 * QN:3 * QN].rearrange("p (g t) -> p g t", g=3), hTf[:, 0:QN].rearrange("p (g t) -> p g t", g=3)]
        vt = [xTb[:, i * QN:(i + 1) * QN].rearrange("p (g b d) -> p g b d", g=3, b=16) for i in range(2)]
        pT = [hTf[:, QN + i * 512:QN + (i + 1) * 512] for i in range(3)]
        mh = [xTb[:, 2 * QN + i * TC:2 * QN + (i + 1) * TC] for i in range(2)]
        mbq = aTf[:, 3 * QN:3 * QN + 1792].bitcast(F32).rearrange("p (m k) -> p m k", m=7)
        smk = mTf[:, 0:2048]
        junk = mTf[:, 2048:3072].bitcast(BF16)
        acc2 = mTf[:, 4096:4224]
        D3 = mTf[:, 3072:3456]
        cbc = mTf[:, 3456:3840]
        acc = mTf[:, 3840:4096]
        ones_f = kb.sb("ones_f", [128, 128], F32, es)
        masks = kb.sb("masks", [128, 7, 128], BF16, es)
        dma("pool", masks[:], c_masks[:, :, :], writes=["masks"])
        l0 = kb.sb("l0", [128, 16, 3], F32, es)
        colm = kb.sb("colm", [128, 8, 3], F32, es)
        dma("sp", mbq, c_mbq[:, :, :], writes=["mbq"])
        op("dve", lambda e: e.memset(ones_f[:], 1.0), writes=["ones_f"])

        def load_head(h):
            b = h % 2
            for g in range(3):
                dma("sp", qTt[b][:, g, :], qT_s[g, h, :, :], reads=["qT_s"], writes=[("qTt", b, g)])
                dma("sp", kTt[b][:, g, :], kT_s[g, h, :, :], reads=["kT_s"], writes=[("kTt", b, g)])
                dma("sp", vt[b][:, g, :, :], v_s[g, :, h * 128:(h + 1) * 128].rearrange("(b p) d -> p b d", p=128),
                    reads=["v_s"], writes=[("vt", b, g)])

        load_head(0)
        pidx = 0
        it = 0
        for h in range(8):
            if h + 1 < 8:
                load_head(h + 1)
            b = h % 2
            for qb in range(16):
                groups = []
                for g in range(3):
                    nb = WIN[g] // 128
                    lst = []
                    for kbk in range(max(0, qb - nb), qb + 1):
                        db = qb - kbk
                        if g == 0:
                            mi = 0 if db == 0 else 1
                        elif g == 1:
                            mi = 2 if db == 0 else (4 if db == 4 else 3)
                        else:
                            mi = 5 if db == 0 else 6
                        lst.append((kbk, mi))
                    groups.append(lst)
                po, pok = psum(4, 5) if it % 2 == 0 else psum(6, 7)
                pc, pck = psum(5, 6) if it % 2 == 0 else psum(7, 8)
                it += 1
                qsl = qTt[b][:, :, qb * 128:(qb + 1) * 128]
                for g in range(3):
                    lst = groups[g]
                    nk = len(lst) * 128
                    for c4 in range(0, len(lst), 4):
                        ch = lst[c4:c4 + 4]
                        ps_, psk = psum(2, 4)
                        k0 = ch[0][0]
                        op("pe", lambda e: e.matmul(ps_[:, 0:len(ch) * 128], qsl[:, g, :], kTt[b][:, g, k0 * 128:(k0 + len(ch)) * 128],
                                                    start=True, stop=True),
                           reads=[("qTt", b, g), ("kTt", b, g)], writes=[psk])
                        for ci, (kbk, mi) in enumerate(ch):
                            op("dve", lambda e: e.tensor_tensor(out=smk[:, (c4 + ci) * 128:(c4 + ci + 1) * 128], in0=ps_[:, ci * 128:(ci + 1) * 128],
                                                                in1=mbq[:, mi, :], op=ALU.add),
                               reads=[psk, "mbq"], writes=[("smk", c4 + ci)])
                    smks = [("smk", k_) for k_ in range(len(lst))]
                    op("dve", lambda e: e.tensor_reduce(out=colm[:, 0, g:g + 1], in_=smk[:, 0:nk], axis=AX.X, op=ALU.max),
                       reads=smks, writes=[("c0", g)])
                    op("dve", lambda e: e.tensor_scalar(out=colm[:, 1, g:g + 1], in0=colm[:, 0, g:g + 1], scalar1=-SCALE, scalar2=None, op0=ALU.mult),
                       reads=[("c0", g)], writes=[("c1", g)])
                    op("act", lambda e: e.activation(out=junk[:, 0:nk], in_=smk[:, 0:nk], func=AF.Exp, bias=colm[:, 1, g:g + 1], scale=SCALE,
                                                     accum_out=colm[:, 2, g:g + 1]),
                       reads=smks + [("c1", g)], writes=[("c2", g)])
                    done = 0
                    for c4 in range(0, len(lst), 4):
                        ch = lst[c4:c4 + 4]
                        pb, pk = psum(0, 2)
                        for ci, (kbk, mi) in enumerate(ch):
                            op("pe", lambda e: e.matmul(pb[:, ci * 128:(ci + 1) * 128], kTt[b][:, g, kbk * 128:(kbk + 1) * 128], qsl[:, g, :],
                                                        start=True, stop=True),
                               reads=[("kTt", b, g), ("qTt", b, g)], writes=[pk], signal=(ci == len(ch) - 1))
                        p_ = pT[pidx % 3]
                        pk_ = "pT%d" % (pidx % 3)
                        pidx += 1
                        nc_ = len(ch) * 128
                        op("act", lambda e: e.activation(out=p_[:, 0:nc_], in_=pb[:, 0:nc_], func=AF.Exp, scale=SCALE), reads=[pk], writes=[pk_])
                        for ci, (kbk, mi) in enumerate(ch):
                            op("pool", lambda e: e.tensor_tensor(out=p_[:, ci * 128:(ci + 1) * 128], in0=p_[:, ci * 128:(ci + 1) * 128],
                                                                 in1=masks[:, mi, :], op=ALU.mult),
                               reads=[pk_, "masks"], writes=[pk_])
                        for ci, (kbk, mi) in enumerate(ch):
                            op("pe", lambda e: e.matmul(po[:, g * 128:(g + 1) * 128], vt[b][:, g, kbk, :], p_[:, ci * 128:(ci + 1) * 128],
                                                        start=(done == 0), stop=(done == len(lst) - 1)),
                               reads=[("vt", b, g), pk_], writes=[pok], signal=True)
                            done += 1
                c1s = [("c1", g_) for g_ in range(3)]
                c2s = [("c2", g_) for g_ in range(3)]
                op("act", lambda e: e.activation(out=colm[:, 3, :], in_=colm[:, 1, :], func=AF.Exp, scale=-1.0), reads=c1s, writes=["c3"])
                op("dve", lambda e: e.tensor_tensor(out=colm[:, 4, :], in0=colm[:, 2, :], in1=colm[:, 3, :], op=ALU.mult), reads=c2s + ["c3"], writes=["c4"])
                op("dve", lambda e: e.tensor_reduce(out=colm[:, 7, 0:1], in_=colm[:, 4, :], axis=AX.X, op=ALU.add), reads=["c4"], writes=["c7"])
                if h == 0:
                    op("dve", lambda e: e.tensor_copy(out=l0[:, qb, :], in_=colm[:, 2, :]), reads=c2s, writes=["l0"])
                op("dve", lambda e: e.tensor_scalar(out=colm[:, 5, :], in0=l0[:, qb, :], scalar1=colm[:, 7, 0:1], scalar2=None, op0=ALU.mult),
                   reads=["c7", "l0"], writes=["c5"])
                op("dve", lambda e: e.reciprocal(colm[:, 5, :], colm[:, 5, :]), reads=["c5"], writes=["c5"])
                op("dve", lambda e: e.tensor_tensor(out=colm[:, 6, :], in0=colm[:, 2, :], in1=colm[:, 5, :], op=ALU.mult), reads=c2s + ["c5"], writes=["c6"])
                for g in range(3):
                    op("dve", lambda e: e.tensor_scalar(out=D3[:, g * 128:(g + 1) * 128], in0=ident_f[:, :], scalar1=colm[:, 6, g:g + 1], scalar2=None, op0=ALU.mult),
                       reads=["c6", "ident_f"], writes=[("D3", g)])
                op("pe", lambda e: e.matmul(pc[:, 0:384], ones_f[:, :], D3[:, :], start=True, stop=True), reads=["ones_f"] + [("D3", g_) for g_ in range(3)], writes=[pck])
                op("act", lambda e: e.copy(cbc[:, :], pc[:, 0:384]), reads=[pck], writes=["cbc"])
                op("dve", lambda e: e.tensor_tensor(out=acc[:, 0:128], in0=po[:, 0:128], in1=cbc[:, 0:128], op=ALU.mult), reads=[pok, "cbc"], writes=["acc0"])
                op("dve", lambda e: e.tensor_tensor(out=acc[:, 128:256], in0=po[:, 128:256], in1=cbc[:, 128:256], op=ALU.mult), reads=[pok, "cbc"], writes=["acc1"])
                op("dve", lambda e: e.tensor_tensor(out=acc2, in0=po[:, 256:384], in1=cbc[:, 256:384], op=ALU.mult), reads=[pok, "cbc"], writes=["acc2"])
                op("dve", lambda e: e.tensor_tensor(out=acc[:, 0:128], in0=acc[:, 0:128], in1=acc[:, 128:256], op=ALU.add), reads=["acc0", "acc1"], writes=["acc0"])
                op("dve", lambda e: e.tensor_tensor(out=mh[b][:, qb * 128:(qb + 1) * 128], in0=acc[:, 0:128], in1=acc2, op=ALU.add),
                   reads=["acc0", "acc2"], writes=["mh%d" % b])
            dma("sp", mT_s[h, :, 0:T], mh[b][:, 0:T], reads=["mh%d" % b], writes=["mT_s"])

        kb.barrier()
        cache = [mTf[:, i * 2048:(i + 1) * 2048] for i in range(2)]
        cb16 = [mTf[:, 4096 + i * 512:4096 + (i + 1) * 512].bitcast(BF16) for i in range(2)]
        vkeep = [mTf[:, 5120 + g * 512:5120 + (g + 1) * 512].bitcast(BF16) for g in range(3)]
        ckT = kb.sb("ckT", [128, 8, 128], BF16, es)
        qk = kb.sb("qk", [128, 24 * NS], F32, es)
        qkr = kb.sb("qkr", [1, 24 * NS], F32, es)
        srow = aTf[0:1, 0:2064].bitcast(F32).rearrange("p (h k) -> p h k", h=8)
        prow = aTf[0:1, 2064:2064 + 6192].bitcast(F32).rearrange("p (g h k) -> p g h k", g=3, h=8)
        rw = kb.sb("rw", [1, 12, 24], F32, es)
        one1 = kb.sb("one1", [1, 1], F32, es)
        pSk = kb.sb("pSk", [128, 3, 8], BF16, es)
        pnb = kb.sb("pnb", [128, 3, 8], F32, es)
        og = kb.sb("og", [128, 3, 8], F32, es)
        cbs = kb.sb("cbs", [128, 3, 8], F32, es)
        so = kb.sb("so", [128, 8], F32, es)
        sob = kb.sb("sob", [128, 8, NS], BF16, es)
        op("dve", lambda e: e.memset(one1[:], 1.0), writes=["one1"])
        op("dve", lambda e: e.tensor_tensor(out=qk[:, :], in0=sqT[:].rearrange("p a s -> p (a s)"), in1=skT[:].rearrange("p a s -> p (a s)"), op=ALU.mult),
           reads=["sqT", "skT"], writes=["qk"])
        pb, pk = psum(0, 2)
        op("pe", lambda e: e.matmul(pb[0:1, 0:24 * NS], ones_f[:, 0:1], qk[:, :], start=True, stop=True), reads=["ones_f", "qk"], writes=[pk])
        op("act", lambda e: e.copy(qkr[:, :], pb[0:1, 0:24 * NS]), reads=[pk], writes=["qkr"])
        qkr3 = qkr[:].rearrange("p (a s) -> p a s", s=NS)
        rwm, rwmm, rwl, rwe, rwt, rwc = (rw[:, k, :].rearrange("p (g h) -> p g h", g=3) for k in range(6))
        for s_ in range(NS):
            pov, povk = psum(4, 5)
            for g in range(3):
                cbuf = cache[g % 2]
                ck = "cache%d" % (g % 2)
                src = ckv[g][li, s_, :, :].rearrange("(j d) c -> j d c", d=DIL[g])[:, 0, :]
                dma("sp", cbuf[:, :], src, writes=[ck])
                c16 = cb16[g % 2]
                c16k = "cb16_%d" % (g % 2)
                op("pool", lambda e: e.tensor_copy(out=c16[:, 0:1024], in_=cbuf[:, 0:1024]), reads=[ck], writes=[c16k])
                op("act", lambda e: e.copy(vkeep[g][:, :], cbuf[:, 1024:2048]), reads=[ck], writes=["vk%d" % g])
                for h in range(8):
                    pt, ptk = psum(0, 2)
                    ptb = pt[:, 0:64].bitcast(BF16)
                    op("pe", lambda e: e.transpose(ptb[:, 0:128], c16[:, h * 128:(h + 1) * 128], ident_b[:, :]),
                       reads=[c16k, "ident_b"], writes=[ptk])
                    op("dve", lambda e: e.tensor_copy(out=ckT[:, h, :], in_=ptb[:, 0:128]), reads=[ptk], writes=["ckT"])
                for hq in range(2):
                    pr, prk = psum(2, 4)
                    for hh in range(4):
                        h = hq * 4 + hh
                        op("pe", lambda e: e.matmul(pr[0:1, hh * 128:(hh + 1) * 128], sqT[:, g * 8 + h, s_:s_ + 1], ckT[:, h, :], start=True, stop=True),
                           reads=["sqT", "ckT"], writes=[prk], signal=(hh == 3))
                    op("act", lambda e: e.copy(srow[0:1, hq * 4:(hq + 1) * 4, 0:128], pr[0:1, :].rearrange("p (h k) -> p h k", h=4)),
                       reads=[prk], writes=["srow"])
                op("dve", lambda e: e.tensor_copy(out=srow[0:1, :, 128:129], in_=qkr3[0:1, g * 8:(g + 1) * 8, s_:s_ + 1]), reads=["qkr"], writes=["srow"])
                op("dve", lambda e: e.tensor_reduce(out=rwm[0:1, g, :], in_=srow[0:1, :, :], axis=AX.X, op=ALU.max), reads=["srow"], writes=["rw"])
                op("dve", lambda e: e.tensor_tensor(out=srow[0:1, :, :], in0=srow[0:1, :, :],
                                                    in1=rwm[0:1, g, :].unsqueeze(2).broadcast_to([1, 8, 129]), op=ALU.subtract),
                   reads=["srow", "rw"], writes=["srow"])
                op("act", lambda e: e.activation(out=prow[0:1, g, :, :], in_=srow[0:1, :, :], func=AF.Exp, scale=SCALE), reads=["srow"], writes=["prow"])
                op("dve", lambda e: e.tensor_reduce(out=rwl[0:1, g, :], in_=prow[0:1, g, :, :], axis=AX.X, op=ALU.add), reads=["prow"], writes=["rw"])
                pp, ppk = psum(2, 4)
                for h in range(8):
                    op("pe", lambda e: e.matmul(pp[:, h:h + 1], prow[0:1, g, h, 0:128], one1[0:1, 0:1], start=True, stop=True),
                       reads=["prow", "one1"], writes=[ppk], signal=(h == 7))
                op("dve", lambda e: e.tensor_copy(out=pSk[:, g, :], in_=pp[:, 0:8]), reads=[ppk], writes=["pSk"])
                pn_, pnk = psum(2, 4)
                op("pe", lambda e: e.matmul(pn_[:, 0:8], ones_f[0:1, :], prow[0:1, g, :, 128], start=True, stop=True), reads=["ones_f", "prow"], writes=[pnk])
                op("dve", lambda e: e.tensor_copy(out=pnb[:, g, :], in_=pn_[:, 0:8]), reads=[pnk], writes=["pnb"])
                for h in range(8):
                    op("pe", lambda e: e.matmul(pov[:, g * 8 + h:g * 8 + h + 1], vkeep[g][:, h * 128:(h + 1) * 128], pSk[:, g, h:h + 1], start=True, stop=True),
                       reads=["vk%d" % g, "pSk"], writes=[povk], signal=(h == 7))
            op("dve", lambda e: e.tensor_tensor(out=og[:, :, :], in0=svT[:, :, s_].rearrange("p (g h) -> p g h", g=3), in1=pnb[:, :, :], op=ALU.mult),
               reads=["svT", "pnb"], writes=["og"])
            op("dve", lambda e: e.tensor_tensor(out=og[:, :, :], in0=og[:, :, :], in1=pov[:, 0:24].rearrange("p (g h) -> p g h", g=3), op=ALU.add),
               reads=["og", povk], writes=["og"])
            op("dve", lambda e: e.tensor_scalar(out=rwmm[0:1, :, :], in0=rwm[0:1, :, :], scalar1=SCALE, scalar2=None, op0=ALU.mult), reads=["rw"], writes=["rw"])
            op("act", lambda e: e.activation(out=rwe[0:1, :, :], in_=rwmm[0:1, :, :], func=AF.Exp), reads=["rw"], writes=["rw"])
            op("dve", lambda e: e.tensor_tensor(out=rwe[0:1, :, :], in0=rwe[0:1, :, :], in1=rwl[0:1, :, :], op=ALU.mult), reads=["rw"], writes=["rw"])
            op("dve", lambda e: e.tensor_tensor(out=rwt[0:1, 0, :], in0=rwe[0:1, 0, :], in1=rwe[0:1, 1, :], op=ALU.add), reads=["rw"], writes=["rw"])
            op("dve", lambda e: e.tensor_tensor(out=rwt[0:1, 0, :], in0=rwt[0:1, 0, :], in1=rwe[0:1, 2, :], op=ALU.add), reads=["rw"], writes=["rw"])
            op("dve", lambda e: e.tensor_tensor(out=rwc[0:1, :, :], in0=rwt[0:1, 0, :].unsqueeze(1).broadcast_to([1, 3, 8]),
                                                in1=rwl[0:1, :, 0:1].broadcast_to([1, 3, 8]), op=ALU.mult), reads=["rw"], writes=["rw"])
            op("dve", lambda e: e.reciprocal(rwc[0:1, :, :], rwc[0:1, :, :]), reads=["rw"], writes=["rw"])
            op("dve", lambda e: e.tensor_tensor(out=rwc[0:1, :, :], in0=rwc[0:1, :, :], in1=rwe[0:1, :, :], op=ALU.mult), reads=["rw"], writes=["rw"])
            pcb, pcbk = psum(2, 4)
            op("pe", lambda e: e.matmul(pcb[:, 0:24], ones_f[0:1, :], rw[0:1, 5, :], start=True, stop=True), reads=["ones_f", "rw"], writes=[pcbk])
            op("dve", lambda e: e.tensor_tensor(out=og[:, :, :], in0=og[:, :, :], in1=pcb[:, 0:24].rearrange("p (g h) -> p g h", g=3), op=ALU.mult),
               reads=["og", pcbk], writes=["og"])
            op("dve", lambda e: e.tensor_tensor(out=so[:, :], in0=og[:, 0, :], in1=og[:, 1, :], op=ALU.add), reads=["og"], writes=["so"])
            op("dve", lambda e: e.tensor_tensor(out=sob[:, :, s_], in0=so[:, :], in1=og[:, 2, :], op=ALU.add), reads=["so", "og"], writes=["sob"])
        with nc.allow_non_contiguous_dma(reason="tiny sample columns"):
            dma("sp", mT_s[:, :, T:TC].rearrange("h p s -> p h s"), sob[:, :, :], reads=["sob"], writes=["mT_s"])
        kb.barrier()
        es.close()

    LCH = 32
    NCH = T // LCH

    def win_phase(li, t, segs):
        g0 = t * TW
        for dc in range(KC):
            wv, wk = wr.get(("win", li, dc))
            w3 = wv[:, 0:KC * 128].rearrange("p (k c) -> p k c", k=KC)
            for (c0, n, S) in segs:
                pb, pk = psum()
                for kc in range(KC):
                    op("pe", lambda e: e.matmul(pb[:, 0:n], w3[:, kc, :], hT[:, kc, c0:c0 + n], start=(kc == 0), stop=(kc == KC - 1)),
                       reads=["hT", wk], writes=[pk], signal=(kc == KC - 1))
                r = rr["cg"] % 2
                rr["cg"] += 1
                op("act", lambda e: e.copy(cgt[r][:, 0:n], pb[:, 0:n]), reads=[pk], writes=["cg%d" % r])
                gc = g0 if S == 1 else T
                if S == 1:
                    dma("sp", uT_s[dc, :, gc:gc + n], cgt[r][:, 0:n], reads=["cg%d" % r], writes=["uT_s"])
                else:
                    with nc.allow_non_contiguous_dma(reason="tiny sample columns"):
                        dma("sp", uT_s[dc, :, gc:gc + n], cgt[r][:, 0:n], reads=["cg%d" % r], writes=["uT_s"])

    def mixer_out_ssm(li, t, segs):
        g0 = t * TW
        minT = aT[:, 0:KC, :]
        dma("sp", minT[:, :, 0:TW], zT_s[:, :, g0:g0 + TW].rearrange("k p t -> p k t"), reads=["zT_s"], writes=["aT"])
        if len(segs) > 1:
            with nc.allow_non_contiguous_dma(reason="tiny sample columns"):
                dma("sp", minT[:, :, TW:TWS], zT_s[:, :, T:TC].rearrange("k p t -> p k t"), reads=["zT_s"], writes=["aT"])
        psts = [stats_begin() if si == 0 else psum(6, 7) for si in range(len(segs))]
        for dc in range(KC):
            wv, wk = wr.get(("glu", li, dc))
            wval = wv[:, 0:KC * 128].rearrange("p (k c) -> p k c", k=KC)
            wgat = wv[:, KC * 128:2 * KC * 128].rearrange("p (k c) -> p k c", k=KC)
            for si, (c0, n, S) in enumerate(segs):
                pv, pvk = psum()
                pg, pgk = psum()
                for kc in range(KC):
                    op("pe", lambda e: e.matmul(pv[:, 0:n], wval[:, kc, :], minT[:, kc, c0:c0 + n], start=(kc == 0), stop=(kc == KC - 1)),
                       reads=["aT", wk], writes=[pvk], signal=(kc == KC - 1))
                for kc in range(KC):
                    op("pe", lambda e: e.matmul(pg[:, 0:n], wgat[:, kc, :], minT[:, kc, c0:c0 + n], start=(kc == 0), stop=(kc == KC - 1)),
                       reads=["aT", wk], writes=[pgk], signal=(kc == KC - 1))
                r = rr["cg"] % 2
                rr["cg"] += 1
                op("act", lambda e: e.activation(out=sgt[r][:, 0:n], in_=pg[:, 0:n], func=AF.Sigmoid), reads=[pgk], writes=["sg%d" % r])
                op("dve", lambda e: e.tensor_tensor(out=mT[:, dc, c0:c0 + n], in0=pv[:, 0:n], in1=sgt[r][:, 0:n], op=ALU.mult),
                   reads=[pvk, "sg%d" % r], writes=["mT"])
                stats_add(psts[si], mT[:, dc, c0:c0 + n], "mT", c0, n, dc == 0, dc == KC - 1)
        for si, (c0, n, S) in enumerate(segs):
            stats_end(psts[si], c0, n)

    def ssm_phase(li):
        kb.barrier()
        es = contextlib.ExitStack()
        aTf = aT[:].rearrange("p a b -> p (a b)")
        hTf = hT[:].rearrange("p a b -> p (a b)")
        xTf = xT[:].rearrange("p a b -> p (a b)")
        mTf = mT[:].rearrange("p a b -> p (a b)")
        Wb = aTf[:, 0:16384].rearrange("p (k g m) -> p k g m", k=KC, g=8)
        npi_t = aTf[:, 16384:16384 + 4096].bitcast(F32).rearrange("p (a i) -> p a i", i=LCH)
        Wc = xTf[:, 0:8192].bitcast(BF16).rearrange("p (k j r m) -> p k j r m", k=KC, j=4, r=2)
        pr_parts = [cgt[0], cgt[1], cvt[0], cvt[1]]
        pi_parts = [sgt[0], sgt[1], rstd, rtmp]

        def tab(parts, pair):
            return parts[pair // 16][:, (pair % 16) * LCH:(pair % 16 + 1) * LCH]
        small = hTf[:, 0:3840].bitcast(F32).rearrange("p (k w) -> p k w", w=64)
        smallB = hTf[0:64, 3840:3840 + 4096].bitcast(F32).rearrange("p (k w) -> p k w", w=128)
        dsk = kb.sb("dsk", [128, KC], F32, es)
        gmask = kb.sb("gmask", [128, 8], F32, es)
        nat = kb.sb("nat", [128, 128], F32, es)
        dma("sp", gmask[:], c_gmask[:, :], writes=["gmask"])
        dma("sp", nat[0:KC, :], d_skip[li, :].rearrange("(k p) -> k p", p=128), writes=["nat"])
        pb, pk = psum(5, 8)
        op("pe", lambda e: e.transpose(pb[:, 0:KC], nat[0:KC, :], ident_f[0:KC, 0:KC]), reads=["nat", "ident_f"], writes=[pk])
        op("act", lambda e: e.copy(dsk[:, :], pb[:, 0:KC]), reads=[pk], writes=["dsk"])

        def abar_chain(P, W, sm, lam_r_ap, lam_i_ap, ldt_ap, key):
            S_ = lambda k: sm[0:P, k, 0:W]
            o = lambda fn, **kw: op("dve", fn, reads=[key], writes=[key])
            o(lambda e: e.tensor_scalar(out=S_(0), in0=lam_r_ap, scalar1=-1e-4, scalar2=None, op0=ALU.min))
            o(lambda e: e.tensor_copy(out=S_(1), in_=lam_i_ap))
            op("act", lambda e: e.activation(out=S_(2), in_=ldt_ap, func=AF.Exp), reads=[key], writes=[key])
            o(lambda e: e.tensor_tensor(out=S_(3), in0=S_(1), in1=S_(2), op=ALU.mult))
            o(lambda e: e.tensor_scalar(out=S_(3), in0=S_(3), scalar1=1.0 / 16.0, scalar2=None, op0=ALU.mult))
            o(lambda e: e.tensor_tensor(out=S_(4), in0=S_(3), in1=S_(3), op=ALU.mult))
            o(lambda e: e.tensor_scalar(out=S_(5), in0=S_(4), scalar1=1.0 / 362880.0, scalar2=None, op0=ALU.mult))
            for cf in (-1.0 / 5040.0, 1.0 / 120.0, -1.0 / 6.0):
                o(lambda e: e.scalar_tensor_tensor(out=S_(5), in0=S_(5), scalar=cf, in1=S_(4), op0=ALU.add, op1=ALU.mult))
            o(lambda e: e.scalar_tensor_tensor(out=S_(5), in0=S_(5), scalar=1.0, in1=S_(3), op0=ALU.add, op1=ALU.mult))
            o(lambda e: e.tensor_scalar(out=S_(6), in0=S_(4), scalar1=-1.0 / 3628800.0, scalar2=None, op0=ALU.mult))
            for cf in (1.0 / 40320.0, -1.0 / 720.0, 1.0 / 24.0, -0.5):
                o(lambda e: e.scalar_tensor_tensor(out=S_(6), in0=S_(6), scalar=cf, in1=S_(4), op0=ALU.add, op1=ALU.mult))
            o(lambda e: e.tensor_scalar(out=S_(6), in0=S_(6), scalar1=1.0, scalar2=None, op0=ALU.add))
            for _ in range(4):
                o(lambda e: e.tensor_tensor(out=S_(7), in0=S_(5), in1=S_(6), op=ALU.mult))
                o(lambda e: e.tensor_tensor(out=S_(11), in0=S_(5), in1=S_(5), op=ALU.mult))
                o(lambda e: e.tensor_scalar(out=S_(6), in0=S_(11), scalar1=-2.0, scalar2=1.0, op0=ALU.mult, op1=ALU.add))
                o(lambda e: e.tensor_scalar(out=S_(5), in0=S_(7), scalar1=2.0, scalar2=None, op0=ALU.mult))
            o(lambda e: e.tensor_tensor(out=S_(11), in0=S_(0), in1=S_(2), op=ALU.mult))
            op("act", lambda e: e.activation(out=S_(8), in_=S_(11), func=AF.Exp), reads=[key], writes=[key])
            o(lambda e: e.tensor_tensor(out=S_(9), in0=S_(8), in1=S_(6), op=ALU.mult))
            o(lambda e: e.tensor_tensor(out=S_(10), in0=S_(8), in1=S_(5), op=ALU.mult))
            return S_(9), S_(10), S_(0), S_(1)

        def load_A(dst, src2d):
            dma("sp", nat[0:64, :], src2d.rearrange("(g s) p -> g (s p)", s=2), writes=["nat"])
            pb, pk = psum(5, 8)
            op("pe", lambda e: e.transpose(pb[:, 0:64], nat[0:64, :], ident_f[0:64, 0:64]), reads=["nat", "ident_f"], writes=[pk])
            op("act", lambda e: e.copy(dst, pb[:, 0:64]), reads=[pk], writes=["small"])
        A_ = lambda k: small[:, k, :]
        load_A(A_(12), lam_re[li, :, :])
        load_A(A_(13), lam_im[li, :, :])
        dma("sp", nat[0:64, 0:2], log_dt[li, :].rearrange("(g s) -> g s", s=2), writes=["nat"])
        op("dve", lambda e: e.tensor_copy(out=nat[0:64, 64:128].rearrange("p (s q) -> p s q", s=2)[:, :, :] if False else smallB[0:64, 15, :].rearrange("p (s q) -> p s q", s=2),
                                          in_=nat[0:64, 0:2].unsqueeze(2).broadcast_to([64, 2, 64])), reads=["nat"], writes=["smallB"])
        pb, pk = psum(5, 8)
        op("pe", lambda e: e.transpose(pb[:, 0:64], smallB[0:64, 15, :], ident_f[0:64, 0:64]), reads=["smallB", "ident_f"], writes=[pk])
        op("act", lambda e: e.copy(A_(14), pb[:, 0:64]), reads=[pk], writes=["small"])
        arA, aiA, _, _ = abar_chain(128, 64, small, A_(12), A_(13), A_(14), "small")
        for pair0 in range(0, 64, 16):
            pass
        prv = lambda i: [p_[:, :].rearrange("p (a i) -> p a i", i=LCH)[:, :, i] for p_ in pr_parts]
        piv = lambda i: [p_[:, 0:512].rearrange("p (a i) -> p a i", i=LCH)[:, :, i] for p_ in pi_parts]
        tkeys = ["cg0", "cg1", "cv0", "cv1", "sg0", "sg1", "rstd", "rtmp", "npi"]
        for q in range(4):
            op("dve", lambda e: e.tensor_copy(out=prv(0)[q], in_=arA[:, q * 16:(q + 1) * 16]), reads=["small"], writes=tkeys)
            op("dve", lambda e: e.tensor_copy(out=piv(0)[q], in_=aiA[:, q * 16:(q + 1) * 16]), reads=["small"], writes=tkeys)
        for i in range(1, LCH):
            for q in range(4):
                a_r = arA[:, q * 16:(q + 1) * 16]
                a_i = aiA[:, q * 16:(q + 1) * 16]
                t1, t2 = small[:, 15, 0:16], small[:, 16, 0:16]
                op("dve", lambda e: e.tensor_tensor(out=t1, in0=prv(i - 1)[q], in1=a_r, op=ALU.mult), reads=tkeys + ["small"], writes=["small"])
                op("dve", lambda e: e.tensor_tensor(out=t2, in0=piv(i - 1)[q], in1=a_i, op=ALU.mult), reads=tkeys + ["small"], writes=["small"])
                op("dve", lambda e: e.tensor_tensor(out=prv(i)[q], in0=t1, in1=t2, op=ALU.subtract), reads=["small"], writes=tkeys)
                op("dve", lambda e: e.tensor_tensor(out=t1, in0=prv(i - 1)[q], in1=a_i, op=ALU.mult), reads=tkeys + ["small"], writes=["small"])
                op("dve", lambda e: e.tensor_tensor(out=t2, in0=piv(i - 1)[q], in1=a_r, op=ALU.mult), reads=tkeys + ["small"], writes=["small"])
                op("dve", lambda e: e.tensor_tensor(out=piv(i)[q], in0=t1, in1=t2, op=ALU.add), reads=["small"], writes=tkeys)
        for q in range(4):
            op("dve", lambda e: e.tensor_scalar(out=npi_t[:, q * 16:(q + 1) * 16, :], in0=pi_parts[q][:, 0:512].rearrange("p (a i) -> p a i", i=LCH),
                                                scalar1=-1.0, scalar2=None, op0=ALU.mult), reads=tkeys, writes=tkeys)

        A8 = kb.sb("A8", [128, 3, 64, 8], F32, es)
        Ar_all = [p_[:, 0:512].rearrange("p (a i) -> p a i", i=LCH)[:, :, LCH - 1] for p_ in pr_parts]
        Ai_all = [p_[:, 0:512].rearrange("p (a i) -> p a i", i=LCH)[:, :, LCH - 1] for p_ in pi_parts]
        for q in range(4):
            qs = slice(q * 16, (q + 1) * 16)
            op("dve", lambda e: e.tensor_copy(out=A8[:, 0, qs, 0], in_=Ar_all[q]), reads=tkeys, writes=["A8"])
            op("dve", lambda e: e.tensor_copy(out=A8[:, 1, qs, 0], in_=Ai_all[q]), reads=tkeys, writes=["A8"])
            for j in range(1, 8):
                t1, t2 = small[:, 15, 0:16], small[:, 16, 0:16]
                op("dve", lambda e: e.tensor_tensor(out=t1, in0=A8[:, 0, qs, j - 1], in1=Ar_all[q], op=ALU.mult), reads=tkeys + ["A8", "small"], writes=["small"])
                op("dve", lambda e: e.tensor_tensor(out=t2, in0=A8[:, 1, qs, j - 1], in1=Ai_all[q], op=ALU.mult), reads=tkeys + ["A8", "small"], writes=["small"])
                op("dve", lambda e: e.tensor_tensor(out=A8[:, 0, qs, j], in0=t1, in1=t2, op=ALU.subtract), reads=["small"], writes=["A8"])
                op("dve", lambda e: e.tensor_tensor(out=t1, in0=A8[:, 0, qs, j - 1], in1=Ai_all[q], op=ALU.mult), reads=tkeys + ["A8", "small"], writes=["small"])
                op("dve", lambda e: e.tensor_tensor(out=t2, in0=A8[:, 1, qs, j - 1], in1=Ar_all[q], op=ALU.mult), reads=tkeys + ["A8", "small"], writes=["small"])
                op("dve", lambda e: e.tensor_tensor(out=A8[:, 1, qs, j], in0=t1, in1=t2, op=ALU.add), reads=["small"], writes=["A8"])
        op("dve", lambda e: e.tensor_scalar(out=A8[:, 2, :, :], in0=A8[:, 1, :, :], scalar1=-1.0, scalar2=None, op0=ALU.mult), reads=["A8"], writes=["A8"])

        B_ = lambda k: smallB[0:64, k, :]
        for (dst, src) in ((B_(12), lam_re), (B_(13), lam_im)):
            dma("sp", nat[:, 0:64], src[li, :, :], writes=["nat"])
            pb, pk = psum(5, 8)
            op("pe", lambda e: e.transpose(pb[0:64, 0:128], nat[:, 0:64], ident_f[:, :]), reads=["nat", "ident_f"], writes=[pk])
            op("act", lambda e: e.copy(dst, pb[0:64, 0:128]), reads=[pk], writes=["smallB"])
        dma("sp", B_(14), log_dt[li:li + 1, :].partition_broadcast(64).rearrange("p a g -> p (a g)") if False else log_dt[li:li + 1, :].broadcast_to([64, 128]), writes=["smallB"])
        arB, aiB, lrB, liB = abar_chain(64, 128, smallB, B_(12), B_(13), B_(14), "smallB")
        ob = lambda fn: op("dve", fn, reads=["smallB"], writes=["smallB"])
        ob(lambda e: e.tensor_scalar(out=B_(11), in0=arB, scalar1=-1.0, scalar2=None, op0=ALU.add))
        ob(lambda e: e.tensor_tensor(out=B_(7), in0=lrB, in1=lrB, op=ALU.mult))
        ob(lambda e: e.tensor_tensor(out=B_(2), in0=liB, in1=liB, op=ALU.mult))
        ob(lambda e: e.tensor_tensor(out=B_(7), in0=B_(7), in1=B_(2), op=ALU.add))
        ob(lambda e: e.reciprocal(B_(7), B_(7)))
        ob(lambda e: e.tensor_tensor(out=B_(3), in0=B_(11), in1=lrB, op=ALU.mult))
        ob(lambda e: e.tensor_tensor(out=B_(2), in0=aiB, in1=liB, op=ALU.mult))
        ob(lambda e: e.tensor_tensor(out=B_(3), in0=B_(3), in1=B_(2), op=ALU.add))
        ob(lambda e: e.tensor_tensor(out=B_(3), in0=B_(3), in1=B_(7), op=ALU.mult))
        ob(lambda e: e.tensor_tensor(out=B_(4), in0=aiB, in1=lrB, op=ALU.mult))
        ob(lambda e: e.tensor_tensor(out=B_(2), in0=B_(11), in1=liB, op=ALU.mult))
        ob(lambda e: e.tensor_tensor(out=B_(4), in0=B_(4), in1=B_(2), op=ALU.subtract))
        ob(lambda e: e.tensor_tensor(out=B_(4), in0=B_(4), in1=B_(7), op=ALU.mult))
        bre = mTf[0:64, 0:2048].rearrange("p (g c) -> p g c", c=16)
        bim = mTf[0:64, 2048:4096].rearrange("p (g c) -> p g c", c=16)
        bbr = mTf[0:64, 4096:6144].rearrange("p (g c) -> p g c", c=16)
        bbi = mTf[0:64, 6144:8192].rearrange("p (g c) -> p g c", c=16)
        with nc.allow_non_contiguous_dma(reason="b tensors 64B runs"):
            dma("sp", bre, b_re[li, :, :, :].rearrange("g p c -> p g c"), writes=["bb"])
            dma("sp", bim, b_im[li, :, :, :].rearrange("g p c -> p g c"), writes=["bb"])
        cr = B_(3).unsqueeze(2).broadcast_to([64, 128, 16])
        ci = B_(4).unsqueeze(2).broadcast_to([64, 128, 16])
        o2 = lambda fn: op("dve", fn, reads=["bb", "smallB"], writes=["bb"])
        o2(lambda e: e.tensor_tensor(out=bbr, in0=bre, in1=cr, op=ALU.mult))
        o2(lambda e: e.tensor_tensor(out=bbi, in0=bim, in1=ci, op=ALU.mult))
        o2(lambda e: e.tensor_tensor(out=bbr, in0=bbr, in1=bbi, op=ALU.subtract))
        o2(lambda e: e.tensor_tensor(out=bbi, in0=bre, in1=ci, op=ALU.mult))
        o2(lambda e: e.tensor_tensor(out=bre, in0=bim, in1=cr, op=ALU.mult))
        o2(lambda e: e.tensor_tensor(out=bbi, in0=bbi, in1=bre, op=ALU.add))
        for kc in range(KC):
            for r_, src in ((0, bbr), (1, bbi)):
                pb, pk = psum(5, 8)
                op("pe", lambda e: e.transpose(pb[:, 0:64], src[:, kc * 8:(kc + 1) * 8, :].rearrange("p g c -> p (g c)"), ident_f[0:64, 0:64]),
                   reads=["bb", "ident_f"], writes=[pk])
                for gl in range(8):
                    op("act" if gl % 2 else "dve",
                       (lambda e: e.activation(out=Wb[:, kc, gl, r_ * 64:(r_ + 1) * 64], in_=pb[:, 0:64], func=AF.Identity, scale=gmask[:, gl:gl + 1])) if gl % 2 else
                       (lambda e: e.tensor_scalar(out=Wb[:, kc, gl, r_ * 64:(r_ + 1) * 64], in0=pb[:, 0:64], scalar1=gmask[:, gl:gl + 1], scalar2=None, op0=ALU.mult)),
                       reads=[pk, "gmask"], writes=["Wb"])
        kb.barrier()
        Cn = [hTf[0:64, r_ * 4096:(r_ + 1) * 4096].bitcast(F32).rearrange("p (s c q) -> p s c q", s=2, c=16) for r_ in range(2)]
        dma("sp", Cn[0], c_re[li, :, :, :].rearrange("(g s) c q -> g s c q", s=2), writes=["Cn"])
        dma("sp", Cn[1], c_im[li, :, :, :].rearrange("(g s) c q -> g s c q", s=2), writes=["Cn"])
        op("dve", lambda e: e.memset(xTf[:, 0:8192], 0.0), writes=["Wc"])
        for r_ in range(2):
            for c in range(16):
                pb, pk = psum(5, 8)
                op("dve", lambda e: e.tensor_copy(out=nat[0:64, :].rearrange("p (s q) -> p s q", s=2), in_=Cn[r_][:, :, c, :]), reads=["Cn"], writes=["nat"])
                op("pe", lambda e: e.transpose(pb[:, 0:64], nat[0:64, :], ident_f[0:64, 0:64]), reads=["nat", "ident_f"], writes=[pk])
                for s in range(2):
                    for j4 in range(4):
                        src = pb[s * 64:(s + 1) * 64, 0:64].rearrange("p (k j) -> p k j", j=4)[:, :, j4]
                        dst = Wc[s * 64:(s + 1) * 64, :, j4, r_, 32 * j4 + 16 * s + c]
                        sc = 1.0 if r_ == 0 else -1.0
                        if (s + j4) % 2:
                            op("act", lambda e: e.mul(dst, src, sc), reads=[pk], writes=["Wc"])
                        else:
                            op("dve", lambda e: e.tensor_scalar(out=dst, in0=src, scalar1=sc, scalar2=None, op0=ALU.mult), reads=[pk], writes=["Wc"])
        kb.barrier()
        XR = [mTf[:, (2 * q) * TC:(2 * q + 1) * TC] for q in range(2)]
        XI = [mTf[:, (2 * q + 1) * TC:(2 * q + 2) * TC] for q in range(2)]
        ya = kb.sb("ya", [128, TW], F32, es)
        yb2 = aTf[:, 20480:20480 + 1024].bitcast(F32)
        Xs = kb.sb("Xs", [128, 4, NCH], F32, es)
        h0t = kb.sb("h0t", [128, 64, NS, 2], F32, es)
        h0 = h0t[:]
        sto = kb.sb("sto", [128, 64, 2], F32, es)
        stos = kb.sb("stos", [128, NS, 64, 2], F32, es)
        ubf = hTf[:, 0:TC]
        xbr = hTf[:, TC:2 * TC]
        xbi = hTf[:, 2 * TC:3 * TC]
        zst = hTf[:, 3 * TC:4 * TC]
        for s_ in range(NS):
            with nc.allow_non_contiguous_dma(reason="state 8B runs"):
                dma("sp", h0[:, :, s_, :], st_ssm[li, s_, :].rearrange("(a p r) -> p a r", p=128, r=2), writes=["h0"])
        coltiles = [(tq * TW, TW) for tq in range(NT)] + [(T, NS)]
        X3R = [x[:, 0:T].rearrange("p (l n) -> p l n", n=NCH) for x in XR]
        X3I = [x[:, 0:T].rearrange("p (l n) -> p l n", n=NCH) for x in XI]
        STT = lambda o_, a_, sc_, b_: (lambda e: e.scalar_tensor_tensor(out=o_, in0=a_, scalar=sc_, in1=b_, op0=ALU.mult, op1=ALU.add))
        for kc in range(KC):
            dma("pool", ubf[:, :], uT_s[kc, :, :], reads=["uT_s"], writes=["ubf"])
            ybanks = [(pbank[k], ("ps", k)) for k in range(5)]
            for jp in range(2):
                pairs = [kc * 4 + 2 * jp + q for q in range(2)]
                tabs = []
                for q in range(2):
                    pair = pairs[q]
                    j4 = 2 * jp + q
                    PR, PI, NPI = tab(pr_parts, pair), tab(pi_parts, pair), npi_t[:, pair, :]
                    tabs.append((PR, PI, NPI))
                    for (c0, n) in coltiles:
                        for r_, dstx, dk in ((0, XR[q], ("xr", q)), (1, XI[q], ("xi", q))):
                            pb, pk = psum(5, 8)
                            for s in range(2):
                                op("pe", lambda e: e.matmul(pb[s * 64:(s + 1) * 64, 0:n], Wb[:, kc, 2 * j4 + s, r_ * 64:(r_ + 1) * 64], ubf[:, c0:c0 + n],
                                                            start=True, stop=True), reads=["Wb", "ubf"], writes=[pk], signal=(s == 1))
                            if n == TW:
                                n0 = c0 // LCH
                                dview = dstx[:, 0:T].rearrange("p (l n) -> p l n", n=NCH)[:, :, n0:n0 + TW // LCH]
                                op("act", lambda e: e.copy(dview, pb[:, 0:TW].rearrange("p (n l) -> p l n", l=LCH)), reads=[pk], writes=[dk])
                            else:
                                op("act", lambda e: e.copy(dstx[:, c0:c0 + n], pb[:, 0:n]), reads=[pk], writes=[dk])
                allk = [("xr", 0), ("xi", 0), ("xr", 1), ("xi", 1), "Xs"] + tkeys
                kb.fence("dve", reads=allk, writes=allk)
                ox = kb.opx
                l2 = [0, 0]
                l4 = [0, 0]
                for i in range(1, LCH):
                    c1 = [0, 0]
                    c3 = [0, 0]
                    for q in range(2):
                        PR, PI, NPI = tabs[q]
                        c1[q] = ox("dve", STT(X3R[q][:, i, :], X3R[q][:, i - 1, :], PR[:, 0:1], X3R[q][:, i, :]), after=[l2[q]])
                    for q in range(2):
                        PR, PI, NPI = tabs[q]
                        c3[q] = ox("dve", STT(X3I[q][:, i, :], X3I[q][:, i - 1, :], PR[:, 0:1], X3I[q][:, i, :]), after=[l4[q]])
                    for q in range(2):
                        PR, PI, NPI = tabs[q]
                        l2n = ox("dve", STT(X3R[q][:, i, :], X3I[q][:, i - 1, :], NPI[:, 0:1], X3R[q][:, i, :]), after=[c1[q], l4[q]])
                        l2[q] = l2n
                    for q in range(2):
                        PR, PI, NPI = tabs[q]
                        l4[q] = ox("dve", STT(X3I[q][:, i, :], X3R[q][:, i - 1, :], PI[:, 0:1], X3I[q][:, i, :]), after=[c3[q], l2[q] if False else c1[q]])
                XRs = [Xs[:, 2 * q, :] for q in range(2)]
                XIs = [Xs[:, 2 * q + 1, :] for q in range(2)]
                e2 = [0, 0]
                e4 = [0, 0]
                for q in range(2):
                    e2[q] = ox("dve", lambda e: e.tensor_copy(out=XRs[q], in_=X3R[q][:, LCH - 1, :]), after=[l2[q], l4[q]])
                    e4[q] = ox("dve", lambda e: e.tensor_copy(out=XIs[q], in_=X3I[q][:, LCH - 1, :]), after=[l2[q], l4[q]])
                X8R = [x.rearrange("p (m j) -> p m j", j=8) for x in XRs]
                X8I = [x.rearrange("p (m j) -> p m j", j=8) for x in XIs]
                A8q = [(A8[:, 0, pairs[q], :], A8[:, 1, pairs[q], :], A8[:, 2, pairs[q], :]) for q in range(2)]

                def cstep(dstR, dstI, srcR, srcI, coef, dR, dI):
                    c1 = [0, 0]
                    c3 = [0, 0]
                    o2 = [0, 0]
                    o4 = [0, 0]
                    for q in range(2):
                        c1[q] = ox("dve", STT(dstR[q], srcR[q], coef[q][0], dstR[q]), after=[dR[q], dI[q]])
                    for q in range(2):
                        c3[q] = ox("dve", STT(dstI[q], srcI[q], coef[q][0], dstI[q]), after=[dR[q], dI[q]])
                    for q in range(2):
                        o2[q] = ox("dve", STT(dstR[q], srcI[q], coef[q][2], dstR[q]), after=[c1[q]])
                    for q in range(2):
                        o4[q] = ox("dve", STT(dstI[q], srcR[q], coef[q][1], dstI[q]), after=[c3[q]])
                    return o2, o4
                for j in range(1, 8):
                    e2, e4 = cstep([x[:, :, j] for x in X8R], [x[:, :, j] for x in X8I], [x[:, :, j - 1] for x in X8R], [x[:, :, j - 1] for x in X8I],
                                   [(A8q[q][0][:, 0:1], A8q[q][1][:, 0:1], A8q[q][2][:, 0:1]) for q in range(2)], e2, e4)
                for m_ in range(1, 8):
                    e2, e4 = cstep([x[:, m_, 7:8] for x in X8R], [x[:, m_, 7:8] for x in X8I], [x[:, m_ - 1, 7:8] for x in X8R], [x[:, m_ - 1, 7:8] for x in X8I],
                                   [(A8q[q][0][:, 7:8], A8q[q][1][:, 7:8], A8q[q][2][:, 7:8]) for q in range(2)], e2, e4)
                f2, f4 = e2, e4
                for j in range(7):
                    g2, g4 = cstep([x[:, 1:8, j] for x in X8R], [x[:, 1:8, j] for x in X8I], [x[:, 0:7, 7] for x in X8R], [x[:, 0:7, 7] for x in X8I],
                                   [(A8q[q][0][:, j:j + 1], A8q[q][1][:, j:j + 1], A8q[q][2][:, j:j + 1]) for q in range(2)], e2, e4)
                    f2 = [max(f2[q], g2[q]) for q in range(2)]
                    f4 = [max(f4[q], g4[q]) for q in range(2)]
                e2, e4 = f2, f4
                for i in range(LCH):
                    ca = [0, 0]
                    cc = [0, 0]
                    for q in range(2):
                        PR, PI, NPI = tabs[q]
                        ca[q] = ox("dve", STT(X3R[q][:, i, 1:NCH], XRs[q][:, 0:NCH - 1], PR[:, i:i + 1], X3R[q][:, i, 1:NCH]), after=[e2[q], e4[q]])
                    for q in range(2):
                        PR, PI, NPI = tabs[q]
                        cc[q] = ox("dve", STT(X3I[q][:, i, 1:NCH], XIs[q][:, 0:NCH - 1], PR[:, i:i + 1], X3I[q][:, i, 1:NCH]), after=[e2[q], e4[q]])
                    for q in range(2):
                        PR, PI, NPI = tabs[q]
                        ox("dve", STT(X3R[q][:, i, 1:NCH], XIs[q][:, 0:NCH - 1], NPI[:, i:i + 1], X3R[q][:, i, 1:NCH]), after=[ca[q]])
                    for q in range(2):
                        PR, PI, NPI = tabs[q]
                        ox("dve", STT(X3I[q][:, i, 1:NCH], XRs[q][:, 0:NCH - 1], PI[:, i:i + 1], X3I[q][:, i, 1:NCH]), after=[cc[q]])
                kb.mark("dve", writes=[("xr", 0), ("xi", 0), ("xr", 1), ("xi", 1), "Xs"], reads=tkeys)
                for q in range(2):
                    pair = pairs[q]
                    j4 = 2 * jp + q
                    PR, PI, NPI = tabs[q]
                    ar, ai, nai = PR[:, 0:1], PI[:, 0:1], NPI[:, 0:1]
                    xr, xi = XR[q], XI[q]
                    xk, ik = ("xr", q), ("xi", q)
                    sc = lambda fn, rd, wrt: op("dve", fn, reads=rd + tkeys, writes=wrt)
                    xs_r, xs_i = xr[:, T:TC], xi[:, T:TC]
                    sc(STT(xs_r, h0[:, pair, :, 0], ar, xs_r), ["h0", xk], [xk])
                    sc(STT(xs_i, h0[:, pair, :, 1], ar, xs_i), ["h0", ik], [ik])
                    sc(STT(xs_r, h0[:, pair, :, 1], nai, xs_r), ["h0", xk], [xk])
                    sc(STT(xs_i, h0[:, pair, :, 0], ai, xs_i), ["h0", ik], [ik])
                    op("act", lambda e: e.copy(sto[:, pair, 0:1], xr[:, T - 1:T]), reads=[xk], writes=["sto"])
                    op("act", lambda e: e.copy(sto[:, pair, 1:2], xi[:, T - 1:T]), reads=[ik], writes=["sto"])
                    op("act", lambda e: e.copy(stos[:, :, pair, 0], xr[:, T:TC]), reads=[xk], writes=["stos"])
                    op("act", lambda e: e.copy(stos[:, :, pair, 1], xi[:, T:TC]), reads=[ik], writes=["stos"])
                    op("act", lambda e: e.copy(xbr[:, 0:T].rearrange("p (n l) -> p n l", l=LCH), xr[:, 0:T].rearrange("p (l n) -> p n l", n=NCH)), reads=[xk], writes=["xbr"])
                    op("act", lambda e: e.copy(xbr[:, T:TC], xr[:, T:TC]), reads=[xk], writes=["xbr"])
                    op("pool", lambda e: e.tensor_copy(out=xbi[:, 0:T].rearrange("p (n l) -> p n l", l=LCH), in_=xi[:, 0:T].rearrange("p (l n) -> p n l", n=NCH)), reads=[ik], writes=["xbi"])
                    op("pool", lambda e: e.tensor_copy(out=xbi[:, T:TC], in_=xi[:, T:TC]), reads=[ik], writes=["xbi"])
                    for ti, (c0, n) in enumerate(coltiles):
                        yb, ybk = ybanks[ti]
                        op("pe", lambda e: e.matmul(yb[:, 0:n], Wc[:, kc, j4, 0, :], xbr[:, c0:c0 + n], start=(j4 == 0), stop=False),
                           reads=["Wc", "xbr"], writes=[ybk], signal=False)
                        op("pe", lambda e: e.matmul(yb[:, 0:n], Wc[:, kc, j4, 1, :], xbi[:, c0:c0 + n], start=False, stop=(j4 == 3)),
                           reads=["Wc", "xbi"], writes=[ybk], signal=True)
            for ti, (c0, n) in enumerate(coltiles):
                yb, ybk = ybanks[ti]
                y_, w_ = ya[:, 0:n], yb2[:, 0:n]
                if n == TW:
                    dma("sp", y_, uT_s[kc, :, c0:c0 + n], reads=["uT_s"], writes=["ya"])
                else:
                    with nc.allow_non_contiguous_dma(reason="tiny sample columns"):
                        dma("sp", y_, uT_s[kc, :, c0:c0 + n], reads=["uT_s"], writes=["ya"])
                op("dve", lambda e: e.scalar_tensor_tensor(out=y_, in0=y_, scalar=dsk[:, kc:kc + 1], in1=yb[:, 0:n], op0=ALU.mult, op1=ALU.add),
                   reads=["ya", "dsk", ybk], writes=["ya"])
                op("dve", lambda e: e.tensor_tensor(out=w_, in0=y_, in1=y_, op=ALU.mult), reads=["ya"], writes=["yb2"])
                op("dve", lambda e: e.tensor_scalar(out=w_, in0=w_, scalar1=0.044715, scalar2=1.0, op0=ALU.mult, op1=ALU.add), reads=["yb2"], writes=["yb2"])
                op("dve", lambda e: e.tensor_tensor(out=w_, in0=w_, in1=y_, op=ALU.mult), reads=["yb2", "ya"], writes=["yb2"])
                op("act", lambda e: e.activation(out=w_, in_=w_, func=AF.Sigmoid, scale=GELU_C), reads=["yb2"], writes=["yb2"])
                op("dve", lambda e: e.tensor_tensor(out=zst[:, c0:c0 + n], in0=y_, in1=w_, op=ALU.mult), reads=["ya", "yb2"], writes=["zst"])
            dma("sp", zT_s[kc, :, :], zst[:, :], reads=["zst"], writes=["zT_s"])
        with nc.allow_non_contiguous_dma(reason="state 8B runs"):
            dma("sp", ssm_p[li, :].rearrange("(a p r) -> p a r", p=128, r=2), sto[:, :, :], reads=["sto"])
            for s_ in range(NS):
                dma("sp", ssm_s[li, s_, :].rearrange("(a p r) -> p a r", p=128, r=2), stos[:, s_, :, :], reads=["stos"])
        kb.barrier()
        es.close()

    def tile_tail(i_next, t, segs):
        prenorm(i_next, 0, segs)
        if i_next % 2 == 0:
            qkv_phase(i_next // 2, t, segs)
        else:
            win_phase(i_next // 2, t, segs)

    def mixer_out_attn(li, t, segs):
        g0 = t * TW
        minT = aT[:, 0:8, :]
        dma("sp", minT[:, :, 0:TW], mT_s[:, :, g0:g0 + TW].rearrange("h p t -> p h t"), reads=["mT_s"], writes=["aT"])
        if len(segs) > 1:
            with nc.allow_non_contiguous_dma(reason="tiny sample columns"):
                dma("sp", minT[:, :, TW:TWS], mT_s[:, :, T:TC].rearrange("h p t -> p h t"), reads=["mT_s"], writes=["aT"])
        psts = [stats_begin() if si == 0 else psum(6, 7) for si in range(len(segs))]
        for dc in range(KC):
            wv, wk = wr.get(("wo", li, dc))
            w3 = wv[:, 0:8 * 128].rearrange("p (k c) -> p k c", k=8)
            for si, (c0, n, S) in enumerate(segs):
                pb, pk = psum()
                for hh in range(8):
                    op("pe", lambda e: e.matmul(pb[:, 0:n], w3[:, hh, :], minT[:, hh, c0:c0 + n], start=(hh == 0), stop=(hh == 7)),
                       reads=["aT", wk], writes=[pk], signal=(hh == 7))
                op("act", lambda e: e.copy(mT[:, dc, c0:c0 + n], pb[:, 0:n]), reads=[pk], writes=["mT"])
                stats_add(psts[si], pb[:, 0:n], pk, c0, n, dc == 0, dc == KC - 1)
        for si, (c0, n, S) in enumerate(segs):
            stats_end(psts[si], c0, n)

    def ffn(i, t, segs):
        for j in range(NFC):
            wv, wk = wr.get(("up", i, j))
            wg = wv[:, 0:KC * 128].rearrange("p (k c) -> p k c", k=KC)
            wvv = wv[:, KC * 128:2 * KC * 128].rearrange("p (k c) -> p k c", k=KC)
            for (c0, n, S) in segs:
                pg, pgk = psum()
                pv, pvk = psum()
                for kc in range(KC):
                    op("pe", lambda e: e.matmul(pg[:, 0:n], wg[:, kc, :], hT[:, kc, c0:c0 + n], start=(kc == 0), stop=(kc == KC - 1)),
                       reads=["hT", wk], writes=[pgk], signal=(kc == KC - 1))
                for kc in range(KC):
                    op("pe", lambda e: e.matmul(pv[:, 0:n], wvv[:, kc, :], hT[:, kc, c0:c0 + n], start=(kc == 0), stop=(kc == KC - 1)),
                       reads=["hT", wk], writes=[pvk], signal=(kc == KC - 1))
                r = rr["cg"] % 2
                rr["cg"] += 1
                halves = []
                for (pp, ppk, jj, ct_, ck_) in ((pg, pgk, j, cgt[r], "cg%d" % r), (pv, pvk, NFC + j, cvt[r], "cv%d" % r)):
                    if S == 1:
                        halo, hk = uprev[:, jj, :], ("uprev", jj)
                    else:
                        halo, hk = cst[:, jj, :, :].rearrange("p a b -> p (a b)"), "cst"
                    halves.append((pp, ppk, jj, ct_, ck_, halo, hk, cw[:, 0, jj:jj + 1], cw[:, 1, jj:jj + 1], cw[:, 2, jj:jj + 1], cw[:, 3, jj:jj + 1]))
                m2 = min(2 * S, n)
                for (pp, ppk, jj, ct_, ck_, halo, hk, w0, w1, w2, bb) in halves:
                    op("act", lambda e: e.activation(out=ct_[:, 0:n], in_=pp[:, 0:n], func=AF.Identity, bias=bb, scale=w2),
                       reads=[ppk, "cw"], writes=[ck_])
                if n > S:
                    for (pp, ppk, jj, ct_, ck_, halo, hk, w0, w1, w2, bb) in halves:
                        op("dve", lambda e: e.scalar_tensor_tensor(out=ct_[:, S:n], in0=pp[:, 0:n - S], scalar=w1, in1=ct_[:, S:n],
                                                                   op0=ALU.mult, op1=ALU.add), reads=[ppk, "cw", ck_], writes=[ck_])
                if n > 2 * S:
                    for (pp, ppk, jj, ct_, ck_, halo, hk, w0, w1, w2, bb) in halves:
                        op("dve", lambda e: e.scalar_tensor_tensor(out=ct_[:, 2 * S:n], in0=pp[:, 0:n - 2 * S], scalar=w0, in1=ct_[:, 2 * S:n],
                                                                   op0=ALU.mult, op1=ALU.add), reads=[ppk, "cw", ck_], writes=[ck_])
                for (pp, ppk, jj, ct_, ck_, halo, hk, w0, w1, w2, bb) in halves:
                    op("dve", lambda e: e.scalar_tensor_tensor(out=ct_[:, 0:S], in0=halo[:, S:2 * S], scalar=w1, in1=ct_[:, 0:S],
                                                               op0=ALU.mult, op1=ALU.add), reads=[hk, "cw", ck_], writes=[ck_])
                for (pp, ppk, jj, ct_, ck_, halo, hk, w0, w1, w2, bb) in halves:
                    op("dve", lambda e: e.scalar_tensor_tensor(out=ct_[:, 0:m2], in0=halo[:, 0:m2], scalar=w0, in1=ct_[:, 0:m2],
                                                               op0=ALU.mult, op1=ALU.add), reads=[hk, "cw", ck_], writes=[ck_])
                for (pp, ppk, jj, ct_, ck_, halo, hk, w0, w1, w2, bb) in halves:
                    if S == 1:
                        op("dve", lambda e: e.tensor_copy(out=uprev[:, jj, :], in_=pp[:, n - 2:n]), reads=[ppk], writes=[hk])
                    else:
                        op("dve", lambda e: e.tensor_copy(out=csout[:, jj, 0, :], in_=cst[:, jj, 1, :]), reads=["cst"], writes=["csout"])
                        op("dve", lambda e: e.tensor_copy(out=csout[:, jj, 1, :], in_=pp[:, 0:NS]), reads=[ppk], writes=["csout"])
                sg_ = sgt[r]
                op("act", lambda e: e.activation(out=sg_[:, 0:n], in_=cgt[r][:, 0:n], func=AF.Silu), reads=["cg%d" % r], writes=["sg%d" % r])
                op("dve", lambda e: e.tensor_tensor(out=aT[:, j, c0:c0 + n], in0=sg_[:, 0:n], in1=cvt[r][:, 0:n], op=ALU.mult),
                   reads=["sg%d" % r, "cv%d" % r], writes=["aT"])
        psts = [stats_begin() if si == 0 else psum(6, 7) for si in range(len(segs))]
        for dc in range(KC):
            wv, wk = wr.get(("down", i, dc))
            w3 = wv[:, 0:NFC * 128].rearrange("p (k c) -> p k c", k=NFC)
            for si, (c0, n, S) in enumerate(segs):
                pb, pk = psum()
                for j in range(NFC):
                    op("pe", lambda e: e.matmul(pb[:, 0:n], w3[:, j, :], aT[:, j, c0:c0 + n], start=(j == 0), stop=(j == NFC - 1)),
                       reads=["aT", wk], writes=[pk], signal=(j == NFC - 1))
                op("act", lambda e: e.copy(mT[:, dc, c0:c0 + n], pb[:, 0:n]), reads=[pk], writes=["mT"])
                stats_add(psts[si], pb[:, 0:n], pk, c0, n, dc == 0, dc == KC - 1)
        for si, (c0, n, S) in enumerate(segs):
            stats_end(psts[si], c0, n)

    def out_y(t, segs):
        ytok = mT_flat
        for bl in range(4):
            for q4 in range(4):
                pb, pk = psum()
                for k4 in range(4):
                    kc = q4 * 4 + k4
                    op("pe", lambda e: e.transpose(pb[:, k4 * 128:(k4 + 1) * 128], xT[:, kc, bl * 128:(bl + 1) * 128], ident_f[:, :]),
                       reads=["xT", "ident_f"], writes=[pk], signal=(k4 == 3))
                op("act", lambda e: e.copy(ytok[:, (bl % 2) * D + q4 * 512:(bl % 2) * D + (q4 + 1) * 512], pb[:, :]), reads=[pk], writes=["mT"])
            dma("sp", y_p[t * TW + bl * 128:t * TW + (bl + 1) * 128, :], ytok[:, (bl % 2) * D:(bl % 2 + 1) * D], reads=["mT"])
        if len(segs) > 1:
            pb, pk = psum()
            for kc in range(KC):
                op("pe", lambda e: e.transpose(pb[0:NS, (kc % 4) * 128:(kc % 4 + 1) * 128], xT[:, kc, TW:TWS], ident_f[:, :]),
                   reads=["xT", "ident_f"], writes=[pk], signal=True)
                if kc % 4 == 3:
                    q4 = kc // 4
                    op("act", lambda e: e.copy(ytok[0:NS, q4 * 512:(q4 + 1) * 512], pb[0:NS, :]), reads=[pk], writes=["mT"])
            dma("sp", y_s[:, :], ytok[0:NS, 0:D], reads=["mT"])

    xtok = mT_flat
    for t in range(NT):
        segs = segs_of(t)
        for bl in range(4):
            xo = (bl % 2) * D
            dma("sp", xtok[:, xo:xo + D], x_p[t * TW + bl * 128:t * TW + (bl + 1) * 128, :], writes=["mT"])
            for q4 in range(4):
                pb, pk = psum()
                for k4 in range(4):
                    kc = q4 * 4 + k4
                    op("pe", lambda e: e.transpose(pb[:, k4 * 128:(k4 + 1) * 128], xtok[:, xo + kc * 128:xo + (kc + 1) * 128], ident_f[:, :]),
                       reads=["mT", "ident_f"], writes=[pk], signal=(k4 == 3))
                op("act", lambda e: e.copy(xT[:, q4 * 4:(q4 + 1) * 4, bl * 128:(bl + 1) * 128], pb[:, :].rearrange("p (k t) -> p k t", k=4)),
                   reads=[pk], writes=["xT"])
        if len(segs) > 1:
            dma("sp", xtok[0:NS, 0:D], x_s[:, :], writes=["mT"])
            pb, pk = psum()
            for kc in range(KC):
                op("pe", lambda e: e.transpose(pb[:, kc * NS:(kc + 1) * NS], xtok[0:NS, kc * 128:(kc + 1) * 128], ident_f[0:NS, 0:NS]),
                   reads=["mT", "ident_f"], writes=[pk], signal=(kc == KC - 1))
            op("act", lambda e: e.copy(xT[:, :, TW:TWS], pb[:, 0:KC * NS].rearrange("p (k s) -> p k s", k=KC)), reads=[pk], writes=["xT"])
        xT_store(t, segs)
        tile_tail(0, t, segs)

    for i in range(nlayers):
        li = i // 2
        if i % 2 == 0:
            attn_phase(li)
        else:
            ssm_phase(li)
        for k3 in range(3):
            fm_load(cw[:, k3, :], "cw", conv_w[i, k3, :])
        fm_load(cw[:, 3, :], "cw", conv_b[i, :])
        for s_ in range(NS):
            for r_ in range(2):
                fm_load(cst[:, :, r_, s_], "cst", st_conv[i, s_, r_, :])
        op("dve", lambda e: e.memset(uprev[:], 0.0), writes=[("uprev", jj_) for jj_ in range(NUC)])
        for t in range(NT):
            segs = segs_of(t)
            xT_load(t, segs)
            if i % 2 == 0:
                mixer_out_attn(li, t, segs)
            else:
                mixer_out_ssm(li, t, segs)
            if dbg and i == 0:
                dma("sp", dbg_m[:, :, t * TW:(t + 1) * TW].rearrange("k p t -> p k t"), mT[:, :, 0:TW], reads=["mT"])
                if len(segs) > 1:
                    dma("sp", dbg_m[:, :, T:TC].rearrange("k p t -> p k t"), mT[:, :, TW:TWS], reads=["mT"])
            postnorm_residual(i, 1, segs)
            if dbg and i == 0:
                dma("sp", dbg_x1[:, :, t * TW:(t + 1) * TW].rearrange("k p t -> p k t"), xT[:, :, 0:TW], reads=["xT"])
                if len(segs) > 1:
                    dma("sp", dbg_x1[:, :, T:TC].rearrange("k p t -> p k t"), xT[:, :, TW:TWS], reads=["xT"])
            prenorm(i, 2, segs)
            ffn(i, t, segs)
            postnorm_residual(i, 3, segs)
            if i + 1 < nlayers:
                xT_store(t, segs)
                tile_tail(i + 1, t, segs)
            else:
                out_y(t, segs)
        for r_ in range(2):
            fm_store(uprev[:, :, r_], [("uprev", jj_) for jj_ in range(NUC)], conv_p[i, r_, :])
            for s_ in range(NS):
                fm_store(csout[:, :, r_, s_], "csout", conv_s[i, s_, r_, :])

    def _unused():
        pass

    assert wr.consumed == len(wr.plan), (wr.consumed, len(wr.plan))
    kb.barrier()
    kb.es.close()
    return nc


def _consts():
    ident = np.eye(128, dtype=np.float32)
    k = np.arange(128)[:, None]
    q = np.arange(128)[None, :]
    m = np.zeros((128, 7, 128), np.float32)
    m[:, 0] = (q >= k)
    m[:, 1] = (q <= k)
    r4 = ((q - k) % 4 == 0)
    m[:, 2] = r4 & (q >= k)
    m[:, 3] = r4
    m[:, 4] = r4 & (q <= k)
    r16 = ((q - k) % 16 == 0)
    m[:, 5] = r16 & (q >= k)
    m[:, 6] = r16
    half = 16
    inv = (np.float32(500000.0) ** (-(np.arange(half, dtype=np.float32) * np.float32(2.0 / 32)))).astype(np.float32)
    pos = np.zeros((128, 17), np.float32)
    for b in range(16):
        pos[:, b] = b * 128 + np.arange(128)
    pos[:, 16] = PAST
    ang = (pos[:, :, None] * inv[None, None, :]).astype(np.float32)
    rope = np.concatenate([np.cos(ang), np.sin(ang)], axis=-1).astype(np.float32)
    mbq = np.where(m.transpose(2, 1, 0) > 0.5, 0.0, -30000.0).astype(np.float32)
    gmask = (np.arange(128)[:, None] // 16 == np.arange(8)[None, :]).astype(np.float32)
    return ident, m, rope, np.ascontiguousarray(mbq), gmask


_NC_CACHE = {}


def kernel(**inp):
    f = lambda a: np.ascontiguousarray(np.asarray(a, dtype=np.float32))
    ident, masks, rope, mbq, gmask = _consts()
    if "nc" not in _NC_CACHE:
        _NC_CACHE["nc"] = build(4)
    nc = _NC_CACHE["nc"]
    shared = {
        "norm_g": f(inp["norm_g"]).reshape(16, D), "w_qkv": f(inp["w_qkv"]), "w_o": f(inp["w_attn_o"]),
        "w_in": f(inp["w_ssm_in"]), "lam_re": f(inp["lambda_re"]), "lam_im": f(inp["lambda_im"]),
        "log_dt": f(inp["log_dt"]), "b_re": f(inp["b_re"]), "b_im": f(inp["b_im"]), "c_re": f(inp["c_re"]),
        "c_im": f(inp["c_im"]), "d_skip": f(inp["d_skip"]), "w_glu": f(inp["w_glu"]), "w_up": f(inp["w_up"]),
        "conv_w": f(inp["conv_w"]), "conv_b": f(inp["conv_b"]), "w_down": f(inp["w_down"]),
        "c_ident": ident, "c_masks": masks, "c_rope": rope, "c_mbq": mbq, "c_gmask": gmask,
    }
    xp = f(inp["x_prompt"])
    xs = f(inp["x_sample"]).reshape(8, D)
    ck = [f(inp["cache_kv_g0"]), f(inp["cache_kv_g1"]), f(inp["cache_kv_g2"])]
    ss = f(inp["state_ssm"])
    sc = f(inp["state_conv"])
    in_maps = []
    for c in range(NCORES):
        m = dict(shared)
        m["x_p"] = xp[c]
        sl = slice(2 * c, 2 * c + 2)
        m["x_s"] = np.ascontiguousarray(xs[sl])
        for g in range(3):
            m["ckv%d" % g] = np.ascontiguousarray(ck[g][:, sl].reshape(2, NS, WIN[g], 2048))
        m["st_ssm"] = np.ascontiguousarray(ss[:, sl].reshape(2, NS, 128 * 64 * 2))
        m["st_conv"] = np.ascontiguousarray(sc[:, sl])
        in_maps.append(m)
    res = run_bass_kernel_spmd(nc, in_maps, core_ids=list(range(NCORES)))
    R = res.results
    y_prompt = np.stack([R[c]["y_p"] for c in range(NCORES)], 0)
    y_sample = np.concatenate([R[c]["y_s"] for c in range(NCORES)], 0).reshape(8, 1, D)
    outs = [y_prompt, y_sample]
    for g in range(3):
        keep = min(WIN[g], T)
        outs.append(np.stack([R[c]["kvp%d" % g] for c in range(NCORES)], 1).reshape(2, 4, keep, 2, 8, 128))
        outs.append(np.concatenate([R[c]["kvs%d" % g] for c in range(NCORES)], 1).reshape(2, 8, 1, 2, 8, 128))
    outs.append(np.stack([R[c]["ssm_p"] for c in range(NCORES)], 1).reshape(2, 4, 128, 64, 2))
    outs.append(np.concatenate([R[c]["ssm_s"] for c in range(NCORES)], 1).reshape(2, 8, 128, 64, 2))
    outs.append(np.stack([R[c]["conv_p"] for c in range(NCORES)], 1).reshape(4, 4, 2, 2 * DFF))
    outs.append(np.concatenate([R[c]["conv_s"] for c in range(NCORES)], 1).reshape(4, 8, 2, 2 * DFF))
    return tuple(np.ascontiguousarray(o.astype(np.float32)) for o in outs)
```

```python
import contextlib
import math
import numpy as np
import concourse.bass as bass
import concourse.mybir as mybir
from concourse.bass_utils import run_bass_kernel_spmd

F32 = mybir.dt.float32
BF16 = mybir.dt.bfloat16
AF = mybir.ActivationFunctionType
ALU = mybir.AluOpType
AX = mybir.AxisListType

T = 2048
D = 2048
KC = 16
NS = 2
TC = T + NS
TW = 512
NT = T // TW
TWS = TW + NS
DFF = 5504
NFC = 43
NUC = 86
NCORES = 4
PAST = 16384
WIN = (128, 512, 2048)
DIL = (1, 4, 16)
EPS = 1e-6
SLOT = 5504
NSLOT = 4
SCALE = 128 ** -0.5
GELU_C = 2.0 * math.sqrt(2.0 / math.pi)


class KB:
    NDS = 12

    def __init__(self):
        self.nc = bass.Bass("TRN2", target_bir_lowering=False)
        nc = self.nc
        self.es = contextlib.ExitStack()
        self.engs = {"pe": nc.tensor, "dve": nc.vector, "act": nc.scalar, "pool": nc.gpsimd, "sp": nc.sync}
        self.semh = {}
        for k in self.engs:
            self.semh[("e", k)] = self.es.enter_context(nc.semaphore("se_" + k))
        self.cnt = {k: 0 for k in self.engs}
        self.waited = {k: {} for k in self.engs}
        self.pending = {k: ([], []) for k in self.engs}
        self.lastw = {}
        self.readers = {}
        self.dcnt = {}
        self.dnext = {}
        for q in ("sp", "pool", "act"):
            for i in range(self.NDS):
                self.semh[("d", q, i)] = self.es.enter_context(nc.semaphore("sd_%s_%d" % (q, i)))
                self.dcnt[("d", q, i)] = 0
            self.dnext[q] = 0
        self.nins = 0

    def sb(self, name, shape, dt, es=None):
        self._uid = getattr(self, "_uid", 0) + 1
        return (es or self.es).enter_context(self.nc.sbuf_tensor("%s_%d" % (name, self._uid), list(shape), dt))

    def ps(self, name, shape, dt, es=None):
        return (es or self.es).enter_context(self.nc.psum_tensor(name, list(shape), dt))

    def dram(self, name, shape, dt, kind):
        return self.nc.dram_tensor(name, list(shape), dt, kind=kind).ap()

    def _deps(self, reads, writes):
        deps = {}
        for b in reads:
            lw = self.lastw.get(b)
            if lw is not None:
                deps[lw[0]] = max(deps.get(lw[0], 0), lw[1])
        for b in writes:
            lw = self.lastw.get(b)
            if lw is not None:
                deps[lw[0]] = max(deps.get(lw[0], 0), lw[1])
            for sk, v in self.readers.get(b, {}).items():
                deps[sk] = max(deps.get(sk, 0), v)
        return deps

    def _wait(self, eng, deps):
        e = self.engs[eng]
        w = self.waited[eng]
        for sk, v in deps.items():
            if eng == "pe" and sk == ("e", "pe"):
                continue
            if w.get(sk, 0) < v:
                e.wait_ge(self.semh[sk], v)
                w[sk] = v
                self.nins += 1

    def fence(self, eng, reads=(), writes=()):
        self._wait(eng, self._deps(reads, writes))

    def mark(self, eng, writes=(), reads=()):
        sk = ("e", eng)
        v = self.cnt[eng]
        for b in writes:
            self.lastw[b] = (sk, v)
            self.readers[b] = {}
        for b in reads:
            self.readers.setdefault(b, {})[sk] = v

    def poison(self, keys):
        allr = {("e", k): v for k, v in self.cnt.items() if v > 0}
        for sk, v in self.dcnt.items():
            if v > 0:
                allr[sk] = v
        for b in keys:
            self.readers[b] = dict(allr)

    def opx(self, eng, fn, after=()):
        sk = ("e", eng)
        v = max(after) if after else 0
        if v > self.waited[eng].get(sk, 0):
            self.engs[eng].wait_ge(self.semh[sk], v)
            self.waited[eng][sk] = v
            self.nins += 1
        ins = fn(self.engs[eng])
        self.nins += 1
        self.cnt[eng] += 1
        ins.then_inc(self.semh[sk], 1)
        return self.cnt[eng]

    def op(self, eng, fn, reads=(), writes=(), signal=True):
        self._wait(eng, self._deps(reads, writes))
        ins = fn(self.engs[eng])
        self.nins += 1
        pr, pw = self.pending[eng]
        pr.extend(reads)
        pw.extend(writes)
        if signal:
            self.cnt[eng] += 1
            sk = ("e", eng)
            ins.then_inc(self.semh[sk], 1)
            v = self.cnt[eng]
            for b in pw:
                self.lastw[b] = (sk, v)
                self.readers[b] = {}
            for b in pr:
                if b not in pw:
                    self.readers.setdefault(b, {})[sk] = v
            self.pending[eng] = ([], [])
        return ins

    def dma(self, q, out, in_, reads=(), writes=(), **kw):
        self._wait(q, self._deps(reads, writes))
        i = self.dnext[q]
        self.dnext[q] = (i + 1) % self.NDS
        sk = ("d", q, i)
        prev = self.dcnt[sk]
        if prev > 0 and self.waited[q].get(sk, 0) < prev:
            self.engs[q].wait_ge(self.semh[sk], prev)
            self.waited[q][sk] = prev
        ins = self.engs[q].dma_start(out=out, in_=in_, **kw)
        self.nins += 1
        self.dcnt[sk] += 16
        v = self.dcnt[sk]
        ins.then_inc(self.semh[sk], 16)
        for b in writes:
            self.lastw[b] = (sk, v)
            self.readers[b] = {}
        for b in reads:
            if b not in writes:
                self.readers.setdefault(b, {})[sk] = v
        return ins

    def barrier(self, engines=None):
        cur = {}
        for k in self.engs:
            assert not self.pending[k][0] and not self.pending[k][1]
            cur[("e", k)] = self.cnt[k]
        for sk, v in self.dcnt.items():
            cur[sk] = v
        for eng in (engines or list(self.engs)):
            for sk, v in cur.items():
                if sk == ("e", eng):
                    continue
                if v > 0 and self.waited[eng].get(sk, 0) < v:
                    self.engs[eng].wait_ge(self.semh[sk], v)
                    self.waited[eng][sk] = v
                    self.nins += 1
        if engines is None:
            self.lastw = {}
            self.readers = {}


class WRing:
    def __init__(self, kb, tensor):
        self.kb = kb
        self.t = tensor
        self.plan = []
        self.emitted = 0
        self.consumed = 0

    def add(self, tag, pieces):
        self.plan.append((tag, pieces))

    def _emit(self, k):
        tag, pieces = self.plan[k]
        s = k % NSLOT
        for (off, shp, src) in pieces:
            n = int(np.prod(shp))
            dst = self.t[:, s, off:off + n]
            if len(shp) == 2:
                dst = dst.rearrange("p (a b) -> p a b", a=shp[0])
            self.kb.dma("pool", dst, src, writes=[("w", s)])

    def get(self, tag):
        k = self.consumed
        assert self.plan[k][0] == tag, (self.plan[k][0], tag)
        while self.emitted < min(len(self.plan), k + NSLOT):
            self._emit(self.emitted)
            self.emitted += 1
        self.consumed += 1
        s = k % NSLOT
        return self.t[:, s, :], ("w", s)


def build(nlayers=4, dbg=False):
    kb = KB()
    nc = kb.nc
    op, dma = kb.op, kb.dma

    def din(name, shape, dt=F32):
        return kb.dram(name, shape, dt, "ExternalInput")

    def dout(name, shape, dt=F32):
        return kb.dram(name, shape, dt, "ExternalOutput")

    x_p = din("x_p", [T, D])
    x_s = din("x_s", [NS, D])
    ckv = [din("ckv%d" % g, [2, NS, WIN[g], 2 * 1024]) for g in range(3)]
    st_ssm = din("st_ssm", [2, NS, 128 * 64 * 2])
    st_conv = din("st_conv", [4, NS, 2, 2 * DFF])
    norm_g = din("norm_g", [16, D])
    w_qkv = din("w_qkv", [2, D, 9216])
    w_o = din("w_o", [2, 1024, D])
    w_in = din("w_in", [2, D, D])
    lam_re = din("lam_re", [2, 128, 64])
    lam_im = din("lam_im", [2, 128, 64])
    log_dt = din("log_dt", [2, 128])
    b_re = din("b_re", [2, 128, 64, 16])
    b_im = din("b_im", [2, 128, 64, 16])
    c_re = din("c_re", [2, 128, 16, 64])
    c_im = din("c_im", [2, 128, 16, 64])
    d_skip = din("d_skip", [2, D])
    w_glu = din("w_glu", [2, D, 2 * D])
    w_up = din("w_up", [4, D, 2 * DFF])
    conv_w = din("conv_w", [4, 3, 2 * DFF])
    conv_b = din("conv_b", [4, 2 * DFF])
    w_down = din("w_down", [4, DFF, D])
    c_ident = din("c_ident", [128, 128])
    c_masks = din("c_masks", [128, 7, 128])
    c_rope = din("c_rope", [128, 17, 32])
    c_mbq = din("c_mbq", [128, 7, 128])
    c_gmask = din("c_gmask", [128, 8])

    y_p = dout("y_p", [T, D])
    y_s = dout("y_s", [NS, D])
    kvp = [dout("kvp%d" % g, [2, min(WIN[g], T), 2048]) for g in range(3)]
    kvs = [dout("kvs%d" % g, [2, NS, 2048]) for g in range(3)]
    ssm_p = dout("ssm_p", [2, 128 * 64 * 2])
    ssm_s = dout("ssm_s", [2, NS, 128 * 64 * 2])
    conv_p = dout("conv_p", [4, 2, 2 * DFF])
    conv_s = dout("conv_s", [4, NS, 2, 2 * DFF])

    IK = "ExternalOutput" if dbg else "Internal"
    xT_s = kb.dram("xT_s", [KC, 128, TC], F32, "Internal")
    qT_s = kb.dram("qT_s", [3, 8, 128, T], BF16, IK)
    kT_s = kb.dram("kT_s", [3, 8, 128, T], BF16, IK)
    v_s = kb.dram("v_s", [3, T, 1024], BF16, IK)
    mT_s = kb.dram("mT_s", [8, 128, TC], BF16, IK)
    dbg_x1 = kb.dram("dbg_x1", [KC, 128, TC], F32, "ExternalOutput") if dbg else None
    dbg_m = kb.dram("dbg_m", [KC, 128, TC], F32, "ExternalOutput") if dbg else None
    uT_s = kb.dram("uT_s", [KC, 128, TC], F32, "Internal")
    zT_s = kb.dram("zT_s", [KC, 128, TC], BF16, "Internal")

    ident_f = kb.sb("ident_f", [128, 128], F32)
    ident_b = kb.sb("ident_b", [128, 128], BF16)
    ones_b = kb.sb("ones_b", [128, 128], BF16)
    rope = kb.sb("rope", [128, 17, 32], F32)
    gsc = kb.sb("gsc", [128, KC, 16], F32)
    wring_t = kb.sb("wring", [128, NSLOT, SLOT], BF16)
    wr = WRing(kb, wring_t)
    sqT = kb.sb("sqT", [128, 24, NS], BF16)
    skT = kb.sb("skT", [128, 24, NS], BF16)
    svT = kb.sb("svT", [128, 24, NS], F32)
    epsb = kb.sb("epsb", [128, 1], F32)

    pbank = [kb.ps("pb%d" % i, [128, 512], F32) for i in range(8)]
    prr = [0]

    def psum(lo=0, hi=6):
        i = lo + prr[0] % (hi - lo)
        prr[0] += 1
        return pbank[i], ("ps", i)

    dma("sp", ident_f[:], c_ident[:, :], writes=["ident_f"])
    dma("pool", ident_b[:], c_ident[:, :], writes=["ident_b"])
    dma("sp", rope[:], c_rope[:, :, :], writes=["rope"])
    op("dve", lambda e: e.memset(ones_b[:], 1.0), writes=["ones_b"])
    op("dve", lambda e: e.memset(epsb[:], EPS), writes=["epsb"])

    es0 = contextlib.ExitStack()
    gtmp = kb.sb("gtmp", [16, D], F32, es0)
    dma("sp", gtmp[:], norm_g[:, :], writes=["gtmp"])
    for kc in range(KC):
        pb, pk = psum()
        op("pe", lambda e: e.transpose(pb[:, 0:16], gtmp[0:16, kc * 128:(kc + 1) * 128], ident_f[0:16, 0:16]),
           reads=["gtmp", "ident_f"], writes=[pk])
        op("act", lambda e: e.copy(gsc[:, kc, :], pb[:, 0:16]), reads=[pk], writes=["gsc"])
    kb.barrier()
    es0.close()

    def gs(i, j, kc):
        return gsc[:, kc, 4 * i + j:4 * i + j + 1]

    def plan_qkv(li):
        for ct in range(36):
            src = w_qkv[li, :, ct * 256:(ct + 1) * 256].rearrange("(k p) c -> p k c", p=128)
            wr.add(("qkv", li, ct), [(0, [KC, 256], src)])

    def plan_win(li):
        for dc in range(KC):
            src = w_in[li, :, dc * 128:(dc + 1) * 128].rearrange("(k p) c -> p k c", p=128)
            wr.add(("win", li, dc), [(0, [KC, 128], src)])

    def plan_tile(i):
        li = i // 2
        if i % 2 == 0:
            for dc in range(KC):
                src = w_o[li, :, dc * 128:(dc + 1) * 128].rearrange("(k p) c -> p k c", p=128)
                wr.add(("wo", li, dc), [(0, [8, 128], src)])
        else:
            for dc in range(KC):
                sv = w_glu[li, :, dc * 128:(dc + 1) * 128].rearrange("(k p) c -> p k c", p=128)
                sg = w_glu[li, :, D + dc * 128:D + (dc + 1) * 128].rearrange("(k p) c -> p k c", p=128)
                wr.add(("glu", li, dc), [(0, [KC, 128], sv), (KC * 128, [KC, 128], sg)])
        for j in range(NFC):
            sg = w_up[i, :, j * 128:(j + 1) * 128].rearrange("(k p) c -> p k c", p=128)
            sv = w_up[i, :, DFF + j * 128:DFF + (j + 1) * 128].rearrange("(k p) c -> p k c", p=128)
            wr.add(("up", i, j), [(0, [KC, 128], sg), (KC * 128, [KC, 128], sv)])
        for dc in range(KC):
            src = w_down[i, :, dc * 128:(dc + 1) * 128].rearrange("(j p) c -> p j c", p=128)
            wr.add(("down", i, dc), [(0, [NFC, 128], src)])
        if i + 1 < nlayers:
            if (i + 1) % 2 == 0:
                plan_qkv((i + 1) // 2)
            else:
                plan_win((i + 1) // 2)

    for t in range(NT):
        plan_qkv(0)
    for i in range(nlayers):
        for t in range(NT):
            plan_tile(i)

    xT = kb.sb("xT", [128, KC, TWS], F32)
    mT = kb.sb("mT", [128, KC, TWS], F32)
    hT = kb.sb("hT", [128, KC, TWS], BF16)
    aT = kb.sb("aT", [128, NFC, TWS], BF16)
    sq = [kb.sb("sq%d" % i, [128, TW], BF16) for i in range(2)]
    rstd = kb.sb("rstd", [128, TWS], F32)
    rtmp = kb.sb("rtmp", [128, TWS], F32)
    cgt = [kb.sb("cg%d" % i, [128, TW], F32) for i in range(2)]
    cvt = [kb.sb("cv%d" % i, [128, TW], F32) for i in range(2)]
    sgt = [kb.sb("sg%d" % i, [128, TW], F32) for i in range(2)]
    cw = kb.sb("cw", [128, 4, NUC], F32)
    uprev = kb.sb("uprev", [128, NUC, 2], F32)
    cst = kb.sb("cst", [128, NUC, 2, NS], F32)
    csout = kb.sb("csout", [128, NUC, 2, NS], F32)
    vtmp = kb.sb("vtmp", [NUC, 128], F32)
    vtmp2 = kb.sb("vtmp2", [NUC, 128], F32)
    rr = {"sq": 0, "cg": 0, "stg": 0}

    mT_flat = mT[:].rearrange("p a b -> p (a b)")
    aT_flat = aT[:].rearrange("p a b -> p (a b)")

    def segs_of(t):
        s = [(0, TW, 1)]
        if t == NT - 1:
            s.append((TW, NS, NS))
        return s

    def fm_load(dst_ap, dst_key, dram_row):
        dma("sp", vtmp[:], dram_row.rearrange("(c p) -> c p", p=128), writes=["vtmp"])
        pb, pk = psum()
        op("pe", lambda e: e.transpose(pb[:, 0:NUC], vtmp[:, :], ident_f[0:NUC, 0:NUC]),
           reads=["vtmp", "ident_f"], writes=[pk])
        op("act", lambda e: e.copy(dst_ap, pb[:, 0:NUC]), reads=[pk], writes=[dst_key])

    def fm_store(src_ap, src_key, dram_row):
        op("act", lambda e: e.copy(rtmp[:, 0:NUC], src_ap), reads=(src_key if isinstance(src_key, list) else [src_key]), writes=["rtmp"])
        pb, pk = psum()
        op("pe", lambda e: e.transpose(pb[0:NUC, 0:128], rtmp[:, 0:NUC], ident_f[:, :]),
           reads=["rtmp", "ident_f"], writes=[pk])
        op("act", lambda e: e.copy(vtmp2[:, :], pb[0:NUC, 0:128]), reads=[pk], writes=["vtmp2"])
        dma("sp", dram_row.rearrange("(c p) -> c p", p=128), vtmp2[:], reads=["vtmp2"])

    def stats_begin():
        return psum(7, 8)

    def stats_add(pst, src_ap, src_key, c0, n, first, last):
        pb, pk = pst
        s = sq[rr["sq"] % 2]
        sk = "sq%d" % (rr["sq"] % 2)
        rr["sq"] += 1
        op("act", lambda e: e.activation(out=s[:, 0:n], in_=src_ap, func=AF.Square), reads=[src_key], writes=[sk])
        op("pe", lambda e: e.matmul(pb[:, 0:n], ones_b[:, :], s[:, 0:n], start=first, stop=last),
           reads=[sk, "ones_b"], writes=[pk], signal=True)

    def stats_end(pst, c0, n):
        pb, pk = pst
        op("act", lambda e: e.activation(out=rtmp[:, c0:c0 + n], in_=pb[:, 0:n], func=AF.Sqrt, bias=epsb[:, 0:1], scale=1.0 / D),
           reads=[pk, "epsb"], writes=["rtmp"])
        op("dve", lambda e: e.reciprocal(rstd[:, c0:c0 + n], rtmp[:, c0:c0 + n]), reads=["rtmp"], writes=["rstd"])

    def prenorm(i, j, segs):
        for (c0, n, S) in segs:
            pst = stats_begin()
            for kc in range(KC):
                stats_add(pst, xT[:, kc, c0:c0 + n], "xT", c0, n, kc == 0, kc == KC - 1)
            stats_end(pst, c0, n)
            for kc in range(KC):
                op("dve", lambda e: e.scalar_tensor_tensor(out=hT[:, kc, c0:c0 + n], in0=xT[:, kc, c0:c0 + n],
                                                           scalar=gs(i, j, kc), in1=rstd[:, c0:c0 + n],
                                                           op0=ALU.mult, op1=ALU.mult),
                   reads=["xT", "rstd", "gsc"], writes=["hT"])

    def postnorm_residual(i, j, segs):
        for (c0, n, S) in segs:
            for kc in range(KC):
                op("dve", lambda e: e.scalar_tensor_tensor(out=mT[:, kc, c0:c0 + n], in0=mT[:, kc, c0:c0 + n],
                                                           scalar=gs(i, j, kc), in1=rstd[:, c0:c0 + n],
                                                           op0=ALU.mult, op1=ALU.mult),
                   reads=["mT", "rstd", "gsc"], writes=["mT"])
                op("dve", lambda e: e.tensor_tensor(out=xT[:, kc, c0:c0 + n], in0=xT[:, kc, c0:c0 + n],
                                                    in1=mT[:, kc, c0:c0 + n], op=ALU.add),
                   reads=["mT", "xT"], writes=["xT"])

    def xT_store(t, segs):
        g0 = t * TW
        dma("sp", xT_s[:, :, g0:g0 + TW].rearrange("k p t -> p k t"), xT[:, :, 0:TW], reads=["xT"], writes=["xT_s%d" % t])
        if len(segs) > 1:
            dma("sp", xT_s[:, :, T:TC].rearrange("k p t -> p k t"), xT[:, :, TW:TWS], reads=["xT"], writes=["xT_ss"])

    def xT_load(t, segs):
        g0 = t * TW
        dma("sp", xT[:, :, 0:TW], xT_s[:, :, g0:g0 + TW].rearrange("k p t -> p k t"), reads=["xT_s%d" % t], writes=["xT"])
        if len(segs) > 1:
            dma("sp", xT[:, :, TW:TWS], xT_s[:, :, T:TC].rearrange("k p t -> p k t"), reads=["xT_ss"], writes=["xT"])

    stg_f = mT_flat
    stg_b = aT_flat
    rt = kb.sb("ropetmp", [128, 4, 2, 16], F32)

    def qkv_phase(li, t, segs):
        blocks = [(b, 128) for b in range(4)]
        if len(segs) > 1:
            blocks.append((4, NS))
        ffk = ["cg0", "cg1", "cv0", "cv1", "sg0", "sg1"]
        for eng_ in ("act", "dve", "pe", "sp"):
            kb.fence(eng_, writes=ffk)
        stf_t = [cgt[0], cgt[1], cvt[0], cvt[1]]
        stb_t = [sgt[0][:, :].bitcast(BF16), sgt[1][:, :].bitcast(BF16)]
        for ct in range(36):
            wv, wk = wr.get(("qkv", li, ct))
            w3 = wv[:, 0:KC * 256].rearrange("p (k c) -> p k c", k=KC)
            s = ct // 12
            g = (ct % 12) // 4
            hp = ct % 4
            for (bl, m) in blocks:
                gb = t * 4 + bl if bl < 4 else 16
                pb, pk = psum()
                for kc in range(KC):
                    lhs = hT[:, kc, bl * 128:bl * 128 + m] if bl < 4 else hT[:, kc, TW:TWS]
                    op("pe", lambda e: e.matmul(pb[0:m, 0:256], lhs, w3[:, kc, :], start=(kc == 0), stop=(kc == KC - 1)),
                       reads=["hT", wk], writes=[pk], signal=(kc == KC - 1))
                slot = rr["stg"] % 4
                rr["stg"] += 1
                sf = stf_t[slot][:, 0:256]
                sfk = ("sf", slot)
                op("act", lambda e: e.copy(sf[0:m, :], pb[0:m, 0:256]), reads=[pk], writes=[sfk])
                sf3 = sf.rearrange("p (h d) -> p h d", h=2)
                if s < 2:
                    cosb = rope[0:m, gb, 0:16].unsqueeze(1).broadcast_to([m, 2, 16])
                    sinb = rope[0:m, gb, 16:32].unsqueeze(1).broadcast_to([m, 2, 16])
                    x1 = sf3[0:m, :, 0:16]
                    x2 = sf3[0:m, :, 16:32]
                    t1, t2, t3, t4 = (rt[0:m, k, :, :] for k in range(4))
                    op("dve", lambda e: e.tensor_tensor(out=t1, in0=x1, in1=cosb, op=ALU.mult), reads=[sfk, "rope"], writes=["rt1"])
                    op("dve", lambda e: e.tensor_tensor(out=t2, in0=x2, in1=sinb, op=ALU.mult), reads=[sfk, "rope"], writes=["rt2"])
                    op("dve", lambda e: e.tensor_tensor(out=t3, in0=x2, in1=cosb, op=ALU.mult), reads=[sfk, "rope"], writes=["rt3"])
                    op("dve", lambda e: e.tensor_tensor(out=t4, in0=x1, in1=sinb, op=ALU.mult), reads=[sfk, "rope"], writes=["rt4"])
                    op("dve", lambda e: e.tensor_tensor(out=x1, in0=t1, in1=t2, op=ALU.subtract), reads=["rt1", "rt2", "rt3", "rt4"], writes=[sfk])
                    op("dve", lambda e: e.tensor_tensor(out=x2, in0=t3, in1=t4, op=ALU.add), reads=["rt3", "rt4"], writes=[sfk])
                if s >= 1:
                    half = (s - 1) * 1024 + hp * 256
                    if bl < 4:
                        keep = min(WIN[g], T)
                        row0 = gb * 128 - (T - keep)
                        if row0 >= 0:
                            dma("sp", kvp[g][li, row0:row0 + 128, half:half + 256], sf[:, :], reads=[sfk])
                    else:
                        dma("sp", kvs[g][li, :, half:half + 256], sf[0:m, :], reads=[sfk])
                sb_ = stb_t[0][:, slot * 256:(slot + 1) * 256]
                sbk = ("sb", slot)
                tbk = ("tb", slot)
                if not (bl == 4 and s == 2):
                    op("act", lambda e: e.copy(sb_[0:m, :], sf[0:m, :]), reads=[sfk], writes=[sbk])
                if s == 2 and bl < 4:
                    dma("sp", v_s[g, gb * 128:(gb + 1) * 128, hp * 256:(hp + 1) * 256], sb_[:, :], reads=[sbk], writes=["v_s"])
                    continue
                if s == 2:
                    pt, ptk = psum()
                    for hh in range(2):
                        op("pe", lambda e: e.transpose(pt[:, hh * NS:(hh + 1) * NS], sf[0:m, hh * 128:(hh + 1) * 128], ident_f[0:m, 0:m]),
                           reads=[sfk, "ident_f"], writes=[ptk], signal=(hh == 1))
                    op("dve", lambda e: e.tensor_copy(out=svT[:, g * 8 + 2 * hp:g * 8 + 2 * hp + 2, :],
                                                      in_=pt[:, 0:2 * NS].rearrange("p (h s) -> p h s", h=2)),
                       reads=[ptk], writes=["svT"])
                    continue
                pt, ptk = psum()
                ptb = pt[:, 0:256].bitcast(BF16)
                for hh in range(2):
                    op("pe", lambda e: e.transpose(ptb[:, hh * 128:hh * 128 + m], sb_[0:m, hh * 128:(hh + 1) * 128], ident_b[0:m, 0:m]),
                       reads=[sbk, "ident_b"], writes=[ptk], signal=(hh == 1))
                if bl < 4:
                    tb = stb_t[1][:, slot * 256:(slot + 1) * 256]
                    op("dve", lambda e: e.tensor_copy(out=tb, in_=ptb[:, 0:256]), reads=[ptk], writes=[tbk])
                    dst = (qT_s if s == 0 else kT_s)[g, 2 * hp:2 * hp + 2, :, gb * 128:(gb + 1) * 128].rearrange("h p t -> p h t")
                    dma("sp", dst, tb.rearrange("p (h t) -> p h t", h=2), reads=[tbk], writes=["qT_s" if s == 0 else "kT_s"])
                else:
                    dstt = sqT if s == 0 else skT
                    op("dve", lambda e: e.tensor_copy(out=dstt[:, g * 8 + 2 * hp:g * 8 + 2 * hp + 2, :],
                                                      in_=ptb[:, 0:256].rearrange("p (h t) -> p h t", h=2)[:, :, 0:NS]),
                       reads=[ptk], writes=["sqT" if s == 0 else "skT"])

    _qkv_inner = qkv_phase

    def qkv_phase(li, t, segs):
        _qkv_inner(li, t, segs)
        kb.poison(["cg0", "cg1", "cv0", "cv1", "sg0", "sg1"])

    def attn_phase(li):
        kb.barrier()
        es = contextlib.ExitStack()
        aTf = aT[:].rearrange("p a b -> p (a b)")
        hTf = hT[:].rearrange("p a b -> p (a b)")
        xTb = xT[:].rearrange("p a b -> p (a b)").bitcast(BF16)
        mTf = mT[:].rearrange("p a b -> p (a b)")
        QN = 3 * T
        qTt = [aTf[:, i * QN:(i + 1) * QN].rearrange("p (g t) -> p g t", g=3) for i in range(2)]
        kTt = [aTf[:, 2 * QN:3 * QN].rearrange("p (g t) -> p g t", g=3), hTf[:, 0:QN].rearrange("p (g t) -> p g t", g=3)]
        vt = [xTb[:, i * QN:(i + 1) * QN].rearrange("p (g b d) -> p g b d", g=3, b=16) for i in range(2)]
        pT = [hTf[:, QN + i * 512:QN + (i + 1) * 512] for i in range(3)]
        mh = [xTb[:, 2 * QN + i * TC:2 * QN + (i + 1) * TC] for i in range(2)]
        mbq = aTf[:, 3 * QN:3 * QN + 1792].bitcast(F32).rearrange("p (m k) -> p m k", m=7)
        smk = mTf[:, 0:2048]
        junk = mTf[:, 2048:3072].bitcast(BF16)
        acc2 = mTf[:, 4096:4224]
        D3 = mTf[:, 3072:3456]
        cbc = mTf[:, 3456:3840]
        acc = mTf[:, 3840:4096]
        ones_f = kb.sb("ones_f", [128, 128], F32, es)
        masks = kb.sb("masks", [128, 7, 128], BF16, es)
        dma("pool", masks[:], c_masks[:, :, :], writes=["masks"])
        l0 = kb.sb("l0", [128, 16, 3], F32, es)
        colm = kb.sb("colm", [128, 8, 3], F32, es)
        dma("sp", mbq, c_mbq[:, :, :], writes=["mbq"])
        op("dve", lambda e: e.memset(ones_f[:], 1.0), writes=["ones_f"])

        def load_head(h):
            b = h % 2
            for g in range(3):
                dma("sp", qTt[b][:, g, :], qT_s[g, h, :, :], reads=["qT_s"], writes=[("qTt", b, g)])
                dma("sp", kTt[b][:, g, :], kT_s[g, h, :, :], reads=["kT_s"], writes=[("kTt", b, g)])
                dma("sp", vt[b][:, g, :, :], v_s[g, :, h * 128:(h + 1) * 128].rearrange("(b p) d -> p b d", p=128),
                    reads=["v_s"], writes=[("vt", b, g)])

        smk2 = [mTf[:, 0:2048], mTf[:, 4224:6272]]
        D32 = [mTf[:, 3072:3456], mTf[:, 6272:6656]]
        cbc2 = [mTf[:, 3456:3840], mTf[:, 6656:7040]]
        acc_2 = [mTf[:, 3840:4096], mTf[:, 7040:7296]]
        acc22 = [mTf[:, 4096:4224], mTf[:, 7296:7424]]
        colm2t = kb.sb("colm2", [128, 2, 8, 3], F32, es)
        pst = {"pidx": 0}

        def iteration(h, qb, sl):
            b = h % 2
            smk, D3, cbc, acc, acc2 = smk2[sl], D32[sl], cbc2[sl], acc_2[sl], acc22[sl]
            colm = colm2t[:, sl, :, :]
            K = lambda name: (name, sl)
            groups = []
            for g in range(3):
                nb = WIN[g] // 128
                lst = []
                for kbk in range(max(0, qb - nb), qb + 1):
                    db = qb - kbk
                    if g == 0:
                        mi = 0 if db == 0 else 1
                    elif g == 1:
                        mi = 2 if db == 0 else (4 if db == 4 else 3)
                    else:
                        mi = 5 if db == 0 else 6
                    lst.append((kbk, mi))
                groups.append(lst)
            po, pok = psum(4, 5) if sl == 0 else psum(6, 7)
            pc, pck = psum(5, 6) if sl == 0 else psum(7, 8)
            qsl = qTt[b][:, :, qb * 128:(qb + 1) * 128]
            for g in range(3):
                lst = groups[g]
                nk = len(lst) * 128
                for c4 in range(0, len(lst), 4):
                    ch = lst[c4:c4 + 4]
                    ps_, psk = psum(2, 4)
                    k0 = ch[0][0]
                    op("pe", lambda e: e.matmul(ps_[:, 0:len(ch) * 128], qsl[:, g, :], kTt[b][:, g, k0 * 128:(k0 + len(ch)) * 128],
                                                start=True, stop=True),
                       reads=[("qTt", b, g), ("kTt", b, g)], writes=[psk])
                    for ci, (kbk, mi) in enumerate(ch):
                        op("dve", lambda e: e.tensor_tensor(out=smk[:, (c4 + ci) * 128:(c4 + ci + 1) * 128], in0=ps_[:, ci * 128:(ci + 1) * 128],
                                                            in1=mbq[:, mi, :], op=ALU.add),
                           reads=[psk, "mbq"], writes=[("smk", sl, c4 + ci)])
                    yield
                smks = [("smk", sl, k_) for k_ in range(len(lst))]
                op("dve", lambda e: e.tensor_reduce(out=colm[:, 0, g:g + 1], in_=smk[:, 0:nk], axis=AX.X, op=ALU.max),
                   reads=smks, writes=[("c0", sl, g)])
                op("dve", lambda e: e.tensor_scalar(out=colm[:, 1, g:g + 1], in0=colm[:, 0, g:g + 1], scalar1=-SCALE, scalar2=None, op0=ALU.mult),
                   reads=[("c0", sl, g)], writes=[("c1", sl, g)])
                op("act", lambda e: e.activation(out=junk[:, 0:nk], in_=smk[:, 0:nk], func=AF.Exp, bias=colm[:, 1, g:g + 1], scale=SCALE,
                                                 accum_out=colm[:, 2, g:g + 1]),
                   reads=smks + [("c1", sl, g)], writes=[("c2", sl, g)])
                yield
                done = 0
                for c4 in range(0, len(lst), 4):
                    ch = lst[c4:c4 + 4]
                    pb, pk = psum(0, 2)
                    for ci, (kbk, mi) in enumerate(ch):
                        op("pe", lambda e: e.matmul(pb[:, ci * 128:(ci + 1) * 128], kTt[b][:, g, kbk * 128:(kbk + 1) * 128], qsl[:, g, :],
                                                    start=True, stop=True),
                           reads=[("kTt", b, g), ("qTt", b, g)], writes=[pk], signal=(ci == len(ch) - 1))
                    p_ = pT[pst["pidx"] % 3]
                    pk_ = "pT%d" % (pst["pidx"] % 3)
                    pst["pidx"] += 1
                    nc_ = len(ch) * 128
                    op("act", lambda e: e.activation(out=p_[:, 0:nc_], in_=pb[:, 0:nc_], func=AF.Exp, scale=SCALE), reads=[pk], writes=[pk_])
                    yield
                    for ci, (kbk, mi) in enumerate(ch):
                        op("pool", lambda e: e.tensor_tensor(out=p_[:, ci * 128:(ci + 1) * 128], in0=p_[:, ci * 128:(ci + 1) * 128],
                                                             in1=masks[:, mi, :], op=ALU.mult),
                           reads=[pk_, "masks"], writes=[pk_])
                    for ci, (kbk, mi) in enumerate(ch):
                        op("pe", lambda e: e.matmul(po[:, g * 128:(g + 1) * 128], vt[b][:, g, kbk, :], p_[:, ci * 128:(ci + 1) * 128],
                                                    start=(done == 0), stop=(done == len(lst) - 1)),
                           reads=[("vt", b, g), pk_], writes=[pok], signal=True)
                        done += 1
                    yield
            c1s = [("c1", sl, g_) for g_ in range(3)]
            c2s = [("c2", sl, g_) for g_ in range(3)]
            op("act", lambda e: e.activation(out=colm[:, 3, :], in_=colm[:, 1, :], func=AF.Exp, scale=-1.0), reads=c1s, writes=[K("c3")])
            op("dve", lambda e: e.tensor_tensor(out=colm[:, 4, :], in0=colm[:, 2, :], in1=colm[:, 3, :], op=ALU.mult), reads=c2s + [K("c3")], writes=[K("c4")])
            yield
            op("dve", lambda e: e.tensor_reduce(out=colm[:, 7, 0:1], in_=colm[:, 4, :], axis=AX.X, op=ALU.add), reads=[K("c4")], writes=[K("c7")])
            if h == 0:
                op("dve", lambda e: e.tensor_copy(out=l0[:, qb, :], in_=colm[:, 2, :]), reads=c2s, writes=[("l0", qb)])
            yield
            op("dve", lambda e: e.tensor_scalar(out=colm[:, 5, :], in0=l0[:, qb, :], scalar1=colm[:, 7, 0:1], scalar2=None, op0=ALU.mult),
               reads=[K("c7"), ("l0", qb)], writes=[K("c5")])
            yield
            op("dve", lambda e: e.reciprocal(colm[:, 5, :], colm[:, 5, :]), reads=[K("c5")], writes=[K("c5")])
            yield
            op("dve", lambda e: e.tensor_tensor(out=colm[:, 6, :], in0=colm[:, 2, :], in1=colm[:, 5, :], op=ALU.mult), reads=c2s + [K("c5")], writes=[K("c6")])
            yield
            for g in range(3):
                op("dve", lambda e: e.tensor_scalar(out=D3[:, g * 128:(g + 1) * 128], in0=ident_f[:, :], scalar1=colm[:, 6, g:g + 1], scalar2=None, op0=ALU.mult),
                   reads=[K("c6"), "ident_f"], writes=[("D3", sl, g)])
            op("pe", lambda e: e.matmul(pc[:, 0:384], ones_f[:, :], D3[:, :], start=True, stop=True), reads=["ones_f"] + [("D3", sl, g_) for g_ in range(3)], writes=[pck])
            op("act", lambda e: e.copy(cbc[:, :], pc[:, 0:384]), reads=[pck], writes=[K("cbc")])
            yield
            op("dve", lambda e: e.tensor_tensor(out=acc[:, 0:128], in0=po[:, 0:128], in1=cbc[:, 0:128], op=ALU.mult), reads=[pok, K("cbc")], writes=[K("acc0")])
            op("dve", lambda e: e.tensor_tensor(out=acc[:, 128:256], in0=po[:, 128:256], in1=cbc[:, 128:256], op=ALU.mult), reads=[pok, K("cbc")], writes=[K("acc1")])
            op("dve", lambda e: e.tensor_tensor(out=acc2, in0=po[:, 256:384], in1=cbc[:, 256:384], op=ALU.mult), reads=[pok, K("cbc")], writes=[K("acc2")])
            yield
            op("dve", lambda e: e.tensor_tensor(out=acc[:, 0:128], in0=acc[:, 0:128], in1=acc[:, 128:256], op=ALU.add), reads=[K("acc0"), K("acc1")], writes=[K("acc0")])
            yield
            op("dve", lambda e: e.tensor_tensor(out=mh[b][:, qb * 128:(qb + 1) * 128], in0=acc[:, 0:128], in1=acc2, op=ALU.add),
               reads=[K("acc0"), K("acc2")], writes=[("mh", b, qb)])

        load_head(0)
        for h in range(8):
            if h + 1 < 8:
                load_head(h + 1)
            b = h % 2
            for qb0 in range(0, 16, 2):
                gens = [iteration(h, qb0, 0), iteration(h, qb0 + 1, 1)]
                alive = [True, True]
                while any(alive):
                    for gi in range(2):
                        if alive[gi]:
                            try:
                                next(gens[gi])
                            except StopIteration:
                                alive[gi] = False
            dma("sp", mT_s[h, :, 0:T], mh[b][:, 0:T], reads=[("mh", b, q_) for q_ in range(16)], writes=["mT_s"])

        kb.barrier()
        cache = [mTf[:, i * 2048:(i + 1) * 2048] for i in range(2)]
        cb16 = [mTf[:, 4096 + i * 512:4096 + (i + 1) * 512].bitcast(BF16) for i in range(2)]
        vkeep = [mTf[:, 5120 + g * 512:5120 + (g + 1) * 512].bitcast(BF16) for g in range(3)]
        ckT = kb.sb("ckT", [128, 8, 128], BF16, es)
        qk = kb.sb("qk", [128, 24 * NS], F32, es)
        qkr = kb.sb("qkr", [1, 24 * NS], F32, es)
        srow = aTf[0:1, 0:2064].bitcast(F32).rearrange("p (h k) -> p h k", h=8)
        prow = aTf[0:1, 2064:2064 + 6192].bitcast(F32).rearrange("p (g h k) -> p g h k", g=3, h=8)
        rw = kb.sb("rw", [1, 12, 24], F32, es)
        one1 = kb.sb("one1", [1, 1], F32, es)
        pSk = kb.sb("pSk", [128, 3, 8], BF16, es)
        pnb = kb.sb("pnb", [128, 3, 8], F32, es)
        og = kb.sb("og", [128, 3, 8], F32, es)
        cbs = kb.sb("cbs", [128, 3, 8], F32, es)
        so = kb.sb("so", [128, 8], F32, es)
        sob = kb.sb("sob", [128, 8, NS], BF16, es)
        op("dve", lambda e: e.memset(one1[:], 1.0), writes=["one1"])
        op("dve", lambda e: e.tensor_tensor(out=qk[:, :], in0=sqT[:].rearrange("p a s -> p (a s)"), in1=skT[:].rearrange("p a s -> p (a s)"), op=ALU.mult),
           reads=["sqT", "skT"], writes=["qk"])
        pb, pk = psum(0, 2)
        op("pe", lambda e: e.matmul(pb[0:1, 0:24 * NS], ones_f[:, 0:1], qk[:, :], start=True, stop=True), reads=["ones_f", "qk"], writes=[pk])
        op("act", lambda e: e.copy(qkr[:, :], pb[0:1, 0:24 * NS]), reads=[pk], writes=["qkr"])
        qkr3 = qkr[:].rearrange("p (a s) -> p a s", s=NS)
        rwm, rwmm, rwl, rwe, rwt, rwc = (rw[:, k, :].rearrange("p (g h) -> p g h", g=3) for k in range(6))
        for s_ in range(NS):
            pov, povk = psum(4, 5)
            for g in range(3):
                cbuf = cache[g % 2]
                ck = "cache%d" % (g % 2)
                src = ckv[g][li, s_, :, :].rearrange("(j d) c -> j d c", d=DIL[g])[:, 0, :]
                dma("sp", cbuf[:, :], src, writes=[ck])
                c16 = cb16[g % 2]
                c16k = "cb16_%d" % (g % 2)
                op("pool", lambda e: e.tensor_copy(out=c16[:, 0:1024], in_=cbuf[:, 0:1024]), reads=[ck], writes=[c16k])
                op("act", lambda e: e.copy(vkeep[g][:, :], cbuf[:, 1024:2048]), reads=[ck], writes=["vk%d" % g])
                for h in range(8):
                    pt, ptk = psum(0, 2)
                    ptb = pt[:, 0:64].bitcast(BF16)
                    op("pe", lambda e: e.transpose(ptb[:, 0:128], c16[:, h * 128:(h + 1) * 128], ident_b[:, :]),
                       reads=[c16k, "ident_b"], writes=[ptk])
                    op("dve", lambda e: e.tensor_copy(out=ckT[:, h, :], in_=ptb[:, 0:128]), reads=[ptk], writes=["ckT"])
                for hq in range(2):
                    pr, prk = psum(2, 4)
                    for hh in range(4):
                        h = hq * 4 + hh
                        op("pe", lambda e: e.matmul(pr[0:1, hh * 128:(hh + 1) * 128], sqT[:, g * 8 + h, s_:s_ + 1], ckT[:, h, :], start=True, stop=True),
                           reads=["sqT", "ckT"], writes=[prk], signal=(hh == 3))
                    op("act", lambda e: e.copy(srow[0:1, hq * 4:(hq + 1) * 4, 0:128], pr[0:1, :].rearrange("p (h k) -> p h k", h=4)),
                       reads=[prk], writes=["srow"])
                op("dve", lambda e: e.tensor_copy(out=srow[0:1, :, 128:129], in_=qkr3[0:1, g * 8:(g + 1) * 8, s_:s_ + 1]), reads=["qkr"], writes=["srow"])
                op("dve", lambda e: e.tensor_reduce(out=rwm[0:1, g, :], in_=srow[0:1, :, :], axis=AX.X, op=ALU.max), reads=["srow"], writes=["rw"])
                op("dve", lambda e: e.tensor_tensor(out=srow[0:1, :, :], in0=srow[0:1, :, :],
                                                    in1=rwm[0:1, g, :].unsqueeze(2).broadcast_to([1, 8, 129]), op=ALU.subtract),
                   reads=["srow", "rw"], writes=["srow"])
                op("act", lambda e: e.activation(out=prow[0:1, g, :, :], in_=srow[0:1, :, :], func=AF.Exp, scale=SCALE), reads=["srow"], writes=["prow"])
                op("dve", lambda e: e.tensor_reduce(out=rwl[0:1, g, :], in_=prow[0:1, g, :, :], axis=AX.X, op=ALU.add), reads=["prow"], writes=["rw"])
                pp, ppk = psum(2, 4)
                for h in range(8):
                    op("pe", lambda e: e.matmul(pp[:, h:h + 1], prow[0:1, g, h, 0:128], one1[0:1, 0:1], start=True, stop=True),
                       reads=["prow", "one1"], writes=[ppk], signal=(h == 7))
                op("dve", lambda e: e.tensor_copy(out=pSk[:, g, :], in_=pp[:, 0:8]), reads=[ppk], writes=["pSk"])
                pn_, pnk = psum(2, 4)
                op("pe", lambda e: e.matmul(pn_[:, 0:8], ones_f[0:1, :], prow[0:1, g, :, 128], start=True, stop=True), reads=["ones_f", "prow"], writes=[pnk])
                op("dve", lambda e: e.tensor_copy(out=pnb[:, g, :], in_=pn_[:, 0:8]), reads=[pnk], writes=["pnb"])
                for h in range(8):
                    op("pe", lambda e: e.matmul(pov[:, g * 8 + h:g * 8 + h + 1], vkeep[g][:, h * 128:(h + 1) * 128], pSk[:, g, h:h + 1], start=True, stop=True),
                       reads=["vk%d" % g, "pSk"], writes=[povk], signal=(h == 7))
            op("dve", lambda e: e.tensor_tensor(out=og[:, :, :], in0=svT[:, :, s_].rearrange("p (g h) -> p g h", g=3), in1=pnb[:, :, :], op=ALU.mult),
               reads=["svT", "pnb"], writes=["og"])
            op("dve", lambda e: e.tensor_tensor(out=og[:, :, :], in0=og[:, :, :], in1=pov[:, 0:24].rearrange("p (g h) -> p g h", g=3), op=ALU.add),
               reads=["og", povk], writes=["og"])
            op("dve", lambda e: e.tensor_scalar(out=rwmm[0:1, :, :], in0=rwm[0:1, :, :], scalar1=SCALE, scalar2=None, op0=ALU.mult), reads=["rw"], writes=["rw"])
            op("act", lambda e: e.activation(out=rwe[0:1, :, :], in_=rwmm[0:1, :, :], func=AF.Exp), reads=["rw"], writes=["rw"])
            op("dve", lambda e: e.tensor_tensor(out=rwe[0:1, :, :], in0=rwe[0:1, :, :], in1=rwl[0:1, :, :], op=ALU.mult), reads=["rw"], writes=["rw"])
            op("dve", lambda e: e.tensor_tensor(out=rwt[0:1, 0, :], in0=rwe[0:1, 0, :], in1=rwe[0:1, 1, :], op=ALU.add), reads=["rw"], writes=["rw"])
            op("dve", lambda e: e.tensor_tensor(out=rwt[0:1, 0, :], in0=rwt[0:1, 0, :], in1=rwe[0:1, 2, :], op=ALU.add), reads=["rw"], writes=["rw"])
            op("dve", lambda e: e.tensor_tensor(out=rwc[0:1, :, :], in0=rwt[0:1, 0, :].unsqueeze(1).broadcast_to([1, 3, 8]),
                                                in1=rwl[0:1, :, 0:1].broadcast_to([1, 3, 8]), op=ALU.mult), reads=["rw"], writes=["rw"])
            op("dve", lambda e: e.reciprocal(rwc[0:1, :, :], rwc[0:1, :, :]), reads=["rw"], writes=["rw"])
            op("dve", lambda e: e.tensor_tensor(out=rwc[0:1, :, :], in0=rwc[0:1, :, :], in1=rwe[0:1, :, :], op=ALU.mult), reads=["rw"], writes=["rw"])
            pcb, pcbk = psum(2, 4)
            op("pe", lambda e: e.matmul(pcb[:, 0:24], ones_f[0:1, :], rw[0:1, 5, :], start=True, stop=True), reads=["ones_f", "rw"], writes=[pcbk])
            op("dve", lambda e: e.tensor_tensor(out=og[:, :, :], in0=og[:, :, :], in1=pcb[:, 0:24].rearrange("p (g h) -> p g h", g=3), op=ALU.mult),
               reads=["og", pcbk], writes=["og"])
            op("dve", lambda e: e.tensor_tensor(out=so[:, :], in0=og[:, 0, :], in1=og[:, 1, :], op=ALU.add), reads=["og"], writes=["so"])
            op("dve", lambda e: e.tensor_tensor(out=sob[:, :, s_], in0=so[:, :], in1=og[:, 2, :], op=ALU.add), reads=["so", "og"], writes=["sob"])
        with nc.allow_non_contiguous_dma(reason="tiny sample columns"):
            dma("sp", mT_s[:, :, T:TC].rearrange("h p s -> p h s"), sob[:, :, :], reads=["sob"], writes=["mT_s"])
        kb.barrier()
        es.close()

    LCH = 32
    NCH = T // LCH

    def win_phase(li, t, segs):
        g0 = t * TW
        for dc in range(KC):
            wv, wk = wr.get(("win", li, dc))
            w3 = wv[:, 0:KC * 128].rearrange("p (k c) -> p k c", k=KC)
            for (c0, n, S) in segs:
                pb, pk = psum()
                for kc in range(KC):
                    op("pe", lambda e: e.matmul(pb[:, 0:n], w3[:, kc, :], hT[:, kc, c0:c0 + n], start=(kc == 0), stop=(kc == KC - 1)),
                       reads=["hT", wk], writes=[pk], signal=(kc == KC - 1))
                r = rr["cg"] % 2
                rr["cg"] += 1
                op("act", lambda e: e.copy(cgt[r][:, 0:n], pb[:, 0:n]), reads=[pk], writes=["cg%d" % r])
                gc = g0 if S == 1 else T
                if S == 1:
                    dma("sp", uT_s[dc, :, gc:gc + n], cgt[r][:, 0:n], reads=["cg%d" % r], writes=["uT_s"])
                else:
                    with nc.allow_non_contiguous_dma(reason="tiny sample columns"):
                        dma("sp", uT_s[dc, :, gc:gc + n], cgt[r][:, 0:n], reads=["cg%d" % r], writes=["uT_s"])

    def mixer_out_ssm(li, t, segs):
        g0 = t * TW
        minT = aT[:, 0:KC, :]
        dma("sp", minT[:, :, 0:TW], zT_s[:, :, g0:g0 + TW].rearrange("k p t -> p k t"), reads=["zT_s"], writes=["aT"])
        if len(segs) > 1:
            with nc.allow_non_contiguous_dma(reason="tiny sample columns"):
                dma("sp", minT[:, :, TW:TWS], zT_s[:, :, T:TC].rearrange("k p t -> p k t"), reads=["zT_s"], writes=["aT"])
        psts = [stats_begin() if si == 0 else psum(6, 7) for si in range(len(segs))]
        for dc in range(KC):
            wv, wk = wr.get(("glu", li, dc))
            wval = wv[:, 0:KC * 128].rearrange("p (k c) -> p k c", k=KC)
            wgat = wv[:, KC * 128:2 * KC * 128].rearrange("p (k c) -> p k c", k=KC)
            for si, (c0, n, S) in enumerate(segs):
                pv, pvk = psum()
                pg, pgk = psum()
                for kc in range(KC):
                    op("pe", lambda e: e.matmul(pv[:, 0:n], wval[:, kc, :], minT[:, kc, c0:c0 + n], start=(kc == 0), stop=(kc == KC - 1)),
                       reads=["aT", wk], writes=[pvk], signal=(kc == KC - 1))
                for kc in range(KC):
                    op("pe", lambda e: e.matmul(pg[:, 0:n], wgat[:, kc, :], minT[:, kc, c0:c0 + n], start=(kc == 0), stop=(kc == KC - 1)),
                       reads=["aT", wk], writes=[pgk], signal=(kc == KC - 1))
                r = rr["cg"] % 2
                rr["cg"] += 1
                op("act", lambda e: e.activation(out=sgt[r][:, 0:n], in_=pg[:, 0:n], func=AF.Sigmoid), reads=[pgk], writes=["sg%d" % r])
                op("dve", lambda e: e.tensor_tensor(out=mT[:, dc, c0:c0 + n], in0=pv[:, 0:n], in1=sgt[r][:, 0:n], op=ALU.mult),
                   reads=[pvk, "sg%d" % r], writes=["mT"])
                stats_add(psts[si], mT[:, dc, c0:c0 + n], "mT", c0, n, dc == 0, dc == KC - 1)
        for si, (c0, n, S) in enumerate(segs):
            stats_end(psts[si], c0, n)

    def ssm_phase(li):
        kb.barrier()
        es = contextlib.ExitStack()
        aTf = aT[:].rearrange("p a b -> p (a b)")
        hTf = hT[:].rearrange("p a b -> p (a b)")
        xTf = xT[:].rearrange("p a b -> p (a b)")
        mTf = mT[:].rearrange("p a b -> p (a b)")
        Wb = aTf[:, 0:16384].rearrange("p (k g m) -> p k g m", k=KC, g=8)
        npi_t = aTf[:, 16384:16384 + 4096].bitcast(F32).rearrange("p (a i) -> p a i", i=LCH)
        Wc = xTf[:, 0:8192].bitcast(BF16).rearrange("p (k j r m) -> p k j r m", k=KC, j=4, r=2)
        pr_parts = [cgt[0], cgt[1], cvt[0], cvt[1]]
        pi_parts = [sgt[0], sgt[1], rstd, rtmp]

        def tab(parts, pair):
            return parts[pair // 16][:, (pair % 16) * LCH:(pair % 16 + 1) * LCH]
        small = hTf[:, 0:3840].bitcast(F32).rearrange("p (k w) -> p k w", w=64)
        smallB = hTf[0:64, 3840:3840 + 4096].bitcast(F32).rearrange("p (k w) -> p k w", w=128)
        dsk = kb.sb("dsk", [128, KC], F32, es)
        gmask = kb.sb("gmask", [128, 8], F32, es)
        nat = kb.sb("nat", [128, 128], F32, es)
        dma("sp", gmask[:], c_gmask[:, :], writes=["gmask"])
        dma("sp", nat[0:KC, :], d_skip[li, :].rearrange("(k p) -> k p", p=128), writes=["nat"])
        pb, pk = psum(5, 8)
        op("pe", lambda e: e.transpose(pb[:, 0:KC], nat[0:KC, :], ident_f[0:KC, 0:KC]), reads=["nat", "ident_f"], writes=[pk])
        op("act", lambda e: e.copy(dsk[:, :], pb[:, 0:KC]), reads=[pk], writes=["dsk"])

        def abar_chain(P, W, sm, lam_r_ap, lam_i_ap, ldt_ap, key):
            S_ = lambda k: sm[0:P, k, 0:W]
            o = lambda fn, **kw: op("dve", fn, reads=[key], writes=[key])
            o(lambda e: e.tensor_scalar(out=S_(0), in0=lam_r_ap, scalar1=-1e-4, scalar2=None, op0=ALU.min))
            o(lambda e: e.tensor_copy(out=S_(1), in_=lam_i_ap))
            op("act", lambda e: e.activation(out=S_(2), in_=ldt_ap, func=AF.Exp), reads=[key], writes=[key])
            o(lambda e: e.tensor_tensor(out=S_(3), in0=S_(1), in1=S_(2), op=ALU.mult))
            o(lambda e: e.tensor_scalar(out=S_(3), in0=S_(3), scalar1=1.0 / 16.0, scalar2=None, op0=ALU.mult))
            o(lambda e: e.tensor_tensor(out=S_(4), in0=S_(3), in1=S_(3), op=ALU.mult))
            o(lambda e: e.tensor_scalar(out=S_(5), in0=S_(4), scalar1=1.0 / 362880.0, scalar2=None, op0=ALU.mult))
            for cf in (-1.0 / 5040.0, 1.0 / 120.0, -1.0 / 6.0):
                o(lambda e: e.scalar_tensor_tensor(out=S_(5), in0=S_(5), scalar=cf, in1=S_(4), op0=ALU.add, op1=ALU.mult))
            o(lambda e: e.scalar_tensor_tensor(out=S_(5), in0=S_(5), scalar=1.0, in1=S_(3), op0=ALU.add, op1=ALU.mult))
            o(lambda e: e.tensor_scalar(out=S_(6), in0=S_(4), scalar1=-1.0 / 3628800.0, scalar2=None, op0=ALU.mult))
            for cf in (1.0 / 40320.0, -1.0 / 720.0, 1.0 / 24.0, -0.5):
                o(lambda e: e.scalar_tensor_tensor(out=S_(6), in0=S_(6), scalar=cf, in1=S_(4), op0=ALU.add, op1=ALU.mult))
            o(lambda e: e.tensor_scalar(out=S_(6), in0=S_(6), scalar1=1.0, scalar2=None, op0=ALU.add))
            for _ in range(4):
                o(lambda e: e.tensor_tensor(out=S_(7), in0=S_(5), in1=S_(6), op=ALU.mult))
                o(lambda e: e.tensor_tensor(out=S_(11), in0=S_(5), in1=S_(5), op=ALU.mult))
                o(lambda e: e.tensor_scalar(out=S_(6), in0=S_(11), scalar1=-2.0, scalar2=1.0, op0=ALU.mult, op1=ALU.add))
                o(lambda e: e.tensor_scalar(out=S_(5), in0=S_(7), scalar1=2.0, scalar2=None, op0=ALU.mult))
            o(lambda e: e.tensor_tensor(out=S_(11), in0=S_(0), in1=S_(2), op=ALU.mult))
            op("act", lambda e: e.activation(out=S_(8), in_=S_(11), func=AF.Exp), reads=[key], writes=[key])
            o(lambda e: e.tensor_tensor(out=S_(9), in0=S_(8), in1=S_(6), op=ALU.mult))
            o(lambda e: e.tensor_tensor(out=S_(10), in0=S_(8), in1=S_(5), op=ALU.mult))
            return S_(9), S_(10), S_(0), S_(1)

        def load_A(dst, src2d):
            dma("sp", nat[0:64, :], src2d.rearrange("(g s) p -> g (s p)", s=2), writes=["nat"])
            pb, pk = psum(5, 8)
            op("pe", lambda e: e.transpose(pb[:, 0:64], nat[0:64, :], ident_f[0:64, 0:64]), reads=["nat", "ident_f"], writes=[pk])
            op("act", lambda e: e.copy(dst, pb[:, 0:64]), reads=[pk], writes=["small"])
        A_ = lambda k: small[:, k, :]
        load_A(A_(12), lam_re[li, :, :])
        load_A(A_(13), lam_im[li, :, :])
        dma("sp", nat[0:64, 0:2], log_dt[li, :].rearrange("(g s) -> g s", s=2), writes=["nat"])
        op("dve", lambda e: e.tensor_copy(out=nat[0:64, 64:128].rearrange("p (s q) -> p s q", s=2)[:, :, :] if False else smallB[0:64, 15, :].rearrange("p (s q) -> p s q", s=2),
                                          in_=nat[0:64, 0:2].unsqueeze(2).broadcast_to([64, 2, 64])), reads=["nat"], writes=["smallB"])
        pb, pk = psum(5, 8)
        op("pe", lambda e: e.transpose(pb[:, 0:64], smallB[0:64, 15, :], ident_f[0:64, 0:64]), reads=["smallB", "ident_f"], writes=[pk])
        op("act", lambda e: e.copy(A_(14), pb[:, 0:64]), reads=[pk], writes=["small"])
        arA, aiA, _, _ = abar_chain(128, 64, small, A_(12), A_(13), A_(14), "small")
        for pair0 in range(0, 64, 16):
            pass
        prv = lambda i: [p_[:, :].rearrange("p (a i) -> p a i", i=LCH)[:, :, i] for p_ in pr_parts]
        piv = lambda i: [p_[:, 0:512].rearrange("p (a i) -> p a i", i=LCH)[:, :, i] for p_ in pi_parts]
        tkeys = ["cg0", "cg1", "cv0", "cv1", "sg0", "sg1", "rstd", "rtmp", "npi"]
        for q in range(4):
            op("dve", lambda e: e.tensor_copy(out=prv(0)[q], in_=arA[:, q * 16:(q + 1) * 16]), reads=["small"], writes=tkeys)
            op("dve", lambda e: e.tensor_copy(out=piv(0)[q], in_=aiA[:, q * 16:(q + 1) * 16]), reads=["small"], writes=tkeys)
        for i in range(1, LCH):
            for q in range(4):
                a_r = arA[:, q * 16:(q + 1) * 16]
                a_i = aiA[:, q * 16:(q + 1) * 16]
                t1, t2 = small[:, 15, 0:16], small[:, 16, 0:16]
                op("dve", lambda e: e.tensor_tensor(out=t1, in0=prv(i - 1)[q], in1=a_r, op=ALU.mult), reads=tkeys + ["small"], writes=["small"])
                op("dve", lambda e: e.tensor_tensor(out=t2, in0=piv(i - 1)[q], in1=a_i, op=ALU.mult), reads=tkeys + ["small"], writes=["small"])
                op("dve", lambda e: e.tensor_tensor(out=prv(i)[q], in0=t1, in1=t2, op=ALU.subtract), reads=["small"], writes=tkeys)
                op("dve", lambda e: e.tensor_tensor(out=t1, in0=prv(i - 1)[q], in1=a_i, op=ALU.mult), reads=tkeys + ["small"], writes=["small"])
                op("dve", lambda e: e.tensor_tensor(out=t2, in0=piv(i - 1)[q], in1=a_r, op=ALU.mult), reads=tkeys + ["small"], writes=["small"])
                op("dve", lambda e: e.tensor_tensor(out=piv(i)[q], in0=t1, in1=t2, op=ALU.add), reads=["small"], writes=tkeys)
        for q in range(4):
            op("dve", lambda e: e.tensor_scalar(out=npi_t[:, q * 16:(q + 1) * 16, :], in0=pi_parts[q][:, 0:512].rearrange("p (a i) -> p a i", i=LCH),
                                                scalar1=-1.0, scalar2=None, op0=ALU.mult), reads=tkeys, writes=tkeys)

        A8 = kb.sb("A8", [128, 3, 64, 8], F32, es)
        Ar_all = [p_[:, 0:512].rearrange("p (a i) -> p a i", i=LCH)[:, :, LCH - 1] for p_ in pr_parts]
        Ai_all = [p_[:, 0:512].rearrange("p (a i) -> p a i", i=LCH)[:, :, LCH - 1] for p_ in pi_parts]
        for q in range(4):
            qs = slice(q * 16, (q + 1) * 16)
            op("dve", lambda e: e.tensor_copy(out=A8[:, 0, qs, 0], in_=Ar_all[q]), reads=tkeys, writes=["A8"])
            op("dve", lambda e: e.tensor_copy(out=A8[:, 1, qs, 0], in_=Ai_all[q]), reads=tkeys, writes=["A8"])
            for j in range(1, 8):
                t1, t2 = small[:, 15, 0:16], small[:, 16, 0:16]
                op("dve", lambda e: e.tensor_tensor(out=t1, in0=A8[:, 0, qs, j - 1], in1=Ar_all[q], op=ALU.mult), reads=tkeys + ["A8", "small"], writes=["small"])
                op("dve", lambda e: e.tensor_tensor(out=t2, in0=A8[:, 1, qs, j - 1], in1=Ai_all[q], op=ALU.mult), reads=tkeys + ["A8", "small"], writes=["small"])
                op("dve", lambda e: e.tensor_tensor(out=A8[:, 0, qs, j], in0=t1, in1=t2, op=ALU.subtract), reads=["small"], writes=["A8"])
                op("dve", lambda e: e.tensor_tensor(out=t1, in0=A8[:, 0, qs, j - 1], in1=Ai_all[q], op=ALU.mult), reads=tkeys + ["A8", "small"], writes=["small"])
                op("dve", lambda e: e.tensor_tensor(out=t2, in0=A8[:, 1, qs, j - 1], in1=Ar_all[q], op=ALU.mult), reads=tkeys + ["A8", "small"], writes=["small"])
                op("dve", lambda e: e.tensor_tensor(out=A8[:, 1, qs, j], in0=t1, in1=t2, op=ALU.add), reads=["small"], writes=["A8"])
        op("dve", lambda e: e.tensor_scalar(out=A8[:, 2, :, :], in0=A8[:, 1, :, :], scalar1=-1.0, scalar2=None, op0=ALU.mult), reads=["A8"], writes=["A8"])

        B_ = lambda k: smallB[0:64, k, :]
        for (dst, src) in ((B_(12), lam_re), (B_(13), lam_im)):
            dma("sp", nat[:, 0:64], src[li, :, :], writes=["nat"])
            pb, pk = psum(5, 8)
            op("pe", lambda e: e.transpose(pb[0:64, 0:128], nat[:, 0:64], ident_f[:, :]), reads=["nat", "ident_f"], writes=[pk])
            op("act", lambda e: e.copy(dst, pb[0:64, 0:128]), reads=[pk], writes=["smallB"])
        dma("sp", B_(14), log_dt[li:li + 1, :].partition_broadcast(64).rearrange("p a g -> p (a g)") if False else log_dt[li:li + 1, :].broadcast_to([64, 128]), writes=["smallB"])
        arB, aiB, lrB, liB = abar_chain(64, 128, smallB, B_(12), B_(13), B_(14), "smallB")
        ob = lambda fn: op("dve", fn, reads=["smallB"], writes=["smallB"])
        ob(lambda e: e.tensor_scalar(out=B_(11), in0=arB, scalar1=-1.0, scalar2=None, op0=ALU.add))
        ob(lambda e: e.tensor_tensor(out=B_(7), in0=lrB, in1=lrB, op=ALU.mult))
        ob(lambda e: e.tensor_tensor(out=B_(2), in0=liB, in1=liB, op=ALU.mult))
        ob(lambda e: e.tensor_tensor(out=B_(7), in0=B_(7), in1=B_(2), op=ALU.add))
        ob(lambda e: e.reciprocal(B_(7), B_(7)))
        ob(lambda e: e.tensor_tensor(out=B_(3), in0=B_(11), in1=lrB, op=ALU.mult))
        ob(lambda e: e.tensor_tensor(out=B_(2), in0=aiB, in1=liB, op=ALU.mult))
        ob(lambda e: e.tensor_tensor(out=B_(3), in0=B_(3), in1=B_(2), op=ALU.add))
        ob(lambda e: e.tensor_tensor(out=B_(3), in0=B_(3), in1=B_(7), op=ALU.mult))
        ob(lambda e: e.tensor_tensor(out=B_(4), in0=aiB, in1=lrB, op=ALU.mult))
        ob(lambda e: e.tensor_tensor(out=B_(2), in0=B_(11), in1=liB, op=ALU.mult))
        ob(lambda e: e.tensor_tensor(out=B_(4), in0=B_(4), in1=B_(2), op=ALU.subtract))
        ob(lambda e: e.tensor_tensor(out=B_(4), in0=B_(4), in1=B_(7), op=ALU.mult))
        bre = mTf[0:64, 0:2048].rearrange("p (g c) -> p g c", c=16)
        bim = mTf[0:64, 2048:4096].rearrange("p (g c) -> p g c", c=16)
        bbr = mTf[0:64, 4096:6144].rearrange("p (g c) -> p g c", c=16)
        bbi = mTf[0:64, 6144:8192].rearrange("p (g c) -> p g c", c=16)
        with nc.allow_non_contiguous_dma(reason="b tensors 64B runs"):
            dma("sp", bre, b_re[li, :, :, :].rearrange("g p c -> p g c"), writes=["bb"])
            dma("sp", bim, b_im[li, :, :, :].rearrange("g p c -> p g c"), writes=["bb"])
        cr = B_(3).unsqueeze(2).broadcast_to([64, 128, 16])
        ci = B_(4).unsqueeze(2).broadcast_to([64, 128, 16])
        o2 = lambda fn: op("dve", fn, reads=["bb", "smallB"], writes=["bb"])
        o2(lambda e: e.tensor_tensor(out=bbr, in0=bre, in1=cr, op=ALU.mult))
        o2(lambda e: e.tensor_tensor(out=bbi, in0=bim, in1=ci, op=ALU.mult))
        o2(lambda e: e.tensor_tensor(out=bbr, in0=bbr, in1=bbi, op=ALU.subtract))
        o2(lambda e: e.tensor_tensor(out=bbi, in0=bre, in1=ci, op=ALU.mult))
        o2(lambda e: e.tensor_tensor(out=bre, in0=bim, in1=cr, op=ALU.mult))
        o2(lambda e: e.tensor_tensor(out=bbi, in0=bbi, in1=bre, op=ALU.add))
        for kc in range(KC):
            for r_, src in ((0, bbr), (1, bbi)):
                pb, pk = psum(5, 8)
                op("pe", lambda e: e.transpose(pb[:, 0:64], src[:, kc * 8:(kc + 1) * 8, :].rearrange("p g c -> p (g c)"), ident_f[0:64, 0:64]),
                   reads=["bb", "ident_f"], writes=[pk])
                for gl in range(8):
                    op("act" if gl % 2 else "dve",
                       (lambda e: e.activation(out=Wb[:, kc, gl, r_ * 64:(r_ + 1) * 64], in_=pb[:, 0:64], func=AF.Identity, scale=gmask[:, gl:gl + 1])) if gl % 2 else
                       (lambda e: e.tensor_scalar(out=Wb[:, kc, gl, r_ * 64:(r_ + 1) * 64], in0=pb[:, 0:64], scalar1=gmask[:, gl:gl + 1], scalar2=None, op0=ALU.mult)),
                       reads=[pk, "gmask"], writes=["Wb"])
        kb.barrier()
        Cn = [hTf[0:64, r_ * 4096:(r_ + 1) * 4096].bitcast(F32).rearrange("p (s c q) -> p s c q", s=2, c=16) for r_ in range(2)]
        dma("sp", Cn[0], c_re[li, :, :, :].rearrange("(g s) c q -> g s c q", s=2), writes=["Cn"])
        dma("sp", Cn[1], c_im[li, :, :, :].rearrange("(g s) c q -> g s c q", s=2), writes=["Cn"])
        op("dve", lambda e: e.memset(xTf[:, 0:8192], 0.0), writes=["Wc"])
        for r_ in range(2):
            for c in range(16):
                pb, pk = psum(5, 8)
                op("dve", lambda e: e.tensor_copy(out=nat[0:64, :].rearrange("p (s q) -> p s q", s=2), in_=Cn[r_][:, :, c, :]), reads=["Cn"], writes=["nat"])
                op("pe", lambda e: e.transpose(pb[:, 0:64], nat[0:64, :], ident_f[0:64, 0:64]), reads=["nat", "ident_f"], writes=[pk])
                for s in range(2):
                    for j4 in range(4):
                        src = pb[s * 64:(s + 1) * 64, 0:64].rearrange("p (k j) -> p k j", j=4)[:, :, j4]
                        dst = Wc[s * 64:(s + 1) * 64, :, j4, r_, 32 * j4 + 16 * s + c]
                        sc = 1.0 if r_ == 0 else -1.0
                        if (s + j4) % 2:
                            op("act", lambda e: e.mul(dst, src, sc), reads=[pk], writes=["Wc"])
                        else:
                            op("dve", lambda e: e.tensor_scalar(out=dst, in0=src, scalar1=sc, scalar2=None, op0=ALU.mult), reads=[pk], writes=["Wc"])
        kb.barrier()
        XR = [mTf[:, (2 * q) * TC:(2 * q + 1) * TC] for q in range(2)]
        XI = [mTf[:, (2 * q + 1) * TC:(2 * q + 2) * TC] for q in range(2)]
        ya = kb.sb("ya", [128, TW], F32, es)
        yb2 = aTf[:, 20480:20480 + 1024].bitcast(F32)
        Xs = kb.sb("Xs", [128, 4, NCH], F32, es)
        h0t = kb.sb("h0t", [128, 64, NS, 2], F32, es)
        h0 = h0t[:]
        sto = kb.sb("sto", [128, 64, 2], F32, es)
        stos = kb.sb("stos", [128, NS, 64, 2], F32, es)
        ubf = hTf[:, 0:TC]
        xbr = hTf[:, TC:2 * TC]
        xbi = hTf[:, 2 * TC:3 * TC]
        zst = hTf[:, 3 * TC:4 * TC]
        for s_ in range(NS):
            with nc.allow_non_contiguous_dma(reason="state 8B runs"):
                dma("sp", h0[:, :, s_, :], st_ssm[li, s_, :].rearrange("(a p r) -> p a r", p=128, r=2), writes=["h0"])
        coltiles = [(tq * TW, TW) for tq in range(NT)] + [(T, NS)]
        NBK = T // 4
        XAR = [x[:, 0:T].rearrange("p (r m) -> p r m", m=NBK) for x in XR]
        XAI = [x[:, 0:T].rearrange("p (r m) -> p r m", m=NBK) for x in XI]
        X3R = [x[:, 3, :].rearrange("p (n i) -> p i n", i=8) for x in XAR]
        X3I = [x[:, 3, :].rearrange("p (n i) -> p i n", i=8) for x in XAI]
        STT = lambda o_, a_, sc_, b_: (lambda e: e.scalar_tensor_tensor(out=o_, in0=a_, scalar=sc_, in1=b_, op0=ALU.mult, op1=ALU.add))
        for kc in range(KC):
            dma("pool", ubf[:, :], uT_s[kc, :, :], reads=["uT_s"], writes=["ubf"])
            ybanks = [(pbank[k], ("ps", k)) for k in range(5)]
            for jp in range(2):
                pairs = [kc * 4 + 2 * jp + q for q in range(2)]
                tabs = []
                for q in range(2):
                    pair = pairs[q]
                    j4 = 2 * jp + q
                    PR, PI, NPI = tab(pr_parts, pair), tab(pi_parts, pair), npi_t[:, pair, :]
                    tabs.append((PR, PI, NPI))
                    for (c0, n) in coltiles:
                        for r_, dstx, dk in ((0, XR[q], ("xr", q)), (1, XI[q], ("xi", q))):
                            pb, pk = psum(5, 8)
                            for s in range(2):
                                op("pe", lambda e: e.matmul(pb[s * 64:(s + 1) * 64, 0:n], Wb[:, kc, 2 * j4 + s, r_ * 64:(r_ + 1) * 64], ubf[:, c0:c0 + n],
                                                            start=True, stop=True), reads=["Wb", "ubf"], writes=[pk], signal=(s == 1))
                            if n == TW:
                                m0 = c0 // 4
                                dview = dstx[:, 0:T].rearrange("p (r m) -> p r m", m=T // 4)[:, :, m0:m0 + TW // 4]
                                op("act", lambda e: e.copy(dview, pb[:, 0:TW].rearrange("p (m r) -> p r m", r=4)), reads=[pk], writes=[dk])
                            else:
                                op("act", lambda e: e.copy(dstx[:, c0:c0 + n], pb[:, 0:n]), reads=[pk], writes=[dk])
                allk = [("xr", 0), ("xi", 0), ("xr", 1), ("xi", 1), "Xs"] + tkeys
                kb.fence("dve", reads=allk, writes=allk)
                ox = kb.opx
                l2 = [0, 0]
                l4 = [0, 0]

                def cplx_step(dR, dI, sR, sI, ti, l2, l4):
                    c1 = [0, 0]
                    c3 = [0, 0]
                    for q in range(2):
                        c1[q] = ox("dve", STT(dR[q], sR[q], tabs[q][0][:, ti:ti + 1], dR[q]), after=[l2[q], l4[q]])
                    for q in range(2):
                        c3[q] = ox("dve", STT(dI[q], sI[q], tabs[q][0][:, ti:ti + 1], dI[q]), after=[l2[q], l4[q]])
                    n2 = [0, 0]
                    n4 = [0, 0]
                    for q in range(2):
                        n2[q] = ox("dve", STT(dR[q], sI[q], tabs[q][2][:, ti:ti + 1], dR[q]), after=[c1[q]])
                    for q in range(2):
                        n4[q] = ox("dve", STT(dI[q], sR[q], tabs[q][1][:, ti:ti + 1], dI[q]), after=[c3[q]])
                    return n2, n4
                for r in range(1, 4):
                    l2, l4 = cplx_step([x[:, r, :] for x in XAR], [x[:, r, :] for x in XAI], [x[:, r - 1, :] for x in XAR], [x[:, r - 1, :] for x in XAI], 0, l2, l4)
                for i in range(1, 8):
                    l2, l4 = cplx_step([x[:, i, :] for x in X3R], [x[:, i, :] for x in X3I], [x[:, i - 1, :] for x in X3R], [x[:, i - 1, :] for x in X3I], 3, l2, l4)
                XRs = [Xs[:, 2 * q, :] for q in range(2)]
                XIs = [Xs[:, 2 * q + 1, :] for q in range(2)]
                e2 = [0, 0]
                e4 = [0, 0]
                for q in range(2):
                    e2[q] = ox("dve", lambda e: e.tensor_copy(out=XRs[q], in_=X3R[q][:, 7, :]), after=[l2[q], l4[q]])
                    e4[q] = ox("dve", lambda e: e.tensor_copy(out=XIs[q], in_=X3I[q][:, 7, :]), after=[l2[q], l4[q]])
                X8R = [x.rearrange("p (m j) -> p m j", j=8) for x in XRs]
                X8I = [x.rearrange("p (m j) -> p m j", j=8) for x in XIs]
                A8q = [(A8[:, 0, pairs[q], :], A8[:, 1, pairs[q], :], A8[:, 2, pairs[q], :]) for q in range(2)]

                def cstep(dstR, dstI, srcR, srcI, coef, dR, dI):
                    c1 = [0, 0]
                    c3 = [0, 0]
                    o2 = [0, 0]
                    o4 = [0, 0]
                    for q in range(2):
                        c1[q] = ox("dve", STT(dstR[q], srcR[q], coef[q][0], dstR[q]), after=[dR[q], dI[q]])
                    for q in range(2):
                        c3[q] = ox("dve", STT(dstI[q], srcI[q], coef[q][0], dstI[q]), after=[dR[q], dI[q]])
                    for q in range(2):
                        o2[q] = ox("dve", STT(dstR[q], srcI[q], coef[q][2], dstR[q]), after=[c1[q]])
                    for q in range(2):
                        o4[q] = ox("dve", STT(dstI[q], srcR[q], coef[q][1], dstI[q]), after=[c3[q]])
                    return o2, o4
                for j in range(1, 8):
                    e2, e4 = cstep([x[:, :, j] for x in X8R], [x[:, :, j] for x in X8I], [x[:, :, j - 1] for x in X8R], [x[:, :, j - 1] for x in X8I],
                                   [(A8q[q][0][:, 0:1], A8q[q][1][:, 0:1], A8q[q][2][:, 0:1]) for q in range(2)], e2, e4)
                for m_ in range(1, 8):
                    e2, e4 = cstep([x[:, m_, 7:8] for x in X8R], [x[:, m_, 7:8] for x in X8I], [x[:, m_ - 1, 7:8] for x in X8R], [x[:, m_ - 1, 7:8] for x in X8I],
                                   [(A8q[q][0][:, 7:8], A8q[q][1][:, 7:8], A8q[q][2][:, 7:8]) for q in range(2)], e2, e4)
                f2, f4 = e2, e4
                for j in range(7):
                    g2, g4 = cstep([x[:, 1:8, j] for x in X8R], [x[:, 1:8, j] for x in X8I], [x[:, 0:7, 7] for x in X8R], [x[:, 0:7, 7] for x in X8I],
                                   [(A8q[q][0][:, j:j + 1], A8q[q][1][:, j:j + 1], A8q[q][2][:, j:j + 1]) for q in range(2)], e2, e4)
                    f2 = [max(f2[q], g2[q]) for q in range(2)]
                    f4 = [max(f4[q], g4[q]) for q in range(2)]
                e2, e4 = f2, f4
                f2, f4 = list(e2), list(e4)
                for i in range(8):
                    g2, g4 = cplx_step([x[:, i, 1:NCH] for x in X3R], [x[:, i, 1:NCH] for x in X3I], [x[:, 0:NCH - 1] for x in XRs], [x[:, 0:NCH - 1] for x in XIs],
                                       4 * i + 3, e2, e4)
                    f2 = [max(f2[q], g2[q]) for q in range(2)]
                    f4 = [max(f4[q], g4[q]) for q in range(2)]
                for r in range(3):
                    cplx_step([x[:, r, 1:NBK] for x in XAR], [x[:, r, 1:NBK] for x in XAI], [x[:, 3, 0:NBK - 1] for x in XAR], [x[:, 3, 0:NBK - 1] for x in XAI],
                              r, f2, f4)
                kb.mark("dve", writes=[("xr", 0), ("xi", 0), ("xr", 1), ("xi", 1), "Xs"], reads=tkeys)
                for q in range(2):
                    pair = pairs[q]
                    j4 = 2 * jp + q
                    PR, PI, NPI = tabs[q]
                    ar, ai, nai = PR[:, 0:1], PI[:, 0:1], NPI[:, 0:1]
                    xr, xi = XR[q], XI[q]
                    xk, ik = ("xr", q), ("xi", q)
                    sc = lambda fn, rd, wrt: op("dve", fn, reads=rd + tkeys, writes=wrt)
                    xs_r, xs_i = xr[:, T:TC], xi[:, T:TC]
                    sc(STT(xs_r, h0[:, pair, :, 0], ar, xs_r), ["h0", xk], [xk])
                    sc(STT(xs_i, h0[:, pair, :, 1], ar, xs_i), ["h0", ik], [ik])
                    sc(STT(xs_r, h0[:, pair, :, 1], nai, xs_r), ["h0", xk], [xk])
                    sc(STT(xs_i, h0[:, pair, :, 0], ai, xs_i), ["h0", ik], [ik])
                    op("act", lambda e: e.copy(sto[:, pair, 0:1], xr[:, T - 1:T]), reads=[xk], writes=["sto"])
                    op("act", lambda e: e.copy(sto[:, pair, 1:2], xi[:, T - 1:T]), reads=[ik], writes=["sto"])
                    op("act", lambda e: e.copy(stos[:, :, pair, 0], xr[:, T:TC]), reads=[xk], writes=["stos"])
                    op("act", lambda e: e.copy(stos[:, :, pair, 1], xi[:, T:TC]), reads=[ik], writes=["stos"])
                    op("act", lambda e: e.copy(xbr[:, 0:T].rearrange("p (m r) -> p m r", r=4), xr[:, 0:T].rearrange("p (r m) -> p m r", m=T // 4)), reads=[xk], writes=["xbr"])
                    op("act", lambda e: e.copy(xbr[:, T:TC], xr[:, T:TC]), reads=[xk], writes=["xbr"])
                    op("pool", lambda e: e.tensor_copy(out=xbi[:, 0:T].rearrange("p (m r) -> p m r", r=4), in_=xi[:, 0:T].rearrange("p (r m) -> p m r", m=T // 4)), reads=[ik], writes=["xbi"])
                    op("pool", lambda e: e.tensor_copy(out=xbi[:, T:TC], in_=xi[:, T:TC]), reads=[ik], writes=["xbi"])
                    for ti, (c0, n) in enumerate(coltiles):
                        yb, ybk = ybanks[ti]
                        op("pe", lambda e: e.matmul(yb[:, 0:n], Wc[:, kc, j4, 0, :], xbr[:, c0:c0 + n], start=(j4 == 0), stop=False),
                           reads=["Wc", "xbr"], writes=[ybk], signal=False)
                        op("pe", lambda e: e.matmul(yb[:, 0:n], Wc[:, kc, j4, 1, :], xbi[:, c0:c0 + n], start=False, stop=(j4 == 3)),
                           reads=["Wc", "xbi"], writes=[ybk], signal=True)
            for ti, (c0, n) in enumerate(coltiles):
                yb, ybk = ybanks[ti]
                y_, w_ = ya[:, 0:n], yb2[:, 0:n]
                if n == TW:
                    dma("sp", y_, uT_s[kc, :, c0:c0 + n], reads=["uT_s"], writes=["ya"])
                else:
                    with nc.allow_non_contiguous_dma(reason="tiny sample columns"):
                        dma("sp", y_, uT_s[kc, :, c0:c0 + n], reads=["uT_s"], writes=["ya"])
                op("dve", lambda e: e.scalar_tensor_tensor(out=y_, in0=y_, scalar=dsk[:, kc:kc + 1], in1=yb[:, 0:n], op0=ALU.mult, op1=ALU.add),
                   reads=["ya", "dsk", ybk], writes=["ya"])
                op("dve", lambda e: e.tensor_tensor(out=w_, in0=y_, in1=y_, op=ALU.mult), reads=["ya"], writes=["yb2"])
                op("dve", lambda e: e.tensor_scalar(out=w_, in0=w_, scalar1=0.044715, scalar2=1.0, op0=ALU.mult, op1=ALU.add), reads=["yb2"], writes=["yb2"])
                op("dve", lambda e: e.tensor_tensor(out=w_, in0=w_, in1=y_, op=ALU.mult), reads=["yb2", "ya"], writes=["yb2"])
                op("act", lambda e: e.activation(out=w_, in_=w_, func=AF.Sigmoid, scale=GELU_C), reads=["yb2"], writes=["yb2"])
                op("dve", lambda e: e.tensor_tensor(out=zst[:, c0:c0 + n], in0=y_, in1=w_, op=ALU.mult), reads=["ya", "yb2"], writes=["zst"])
            dma("sp", zT_s[kc, :, :], zst[:, :], reads=["zst"], writes=["zT_s"])
        with nc.allow_non_contiguous_dma(reason="state 8B runs"):
            dma("sp", ssm_p[li, :].rearrange("(a p r) -> p a r", p=128, r=2), sto[:, :, :], reads=["sto"])
            for s_ in range(NS):
                dma("sp", ssm_s[li, s_, :].rearrange("(a p r) -> p a r", p=128, r=2), stos[:, s_, :, :], reads=["stos"])
        kb.barrier()
        es.close()

    def tile_tail(i_next, t, segs):
        prenorm(i_next, 0, segs)
        if i_next % 2 == 0:
            qkv_phase(i_next // 2, t, segs)
        else:
            win_phase(i_next // 2, t, segs)

    def mixer_out_attn(li, t, segs):
        g0 = t * TW
        minT = aT[:, 0:8, :]
        dma("sp", minT[:, :, 0:TW], mT_s[:, :, g0:g0 + TW].rearrange("h p t -> p h t"), reads=["mT_s"], writes=["aT"])
        if len(segs) > 1:
            with nc.allow_non_contiguous_dma(reason="tiny sample columns"):
                dma("sp", minT[:, :, TW:TWS], mT_s[:, :, T:TC].rearrange("h p t -> p h t"), reads=["mT_s"], writes=["aT"])
        psts = [stats_begin() if si == 0 else psum(6, 7) for si in range(len(segs))]
        for dc in range(KC):
            wv, wk = wr.get(("wo", li, dc))
            w3 = wv[:, 0:8 * 128].rearrange("p (k c) -> p k c", k=8)
            for si, (c0, n, S) in enumerate(segs):
                pb, pk = psum()
                for hh in range(8):
                    op("pe", lambda e: e.matmul(pb[:, 0:n], w3[:, hh, :], minT[:, hh, c0:c0 + n], start=(hh == 0), stop=(hh == 7)),
                       reads=["aT", wk], writes=[pk], signal=(hh == 7))
                op("act", lambda e: e.copy(mT[:, dc, c0:c0 + n], pb[:, 0:n]), reads=[pk], writes=["mT"])
                stats_add(psts[si], pb[:, 0:n], pk, c0, n, dc == 0, dc == KC - 1)
        for si, (c0, n, S) in enumerate(segs):
            stats_end(psts[si], c0, n)

    def ffn(i, t, segs):
        for j in range(NFC):
            wv, wk = wr.get(("up", i, j))
            wg = wv[:, 0:KC * 128].rearrange("p (k c) -> p k c", k=KC)
            wvv = wv[:, KC * 128:2 * KC * 128].rearrange("p (k c) -> p k c", k=KC)
            for (c0, n, S) in segs:
                pg, pgk = psum()
                pv, pvk = psum()
                for kc in range(KC):
                    op("pe", lambda e: e.matmul(pg[:, 0:n], wg[:, kc, :], hT[:, kc, c0:c0 + n], start=(kc == 0), stop=(kc == KC - 1)),
                       reads=["hT", wk], writes=[pgk], signal=(kc == KC - 1))
                for kc in range(KC):
                    op("pe", lambda e: e.matmul(pv[:, 0:n], wvv[:, kc, :], hT[:, kc, c0:c0 + n], start=(kc == 0), stop=(kc == KC - 1)),
                       reads=["hT", wk], writes=[pvk], signal=(kc == KC - 1))
                r = rr["cg"] % 2
                rr["cg"] += 1
                halves = []
                for (pp, ppk, jj, ct_, ck_) in ((pg, pgk, j, cgt[r], "cg%d" % r), (pv, pvk, NFC + j, cvt[r], "cv%d" % r)):
                    if S == 1:
                        halo, hk = uprev[:, jj, :], ("uprev", jj)
                    else:
                        halo, hk = cst[:, jj, :, :].rearrange("p a b -> p (a b)"), "cst"
                    halves.append((pp, ppk, jj, ct_, ck_, halo, hk, cw[:, 0, jj:jj + 1], cw[:, 1, jj:jj + 1], cw[:, 2, jj:jj + 1], cw[:, 3, jj:jj + 1]))
                m2 = min(2 * S, n)
                for (pp, ppk, jj, ct_, ck_, halo, hk, w0, w1, w2, bb) in halves:
                    op("act", lambda e: e.activation(out=ct_[:, 0:n], in_=pp[:, 0:n], func=AF.Identity, bias=bb, scale=w2),
                       reads=[ppk, "cw"], writes=[ck_])
                if n > S:
                    for (pp, ppk, jj, ct_, ck_, halo, hk, w0, w1, w2, bb) in halves:
                        op("dve", lambda e: e.scalar_tensor_tensor(out=ct_[:, S:n], in0=pp[:, 0:n - S], scalar=w1, in1=ct_[:, S:n],
                                                                   op0=ALU.mult, op1=ALU.add), reads=[ppk, "cw", ck_], writes=[ck_])
                if n > 2 * S:
                    for (pp, ppk, jj, ct_, ck_, halo, hk, w0, w1, w2, bb) in halves:
                        op("dve", lambda e: e.scalar_tensor_tensor(out=ct_[:, 2 * S:n], in0=pp[:, 0:n - 2 * S], scalar=w0, in1=ct_[:, 2 * S:n],
                                                                   op0=ALU.mult, op1=ALU.add), reads=[ppk, "cw", ck_], writes=[ck_])
                for (pp, ppk, jj, ct_, ck_, halo, hk, w0, w1, w2, bb) in halves:
                    op("dve", lambda e: e.scalar_tensor_tensor(out=ct_[:, 0:S], in0=halo[:, S:2 * S], scalar=w1, in1=ct_[:, 0:S],
                                                               op0=ALU.mult, op1=ALU.add), reads=[hk, "cw", ck_], writes=[ck_])
                for (pp, ppk, jj, ct_, ck_, halo, hk, w0, w1, w2, bb) in halves:
                    op("dve", lambda e: e.scalar_tensor_tensor(out=ct_[:, 0:m2], in0=halo[:, 0:m2], scalar=w0, in1=ct_[:, 0:m2],
                                                               op0=ALU.mult, op1=ALU.add), reads=[hk, "cw", ck_], writes=[ck_])
                for (pp, ppk, jj, ct_, ck_, halo, hk, w0, w1, w2, bb) in halves:
                    if S == 1:
                        op("dve", lambda e: e.tensor_copy(out=uprev[:, jj, :], in_=pp[:, n - 2:n]), reads=[ppk], writes=[hk])
                    else:
                        op("dve", lambda e: e.tensor_copy(out=csout[:, jj, 0, :], in_=cst[:, jj, 1, :]), reads=["cst"], writes=["csout"])
                        op("dve", lambda e: e.tensor_copy(out=csout[:, jj, 1, :], in_=pp[:, 0:NS]), reads=[ppk], writes=["csout"])
                sg_ = sgt[r]
                op("act", lambda e: e.activation(out=sg_[:, 0:n], in_=cgt[r][:, 0:n], func=AF.Silu), reads=["cg%d" % r], writes=["sg%d" % r])
                op("dve", lambda e: e.tensor_tensor(out=aT[:, j, c0:c0 + n], in0=sg_[:, 0:n], in1=cvt[r][:, 0:n], op=ALU.mult),
                   reads=["sg%d" % r, "cv%d" % r], writes=["aT"])
        psts = [stats_begin() if si == 0 else psum(6, 7) for si in range(len(segs))]
        for dc in range(KC):
            wv, wk = wr.get(("down", i, dc))
            w3 = wv[:, 0:NFC * 128].rearrange("p (k c) -> p k c", k=NFC)
            for si, (c0, n, S) in enumerate(segs):
                pb, pk = psum()
                for j in range(NFC):
                    op("pe", lambda e: e.matmul(pb[:, 0:n], w3[:, j, :], aT[:, j, c0:c0 + n], start=(j == 0), stop=(j == NFC - 1)),
                       reads=["aT", wk], writes=[pk], signal=(j == NFC - 1))
                op("act", lambda e: e.copy(mT[:, dc, c0:c0 + n], pb[:, 0:n]), reads=[pk], writes=["mT"])
                stats_add(psts[si], pb[:, 0:n], pk, c0, n, dc == 0, dc == KC - 1)
        for si, (c0, n, S) in enumerate(segs):
            stats_end(psts[si], c0, n)

    def out_y(t, segs):
        ytok = mT_flat
        for bl in range(4):
            for q4 in range(4):
                pb, pk = psum()
                for k4 in range(4):
                    kc = q4 * 4 + k4
                    op("pe", lambda e: e.transpose(pb[:, k4 * 128:(k4 + 1) * 128], xT[:, kc, bl * 128:(bl + 1) * 128], ident_f[:, :]),
                       reads=["xT", "ident_f"], writes=[pk], signal=(k4 == 3))
                op("act", lambda e: e.copy(ytok[:, (bl % 2) * D + q4 * 512:(bl % 2) * D + (q4 + 1) * 512], pb[:, :]), reads=[pk], writes=["mT"])
            dma("sp", y_p[t * TW + bl * 128:t * TW + (bl + 1) * 128, :], ytok[:, (bl % 2) * D:(bl % 2 + 1) * D], reads=["mT"])
        if len(segs) > 1:
            pb, pk = psum()
            for kc in range(KC):
                op("pe", lambda e: e.transpose(pb[0:NS, (kc % 4) * 128:(kc % 4 + 1) * 128], xT[:, kc, TW:TWS], ident_f[:, :]),
                   reads=["xT", "ident_f"], writes=[pk], signal=True)
                if kc % 4 == 3:
                    q4 = kc // 4
                    op("act", lambda e: e.copy(ytok[0:NS, q4 * 512:(q4 + 1) * 512], pb[0:NS, :]), reads=[pk], writes=["mT"])
            dma("sp", y_s[:, :], ytok[0:NS, 0:D], reads=["mT"])

    xtok = mT_flat
    for t in range(NT):
        segs = segs_of(t)
        for bl in range(4):
            xo = (bl % 2) * D
            dma("sp", xtok[:, xo:xo + D], x_p[t * TW + bl * 128:t * TW + (bl + 1) * 128, :], writes=["mT"])
            for q4 in range(4):
                pb, pk = psum()
                for k4 in range(4):
                    kc = q4 * 4 + k4
                    op("pe", lambda e: e.transpose(pb[:, k4 * 128:(k4 + 1) * 128], xtok[:, xo + kc * 128:xo + (kc + 1) * 128], ident_f[:, :]),
                       reads=["mT", "ident_f"], writes=[pk], signal=(k4 == 3))
                op("act", lambda e: e.copy(xT[:, q4 * 4:(q4 + 1) * 4, bl * 128:(bl + 1) * 128], pb[:, :].rearrange("p (k t) -> p k t", k=4)),
                   reads=[pk], writes=["xT"])
        if len(segs) > 1:
            dma("sp", xtok[0:NS, 0:D], x_s[:, :], writes=["mT"])
            pb, pk = psum()
            for kc in range(KC):
                op("pe", lambda e: e.transpose(pb[:, kc * NS:(kc + 1) * NS], xtok[0:NS, kc * 128:(kc + 1) * 128], ident_f[0:NS, 0:NS]),
                   reads=["mT", "ident_f"], writes=[pk], signal=(kc == KC - 1))
            op("act", lambda e: e.copy(xT[:, :, TW:TWS], pb[:, 0:KC * NS].rearrange("p (k s) -> p k s", k=KC)), reads=[pk], writes=["xT"])
        xT_store(t, segs)
        tile_tail(0, t, segs)

    for i in range(nlayers):
        li = i // 2
        if i % 2 == 0:
            attn_phase(li)
        else:
            ssm_phase(li)
        for k3 in range(3):
            fm_load(cw[:, k3, :], "cw", conv_w[i, k3, :])
        fm_load(cw[:, 3, :], "cw", conv_b[i, :])
        for s_ in range(NS):
            for r_ in range(2):
                fm_load(cst[:, :, r_, s_], "cst", st_conv[i, s_, r_, :])
        op("dve", lambda e: e.memset(uprev[:], 0.0), writes=[("uprev", jj_) for jj_ in range(NUC)])
        for t in range(NT):
            segs = segs_of(t)
            xT_load(t, segs)
            if i % 2 == 0:
                mixer_out_attn(li, t, segs)
            else:
                mixer_out_ssm(li, t, segs)
            if dbg and i == 0:
                dma("sp", dbg_m[:, :, t * TW:(t + 1) * TW].rearrange("k p t -> p k t"), mT[:, :, 0:TW], reads=["mT"])
                if len(segs) > 1:
                    dma("sp", dbg_m[:, :, T:TC].rearrange("k p t -> p k t"), mT[:, :, TW:TWS], reads=["mT"])
            postnorm_residual(i, 1, segs)
            if dbg and i == 0:
                dma("sp", dbg_x1[:, :, t * TW:(t + 1) * TW].rearrange("k p t -> p k t"), xT[:, :, 0:TW], reads=["xT"])
                if len(segs) > 1:
                    dma("sp", dbg_x1[:, :, T:TC].rearrange("k p t -> p k t"), xT[:, :, TW:TWS], reads=["xT"])
            prenorm(i, 2, segs)
            ffn(i, t, segs)
            postnorm_residual(i, 3, segs)
            if i + 1 < nlayers:
                xT_store(t, segs)
                tile_tail(i + 1, t, segs)
            else:
                out_y(t, segs)
        for r_ in range(2):
            fm_store(uprev[:, :, r_], [("uprev", jj_) for jj_ in range(NUC)], conv_p[i, r_, :])
            for s_ in range(NS):
                fm_store(csout[:, :, r_, s_], "csout", conv_s[i, s_, r_, :])

    def _unused():
        pass

    assert wr.consumed == len(wr.plan), (wr.consumed, len(wr.plan))
    kb.barrier()
    kb.es.close()
    return nc


def _consts():
    ident = np.eye(128, dtype=np.float32)
    k = np.arange(128)[:, None]
    q = np.arange(128)[None, :]
    m = np.zeros((128, 7, 128), np.float32)
    m[:, 0] = (q >= k)
    m[:, 1] = (q <= k)
    r4 = ((q - k) % 4 == 0)
    m[:, 2] = r4 & (q >= k)
    m[:, 3] = r4
    m[:, 4] = r4 & (q <= k)
    r16 = ((q - k) % 16 == 0)
    m[:, 5] = r16 & (q >= k)
    m[:, 6] = r16
    half = 16
    inv = (np.float32(500000.0) ** (-(np.arange(half, dtype=np.float32) * np.float32(2.0 / 32)))).astype(np.float32)
    pos = np.zeros((128, 17), np.float32)
    for b in range(16):
        pos[:, b] = b * 128 + np.arange(128)
    pos[:, 16] = PAST
    ang = (pos[:, :, None] * inv[None, None, :]).astype(np.float32)
    rope = np.concatenate([np.cos(ang), np.sin(ang)], axis=-1).astype(np.float32)
    mbq = np.where(m.transpose(2, 1, 0) > 0.5, 0.0, -30000.0).astype(np.float32)
    gmask = (np.arange(128)[:, None] // 16 == np.arange(8)[None, :]).astype(np.float32)
    return ident, m, rope, np.ascontiguousarray(mbq), gmask


_NC_CACHE = {}


def kernel(**inp):
    f = lambda a: np.ascontiguousarray(np.asarray(a, dtype=np.float32))
    ident, masks, rope, mbq, gmask = _consts()
    if "nc" not in _NC_CACHE:
        _NC_CACHE["nc"] = build(4)
    nc = _NC_CACHE["nc"]
    shared = {
        "norm_g": f(inp["norm_g"]).reshape(16, D), "w_qkv": f(inp["w_qkv"]), "w_o": f(inp["w_attn_o"]),
        "w_in": f(inp["w_ssm_in"]), "lam_re": f(inp["lambda_re"]), "lam_im": f(inp["lambda_im"]),
        "log_dt": f(inp["log_dt"]), "b_re": f(inp["b_re"]), "b_im": f(inp["b_im"]), "c_re": f(inp["c_re"]),
        "c_im": f(inp["c_im"]), "d_skip": f(inp["d_skip"]), "w_glu": f(inp["w_glu"]), "w_up": f(inp["w_up"]),
        "conv_w": f(inp["conv_w"]), "conv_b": f(inp["conv_b"]), "w_down": f(inp["w_down"]),
        "c_ident": ident, "c_masks": masks, "c_rope": rope, "c_mbq": mbq, "c_gmask": gmask,
    }
    xp = f(inp["x_prompt"])
    xs = f(inp["x_sample"]).reshape(8, D)
    ck = [f(inp["cache_kv_g0"]), f(inp["cache_kv_g1"]), f(inp["cache_kv_g2"])]
    ss = f(inp["state_ssm"])
    sc = f(inp["state_conv"])
    in_maps = []
    for c in range(NCORES):
        m = dict(shared)
        m["x_p"] = xp[c]
        sl = slice(2 * c, 2 * c + 2)
        m["x_s"] = np.ascontiguousarray(xs[sl])
        for g in range(3):
            m["ckv%d" % g] = np.ascontiguousarray(ck[g][:, sl].reshape(2, NS, WIN[g], 2048))
        m["st_ssm"] = np.ascontiguousarray(ss[:, sl].reshape(2, NS, 128 * 64 * 2))
        m["st_conv"] = np.ascontiguousarray(sc[:, sl])
        in_maps.append(m)
    res = run_bass_kernel_spmd(nc, in_maps, core_ids=list(range(NCORES)))
    R = res.results
    y_prompt = np.stack([R[c]["y_p"] for c in range(NCORES)], 0)
    y_sample = np.concatenate([R[c]["y_s"] for c in range(NCORES)], 0).reshape(8, 1, D)
    outs = [y_prompt, y_sample]
    for g in range(3):
        keep = min(WIN[g], T)
        outs.append(np.stack([R[c]["kvp%d" % g] for c in range(NCORES)], 1).reshape(2, 4, keep, 2, 8, 128))
        outs.append(np.concatenate([R[c]["kvs%d" % g] for c in range(NCORES)], 1).reshape(2, 8, 1, 2, 8, 128))
    outs.append(np.stack([R[c]["ssm_p"] for c in range(NCORES)], 1).reshape(2, 4, 128, 64, 2))
    outs.append(np.concatenate([R[c]["ssm_s"] for c in range(NCORES)], 1).reshape(2, 8, 128, 64, 2))
    outs.append(np.stack([R[c]["conv_p"] for c in range(NCORES)], 1).reshape(4, 4, 2, 2 * DFF))
    outs.append(np.concatenate([R[c]["conv_s"] for c in range(NCORES)], 1).reshape(4, 8, 2, 2 * DFF))
    return tuple(np.ascontiguousarray(o.astype(np.float32)) for o in outs)
```

```python
import contextlib
import math
import numpy as np
import concourse.bass as bass
import concourse.mybir as mybir
from concourse.bass_utils import run_bass_kernel_spmd

F32 = mybir.dt.float32
BF16 = mybir.dt.bfloat16
AF = mybir.ActivationFunctionType
ALU = mybir.AluOpType
AX = mybir.AxisListType

T = 2048
D = 2048
KC = 16
NS = 2
TC = T + NS
TW = 512
NT = T // TW
TWS = TW + NS
DFF = 5504
NFC = 43
NUC = 86
NCORES = 4
PAST = 16384
WIN = (128, 512, 2048)
DIL = (1, 4, 16)
EPS = 1e-6
SLOT = 5504
NSLOT = 4
SCALE = 128 ** -0.5
GELU_C = 2.0 * math.sqrt(2.0 / math.pi)


class KB:
    NDS = 12

    def __init__(self):
        self.nc = bass.Bass("TRN2", target_bir_lowering=False)
        nc = self.nc
        self.es = contextlib.ExitStack()
        self.engs = {"pe": nc.tensor, "dve": nc.vector, "act": nc.scalar, "pool": nc.gpsimd, "sp": nc.sync}
        self.semh = {}
        for k in self.engs:
            self.semh[("e", k)] = self.es.enter_context(nc.semaphore("se_" + k))
        self.cnt = {k: 0 for k in self.engs}
        self.waited = {k: {} for k in self.engs}
        self.pending = {k: ([], []) for k in self.engs}
        self.lastw = {}
        self.readers = {}
        self.dcnt = {}
        self.dnext = {}
        for q in ("sp", "pool", "act"):
            for i in range(self.NDS):
                self.semh[("d", q, i)] = self.es.enter_context(nc.semaphore("sd_%s_%d" % (q, i)))
                self.dcnt[("d", q, i)] = 0
            self.dnext[q] = 0
        self.nins = 0

    def sb(self, name, shape, dt, es=None):
        self._uid = getattr(self, "_uid", 0) + 1
        return (es or self.es).enter_context(self.nc.sbuf_tensor("%s_%d" % (name, self._uid), list(shape), dt))

    def ps(self, name, shape, dt, es=None):
        return (es or self.es).enter_context(self.nc.psum_tensor(name, list(shape), dt))

    def dram(self, name, shape, dt, kind):
        return self.nc.dram_tensor(name, list(shape), dt, kind=kind).ap()

    def _deps(self, reads, writes):
        deps = {}
        for b in reads:
            lw = self.lastw.get(b)
            if lw is not None:
                deps[lw[0]] = max(deps.get(lw[0], 0), lw[1])
        for b in writes:
            lw = self.lastw.get(b)
            if lw is not None:
                deps[lw[0]] = max(deps.get(lw[0], 0), lw[1])
            for sk, v in self.readers.get(b, {}).items():
                deps[sk] = max(deps.get(sk, 0), v)
        return deps

    def _wait(self, eng, deps):
        e = self.engs[eng]
        w = self.waited[eng]
        for sk, v in deps.items():
            if eng == "pe" and sk == ("e", "pe"):
                continue
            if w.get(sk, 0) < v:
                e.wait_ge(self.semh[sk], v)
                w[sk] = v
                self.nins += 1

    def fence(self, eng, reads=(), writes=()):
        self._wait(eng, self._deps(reads, writes))

    def mark(self, eng, writes=(), reads=()):
        sk = ("e", eng)
        v = self.cnt[eng]
        for b in writes:
            self.lastw[b] = (sk, v)
            self.readers[b] = {}
        for b in reads:
            self.readers.setdefault(b, {})[sk] = v

    def poison(self, keys):
        allr = {("e", k): v for k, v in self.cnt.items() if v > 0}
        for sk, v in self.dcnt.items():
            if v > 0:
                allr[sk] = v
        for b in keys:
            self.readers[b] = dict(allr)

    def opx(self, eng, fn, after=()):
        sk = ("e", eng)
        v = max(after) if after else 0
        if v > self.waited[eng].get(sk, 0):
            self.engs[eng].wait_ge(self.semh[sk], v)
            self.waited[eng][sk] = v
            self.nins += 1
        ins = fn(self.engs[eng])
        self.nins += 1
        self.cnt[eng] += 1
        ins.then_inc(self.semh[sk], 1)
        return self.cnt[eng]

    def op(self, eng, fn, reads=(), writes=(), signal=True):
        self._wait(eng, self._deps(reads, writes))
        ins = fn(self.engs[eng])
        self.nins += 1
        pr, pw = self.pending[eng]
        pr.extend(reads)
        pw.extend(writes)
        if signal:
            self.cnt[eng] += 1
            sk = ("e", eng)
            ins.then_inc(self.semh[sk], 1)
            v = self.cnt[eng]
            for b in pw:
                self.lastw[b] = (sk, v)
                self.readers[b] = {}
            for b in pr:
                if b not in pw:
                    self.readers.setdefault(b, {})[sk] = v
            self.pending[eng] = ([], [])
        return ins

    def dma(self, q, out, in_, reads=(), writes=(), **kw):
        self._wait(q, self._deps(reads, writes))
        i = self.dnext[q]
        self.dnext[q] = (i + 1) % self.NDS
        sk = ("d", q, i)
        prev = self.dcnt[sk]
        if prev > 0 and self.waited[q].get(sk, 0) < prev:
            self.engs[q].wait_ge(self.semh[sk], prev)
            self.waited[q][sk] = prev
        ins = self.engs[q].dma_start(out=out, in_=in_, **kw)
        self.nins += 1
        self.dcnt[sk] += 16
        v = self.dcnt[sk]
        ins.then_inc(self.semh[sk], 16)
        for b in writes:
            self.lastw[b] = (sk, v)
            self.readers[b] = {}
        for b in reads:
            if b not in writes:
                self.readers.setdefault(b, {})[sk] = v
        return ins

    def barrier(self, engines=None):
        cur = {}
        for k in self.engs:
            assert not self.pending[k][0] and not self.pending[k][1]
            cur[("e", k)] = self.cnt[k]
        for sk, v in self.dcnt.items():
            cur[sk] = v
        for eng in (engines or list(self.engs)):
            for sk, v in cur.items():
                if sk == ("e", eng):
                    continue
                if v > 0 and self.waited[eng].get(sk, 0) < v:
                    self.engs[eng].wait_ge(self.semh[sk], v)
                    self.waited[eng][sk] = v
                    self.nins += 1
        if engines is None:
            self.lastw = {}
            self.readers = {}


class WRing:
    def __init__(self, kb, tensor):
        self.kb = kb
        self.t = tensor
        self.plan = []
        self.emitted = 0
        self.consumed = 0

    def add(self, tag, pieces):
        self.plan.append((tag, pieces))

    def _emit(self, k):
        tag, pieces = self.plan[k]
        s = k % NSLOT
        for (off, shp, src) in pieces:
            n = int(np.prod(shp))
            dst = self.t[:, s, off:off + n]
            if len(shp) == 2:
                dst = dst.rearrange("p (a b) -> p a b", a=shp[0])
            self.kb.dma("pool", dst, src, writes=[("w", s)])

    def get(self, tag):
        k = self.consumed
        assert self.plan[k][0] == tag, (self.plan[k][0], tag)
        while self.emitted < min(len(self.plan), k + NSLOT):
            self._emit(self.emitted)
            self.emitted += 1
        self.consumed += 1
        s = k % NSLOT
        return self.t[:, s, :], ("w", s)


def build(nlayers=4, dbg=False):
    kb = KB()
    nc = kb.nc
    op, dma = kb.op, kb.dma

    def din(name, shape, dt=F32):
        return kb.dram(name, shape, dt, "ExternalInput")

    def dout(name, shape, dt=F32):
        return kb.dram(name, shape, dt, "ExternalOutput")

    x_p = din("x_p", [T, D])
    x_s = din("x_s", [NS, D])
    ckv = [din("ckv%d" % g, [2, NS, WIN[g], 2 * 1024]) for g in range(3)]
    st_ssm = din("st_ssm", [2, NS, 128 * 64 * 2])
    st_conv = din("st_conv", [4, NS, 2, 2 * DFF])
    norm_g = din("norm_g", [16, D])
    w_qkv = din("w_qkv", [2, D, 9216])
    w_o = din("w_o", [2, 1024, D])
    w_in = din("w_in", [2, D, D])
    lam_re = din("lam_re", [2, 128, 64])
    lam_im = din("lam_im", [2, 128, 64])
    log_dt = din("log_dt", [2, 128])
    b_re = din("b_re", [2, 128, 64, 16])
    b_im = din("b_im", [2, 128, 64, 16])
    c_re = din("c_re", [2, 128, 16, 64])
    c_im = din("c_im", [2, 128, 16, 64])
    d_skip = din("d_skip", [2, D])
    w_glu = din("w_glu", [2, D, 2 * D])
    w_up = din("w_up", [4, D, 2 * DFF])
    conv_w = din("conv_w", [4, 3, 2 * DFF])
    conv_b = din("conv_b", [4, 2 * DFF])
    w_down = din("w_down", [4, DFF, D])
    c_ident = din("c_ident", [128, 128])
    c_masks = din("c_masks", [128, 7, 128])
    c_rope = din("c_rope", [128, 17, 32])
    c_mbq = din("c_mbq", [128, 7, 128])
    c_gmask = din("c_gmask", [128, 8])

    y_p = dout("y_p", [T, D])
    y_s = dout("y_s", [NS, D])
    kvp = [dout("kvp%d" % g, [2, min(WIN[g], T), 2048]) for g in range(3)]
    kvs = [dout("kvs%d" % g, [2, NS, 2048]) for g in range(3)]
    ssm_p = dout("ssm_p", [2, 128 * 64 * 2])
    ssm_s = dout("ssm_s", [2, NS, 128 * 64 * 2])
    conv_p = dout("conv_p", [4, 2, 2 * DFF])
    conv_s = dout("conv_s", [4, NS, 2, 2 * DFF])

    IK = "ExternalOutput" if dbg else "Internal"
    xT_s = kb.dram("xT_s", [KC, 128, TC], F32, "Internal")
    qT_s = kb.dram("qT_s", [3, 8, 128, T], BF16, IK)
    kT_s = kb.dram("kT_s", [3, 8, 128, T], BF16, IK)
    v_s = kb.dram("v_s", [3, T, 1024], BF16, IK)
    mT_s = kb.dram("mT_s", [8, 128, TC], BF16, IK)
    dbg_x1 = kb.dram("dbg_x1", [KC, 128, TC], F32, "ExternalOutput") if dbg else None
    dbg_m = kb.dram("dbg_m", [KC, 128, TC], F32, "ExternalOutput") if dbg else None
    uT_s = kb.dram("uT_s", [KC, 128, TC], F32, "Internal")
    zT_s = kb.dram("zT_s", [KC, 128, TC], BF16, "Internal")

    ident_f = kb.sb("ident_f", [128, 128], F32)
    ident_b = kb.sb("ident_b", [128, 128], BF16)
    ones_b = kb.sb("ones_b", [128, 128], BF16)
    rope = kb.sb("rope", [128, 17, 32], F32)
    gsc = kb.sb("gsc", [128, KC, 16], F32)
    wring_t = kb.sb("wring", [128, NSLOT, SLOT], BF16)
    wr = WRing(kb, wring_t)
    sqT = kb.sb("sqT", [128, 24, NS], BF16)
    skT = kb.sb("skT", [128, 24, NS], BF16)
    svT = kb.sb("svT", [128, 24, NS], F32)
    epsb = kb.sb("epsb", [128, 1], F32)

    pbank = [kb.ps("pb%d" % i, [128, 512], F32) for i in range(8)]
    prr = [0]

    def psum(lo=0, hi=6):
        i = lo + prr[0] % (hi - lo)
        prr[0] += 1
        return pbank[i], ("ps", i)

    dma("sp", ident_f[:], c_ident[:, :], writes=["ident_f"])
    dma("pool", ident_b[:], c_ident[:, :], writes=["ident_b"])
    dma("sp", rope[:], c_rope[:, :, :], writes=["rope"])
    op("dve", lambda e: e.memset(ones_b[:], 1.0), writes=["ones_b"])
    op("dve", lambda e: e.memset(epsb[:], EPS), writes=["epsb"])

    es0 = contextlib.ExitStack()
    gtmp = kb.sb("gtmp", [16, D], F32, es0)
    dma("sp", gtmp[:], norm_g[:, :], writes=["gtmp"])
    for kc in range(KC):
        pb, pk = psum()
        op("pe", lambda e: e.transpose(pb[:, 0:16], gtmp[0:16, kc * 128:(kc + 1) * 128], ident_f[0:16, 0:16]),
           reads=["gtmp", "ident_f"], writes=[pk])
        op("act", lambda e: e.copy(gsc[:, kc, :], pb[:, 0:16]), reads=[pk], writes=["gsc"])
    kb.barrier()
    es0.close()

    def gs(i, j, kc):
        return gsc[:, kc, 4 * i + j:4 * i + j + 1]

    def plan_qkv(li):
        for ct in range(36):
            src = w_qkv[li, :, ct * 256:(ct + 1) * 256].rearrange("(k p) c -> p k c", p=128)
            wr.add(("qkv", li, ct), [(0, [KC, 256], src)])

    def plan_win(li):
        for dc in range(KC):
            src = w_in[li, :, dc * 128:(dc + 1) * 128].rearrange("(k p) c -> p k c", p=128)
            wr.add(("win", li, dc), [(0, [KC, 128], src)])

    def plan_tile(i):
        li = i // 2
        if i % 2 == 0:
            for dc in range(KC):
                src = w_o[li, :, dc * 128:(dc + 1) * 128].rearrange("(k p) c -> p k c", p=128)
                wr.add(("wo", li, dc), [(0, [8, 128], src)])
        else:
            for dc in range(KC):
                sv = w_glu[li, :, dc * 128:(dc + 1) * 128].rearrange("(k p) c -> p k c", p=128)
                sg = w_glu[li, :, D + dc * 128:D + (dc + 1) * 128].rearrange("(k p) c -> p k c", p=128)
                wr.add(("glu", li, dc), [(0, [KC, 128], sv), (KC * 128, [KC, 128], sg)])
        for j in range(NFC):
            sg = w_up[i, :, j * 128:(j + 1) * 128].rearrange("(k p) c -> p k c", p=128)
            sv = w_up[i, :, DFF + j * 128:DFF + (j + 1) * 128].rearrange("(k p) c -> p k c", p=128)
            wr.add(("up", i, j), [(0, [KC, 128], sg), (KC * 128, [KC, 128], sv)])
        for dc in range(KC):
            src = w_down[i, :, dc * 128:(dc + 1) * 128].rearrange("(j p) c -> p j c", p=128)
            wr.add(("down", i, dc), [(0, [NFC, 128], src)])
        if i + 1 < nlayers:
            if (i + 1) % 2 == 0:
                plan_qkv((i + 1) // 2)
            else:
                plan_win((i + 1) // 2)

    for t in range(NT):
        plan_qkv(0)
    for i in range(nlayers):
        for t in range(NT):
            plan_tile(i)

    xT = kb.sb("xT", [128, KC, TWS], F32)
    mT = kb.sb("mT", [128, KC, TWS], F32)
    hT = kb.sb("hT", [128, KC, TWS], BF16)
    aT = kb.sb("aT", [128, NFC, TWS], BF16)
    sq = [kb.sb("sq%d" % i, [128, TW], BF16) for i in range(2)]
    rstd = kb.sb("rstd", [128, TWS], F32)
    rtmp = kb.sb("rtmp", [128, TWS], F32)
    cgt = [kb.sb("cg%d" % i, [128, TW], F32) for i in range(2)]
    cvt = [kb.sb("cv%d" % i, [128, TW], F32) for i in range(2)]
    sgt = [kb.sb("sg%d" % i, [128, TW], F32) for i in range(2)]
    cw = kb.sb("cw", [128, 4, NUC], F32)
    uprev = kb.sb("uprev", [128, NUC, 2], F32)
    cst = kb.sb("cst", [128, NUC, 2, NS], F32)
    csout = kb.sb("csout", [128, NUC, 2, NS], F32)
    vtmp = kb.sb("vtmp", [NUC, 128], F32)
    vtmp2 = kb.sb("vtmp2", [NUC, 128], F32)
    rr = {"sq": 0, "cg": 0, "stg": 0}

    mT_flat = mT[:].rearrange("p a b -> p (a b)")
    aT_flat = aT[:].rearrange("p a b -> p (a b)")

    def segs_of(t):
        s = [(0, TW, 1)]
        if t == NT - 1:
            s.append((TW, NS, NS))
        return s

    def fm_load(dst_ap, dst_key, dram_row):
        dma("sp", vtmp[:], dram_row.rearrange("(c p) -> c p", p=128), writes=["vtmp"])
        pb, pk = psum()
        op("pe", lambda e: e.transpose(pb[:, 0:NUC], vtmp[:, :], ident_f[0:NUC, 0:NUC]),
           reads=["vtmp", "ident_f"], writes=[pk])
        op("act", lambda e: e.copy(dst_ap, pb[:, 0:NUC]), reads=[pk], writes=[dst_key])

    def fm_store(src_ap, src_key, dram_row):
        op("act", lambda e: e.copy(rtmp[:, 0:NUC], src_ap), reads=(src_key if isinstance(src_key, list) else [src_key]), writes=["rtmp"])
        pb, pk = psum()
        op("pe", lambda e: e.transpose(pb[0:NUC, 0:128], rtmp[:, 0:NUC], ident_f[:, :]),
           reads=["rtmp", "ident_f"], writes=[pk])
        op("act", lambda e: e.copy(vtmp2[:, :], pb[0:NUC, 0:128]), reads=[pk], writes=["vtmp2"])
        dma("sp", dram_row.rearrange("(c p) -> c p", p=128), vtmp2[:], reads=["vtmp2"])

    def stats_begin():
        return psum(7, 8)

    def stats_add(pst, src_ap, src_key, c0, n, first, last):
        pb, pk = pst
        s = sq[rr["sq"] % 2]
        sk = "sq%d" % (rr["sq"] % 2)
        rr["sq"] += 1
        op("act", lambda e: e.activation(out=s[:, 0:n], in_=src_ap, func=AF.Square), reads=[src_key], writes=[sk])
        op("pe", lambda e: e.matmul(pb[:, 0:n], ones_b[:, :], s[:, 0:n], start=first, stop=last),
           reads=[sk, "ones_b"], writes=[pk], signal=True)

    def stats_end(pst, c0, n):
        pb, pk = pst
        op("act", lambda e: e.activation(out=rtmp[:, c0:c0 + n], in_=pb[:, 0:n], func=AF.Sqrt, bias=epsb[:, 0:1], scale=1.0 / D),
           reads=[pk, "epsb"], writes=["rtmp"])
        op("dve", lambda e: e.reciprocal(rstd[:, c0:c0 + n], rtmp[:, c0:c0 + n]), reads=["rtmp"], writes=["rstd"])

    def prenorm(i, j, segs):
        for (c0, n, S) in segs:
            pst = stats_begin()
            for kc in range(KC):
                stats_add(pst, xT[:, kc, c0:c0 + n], "xT", c0, n, kc == 0, kc == KC - 1)
            stats_end(pst, c0, n)
            for kc in range(KC):
                op("dve", lambda e: e.scalar_tensor_tensor(out=hT[:, kc, c0:c0 + n], in0=xT[:, kc, c0:c0 + n],
                                                           scalar=gs(i, j, kc), in1=rstd[:, c0:c0 + n],
                                                           op0=ALU.mult, op1=ALU.mult),
                   reads=["xT", "rstd", "gsc"], writes=["hT"])

    def postnorm_residual(i, j, segs):
        for (c0, n, S) in segs:
            for kc in range(KC):
                op("dve", lambda e: e.scalar_tensor_tensor(out=mT[:, kc, c0:c0 + n], in0=mT[:, kc, c0:c0 + n],
                                                           scalar=gs(i, j, kc), in1=rstd[:, c0:c0 + n],
                                                           op0=ALU.mult, op1=ALU.mult),
                   reads=["mT", "rstd", "gsc"], writes=["mT"])
                op("dve", lambda e: e.tensor_tensor(out=xT[:, kc, c0:c0 + n], in0=xT[:, kc, c0:c0 + n],
                                                    in1=mT[:, kc, c0:c0 + n], op=ALU.add),
                   reads=["mT", "xT"], writes=["xT"])

    def xT_store(t, segs):
        g0 = t * TW
        dma("sp", xT_s[:, :, g0:g0 + TW].rearrange("k p t -> p k t"), xT[:, :, 0:TW], reads=["xT"], writes=["xT_s%d" % t])
        if len(segs) > 1:
            dma("sp", xT_s[:, :, T:TC].rearrange("k p t -> p k t"), xT[:, :, TW:TWS], reads=["xT"], writes=["xT_ss"])

    def xT_load(t, segs):
        g0 = t * TW
        dma("sp", xT[:, :, 0:TW], xT_s[:, :, g0:g0 + TW].rearrange("k p t -> p k t"), reads=["xT_s%d" % t], writes=["xT"])
        if len(segs) > 1:
            dma("sp", xT[:, :, TW:TWS], xT_s[:, :, T:TC].rearrange("k p t -> p k t"), reads=["xT_ss"], writes=["xT"])

    stg_f = mT_flat
    stg_b = aT_flat
    rt = kb.sb("ropetmp", [128, 4, 2, 16], F32)

    def qkv_phase(li, t, segs):
        blocks = [(b, 128) for b in range(4)]
        if len(segs) > 1:
            blocks.append((4, NS))
        ffk = ["cg0", "cg1", "cv0", "cv1", "sg0", "sg1"]
        for eng_ in ("act", "dve", "pe", "sp"):
            kb.fence(eng_, writes=ffk)
        backlog = []
        stf_t = [cgt[0], cgt[1], cvt[0], cvt[1]]
        stb_t = [sgt[0][:, :].bitcast(BF16), sgt[1][:, :].bitcast(BF16)]
        for ct in range(36):
            wv, wk = wr.get(("qkv", li, ct))
            w3 = wv[:, 0:KC * 256].rearrange("p (k c) -> p k c", k=KC)
            s = ct // 12
            g = (ct % 12) // 4
            hp = ct % 4
            for (bl, m) in blocks:
                gb = t * 4 + bl if bl < 4 else 16
                while len(backlog) >= (1 if (s == 2 and bl < 4) else 2):
                    backlog.pop(0)()
                pb, pk = psum()
                for kc in range(KC):
                    lhs = hT[:, kc, bl * 128:bl * 128 + m] if bl < 4 else hT[:, kc, TW:TWS]
                    op("pe", lambda e: e.matmul(pb[0:m, 0:256], lhs, w3[:, kc, :], start=(kc == 0), stop=(kc == KC - 1)),
                       reads=["hT", wk], writes=[pk], signal=(kc == KC - 1))
                slot = rr["stg"] % 4
                rr["stg"] += 1
                sf = stf_t[slot][:, 0:256]
                sfk = ("sf", slot)
                op("act", lambda e: e.copy(sf[0:m, :], pb[0:m, 0:256]), reads=[pk], writes=[sfk])
                sf3 = sf.rearrange("p (h d) -> p h d", h=2)
                if s < 2:
                    cosb = rope[0:m, gb, 0:16].unsqueeze(1).broadcast_to([m, 2, 16])
                    sinb = rope[0:m, gb, 16:32].unsqueeze(1).broadcast_to([m, 2, 16])
                    x1 = sf3[0:m, :, 0:16]
                    x2 = sf3[0:m, :, 16:32]
                    t1, t2, t3, t4 = (rt[0:m, k, :, :] for k in range(4))
                    op("dve", lambda e: e.tensor_tensor(out=t1, in0=x1, in1=cosb, op=ALU.mult), reads=[sfk, "rope"], writes=["rt1"])
                    op("dve", lambda e: e.tensor_tensor(out=t2, in0=x2, in1=sinb, op=ALU.mult), reads=[sfk, "rope"], writes=["rt2"])
                    op("dve", lambda e: e.tensor_tensor(out=t3, in0=x2, in1=cosb, op=ALU.mult), reads=[sfk, "rope"], writes=["rt3"])
                    op("dve", lambda e: e.tensor_tensor(out=t4, in0=x1, in1=sinb, op=ALU.mult), reads=[sfk, "rope"], writes=["rt4"])
                    op("dve", lambda e: e.tensor_tensor(out=x1, in0=t1, in1=t2, op=ALU.subtract), reads=["rt1", "rt2", "rt3", "rt4"], writes=[sfk])
                    op("dve", lambda e: e.tensor_tensor(out=x2, in0=t3, in1=t4, op=ALU.add), reads=["rt3", "rt4"], writes=[sfk])
                if s >= 1:
                    half = (s - 1) * 1024 + hp * 256
                    if bl < 4:
                        keep = min(WIN[g], T)
                        row0 = gb * 128 - (T - keep)
                        if row0 >= 0:
                            dma("sp", kvp[g][li, row0:row0 + 128, half:half + 256], sf[:, :], reads=[sfk])
                    else:
                        dma("sp", kvs[g][li, :, half:half + 256], sf[0:m, :], reads=[sfk])
                sb_ = stb_t[0][:, slot * 256:(slot + 1) * 256]
                sbk = ("sb", slot)
                tbk = ("tb", slot)
                if not (bl == 4 and s == 2):
                    op("act", lambda e: e.copy(sb_[0:m, :], sf[0:m, :]), reads=[sfk], writes=[sbk])
                if s == 2 and bl < 4:
                    dma("sp", v_s[g, gb * 128:(gb + 1) * 128, hp * 256:(hp + 1) * 256], sb_[:, :], reads=[sbk], writes=["v_s"])
                    continue
                def make_back(s=s, g=g, hp=hp, bl=bl, m=m, gb=gb, slot=slot, sf=sf, sfk=sfk, sb_=sb_, sbk=sbk, tbk=tbk):
                    def back():
                        if s == 2:
                            pt, ptk = psum()
                            for hh in range(2):
                                op("pe", lambda e: e.transpose(pt[:, hh * NS:(hh + 1) * NS], sf[0:m, hh * 128:(hh + 1) * 128], ident_f[0:m, 0:m]),
                                   reads=[sfk, "ident_f"], writes=[ptk], signal=(hh == 1))
                            op("dve", lambda e: e.tensor_copy(out=svT[:, g * 8 + 2 * hp:g * 8 + 2 * hp + 2, :],
                                                              in_=pt[:, 0:2 * NS].rearrange("p (h s) -> p h s", h=2)),
                               reads=[ptk], writes=["svT"])
                            return
                        pt, ptk = psum()
                        ptb = pt[:, 0:256].bitcast(BF16)
                        for hh in range(2):
                            op("pe", lambda e: e.transpose(ptb[:, hh * 128:hh * 128 + m], sb_[0:m, hh * 128:(hh + 1) * 128], ident_b[0:m, 0:m]),
                               reads=[sbk, "ident_b"], writes=[ptk], signal=(hh == 1))
                        if bl < 4:
                            tb = stb_t[1][:, slot * 256:(slot + 1) * 256]
                            op("dve", lambda e: e.tensor_copy(out=tb, in_=ptb[:, 0:256]), reads=[ptk], writes=[tbk])
                            dst = (qT_s if s == 0 else kT_s)[g, 2 * hp:2 * hp + 2, :, gb * 128:(gb + 1) * 128].rearrange("h p t -> p h t")
                            dma("sp", dst, tb.rearrange("p (h t) -> p h t", h=2), reads=[tbk], writes=["qT_s" if s == 0 else "kT_s"])
                        else:
                            dstt = sqT if s == 0 else skT
                            op("dve", lambda e: e.tensor_copy(out=dstt[:, g * 8 + 2 * hp:g * 8 + 2 * hp + 2, :],
                                                              in_=ptb[:, 0:256].rearrange("p (h t) -> p h t", h=2)[:, :, 0:NS]),
                               reads=[ptk], writes=["sqT" if s == 0 else "skT"])
                    return back
                backlog.append(make_back())
        while backlog:
            backlog.pop(0)()

    _qkv_inner = qkv_phase

    def qkv_phase(li, t, segs):
        _qkv_inner(li, t, segs)
        kb.poison(["cg0", "cg1", "cv0", "cv1", "sg0", "sg1"])

    def attn_phase(li):
        kb.barrier()
        es = contextlib.ExitStack()
        aTf = aT[:].rearrange("p a b -> p (a b)")
        hTf = hT[:].rearrange("p a b -> p (a b)")
        xTb = xT[:].rearrange("p a b -> p (a b)").bitcast(BF16)
        mTf = mT[:].rearrange("p a b -> p (a b)")
        QN = 3 * T
        qTt = [aTf[:, i * QN:(i + 1) * QN].rearrange("p (g t) -> p g t", g=3) for i in range(2)]
        kTt = [aTf[:, 2 * QN:3 * QN].rearrange("p (g t) -> p g t", g=3), hTf[:, 0:QN].rearrange("p (g t) -> p g t", g=3)]
        vt = [xTb[:, i * QN:(i + 1) * QN].rearrange("p (g b d) -> p g b d", g=3, b=16) for i in range(2)]
        pT = [hTf[:, QN + i * 512:QN + (i + 1) * 512] for i in range(3)]
        mh = [xTb[:, 2 * QN + i * TC:2 * QN + (i + 1) * TC] for i in range(2)]
        mbq = aTf[:, 3 * QN:3 * QN + 1792].bitcast(F32).rearrange("p (m k) -> p m k", m=7)
        smk = mTf[:, 0:2048]
        junk = mTf[:, 2048:3072].bitcast(BF16)
        acc2 = mTf[:, 4096:4224]
        D3 = mTf[:, 3072:3456]
        cbc = mTf[:, 3456:3840]
        acc = mTf[:, 3840:4096]
        ones_f = kb.sb("ones_f", [128, 128], F32, es)
        masks = kb.sb("masks", [128, 7, 128], BF16, es)
        dma("pool", masks[:], c_masks[:, :, :], writes=["masks"])
        l0 = kb.sb("l0", [128, 16, 3], F32, es)
        colm = kb.sb("colm", [128, 8, 3], F32, es)
        dma("sp", mbq, c_mbq[:, :, :], writes=["mbq"])
        op("dve", lambda e: e.memset(ones_f[:], 1.0), writes=["ones_f"])

        def load_head(h):
            b = h % 2
            for g in range(3):
                dma("sp", qTt[b][:, g, :], qT_s[g, h, :, :], reads=["qT_s"], writes=[("qTt", b, g)])
                dma("sp", kTt[b][:, g, :], kT_s[g, h, :, :], reads=["kT_s"], writes=[("kTt", b, g)])
                dma("sp", vt[b][:, g, :, :], v_s[g, :, h * 128:(h + 1) * 128].rearrange("(b p) d -> p b d", p=128),
                    reads=["v_s"], writes=[("vt", b, g)])

        smk2 = [mTf[:, 0:2048], mTf[:, 4224:6272]]
        D32 = [mTf[:, 3072:3456], mTf[:, 6272:6656]]
        cbc2 = [mTf[:, 3456:3840], mTf[:, 6656:7040]]
        acc_2 = [mTf[:, 3840:4096], mTf[:, 7040:7296]]
        acc22 = [mTf[:, 4096:4224], mTf[:, 7296:7424]]
        colm2t = kb.sb("colm2", [128, 2, 8, 3], F32, es)
        pst = {"pidx": 0}

        def iteration(h, qb, sl):
            b = h % 2
            smk, D3, cbc, acc, acc2 = smk2[sl], D32[sl], cbc2[sl], acc_2[sl], acc22[sl]
            colm = colm2t[:, sl, :, :]
            K = lambda name: (name, sl)
            groups = []
            for g in range(3):
                nb = WIN[g] // 128
                lst = []
                for kbk in range(max(0, qb - nb), qb + 1):
                    db = qb - kbk
                    if g == 0:
                        mi = 0 if db == 0 else 1
                    elif g == 1:
                        mi = 2 if db == 0 else (4 if db == 4 else 3)
                    else:
                        mi = 5 if db == 0 else 6
                    lst.append((kbk, mi))
                groups.append(lst)
            po, pok = psum(4, 5) if sl == 0 else psum(6, 7)
            pc, pck = psum(5, 6) if sl == 0 else psum(7, 8)
            qsl = qTt[b][:, :, qb * 128:(qb + 1) * 128]
            for g in range(3):
                lst = groups[g]
                nk = len(lst) * 128
                for c4 in range(0, len(lst), 4):
                    ch = lst[c4:c4 + 4]
                    ps_, psk = psum(2, 4)
                    k0 = ch[0][0]
                    op("pe", lambda e: e.matmul(ps_[:, 0:len(ch) * 128], qsl[:, g, :], kTt[b][:, g, k0 * 128:(k0 + len(ch)) * 128],
                                                start=True, stop=True),
                       reads=[("qTt", b, g), ("kTt", b, g)], writes=[psk])
                    for ci, (kbk, mi) in enumerate(ch):
                        op("dve", lambda e: e.tensor_tensor(out=smk[:, (c4 + ci) * 128:(c4 + ci + 1) * 128], in0=ps_[:, ci * 128:(ci + 1) * 128],
                                                            in1=mbq[:, mi, :], op=ALU.add),
                           reads=[psk, "mbq"], writes=[("smk", sl, c4 + ci)])
                    yield
                smks = [("smk", sl, k_) for k_ in range(len(lst))]
                op("dve", lambda e: e.tensor_reduce(out=colm[:, 0, g:g + 1], in_=smk[:, 0:nk], axis=AX.X, op=ALU.max),
                   reads=smks, writes=[("c0", sl, g)])
                op("dve", lambda e: e.tensor_scalar(out=colm[:, 1, g:g + 1], in0=colm[:, 0, g:g + 1], scalar1=-SCALE, scalar2=None, op0=ALU.mult),
                   reads=[("c0", sl, g)], writes=[("c1", sl, g)])
                op("act", lambda e: e.activation(out=junk[:, 0:nk], in_=smk[:, 0:nk], func=AF.Exp, bias=colm[:, 1, g:g + 1], scale=SCALE,
                                                 accum_out=colm[:, 2, g:g + 1]),
                   reads=smks + [("c1", sl, g)], writes=[("c2", sl, g)])
                yield
                done = 0
                for c4 in range(0, len(lst), 4):
                    ch = lst[c4:c4 + 4]
                    pb, pk = psum(0, 2)
                    for ci, (kbk, mi) in enumerate(ch):
                        op("pe", lambda e: e.matmul(pb[:, ci * 128:(ci + 1) * 128], kTt[b][:, g, kbk * 128:(kbk + 1) * 128], qsl[:, g, :],
                                                    start=True, stop=True),
                           reads=[("kTt", b, g), ("qTt", b, g)], writes=[pk], signal=(ci == len(ch) - 1))
                    p_ = pT[pst["pidx"] % 3]
                    pk_ = "pT%d" % (pst["pidx"] % 3)
                    pst["pidx"] += 1
                    nc_ = len(ch) * 128
                    op("act", lambda e: e.activation(out=p_[:, 0:nc_], in_=pb[:, 0:nc_], func=AF.Exp, scale=SCALE), reads=[pk], writes=[pk_])
                    yield
                    for ci, (kbk, mi) in enumerate(ch):
                        op("pool", lambda e: e.tensor_tensor(out=p_[:, ci * 128:(ci + 1) * 128], in0=p_[:, ci * 128:(ci + 1) * 128],
                                                             in1=masks[:, mi, :], op=ALU.mult),
                           reads=[pk_, "masks"], writes=[pk_])
                    for ci, (kbk, mi) in enumerate(ch):
                        op("pe", lambda e: e.matmul(po[:, g * 128:(g + 1) * 128], vt[b][:, g, kbk, :], p_[:, ci * 128:(ci + 1) * 128],
                                                    start=(done == 0), stop=(done == len(lst) - 1)),
                           reads=[("vt", b, g), pk_], writes=[pok], signal=True)
                        done += 1
                    yield
            c1s = [("c1", sl, g_) for g_ in range(3)]
            c2s = [("c2", sl, g_) for g_ in range(3)]
            op("act", lambda e: e.activation(out=colm[:, 3, :], in_=colm[:, 1, :], func=AF.Exp, scale=-1.0), reads=c1s, writes=[K("c3")])
            op("dve", lambda e: e.tensor_tensor(out=colm[:, 4, :], in0=colm[:, 2, :], in1=colm[:, 3, :], op=ALU.mult), reads=c2s + [K("c3")], writes=[K("c4")])
            yield
            op("dve", lambda e: e.tensor_reduce(out=colm[:, 7, 0:1], in_=colm[:, 4, :], axis=AX.X, op=ALU.add), reads=[K("c4")], writes=[K("c7")])
            if h == 0:
                op("dve", lambda e: e.tensor_copy(out=l0[:, qb, :], in_=colm[:, 2, :]), reads=c2s, writes=[("l0", qb)])
            yield
            op("dve", lambda e: e.tensor_scalar(out=colm[:, 5, :], in0=l0[:, qb, :], scalar1=colm[:, 7, 0:1], scalar2=None, op0=ALU.mult),
               reads=[K("c7"), ("l0", qb)], writes=[K("c5")])
            yield
            op("dve", lambda e: e.reciprocal(colm[:, 5, :], colm[:, 5, :]), reads=[K("c5")], writes=[K("c5")])
            yield
            op("dve", lambda e: e.tensor_tensor(out=colm[:, 6, :], in0=colm[:, 2, :], in1=colm[:, 5, :], op=ALU.mult), reads=c2s + [K("c5")], writes=[K("c6")])
            yield
            for g in range(3):
                op("dve", lambda e: e.tensor_scalar(out=D3[:, g * 128:(g + 1) * 128], in0=ident_f[:, :], scalar1=colm[:, 6, g:g + 1], scalar2=None, op0=ALU.mult),
                   reads=[K("c6"), "ident_f"], writes=[("D3", sl, g)])
            op("pe", lambda e: e.matmul(pc[:, 0:384], ones_f[:, :], D3[:, :], start=True, stop=True), reads=["ones_f"] + [("D3", sl, g_) for g_ in range(3)], writes=[pck])
            op("act", lambda e: e.copy(cbc[:, :], pc[:, 0:384]), reads=[pck], writes=[K("cbc")])
            yield
            op("dve", lambda e: e.tensor_tensor(out=acc[:, 0:128], in0=po[:, 0:128], in1=cbc[:, 0:128], op=ALU.mult), reads=[pok, K("cbc")], writes=[K("acc0")])
            op("dve", lambda e: e.tensor_tensor(out=acc[:, 128:256], in0=po[:, 128:256], in1=cbc[:, 128:256], op=ALU.mult), reads=[pok, K("cbc")], writes=[K("acc1")])
            op("dve", lambda e: e.tensor_tensor(out=acc2, in0=po[:, 256:384], in1=cbc[:, 256:384], op=ALU.mult), reads=[pok, K("cbc")], writes=[K("acc2")])
            yield
            op("dve", lambda e: e.tensor_tensor(out=acc[:, 0:128], in0=acc[:, 0:128], in1=acc[:, 128:256], op=ALU.add), reads=[K("acc0"), K("acc1")], writes=[K("acc0")])
            yield
            op("dve", lambda e: e.tensor_tensor(out=mh[b][:, qb * 128:(qb + 1) * 128], in0=acc[:, 0:128], in1=acc2, op=ALU.add),
               reads=[K("acc0"), K("acc2")], writes=[("mh", b, qb)])

        load_head(0)
        for h in range(8):
            if h + 1 < 8:
                load_head(h + 1)
            b = h % 2
            for qb0 in range(0, 16, 2):
                gens = [iteration(h, qb0, 0), iteration(h, qb0 + 1, 1)]
                alive = [True, True]
                while any(alive):
                    for gi in range(2):
                        if alive[gi]:
                            try:
                                next(gens[gi])
                            except StopIteration:
                                alive[gi] = False
            dma("sp", mT_s[h, :, 0:T], mh[b][:, 0:T], reads=[("mh", b, q_) for q_ in range(16)], writes=["mT_s"])

        kb.barrier()
        cache = [mTf[:, i * 2048:(i + 1) * 2048] for i in range(2)]
        cb16 = [mTf[:, 4096 + i * 512:4096 + (i + 1) * 512].bitcast(BF16) for i in range(2)]
        vkeep = [mTf[:, 5120 + g * 512:5120 + (g + 1) * 512].bitcast(BF16) for g in range(3)]
        ckT = kb.sb("ckT", [128, 8, 128], BF16, es)
        qk = kb.sb("qk", [128, 24 * NS], F32, es)
        qkr = kb.sb("qkr", [1, 24 * NS], F32, es)
        srow = aTf[0:1, 0:2064].bitcast(F32).rearrange("p (h k) -> p h k", h=8)
        prow = aTf[0:1, 2064:2064 + 6192].bitcast(F32).rearrange("p (g h k) -> p g h k", g=3, h=8)
        rw = kb.sb("rw", [1, 12, 24], F32, es)
        one1 = kb.sb("one1", [1, 1], F32, es)
        pSk = kb.sb("pSk", [128, 3, 8], BF16, es)
        pnb = kb.sb("pnb", [128, 3, 8], F32, es)
        og = kb.sb("og", [128, 3, 8], F32, es)
        cbs = kb.sb("cbs", [128, 3, 8], F32, es)
        so = kb.sb("so", [128, 8], F32, es)
        sob = kb.sb("sob", [128, 8, NS], BF16, es)
        op("dve", lambda e: e.memset(one1[:], 1.0), writes=["one1"])
        op("dve", lambda e: e.tensor_tensor(out=qk[:, :], in0=sqT[:].rearrange("p a s -> p (a s)"), in1=skT[:].rearrange("p a s -> p (a s)"), op=ALU.mult),
           reads=["sqT", "skT"], writes=["qk"])
        pb, pk = psum(0, 2)
        op("pe", lambda e: e.matmul(pb[0:1, 0:24 * NS], ones_f[:, 0:1], qk[:, :], start=True, stop=True), reads=["ones_f", "qk"], writes=[pk])
        op("act", lambda e: e.copy(qkr[:, :], pb[0:1, 0:24 * NS]), reads=[pk], writes=["qkr"])
        qkr3 = qkr[:].rearrange("p (a s) -> p a s", s=NS)
        rwm, rwmm, rwl, rwe, rwt, rwc = (rw[:, k, :].rearrange("p (g h) -> p g h", g=3) for k in range(6))
        for s_ in range(NS):
            pov, povk = psum(4, 5)
            for g in range(3):
                cbuf = cache[g % 2]
                ck = "cache%d" % (g % 2)
                src = ckv[g][li, s_, :, :].rearrange("(j d) c -> j d c", d=DIL[g])[:, 0, :]
                dma("sp", cbuf[:, :], src, writes=[ck])
                c16 = cb16[g % 2]
                c16k = "cb16_%d" % (g % 2)
                op("pool", lambda e: e.tensor_copy(out=c16[:, 0:1024], in_=cbuf[:, 0:1024]), reads=[ck], writes=[c16k])
                op("act", lambda e: e.copy(vkeep[g][:, :], cbuf[:, 1024:2048]), reads=[ck], writes=["vk%d" % g])
                for h in range(8):
                    pt, ptk = psum(0, 2)
                    ptb = pt[:, 0:64].bitcast(BF16)
                    op("pe", lambda e: e.transpose(ptb[:, 0:128], c16[:, h * 128:(h + 1) * 128], ident_b[:, :]),
                       reads=[c16k, "ident_b"], writes=[ptk])
                    op("dve", lambda e: e.tensor_copy(out=ckT[:, h, :], in_=ptb[:, 0:128]), reads=[ptk], writes=["ckT"])
                for hq in range(2):
                    pr, prk = psum(2, 4)
                    for hh in range(4):
                        h = hq * 4 + hh
                        op("pe", lambda e: e.matmul(pr[0:1, hh * 128:(hh + 1) * 128], sqT[:, g * 8 + h, s_:s_ + 1], ckT[:, h, :], start=True, stop=True),
                           reads=["sqT", "ckT"], writes=[prk], signal=(hh == 3))
                    op("act", lambda e: e.copy(srow[0:1, hq * 4:(hq + 1) * 4, 0:128], pr[0:1, :].rearrange("p (h k) -> p h k", h=4)),
                       reads=[prk], writes=["srow"])
                op("dve", lambda e: e.tensor_copy(out=srow[0:1, :, 128:129], in_=qkr3[0:1, g * 8:(g + 1) * 8, s_:s_ + 1]), reads=["qkr"], writes=["srow"])
                op("dve", lambda e: e.tensor_reduce(out=rwm[0:1, g, :], in_=srow[0:1, :, :], axis=AX.X, op=ALU.max), reads=["srow"], writes=["rw"])
                op("dve", lambda e: e.tensor_tensor(out=srow[0:1, :, :], in0=srow[0:1, :, :],
                                                    in1=rwm[0:1, g, :].unsqueeze(2).broadcast_to([1, 8, 129]), op=ALU.subtract),
                   reads=["srow", "rw"], writes=["srow"])
                op("act", lambda e: e.activation(out=prow[0:1, g, :, :], in_=srow[0:1, :, :], func=AF.Exp, scale=SCALE), reads=["srow"], writes=["prow"])
                op("dve", lambda e: e.tensor_reduce(out=rwl[0:1, g, :], in_=prow[0:1, g, :, :], axis=AX.X, op=ALU.add), reads=["prow"], writes=["rw"])
                pp, ppk = psum(2, 4)
                for h in range(8):
                    op("pe", lambda e: e.matmul(pp[:, h:h + 1], prow[0:1, g, h, 0:128], one1[0:1, 0:1], start=True, stop=True),
                       reads=["prow", "one1"], writes=[ppk], signal=(h == 7))
                op("dve", lambda e: e.tensor_copy(out=pSk[:, g, :], in_=pp[:, 0:8]), reads=[ppk], writes=["pSk"])
                pn_, pnk = psum(2, 4)
                op("pe", lambda e: e.matmul(pn_[:, 0:8], ones_f[0:1, :], prow[0:1, g, :, 128], start=True, stop=True), reads=["ones_f", "prow"], writes=[pnk])
                op("dve", lambda e: e.tensor_copy(out=pnb[:, g, :], in_=pn_[:, 0:8]), reads=[pnk], writes=["pnb"])
                for h in range(8):
                    op("pe", lambda e: e.matmul(pov[:, g * 8 + h:g * 8 + h + 1], vkeep[g][:, h * 128:(h + 1) * 128], pSk[:, g, h:h + 1], start=True, stop=True),
                       reads=["vk%d" % g, "pSk"], writes=[povk], signal=(h == 7))
            op("dve", lambda e: e.tensor_tensor(out=og[:, :, :], in0=svT[:, :, s_].rearrange("p (g h) -> p g h", g=3), in1=pnb[:, :, :], op=ALU.mult),
               reads=["svT", "pnb"], writes=["og"])
            op("dve", lambda e: e.tensor_tensor(out=og[:, :, :], in0=og[:, :, :], in1=pov[:, 0:24].rearrange("p (g h) -> p g h", g=3), op=ALU.add),
               reads=["og", povk], writes=["og"])
            op("dve", lambda e: e.tensor_scalar(out=rwmm[0:1, :, :], in0=rwm[0:1, :, :], scalar1=SCALE, scalar2=None, op0=ALU.mult), reads=["rw"], writes=["rw"])
            op("act", lambda e: e.activation(out=rwe[0:1, :, :], in_=rwmm[0:1, :, :], func=AF.Exp), reads=["rw"], writes=["rw"])
            op("dve", lambda e: e.tensor_tensor(out=rwe[0:1, :, :], in0=rwe[0:1, :, :], in1=rwl[0:1, :, :], op=ALU.mult), reads=["rw"], writes=["rw"])
            op("dve", lambda e: e.tensor_tensor(out=rwt[0:1, 0, :], in0=rwe[0:1, 0, :], in1=rwe[0:1, 1, :], op=ALU.add), reads=["rw"], writes=["rw"])
            op("dve", lambda e: e.tensor_tensor(out=rwt[0:1, 0, :], in0=rwt[0:1, 0, :], in1=rwe[0:1, 2, :], op=ALU.add), reads=["rw"], writes=["rw"])
            op("dve", lambda e: e.tensor_tensor(out=rwc[0:1, :, :], in0=rwt[0:1, 0, :].unsqueeze(1).broadcast_to([1, 3, 8]),
                                                in1=rwl[0:1, :, 0:1].broadcast_to([1, 3, 8]), op=ALU.mult), reads=["rw"], writes=["rw"])
            op("dve", lambda e: e.reciprocal(rwc[0:1, :, :], rwc[0:1, :, :]), reads=["rw"], writes=["rw"])
            op("dve", lambda e: e.tensor_tensor(out=rwc[0:1, :, :], in0=rwc[0:1, :, :], in1=rwe[0:1, :, :], op=ALU.mult), reads=["rw"], writes=["rw"])
            pcb, pcbk = psum(2, 4)
            op("pe", lambda e: e.matmul(pcb[:, 0:24], ones_f[0:1, :], rw[0:1, 5, :], start=True, stop=True), reads=["ones_f", "rw"], writes=[pcbk])
            op("dve", lambda e: e.tensor_tensor(out=og[:, :, :], in0=og[:, :, :], in1=pcb[:, 0:24].rearrange("p (g h) -> p g h", g=3), op=ALU.mult),
               reads=["og", pcbk], writes=["og"])
            op("dve", lambda e: e.tensor_tensor(out=so[:, :], in0=og[:, 0, :], in1=og[:, 1, :], op=ALU.add), reads=["og"], writes=["so"])
            op("dve", lambda e: e.tensor_tensor(out=sob[:, :, s_], in0=so[:, :], in1=og[:, 2, :], op=ALU.add), reads=["so", "og"], writes=["sob"])
        with nc.allow_non_contiguous_dma(reason="tiny sample columns"):
            dma("sp", mT_s[:, :, T:TC].rearrange("h p s -> p h s"), sob[:, :, :], reads=["sob"], writes=["mT_s"])
        kb.barrier()
        es.close()

    LCH = 32
    NCH = T // LCH

    def win_phase(li, t, segs):
        g0 = t * TW
        for dc in range(KC):
            wv, wk = wr.get(("win", li, dc))
            w3 = wv[:, 0:KC * 128].rearrange("p (k c) -> p k c", k=KC)
            for (c0, n, S) in segs:
                pb, pk = psum()
                for kc in range(KC):
                    op("pe", lambda e: e.matmul(pb[:, 0:n], w3[:, kc, :], hT[:, kc, c0:c0 + n], start=(kc == 0), stop=(kc == KC - 1)),
                       reads=["hT", wk], writes=[pk], signal=(kc == KC - 1))
                r = rr["cg"] % 2
                rr["cg"] += 1
                op("act", lambda e: e.copy(cgt[r][:, 0:n], pb[:, 0:n]), reads=[pk], writes=["cg%d" % r])
                gc = g0 if S == 1 else T
                if S == 1:
                    dma("sp", uT_s[dc, :, gc:gc + n], cgt[r][:, 0:n], reads=["cg%d" % r], writes=["uT_s"])
                else:
                    with nc.allow_non_contiguous_dma(reason="tiny sample columns"):
                        dma("sp", uT_s[dc, :, gc:gc + n], cgt[r][:, 0:n], reads=["cg%d" % r], writes=["uT_s"])

    def mixer_out_ssm(li, t, segs):
        g0 = t * TW
        minT = aT[:, 0:KC, :]
        dma("sp", minT[:, :, 0:TW], zT_s[:, :, g0:g0 + TW].rearrange("k p t -> p k t"), reads=["zT_s"], writes=["aT"])
        if len(segs) > 1:
            with nc.allow_non_contiguous_dma(reason="tiny sample columns"):
                dma("sp", minT[:, :, TW:TWS], zT_s[:, :, T:TC].rearrange("k p t -> p k t"), reads=["zT_s"], writes=["aT"])
        psts = [stats_begin() if si == 0 else psum(6, 7) for si in range(len(segs))]
        for dc in range(KC):
            wv, wk = wr.get(("glu", li, dc))
            wval = wv[:, 0:KC * 128].rearrange("p (k c) -> p k c", k=KC)
            wgat = wv[:, KC * 128:2 * KC * 128].rearrange("p (k c) -> p k c", k=KC)
            for si, (c0, n, S) in enumerate(segs):
                pv, pvk = psum()
                pg, pgk = psum()
                for kc in range(KC):
                    op("pe", lambda e: e.matmul(pv[:, 0:n], wval[:, kc, :], minT[:, kc, c0:c0 + n], start=(kc == 0), stop=(kc == KC - 1)),
                       reads=["aT", wk], writes=[pvk], signal=(kc == KC - 1))
                for kc in range(KC):
                    op("pe", lambda e: e.matmul(pg[:, 0:n], wgat[:, kc, :], minT[:, kc, c0:c0 + n], start=(kc == 0), stop=(kc == KC - 1)),
                       reads=["aT", wk], writes=[pgk], signal=(kc == KC - 1))
                r = rr["cg"] % 2
                rr["cg"] += 1
                op("act", lambda e: e.activation(out=sgt[r][:, 0:n], in_=pg[:, 0:n], func=AF.Sigmoid), reads=[pgk], writes=["sg%d" % r])
                op("dve", lambda e: e.tensor_tensor(out=mT[:, dc, c0:c0 + n], in0=pv[:, 0:n], in1=sgt[r][:, 0:n], op=ALU.mult),
                   reads=[pvk, "sg%d" % r], writes=["mT"])
                stats_add(psts[si], mT[:, dc, c0:c0 + n], "mT", c0, n, dc == 0, dc == KC - 1)
        for si, (c0, n, S) in enumerate(segs):
            stats_end(psts[si], c0, n)

    def ssm_phase(li):
        kb.barrier()
        es = contextlib.ExitStack()
        aTf = aT[:].rearrange("p a b -> p (a b)")
        hTf = hT[:].rearrange("p a b -> p (a b)")
        xTf = xT[:].rearrange("p a b -> p (a b)")
        mTf = mT[:].rearrange("p a b -> p (a b)")
        Wb = aTf[:, 0:16384].rearrange("p (k g m) -> p k g m", k=KC, g=8)
        npi_t = aTf[:, 16384:16384 + 4096].bitcast(F32).rearrange("p (a i) -> p a i", i=LCH)
        Wc = xTf[:, 0:8192].bitcast(BF16).rearrange("p (k j r m) -> p k j r m", k=KC, j=4, r=2)
        pr_parts = [cgt[0], cgt[1], cvt[0], cvt[1]]
        pi_parts = [sgt[0], sgt[1], rstd, rtmp]

        def tab(parts, pair):
            return parts[pair // 16][:, (pair % 16) * LCH:(pair % 16 + 1) * LCH]
        small = hTf[:, 0:3840].bitcast(F32).rearrange("p (k w) -> p k w", w=64)
        smallB = hTf[0:64, 3840:3840 + 4096].bitcast(F32).rearrange("p (k w) -> p k w", w=128)
        dsk = kb.sb("dsk", [128, KC], F32, es)
        gmask = kb.sb("gmask", [128, 8], F32, es)
        nat = kb.sb("nat", [128, 128], F32, es)
        dma("sp", gmask[:], c_gmask[:, :], writes=["gmask"])
        dma("sp", nat[0:KC, :], d_skip[li, :].rearrange("(k p) -> k p", p=128), writes=["nat"])
        pb, pk = psum(5, 8)
        op("pe", lambda e: e.transpose(pb[:, 0:KC], nat[0:KC, :], ident_f[0:KC, 0:KC]), reads=["nat", "ident_f"], writes=[pk])
        op("act", lambda e: e.copy(dsk[:, :], pb[:, 0:KC]), reads=[pk], writes=["dsk"])

        def abar_chain(P, W, sm, lam_r_ap, lam_i_ap, ldt_ap, key):
            S_ = lambda k: sm[0:P, k, 0:W]
            o = lambda fn, **kw: op("dve", fn, reads=[key], writes=[key])
            o(lambda e: e.tensor_scalar(out=S_(0), in0=lam_r_ap, scalar1=-1e-4, scalar2=None, op0=ALU.min))
            o(lambda e: e.tensor_copy(out=S_(1), in_=lam_i_ap))
            op("act", lambda e: e.activation(out=S_(2), in_=ldt_ap, func=AF.Exp), reads=[key], writes=[key])
            o(lambda e: e.tensor_tensor(out=S_(3), in0=S_(1), in1=S_(2), op=ALU.mult))
            o(lambda e: e.tensor_scalar(out=S_(3), in0=S_(3), scalar1=1.0 / 16.0, scalar2=None, op0=ALU.mult))
            o(lambda e: e.tensor_tensor(out=S_(4), in0=S_(3), in1=S_(3), op=ALU.mult))
            o(lambda e: e.tensor_scalar(out=S_(5), in0=S_(4), scalar1=1.0 / 362880.0, scalar2=None, op0=ALU.mult))
            for cf in (-1.0 / 5040.0, 1.0 / 120.0, -1.0 / 6.0):
                o(lambda e: e.scalar_tensor_tensor(out=S_(5), in0=S_(5), scalar=cf, in1=S_(4), op0=ALU.add, op1=ALU.mult))
            o(lambda e: e.scalar_tensor_tensor(out=S_(5), in0=S_(5), scalar=1.0, in1=S_(3), op0=ALU.add, op1=ALU.mult))
            o(lambda e: e.tensor_scalar(out=S_(6), in0=S_(4), scalar1=-1.0 / 3628800.0, scalar2=None, op0=ALU.mult))
            for cf in (1.0 / 40320.0, -1.0 / 720.0, 1.0 / 24.0, -0.5):
                o(lambda e: e.scalar_tensor_tensor(out=S_(6), in0=S_(6), scalar=cf, in1=S_(4), op0=ALU.add, op1=ALU.mult))
            o(lambda e: e.tensor_scalar(out=S_(6), in0=S_(6), scalar1=1.0, scalar2=None, op0=ALU.add))
            for _ in range(4):
                o(lambda e: e.tensor_tensor(out=S_(7), in0=S_(5), in1=S_(6), op=ALU.mult))
                o(lambda e: e.tensor_tensor(out=S_(11), in0=S_(5), in1=S_(5), op=ALU.mult))
                o(lambda e: e.tensor_scalar(out=S_(6), in0=S_(11), scalar1=-2.0, scalar2=1.0, op0=ALU.mult, op1=ALU.add))
                o(lambda e: e.tensor_scalar(out=S_(5), in0=S_(7), scalar1=2.0, scalar2=None, op0=ALU.mult))
            o(lambda e: e.tensor_tensor(out=S_(11), in0=S_(0), in1=S_(2), op=ALU.mult))
            op("act", lambda e: e.activation(out=S_(8), in_=S_(11), func=AF.Exp), reads=[key], writes=[key])
            o(lambda e: e.tensor_tensor(out=S_(9), in0=S_(8), in1=S_(6), op=ALU.mult))
            o(lambda e: e.tensor_tensor(out=S_(10), in0=S_(8), in1=S_(5), op=ALU.mult))
            return S_(9), S_(10), S_(0), S_(1)

        def load_A(dst, src2d):
            dma("sp", nat[0:64, :], src2d.rearrange("(g s) p -> g (s p)", s=2), writes=["nat"])
            pb, pk = psum(5, 8)
            op("pe", lambda e: e.transpose(pb[:, 0:64], nat[0:64, :], ident_f[0:64, 0:64]), reads=["nat", "ident_f"], writes=[pk])
            op("act", lambda e: e.copy(dst, pb[:, 0:64]), reads=[pk], writes=["small"])
        A_ = lambda k: small[:, k, :]
        load_A(A_(12), lam_re[li, :, :])
        load_A(A_(13), lam_im[li, :, :])
        dma("sp", nat[0:64, 0:2], log_dt[li, :].rearrange("(g s) -> g s", s=2), writes=["nat"])
        op("dve", lambda e: e.tensor_copy(out=nat[0:64, 64:128].rearrange("p (s q) -> p s q", s=2)[:, :, :] if False else smallB[0:64, 15, :].rearrange("p (s q) -> p s q", s=2),
                                          in_=nat[0:64, 0:2].unsqueeze(2).broadcast_to([64, 2, 64])), reads=["nat"], writes=["smallB"])
        pb, pk = psum(5, 8)
        op("pe", lambda e: e.transpose(pb[:, 0:64], smallB[0:64, 15, :], ident_f[0:64, 0:64]), reads=["smallB", "ident_f"], writes=[pk])
        op("act", lambda e: e.copy(A_(14), pb[:, 0:64]), reads=[pk], writes=["small"])
        arA, aiA, _, _ = abar_chain(128, 64, small, A_(12), A_(13), A_(14), "small")
        for pair0 in range(0, 64, 16):
            pass
        prv = lambda i: [p_[:, :].rearrange("p (a i) -> p a i", i=LCH)[:, :, i] for p_ in pr_parts]
        piv = lambda i: [p_[:, 0:512].rearrange("p (a i) -> p a i", i=LCH)[:, :, i] for p_ in pi_parts]
        tkeys = ["cg0", "cg1", "cv0", "cv1", "sg0", "sg1", "rstd", "rtmp", "npi"]
        for q in range(4):
            op("dve", lambda e: e.tensor_copy(out=prv(0)[q], in_=arA[:, q * 16:(q + 1) * 16]), reads=["small"], writes=tkeys)
            op("dve", lambda e: e.tensor_copy(out=piv(0)[q], in_=aiA[:, q * 16:(q + 1) * 16]), reads=["small"], writes=tkeys)
        for i in range(1, LCH):
            for q in range(4):
                a_r = arA[:, q * 16:(q + 1) * 16]
                a_i = aiA[:, q * 16:(q + 1) * 16]
                t1, t2 = small[:, 15, 0:16], small[:, 16, 0:16]
                op("dve", lambda e: e.tensor_tensor(out=t1, in0=prv(i - 1)[q], in1=a_r, op=ALU.mult), reads=tkeys + ["small"], writes=["small"])
                op("dve", lambda e: e.tensor_tensor(out=t2, in0=piv(i - 1)[q], in1=a_i, op=ALU.mult), reads=tkeys + ["small"], writes=["small"])
                op("dve", lambda e: e.tensor_tensor(out=prv(i)[q], in0=t1, in1=t2, op=ALU.subtract), reads=["small"], writes=tkeys)
                op("dve", lambda e: e.tensor_tensor(out=t1, in0=prv(i - 1)[q], in1=a_i, op=ALU.mult), reads=tkeys + ["small"], writes=["small"])
                op("dve", lambda e: e.tensor_tensor(out=t2, in0=piv(i - 1)[q], in1=a_r, op=ALU.mult), reads=tkeys + ["small"], writes=["small"])
                op("dve", lambda e: e.tensor_tensor(out=piv(i)[q], in0=t1, in1=t2, op=ALU.add), reads=["small"], writes=tkeys)
        for q in range(4):
            op("dve", lambda e: e.tensor_scalar(out=npi_t[:, q * 16:(q + 1) * 16, :], in0=pi_parts[q][:, 0:512].rearrange("p (a i) -> p a i", i=LCH),
                                                scalar1=-1.0, scalar2=None, op0=ALU.mult), reads=tkeys, writes=tkeys)

        A8 = kb.sb("A8", [128, 3, 64, 8], F32, es)
        Ar_all = [p_[:, 0:512].rearrange("p (a i) -> p a i", i=LCH)[:, :, LCH - 1] for p_ in pr_parts]
        Ai_all = [p_[:, 0:512].rearrange("p (a i) -> p a i", i=LCH)[:, :, LCH - 1] for p_ in pi_parts]
        for q in range(4):
            qs = slice(q * 16, (q + 1) * 16)
            op("dve", lambda e: e.tensor_copy(out=A8[:, 0, qs, 0], in_=Ar_all[q]), reads=tkeys, writes=["A8"])
            op("dve", lambda e: e.tensor_copy(out=A8[:, 1, qs, 0], in_=Ai_all[q]), reads=tkeys, writes=["A8"])
            for j in range(1, 8):
                t1, t2 = small[:, 15, 0:16], small[:, 16, 0:16]
                op("dve", lambda e: e.tensor_tensor(out=t1, in0=A8[:, 0, qs, j - 1], in1=Ar_all[q], op=ALU.mult), reads=tkeys + ["A8", "small"], writes=["small"])
                op("dve", lambda e: e.tensor_tensor(out=t2, in0=A8[:, 1, qs, j - 1], in1=Ai_all[q], op=ALU.mult), reads=tkeys + ["A8", "small"], writes=["small"])
                op("dve", lambda e: e.tensor_tensor(out=A8[:, 0, qs, j], in0=t1, in1=t2, op=ALU.subtract), reads=["small"], writes=["A8"])
                op("dve", lambda e: e.tensor_tensor(out=t1, in0=A8[:, 0, qs, j - 1], in1=Ai_all[q], op=ALU.mult), reads=tkeys + ["A8", "small"], writes=["small"])
                op("dve", lambda e: e.tensor_tensor(out=t2, in0=A8[:, 1, qs, j - 1], in1=Ar_all[q], op=ALU.mult), reads=tkeys + ["A8", "small"], writes=["small"])
                op("dve", lambda e: e.tensor_tensor(out=A8[:, 1, qs, j], in0=t1, in1=t2, op=ALU.add), reads=["small"], writes=["A8"])
        op("dve", lambda e: e.tensor_scalar(out=A8[:, 2, :, :], in0=A8[:, 1, :, :], scalar1=-1.0, scalar2=None, op0=ALU.mult), reads=["A8"], writes=["A8"])

        B_ = lambda k: smallB[0:64, k, :]
        for (dst, src) in ((B_(12), lam_re), (B_(13), lam_im)):
            dma("sp", nat[:, 0:64], src[li, :, :], writes=["nat"])
            pb, pk = psum(5, 8)
            op("pe", lambda e: e.transpose(pb[0:64, 0:128], nat[:, 0:64], ident_f[:, :]), reads=["nat", "ident_f"], writes=[pk])
            op("act", lambda e: e.copy(dst, pb[0:64, 0:128]), reads=[pk], writes=["smallB"])
        dma("sp", B_(14), log_dt[li:li + 1, :].partition_broadcast(64).rearrange("p a g -> p (a g)") if False else log_dt[li:li + 1, :].broadcast_to([64, 128]), writes=["smallB"])
        arB, aiB, lrB, liB = abar_chain(64, 128, smallB, B_(12), B_(13), B_(14), "smallB")
        ob = lambda fn: op("dve", fn, reads=["smallB"], writes=["smallB"])
        ob(lambda e: e.tensor_scalar(out=B_(11), in0=arB, scalar1=-1.0, scalar2=None, op0=ALU.add))
        ob(lambda e: e.tensor_tensor(out=B_(7), in0=lrB, in1=lrB, op=ALU.mult))
        ob(lambda e: e.tensor_tensor(out=B_(2), in0=liB, in1=liB, op=ALU.mult))
        ob(lambda e: e.tensor_tensor(out=B_(7), in0=B_(7), in1=B_(2), op=ALU.add))
        ob(lambda e: e.reciprocal(B_(7), B_(7)))
        ob(lambda e: e.tensor_tensor(out=B_(3), in0=B_(11), in1=lrB, op=ALU.mult))
        ob(lambda e: e.tensor_tensor(out=B_(2), in0=aiB, in1=liB, op=ALU.mult))
        ob(lambda e: e.tensor_tensor(out=B_(3), in0=B_(3), in1=B_(2), op=ALU.add))
        ob(lambda e: e.tensor_tensor(out=B_(3), in0=B_(3), in1=B_(7), op=ALU.mult))
        ob(lambda e: e.tensor_tensor(out=B_(4), in0=aiB, in1=lrB, op=ALU.mult))
        ob(lambda e: e.tensor_tensor(out=B_(2), in0=B_(11), in1=liB, op=ALU.mult))
        ob(lambda e: e.tensor_tensor(out=B_(4), in0=B_(4), in1=B_(2), op=ALU.subtract))
        ob(lambda e: e.tensor_tensor(out=B_(4), in0=B_(4), in1=B_(7), op=ALU.mult))
        bre = mTf[0:64, 0:2048].rearrange("p (g c) -> p g c", c=16)
        bim = mTf[0:64, 2048:4096].rearrange("p (g c) -> p g c", c=16)
        bbr = mTf[0:64, 4096:6144].rearrange("p (g c) -> p g c", c=16)
        bbi = mTf[0:64, 6144:8192].rearrange("p (g c) -> p g c", c=16)
        with nc.allow_non_contiguous_dma(reason="b tensors 64B runs"):
            dma("sp", bre, b_re[li, :, :, :].rearrange("g p c -> p g c"), writes=["bb"])
            dma("sp", bim, b_im[li, :, :, :].rearrange("g p c -> p g c"), writes=["bb"])
        cr = B_(3).unsqueeze(2).broadcast_to([64, 128, 16])
        ci = B_(4).unsqueeze(2).broadcast_to([64, 128, 16])
        o2 = lambda fn: op("dve", fn, reads=["bb", "smallB"], writes=["bb"])
        o2(lambda e: e.tensor_tensor(out=bbr, in0=bre, in1=cr, op=ALU.mult))
        o2(lambda e: e.tensor_tensor(out=bbi, in0=bim, in1=ci, op=ALU.mult))
        o2(lambda e: e.tensor_tensor(out=bbr, in0=bbr, in1=bbi, op=ALU.subtract))
        o2(lambda e: e.tensor_tensor(out=bbi, in0=bre, in1=ci, op=ALU.mult))
        o2(lambda e: e.tensor_tensor(out=bre, in0=bim, in1=cr, op=ALU.mult))
        o2(lambda e: e.tensor_tensor(out=bbi, in0=bbi, in1=bre, op=ALU.add))
        for kc in range(KC):
            for r_, src in ((0, bbr), (1, bbi)):
                pb, pk = psum(5, 8)
                op("pe", lambda e: e.transpose(pb[:, 0:64], src[:, kc * 8:(kc + 1) * 8, :].rearrange("p g c -> p (g c)"), ident_f[0:64, 0:64]),
                   reads=["bb", "ident_f"], writes=[pk])
                for gl in range(8):
                    op("act" if gl % 2 else "dve",
                       (lambda e: e.activation(out=Wb[:, kc, gl, r_ * 64:(r_ + 1) * 64], in_=pb[:, 0:64], func=AF.Identity, scale=gmask[:, gl:gl + 1])) if gl % 2 else
                       (lambda e: e.tensor_scalar(out=Wb[:, kc, gl, r_ * 64:(r_ + 1) * 64], in0=pb[:, 0:64], scalar1=gmask[:, gl:gl + 1], scalar2=None, op0=ALU.mult)),
                       reads=[pk, "gmask"], writes=["Wb"])
        kb.barrier()
        Cn = [hTf[0:64, r_ * 4096:(r_ + 1) * 4096].bitcast(F32).rearrange("p (s c q) -> p s c q", s=2, c=16) for r_ in range(2)]
        dma("sp", Cn[0], c_re[li, :, :, :].rearrange("(g s) c q -> g s c q", s=2), writes=["Cn"])
        dma("sp", Cn[1], c_im[li, :, :, :].rearrange("(g s) c q -> g s c q", s=2), writes=["Cn"])
        op("dve", lambda e: e.memset(xTf[:, 0:8192], 0.0), writes=["Wc"])
        for r_ in range(2):
            for c in range(16):
                pb, pk = psum(5, 8)
                op("dve", lambda e: e.tensor_copy(out=nat[0:64, :].rearrange("p (s q) -> p s q", s=2), in_=Cn[r_][:, :, c, :]), reads=["Cn"], writes=["nat"])
                op("pe", lambda e: e.transpose(pb[:, 0:64], nat[0:64, :], ident_f[0:64, 0:64]), reads=["nat", "ident_f"], writes=[pk])
                for s in range(2):
                    for j4 in range(4):
                        src = pb[s * 64:(s + 1) * 64, 0:64].rearrange("p (k j) -> p k j", j=4)[:, :, j4]
                        dst = Wc[s * 64:(s + 1) * 64, :, j4, r_, 32 * j4 + 16 * s + c]
                        sc = 1.0 if r_ == 0 else -1.0
                        if (s + j4) % 2:
                            op("act", lambda e: e.mul(dst, src, sc), reads=[pk], writes=["Wc"])
                        else:
                            op("dve", lambda e: e.tensor_scalar(out=dst, in0=src, scalar1=sc, scalar2=None, op0=ALU.mult), reads=[pk], writes=["Wc"])
        kb.barrier()
        XR = [mTf[:, (2 * q) * TC:(2 * q + 1) * TC] for q in range(2)]
        XI = [mTf[:, (2 * q + 1) * TC:(2 * q + 2) * TC] for q in range(2)]
        ya = kb.sb("ya", [128, TW], F32, es)
        yb2 = aTf[:, 20480:20480 + 1024].bitcast(F32)
        Xs = kb.sb("Xs", [128, 4, NCH], F32, es)
        h0t = kb.sb("h0t", [128, 64, NS, 2], F32, es)
        h0 = h0t[:]
        sto = kb.sb("sto", [128, 64, 2], F32, es)
        stos = kb.sb("stos", [128, NS, 64, 2], F32, es)
        ubf = hTf[:, 0:TC]
        xbr = hTf[:, TC:2 * TC]
        xbi = hTf[:, 2 * TC:3 * TC]
        zst = hTf[:, 3 * TC:4 * TC]
        for s_ in range(NS):
            with nc.allow_non_contiguous_dma(reason="state 8B runs"):
                dma("sp", h0[:, :, s_, :], st_ssm[li, s_, :].rearrange("(a p r) -> p a r", p=128, r=2), writes=["h0"])
        coltiles = [(tq * TW, TW) for tq in range(NT)] + [(T, NS)]
        NBK = T // 4
        XAR = [x[:, 0:T].rearrange("p (r m) -> p r m", m=NBK) for x in XR]
        XAI = [x[:, 0:T].rearrange("p (r m) -> p r m", m=NBK) for x in XI]
        X3R = [x[:, 3, :].rearrange("p (n i) -> p i n", i=8) for x in XAR]
        X3I = [x[:, 3, :].rearrange("p (n i) -> p i n", i=8) for x in XAI]
        STT = lambda o_, a_, sc_, b_: (lambda e: e.scalar_tensor_tensor(out=o_, in0=a_, scalar=sc_, in1=b_, op0=ALU.mult, op1=ALU.add))
        for kc in range(KC):
            dma("pool", ubf[:, :], uT_s[kc, :, :], reads=["uT_s"], writes=["ubf"])
            ybanks = [(pbank[k], ("ps", k)) for k in range(5)]
            for jp in range(2):
                pairs = [kc * 4 + 2 * jp + q for q in range(2)]
                tabs = []
                for q in range(2):
                    pair = pairs[q]
                    j4 = 2 * jp + q
                    PR, PI, NPI = tab(pr_parts, pair), tab(pi_parts, pair), npi_t[:, pair, :]
                    tabs.append((PR, PI, NPI))
                    for (c0, n) in coltiles:
                        for r_, dstx, dk in ((0, XR[q], ("xr", q)), (1, XI[q], ("xi", q))):
                            pb, pk = psum(5, 8)
                            for s in range(2):
                                op("pe", lambda e: e.matmul(pb[s * 64:(s + 1) * 64, 0:n], Wb[:, kc, 2 * j4 + s, r_ * 64:(r_ + 1) * 64], ubf[:, c0:c0 + n],
                                                            start=True, stop=True), reads=["Wb", "ubf"], writes=[pk], signal=(s == 1))
                            if n == TW:
                                m0 = c0 // 4
                                dview = dstx[:, 0:T].rearrange("p (r m) -> p r m", m=T // 4)[:, :, m0:m0 + TW // 4]
                                op("act", lambda e: e.copy(dview, pb[:, 0:TW].rearrange("p (m r) -> p r m", r=4)), reads=[pk], writes=[dk])
                            else:
                                op("act", lambda e: e.copy(dstx[:, c0:c0 + n], pb[:, 0:n]), reads=[pk], writes=[dk])
                allk = [("xr", 0), ("xi", 0), ("xr", 1), ("xi", 1), "Xs"] + tkeys
                kb.fence("dve", reads=allk, writes=allk)
                ox = kb.opx
                l2 = [0, 0]
                l4 = [0, 0]

                def cplx_step(dR, dI, sR, sI, ti, l2, l4):
                    c1 = [0, 0]
                    c3 = [0, 0]
                    for q in range(2):
                        c1[q] = ox("dve", STT(dR[q], sR[q], tabs[q][0][:, ti:ti + 1], dR[q]), after=[l2[q], l4[q]])
                    for q in range(2):
                        c3[q] = ox("dve", STT(dI[q], sI[q], tabs[q][0][:, ti:ti + 1], dI[q]), after=[l2[q], l4[q]])
                    n2 = [0, 0]
                    n4 = [0, 0]
                    for q in range(2):
                        n2[q] = ox("dve", STT(dR[q], sI[q], tabs[q][2][:, ti:ti + 1], dR[q]), after=[c1[q]])
                    for q in range(2):
                        n4[q] = ox("dve", STT(dI[q], sR[q], tabs[q][1][:, ti:ti + 1], dI[q]), after=[c3[q]])
                    return n2, n4
                for r in range(1, 4):
                    l2, l4 = cplx_step([x[:, r, :] for x in XAR], [x[:, r, :] for x in XAI], [x[:, r - 1, :] for x in XAR], [x[:, r - 1, :] for x in XAI], 0, l2, l4)
                for i in range(1, 8):
                    l2, l4 = cplx_step([x[:, i, :] for x in X3R], [x[:, i, :] for x in X3I], [x[:, i - 1, :] for x in X3R], [x[:, i - 1, :] for x in X3I], 3, l2, l4)
                XRs = [Xs[:, 2 * q, :] for q in range(2)]
                XIs = [Xs[:, 2 * q + 1, :] for q in range(2)]
                e2 = [0, 0]
                e4 = [0, 0]
                for q in range(2):
                    e2[q] = ox("dve", lambda e: e.tensor_copy(out=XRs[q], in_=X3R[q][:, 7, :]), after=[l2[q], l4[q]])
                    e4[q] = ox("dve", lambda e: e.tensor_copy(out=XIs[q], in_=X3I[q][:, 7, :]), after=[l2[q], l4[q]])
                X8R = [x.rearrange("p (m j) -> p m j", j=8) for x in XRs]
                X8I = [x.rearrange("p (m j) -> p m j", j=8) for x in XIs]
                A8q = [(A8[:, 0, pairs[q], :], A8[:, 1, pairs[q], :], A8[:, 2, pairs[q], :]) for q in range(2)]

                def cstep(dstR, dstI, srcR, srcI, coef, dR, dI):
                    c1 = [0, 0]
                    c3 = [0, 0]
                    o2 = [0, 0]
                    o4 = [0, 0]
                    for q in range(2):
                        c1[q] = ox("dve", STT(dstR[q], srcR[q], coef[q][0], dstR[q]), after=[dR[q], dI[q]])
                    for q in range(2):
                        c3[q] = ox("dve", STT(dstI[q], srcI[q], coef[q][0], dstI[q]), after=[dR[q], dI[q]])
                    for q in range(2):
                        o2[q] = ox("dve", STT(dstR[q], srcI[q], coef[q][2], dstR[q]), after=[c1[q]])
                    for q in range(2):
                        o4[q] = ox("dve", STT(dstI[q], srcR[q], coef[q][1], dstI[q]), after=[c3[q]])
                    return o2, o4
                for j in range(1, 8):
                    e2, e4 = cstep([x[:, :, j] for x in X8R], [x[:, :, j] for x in X8I], [x[:, :, j - 1] for x in X8R], [x[:, :, j - 1] for x in X8I],
                                   [(A8q[q][0][:, 0:1], A8q[q][1][:, 0:1], A8q[q][2][:, 0:1]) for q in range(2)], e2, e4)
                for m_ in range(1, 8):
                    e2, e4 = cstep([x[:, m_, 7:8] for x in X8R], [x[:, m_, 7:8] for x in X8I], [x[:, m_ - 1, 7:8] for x in X8R], [x[:, m_ - 1, 7:8] for x in X8I],
                                   [(A8q[q][0][:, 7:8], A8q[q][1][:, 7:8], A8q[q][2][:, 7:8]) for q in range(2)], e2, e4)
                f2, f4 = e2, e4
                for j in range(7):
                    g2, g4 = cstep([x[:, 1:8, j] for x in X8R], [x[:, 1:8, j] for x in X8I], [x[:, 0:7, 7] for x in X8R], [x[:, 0:7, 7] for x in X8I],
                                   [(A8q[q][0][:, j:j + 1], A8q[q][1][:, j:j + 1], A8q[q][2][:, j:j + 1]) for q in range(2)], e2, e4)
                    f2 = [max(f2[q], g2[q]) for q in range(2)]
                    f4 = [max(f4[q], g4[q]) for q in range(2)]
                e2, e4 = f2, f4
                f2, f4 = list(e2), list(e4)
                for i in range(8):
                    g2, g4 = cplx_step([x[:, i, 1:NCH] for x in X3R], [x[:, i, 1:NCH] for x in X3I], [x[:, 0:NCH - 1] for x in XRs], [x[:, 0:NCH - 1] for x in XIs],
                                       4 * i + 3, e2, e4)
                    f2 = [max(f2[q], g2[q]) for q in range(2)]
                    f4 = [max(f4[q], g4[q]) for q in range(2)]
                for r in range(3):
                    cplx_step([x[:, r, 1:NBK] for x in XAR], [x[:, r, 1:NBK] for x in XAI], [x[:, 3, 0:NBK - 1] for x in XAR], [x[:, 3, 0:NBK - 1] for x in XAI],
                              r, f2, f4)
                kb.mark("dve", writes=[("xr", 0), ("xi", 0), ("xr", 1), ("xi", 1), "Xs"], reads=tkeys)
                for q in range(2):
                    pair = pairs[q]
                    j4 = 2 * jp + q
                    PR, PI, NPI = tabs[q]
                    ar, ai, nai = PR[:, 0:1], PI[:, 0:1], NPI[:, 0:1]
                    xr, xi = XR[q], XI[q]
                    xk, ik = ("xr", q), ("xi", q)
                    sc = lambda fn, rd, wrt: op("dve", fn, reads=rd + tkeys, writes=wrt)
                    xs_r, xs_i = xr[:, T:TC], xi[:, T:TC]
                    sc(STT(xs_r, h0[:, pair, :, 0], ar, xs_r), ["h0", xk], [xk])
                    sc(STT(xs_i, h0[:, pair, :, 1], ar, xs_i), ["h0", ik], [ik])
                    sc(STT(xs_r, h0[:, pair, :, 1], nai, xs_r), ["h0", xk], [xk])
                    sc(STT(xs_i, h0[:, pair, :, 0], ai, xs_i), ["h0", ik], [ik])
                    op("act", lambda e: e.copy(sto[:, pair, 0:1], xr[:, T - 1:T]), reads=[xk], writes=["sto"])
                    op("act", lambda e: e.copy(sto[:, pair, 1:2], xi[:, T - 1:T]), reads=[ik], writes=["sto"])
                    op("act", lambda e: e.copy(stos[:, :, pair, 0], xr[:, T:TC]), reads=[xk], writes=["stos"])
                    op("act", lambda e: e.copy(stos[:, :, pair, 1], xi[:, T:TC]), reads=[ik], writes=["stos"])
                    op("act", lambda e: e.copy(xbr[:, 0:T].rearrange("p (m r) -> p m r", r=4), xr[:, 0:T].rearrange("p (r m) -> p m r", m=T // 4)), reads=[xk], writes=["xbr"])
                    op("act", lambda e: e.copy(xbr[:, T:TC], xr[:, T:TC]), reads=[xk], writes=["xbr"])
                    op("pool", lambda e: e.tensor_copy(out=xbi[:, 0:T].rearrange("p (m r) -> p m r", r=4), in_=xi[:, 0:T].rearrange("p (r m) -> p m r", m=T // 4)), reads=[ik], writes=["xbi"])
                    op("pool", lambda e: e.tensor_copy(out=xbi[:, T:TC], in_=xi[:, T:TC]), reads=[ik], writes=["xbi"])
                    for ti, (c0, n) in enumerate(coltiles):
                        yb, ybk = ybanks[ti]
                        op("pe", lambda e: e.matmul(yb[:, 0:n], Wc[:, kc, j4, 0, :], xbr[:, c0:c0 + n], start=(j4 == 0), stop=False),
                           reads=["Wc", "xbr"], writes=[ybk], signal=False)
                        op("pe", lambda e: e.matmul(yb[:, 0:n], Wc[:, kc, j4, 1, :], xbi[:, c0:c0 + n], start=False, stop=(j4 == 3)),
                           reads=["Wc", "xbi"], writes=[ybk], signal=True)
            for ti, (c0, n) in enumerate(coltiles):
                yb, ybk = ybanks[ti]
                y_, w_ = ya[:, 0:n], yb2[:, 0:n]
                if n == TW:
                    dma("sp", y_, uT_s[kc, :, c0:c0 + n], reads=["uT_s"], writes=["ya"])
                else:
                    with nc.allow_non_contiguous_dma(reason="tiny sample columns"):
                        dma("sp", y_, uT_s[kc, :, c0:c0 + n], reads=["uT_s"], writes=["ya"])
                op("dve", lambda e: e.scalar_tensor_tensor(out=y_, in0=y_, scalar=dsk[:, kc:kc + 1], in1=yb[:, 0:n], op0=ALU.mult, op1=ALU.add),
                   reads=["ya", "dsk", ybk], writes=["ya"])
                op("dve", lambda e: e.tensor_tensor(out=w_, in0=y_, in1=y_, op=ALU.mult), reads=["ya"], writes=["yb2"])
                op("dve", lambda e: e.tensor_scalar(out=w_, in0=w_, scalar1=0.044715, scalar2=1.0, op0=ALU.mult, op1=ALU.add), reads=["yb2"], writes=["yb2"])
                op("dve", lambda e: e.tensor_tensor(out=w_, in0=w_, in1=y_, op=ALU.mult), reads=["yb2", "ya"], writes=["yb2"])
                op("act", lambda e: e.activation(out=w_, in_=w_, func=AF.Sigmoid, scale=GELU_C), reads=["yb2"], writes=["yb2"])
                op("dve", lambda e: e.tensor_tensor(out=zst[:, c0:c0 + n], in0=y_, in1=w_, op=ALU.mult), reads=["ya", "yb2"], writes=["zst"])
            dma("sp", zT_s[kc, :, :], zst[:, :], reads=["zst"], writes=["zT_s"])
        with nc.allow_non_contiguous_dma(reason="state 8B runs"):
            dma("sp", ssm_p[li, :].rearrange("(a p r) -> p a r", p=128, r=2), sto[:, :, :], reads=["sto"])
            for s_ in range(NS):
                dma("sp", ssm_s[li, s_, :].rearrange("(a p r) -> p a r", p=128, r=2), stos[:, s_, :, :], reads=["stos"])
        kb.barrier()
        es.close()

    def tile_tail(i_next, t, segs):
        prenorm(i_next, 0, segs)
        if i_next % 2 == 0:
            qkv_phase(i_next // 2, t, segs)
        else:
            win_phase(i_next // 2, t, segs)

    def mixer_out_attn(li, t, segs):
        g0 = t * TW
        minT = aT[:, 0:8, :]
        dma("sp", minT[:, :, 0:TW], mT_s[:, :, g0:g0 + TW].rearrange("h p t -> p h t"), reads=["mT_s"], writes=["aT"])
        if len(segs) > 1:
            with nc.allow_non_contiguous_dma(reason="tiny sample columns"):
                dma("sp", minT[:, :, TW:TWS], mT_s[:, :, T:TC].rearrange("h p t -> p h t"), reads=["mT_s"], writes=["aT"])
        psts = [stats_begin() if si == 0 else psum(6, 7) for si in range(len(segs))]
        for dc in range(KC):
            wv, wk = wr.get(("wo", li, dc))
            w3 = wv[:, 0:8 * 128].rearrange("p (k c) -> p k c", k=8)
            for si, (c0, n, S) in enumerate(segs):
                pb, pk = psum()
                for hh in range(8):
                    op("pe", lambda e: e.matmul(pb[:, 0:n], w3[:, hh, :], minT[:, hh, c0:c0 + n], start=(hh == 0), stop=(hh == 7)),
                       reads=["aT", wk], writes=[pk], signal=(hh == 7))
                op("act", lambda e: e.copy(mT[:, dc, c0:c0 + n], pb[:, 0:n]), reads=[pk], writes=["mT"])
                stats_add(psts[si], pb[:, 0:n], pk, c0, n, dc == 0, dc == KC - 1)
        for si, (c0, n, S) in enumerate(segs):
            stats_end(psts[si], c0, n)

    def ffn(i, t, segs):
        for j in range(NFC):
            wv, wk = wr.get(("up", i, j))
            wg = wv[:, 0:KC * 128].rearrange("p (k c) -> p k c", k=KC)
            wvv = wv[:, KC * 128:2 * KC * 128].rearrange("p (k c) -> p k c", k=KC)
            for (c0, n, S) in segs:
                pg, pgk = psum()
                pv, pvk = psum()
                for kc in range(KC):
                    op("pe", lambda e: e.matmul(pg[:, 0:n], wg[:, kc, :], hT[:, kc, c0:c0 + n], start=(kc == 0), stop=(kc == KC - 1)),
                       reads=["hT", wk], writes=[pgk], signal=(kc == KC - 1))
                for kc in range(KC):
                    op("pe", lambda e: e.matmul(pv[:, 0:n], wvv[:, kc, :], hT[:, kc, c0:c0 + n], start=(kc == 0), stop=(kc == KC - 1)),
                       reads=["hT", wk], writes=[pvk], signal=(kc == KC - 1))
                r = rr["cg"] % 2
                rr["cg"] += 1
                halves = []
                for (pp, ppk, jj, ct_, ck_) in ((pg, pgk, j, cgt[r], "cg%d" % r), (pv, pvk, NFC + j, cvt[r], "cv%d" % r)):
                    if S == 1:
                        halo, hk = uprev[:, jj, :], ("uprev", jj)
                    else:
                        halo, hk = cst[:, jj, :, :].rearrange("p a b -> p (a b)"), "cst"
                    halves.append((pp, ppk, jj, ct_, ck_, halo, hk, cw[:, 0, jj:jj + 1], cw[:, 1, jj:jj + 1], cw[:, 2, jj:jj + 1], cw[:, 3, jj:jj + 1]))
                m2 = min(2 * S, n)
                for (pp, ppk, jj, ct_, ck_, halo, hk, w0, w1, w2, bb) in halves:
                    op("act", lambda e: e.activation(out=ct_[:, 0:n], in_=pp[:, 0:n], func=AF.Identity, bias=bb, scale=w2),
                       reads=[ppk, "cw"], writes=[ck_])
                if n > S:
                    for (pp, ppk, jj, ct_, ck_, halo, hk, w0, w1, w2, bb) in halves:
                        op("dve", lambda e: e.scalar_tensor_tensor(out=ct_[:, S:n], in0=pp[:, 0:n - S], scalar=w1, in1=ct_[:, S:n],
                                                                   op0=ALU.mult, op1=ALU.add), reads=[ppk, "cw", ck_], writes=[ck_])
                if n > 2 * S:
                    for (pp, ppk, jj, ct_, ck_, halo, hk, w0, w1, w2, bb) in halves:
                        op("dve", lambda e: e.scalar_tensor_tensor(out=ct_[:, 2 * S:n], in0=pp[:, 0:n - 2 * S], scalar=w0, in1=ct_[:, 2 * S:n],
                                                                   op0=ALU.mult, op1=ALU.add), reads=[ppk, "cw", ck_], writes=[ck_])
                for (pp, ppk, jj, ct_, ck_, halo, hk, w0, w1, w2, bb) in halves:
                    op("dve", lambda e: e.scalar_tensor_tensor(out=ct_[:, 0:S], in0=halo[:, S:2 * S], scalar=w1, in1=ct_[:, 0:S],
                                                               op0=ALU.mult, op1=ALU.add), reads=[hk, "cw", ck_], writes=[ck_])
                for (pp, ppk, jj, ct_, ck_, halo, hk, w0, w1, w2, bb) in halves:
                    op("dve", lambda e: e.scalar_tensor_tensor(out=ct_[:, 0:m2], in0=halo[:, 0:m2], scalar=w0, in1=ct_[:, 0:m2],
                                                               op0=ALU.mult, op1=ALU.add), reads=[hk, "cw", ck_], writes=[ck_])
                for (pp, ppk, jj, ct_, ck_, halo, hk, w0, w1, w2, bb) in halves:
                    if S == 1:
                        op("dve", lambda e: e.tensor_copy(out=uprev[:, jj, :], in_=pp[:, n - 2:n]), reads=[ppk], writes=[hk])
                    else:
                        op("dve", lambda e: e.tensor_copy(out=csout[:, jj, 0, :], in_=cst[:, jj, 1, :]), reads=["cst"], writes=["csout"])
                        op("dve", lambda e: e.tensor_copy(out=csout[:, jj, 1, :], in_=pp[:, 0:NS]), reads=[ppk], writes=["csout"])
                sg_ = sgt[r]
                op("act", lambda e: e.activation(out=sg_[:, 0:n], in_=cgt[r][:, 0:n], func=AF.Silu), reads=["cg%d" % r], writes=["sg%d" % r])
                op("dve", lambda e: e.tensor_tensor(out=aT[:, j, c0:c0 + n], in0=sg_[:, 0:n], in1=cvt[r][:, 0:n], op=ALU.mult),
                   reads=["sg%d" % r, "cv%d" % r], writes=["aT"])
        psts = [stats_begin() if si == 0 else psum(6, 7) for si in range(len(segs))]
        for dc in range(KC):
            wv, wk = wr.get(("down", i, dc))
            w3 = wv[:, 0:NFC * 128].rearrange("p (k c) -> p k c", k=NFC)
            for si, (c0, n, S) in enumerate(segs):
                pb, pk = psum()
                for j in range(NFC):
                    op("pe", lambda e: e.matmul(pb[:, 0:n], w3[:, j, :], aT[:, j, c0:c0 + n], start=(j == 0), stop=(j == NFC - 1)),
                       reads=["aT", wk], writes=[pk], signal=(j == NFC - 1))
                op("act", lambda e: e.copy(mT[:, dc, c0:c0 + n], pb[:, 0:n]), reads=[pk], writes=["mT"])
                stats_add(psts[si], pb[:, 0:n], pk, c0, n, dc == 0, dc == KC - 1)
        for si, (c0, n, S) in enumerate(segs):
            stats_end(psts[si], c0, n)

    def out_y(t, segs):
        ytok = mT_flat
        for bl in range(4):
            for q4 in range(4):
                pb, pk = psum()
                for k4 in range(4):
                    kc = q4 * 4 + k4
                    op("pe", lambda e: e.transpose(pb[:, k4 * 128:(k4 + 1) * 128], xT[:, kc, bl * 128:(bl + 1) * 128], ident_f[:, :]),
                       reads=["xT", "ident_f"], writes=[pk], signal=(k4 == 3))
                op("act", lambda e: e.copy(ytok[:, (bl % 2) * D + q4 * 512:(bl % 2) * D + (q4 + 1) * 512], pb[:, :]), reads=[pk], writes=["mT"])
            dma("sp", y_p[t * TW + bl * 128:t * TW + (bl + 1) * 128, :], ytok[:, (bl % 2) * D:(bl % 2 + 1) * D], reads=["mT"])
        if len(segs) > 1:
            pb, pk = psum()
            for kc in range(KC):
                op("pe", lambda e: e.transpose(pb[0:NS, (kc % 4) * 128:(kc % 4 + 1) * 128], xT[:, kc, TW:TWS], ident_f[:, :]),
                   reads=["xT", "ident_f"], writes=[pk], signal=True)
                if kc % 4 == 3:
                    q4 = kc // 4
                    op("act", lambda e: e.copy(ytok[0:NS, q4 * 512:(q4 + 1) * 512], pb[0:NS, :]), reads=[pk], writes=["mT"])
            dma("sp", y_s[:, :], ytok[0:NS, 0:D], reads=["mT"])

    xtok = mT_flat
    for t in range(NT):
        segs = segs_of(t)
        for bl in range(4):
            xo = (bl % 2) * D
            dma("sp", xtok[:, xo:xo + D], x_p[t * TW + bl * 128:t * TW + (bl + 1) * 128, :], writes=["mT"])
            for q4 in range(4):
                pb, pk = psum()
                for k4 in range(4):
                    kc = q4 * 4 + k4
                    op("pe", lambda e: e.transpose(pb[:, k4 * 128:(k4 + 1) * 128], xtok[:, xo + kc * 128:xo + (kc + 1) * 128], ident_f[:, :]),
                       reads=["mT", "ident_f"], writes=[pk], signal=(k4 == 3))
                op("act", lambda e: e.copy(xT[:, q4 * 4:(q4 + 1) * 4, bl * 128:(bl + 1) * 128], pb[:, :].rearrange("p (k t) -> p k t", k=4)),
                   reads=[pk], writes=["xT"])
        if len(segs) > 1:
            dma("sp", xtok[0:NS, 0:D], x_s[:, :], writes=["mT"])
            pb, pk = psum()
            for kc in range(KC):
                op("pe", lambda e: e.transpose(pb[:, kc * NS:(kc + 1) * NS], xtok[0:NS, kc * 128:(kc + 1) * 128], ident_f[0:NS, 0:NS]),
                   reads=["mT", "ident_f"], writes=[pk], signal=(kc == KC - 1))
            op("act", lambda e: e.copy(xT[:, :, TW:TWS], pb[:, 0:KC * NS].rearrange("p (k s) -> p k s", k=KC)), reads=[pk], writes=["xT"])
        xT_store(t, segs)
        tile_tail(0, t, segs)

    for i in range(nlayers):
        li = i // 2
        if i % 2 == 0:
            attn_phase(li)
        else:
            ssm_phase(li)
        for k3 in range(3):
            fm_load(cw[:, k3, :], "cw", conv_w[i, k3, :])
        fm_load(cw[:, 3, :], "cw", conv_b[i, :])
        for s_ in range(NS):
            for r_ in range(2):
                fm_load(cst[:, :, r_, s_], "cst", st_conv[i, s_, r_, :])
        op("dve", lambda e: e.memset(uprev[:], 0.0), writes=[("uprev", jj_) for jj_ in range(NUC)])
        for t in range(NT):
            segs = segs_of(t)
            xT_load(t, segs)
            if i % 2 == 0:
                mixer_out_attn(li, t, segs)
            else:
                mixer_out_ssm(li, t, segs)
            if dbg and i == 0:
                dma("sp", dbg_m[:, :, t * TW:(t + 1) * TW].rearrange("k p t -> p k t"), mT[:, :, 0:TW], reads=["mT"])
                if len(segs) > 1:
                    dma("sp", dbg_m[:, :, T:TC].rearrange("k p t -> p k t"), mT[:, :, TW:TWS], reads=["mT"])
            postnorm_residual(i, 1, segs)
            if dbg and i == 0:
                dma("sp", dbg_x1[:, :, t * TW:(t + 1) * TW].rearrange("k p t -> p k t"), xT[:, :, 0:TW], reads=["xT"])
                if len(segs) > 1:
                    dma("sp", dbg_x1[:, :, T:TC].rearrange("k p t -> p k t"), xT[:, :, TW:TWS], reads=["xT"])
            prenorm(i, 2, segs)
            ffn(i, t, segs)
            postnorm_residual(i, 3, segs)
            if i + 1 < nlayers:
                xT_store(t, segs)
                tile_tail(i + 1, t, segs)
            else:
                out_y(t, segs)
        for r_ in range(2):
            fm_store(uprev[:, :, r_], [("uprev", jj_) for jj_ in range(NUC)], conv_p[i, r_, :])
            for s_ in range(NS):
                fm_store(csout[:, :, r_, s_], "csout", conv_s[i, s_, r_, :])

    def _unused():
        pass

    assert wr.consumed == len(wr.plan), (wr.consumed, len(wr.plan))
    kb.barrier()
    kb.es.close()
    return nc


def _consts():
    ident = np.eye(128, dtype=np.float32)
    k = np.arange(128)[:, None]
    q = np.arange(128)[None, :]
    m = np.zeros((128, 7, 128), np.float32)
    m[:, 0] = (q >= k)
    m[:, 1] = (q <= k)
    r4 = ((q - k) % 4 == 0)
    m[:, 2] = r4 & (q >= k)
    m[:, 3] = r4
    m[:, 4] = r4 & (q <= k)
    r16 = ((q - k) % 16 == 0)
    m[:, 5] = r16 & (q >= k)
    m[:, 6] = r16
    half = 16
    inv = (np.float32(500000.0) ** (-(np.arange(half, dtype=np.float32) * np.float32(2.0 / 32)))).astype(np.float32)
    pos = np.zeros((128, 17), np.float32)
    for b in range(16):
        pos[:, b] = b * 128 + np.arange(128)
    pos[:, 16] = PAST
    ang = (pos[:, :, None] * inv[None, None, :]).astype(np.float32)
    rope = np.concatenate([np.cos(ang), np.sin(ang)], axis=-1).astype(np.float32)
    mbq = np.where(m.transpose(2, 1, 0) > 0.5, 0.0, -30000.0).astype(np.float32)
    gmask = (np.arange(128)[:, None] // 16 == np.arange(8)[None, :]).astype(np.float32)
    return ident, m, rope, np.ascontiguousarray(mbq), gmask


_NC_CACHE = {}


def kernel(**inp):
    f = lambda a: np.ascontiguousarray(np.asarray(a, dtype=np.float32))
    ident, masks, rope, mbq, gmask = _consts()
    if "nc" not in _NC_CACHE:
        _NC_CACHE["nc"] = build(4)
    nc = _NC_CACHE["nc"]
    shared = {
        "norm_g": f(inp["norm_g"]).reshape(16, D), "w_qkv": f(inp["w_qkv"]), "w_o": f(inp["w_attn_o"]),
        "w_in": f(inp["w_ssm_in"]), "lam_re": f(inp["lambda_re"]), "lam_im": f(inp["lambda_im"]),
        "log_dt": f(inp["log_dt"]), "b_re": f(inp["b_re"]), "b_im": f(inp["b_im"]), "c_re": f(inp["c_re"]),
        "c_im": f(inp["c_im"]), "d_skip": f(inp["d_skip"]), "w_glu": f(inp["w_glu"]), "w_up": f(inp["w_up"]),
        "conv_w": f(inp["conv_w"]), "conv_b": f(inp["conv_b"]), "w_down": f(inp["w_down"]),
        "c_ident": ident, "c_masks": masks, "c_rope": rope, "c_mbq": mbq, "c_gmask": gmask,
    }
    xp = f(inp["x_prompt"])
    xs = f(inp["x_sample"]).reshape(8, D)
    ck = [f(inp["cache_kv_g0"]), f(inp["cache_kv_g1"]), f(inp["cache_kv_g2"])]
    ss = f(inp["state_ssm"])
    sc = f(inp["state_conv"])
    in_maps = []
    for c in range(NCORES):
        m = dict(shared)
        m["x_p"] = xp[c]
        sl = slice(2 * c, 2 * c + 2)
        m["x_s"] = np.ascontiguousarray(xs[sl])
        for g in range(3):
            m["ckv%d" % g] = np.ascontiguousarray(ck[g][:, sl].reshape(2, NS, WIN[g], 2048))
        m["st_ssm"] = np.ascontiguousarray(ss[:, sl].reshape(2, NS, 128 * 64 * 2))
        m["st_conv"] = np.ascontiguousarray(sc[:, sl])
        in_maps.append(m)
    res = run_bass_kernel_spmd(nc, in_maps, core_ids=list(range(NCORES)))
    R = res.results
    y_prompt = np.stack([R[c]["y_p"] for c in range(NCORES)], 0)
    y_sample = np.concatenate([R[c]["y_s"] for c in range(NCORES)], 0).reshape(8, 1, D)
    outs = [y_prompt, y_sample]
    for g in range(3):
        keep = min(WIN[g], T)
        outs.append(np.stack([R[c]["kvp%d" % g] for c in range(NCORES)], 1).reshape(2, 4, keep, 2, 8, 128))
        outs.append(np.concatenate([R[c]["kvs%d" % g] for c in range(NCORES)], 1).reshape(2, 8, 1, 2, 8, 128))
    outs.append(np.stack([R[c]["ssm_p"] for c in range(NCORES)], 1).reshape(2, 4, 128, 64, 2))
    outs.append(np.concatenate([R[c]["ssm_s"] for c in range(NCORES)], 1).reshape(2, 8, 128, 64, 2))
    outs.append(np.stack([R[c]["conv_p"] for c in range(NCORES)], 1).reshape(4, 4, 2, 2 * DFF))
    outs.append(np.concatenate([R[c]["conv_s"] for c in range(NCORES)], 1).reshape(4, 8, 2, 2 * DFF))
    return tuple(np.ascontiguousarray(o.astype(np.float32)) for o in outs)
```

```python
import contextlib
import math
import numpy as np
import concourse.bass as bass
import concourse.mybir as mybir
from concourse.bass_utils import run_bass_kernel_spmd

F32 = mybir.dt.float32
BF16 = mybir.dt.bfloat16
AF = mybir.ActivationFunctionType
ALU = mybir.AluOpType
AX = mybir.AxisListType

T = 2048
D = 2048
KC = 16
NS = 2
TC = T + NS
TW = 512
NT = T // TW
TWS = TW + NS
DFF = 5504
NFC = 43
NUC = 86
NCORES = 4
PAST = 16384
WIN = (128, 512, 2048)
DIL = (1, 4, 16)
EPS = 1e-6
SLOT = 5504
NSLOT = 4
SCALE = 128 ** -0.5
GELU_C = 2.0 * math.sqrt(2.0 / math.pi)


class KB:
    NDS = 12

    def __init__(self):
        self.nc = bass.Bass("TRN2", target_bir_lowering=False)
        nc = self.nc
        self.es = contextlib.ExitStack()
        self.engs = {"pe": nc.tensor, "dve": nc.vector, "act": nc.scalar, "pool": nc.gpsimd, "sp": nc.sync}
        self.semh = {}
        for k in self.engs:
            self.semh[("e", k)] = self.es.enter_context(nc.semaphore("se_" + k))
        self.cnt = {k: 0 for k in self.engs}
        self.waited = {k: {} for k in self.engs}
        self.pending = {k: ([], []) for k in self.engs}
        self.lastw = {}
        self.readers = {}
        self.dcnt = {}
        self.dnext = {}
        for q in ("sp", "pool", "act"):
            for i in range(self.NDS):
                self.semh[("d", q, i)] = self.es.enter_context(nc.semaphore("sd_%s_%d" % (q, i)))
                self.dcnt[("d", q, i)] = 0
            self.dnext[q] = 0
        self.nins = 0

    def sb(self, name, shape, dt, es=None):
        self._uid = getattr(self, "_uid", 0) + 1
        return (es or self.es).enter_context(self.nc.sbuf_tensor("%s_%d" % (name, self._uid), list(shape), dt))

    def ps(self, name, shape, dt, es=None):
        return (es or self.es).enter_context(self.nc.psum_tensor(name, list(shape), dt))

    def dram(self, name, shape, dt, kind):
        return self.nc.dram_tensor(name, list(shape), dt, kind=kind).ap()

    def _deps(self, reads, writes):
        deps = {}
        for b in reads:
            lw = self.lastw.get(b)
            if lw is not None:
                deps[lw[0]] = max(deps.get(lw[0], 0), lw[1])
        for b in writes:
            lw = self.lastw.get(b)
            if lw is not None:
                deps[lw[0]] = max(deps.get(lw[0], 0), lw[1])
            for sk, v in self.readers.get(b, {}).items():
                deps[sk] = max(deps.get(sk, 0), v)
        return deps

    def _wait(self, eng, deps):
        e = self.engs[eng]
        w = self.waited[eng]
        for sk, v in deps.items():
            if eng == "pe" and sk == ("e", "pe"):
                continue
            if w.get(sk, 0) < v:
                e.wait_ge(self.semh[sk], v)
                w[sk] = v
                self.nins += 1

    def fence(self, eng, reads=(), writes=()):
        self._wait(eng, self._deps(reads, writes))

    def mark(self, eng, writes=(), reads=()):
        sk = ("e", eng)
        v = self.cnt[eng]
        for b in writes:
            self.lastw[b] = (sk, v)
            self.readers[b] = {}
        for b in reads:
            self.readers.setdefault(b, {})[sk] = v

    def poison(self, keys):
        allr = {("e", k): v for k, v in self.cnt.items() if v > 0}
        for sk, v in self.dcnt.items():
            if v > 0:
                allr[sk] = v
        for b in keys:
            self.readers[b] = dict(allr)

    def opx(self, eng, fn, after=()):
        sk = ("e", eng)
        v = max(after) if after else 0
        if v > self.waited[eng].get(sk, 0):
            self.engs[eng].wait_ge(self.semh[sk], v)
            self.waited[eng][sk] = v
            self.nins += 1
        ins = fn(self.engs[eng])
        self.nins += 1
        self.cnt[eng] += 1
        ins.then_inc(self.semh[sk], 1)
        return self.cnt[eng]

    def op(self, eng, fn, reads=(), writes=(), signal=True):
        self._wait(eng, self._deps(reads, writes))
        ins = fn(self.engs[eng])
        self.nins += 1
        pr, pw = self.pending[eng]
        pr.extend(reads)
        pw.extend(writes)
        if signal:
            self.cnt[eng] += 1
            sk = ("e", eng)
            ins.then_inc(self.semh[sk], 1)
            v = self.cnt[eng]
            for b in pw:
                self.lastw[b] = (sk, v)
                self.readers[b] = {}
            for b in pr:
                if b not in pw:
                    self.readers.setdefault(b, {})[sk] = v
            self.pending[eng] = ([], [])
        return ins

    def dma(self, q, out, in_, reads=(), writes=(), **kw):
        self._wait(q, self._deps(reads, writes))
        i = self.dnext[q]
        self.dnext[q] = (i + 1) % self.NDS
        sk = ("d", q, i)
        prev = self.dcnt[sk]
        if prev > 0 and self.waited[q].get(sk, 0) < prev:
            self.engs[q].wait_ge(self.semh[sk], prev)
            self.waited[q][sk] = prev
        ins = self.engs[q].dma_start(out=out, in_=in_, **kw)
        self.nins += 1
        self.dcnt[sk] += 16
        v = self.dcnt[sk]
        ins.then_inc(self.semh[sk], 16)
        for b in writes:
            self.lastw[b] = (sk, v)
            self.readers[b] = {}
        for b in reads:
            if b not in writes:
                self.readers.setdefault(b, {})[sk] = v
        return ins

    def barrier(self, engines=None):
        cur = {}
        for k in self.engs:
            assert not self.pending[k][0] and not self.pending[k][1]
            cur[("e", k)] = self.cnt[k]
        for sk, v in self.dcnt.items():
            cur[sk] = v
        for eng in (engines or list(self.engs)):
            for sk, v in cur.items():
                if sk == ("e", eng):
                    continue
                if v > 0 and self.waited[eng].get(sk, 0) < v:
                    self.engs[eng].wait_ge(self.semh[sk], v)
                    self.waited[eng][sk] = v
                    self.nins += 1
        if engines is None:
            self.lastw = {}
            self.readers = {}


class WRing:
    def __init__(self, kb, tensor):
        self.kb = kb
        self.t = tensor
        self.plan = []
        self.emitted = 0
        self.consumed = 0

    def add(self, tag, pieces):
        self.plan.append((tag, pieces))

    def _emit(self, k):
        tag, pieces = self.plan[k]
        s = k % NSLOT
        for (off, shp, src) in pieces:
            n = int(np.prod(shp))
            dst = self.t[:, s, off:off + n]
            if len(shp) == 2:
                dst = dst.rearrange("p (a b) -> p a b", a=shp[0])
            self.kb.dma("pool", dst, src, writes=[("w", s)])

    def get(self, tag):
        k = self.consumed
        assert self.plan[k][0] == tag, (self.plan[k][0], tag)
        while self.emitted < min(len(self.plan), k + NSLOT):
            self._emit(self.emitted)
            self.emitted += 1
        self.consumed += 1
        s = k % NSLOT
        return self.t[:, s, :], ("w", s)


def build(nlayers=4, dbg=False):
    kb = KB()
    nc = kb.nc
    op, dma = kb.op, kb.dma

    def din(name, shape, dt=F32):
        return kb.dram(name, shape, dt, "ExternalInput")

    def dout(name, shape, dt=F32):
        return kb.dram(name, shape, dt, "ExternalOutput")

    x_p = din("x_p", [T, D])
    x_s = din("x_s", [NS, D])
    ckv = [din("ckv%d" % g, [2, NS, WIN[g], 2 * 1024]) for g in range(3)]
    st_ssm = din("st_ssm", [2, NS, 128 * 64 * 2])
    st_conv = din("st_conv", [4, NS, 2, 2 * DFF])
    norm_g = din("norm_g", [16, D])
    w_qkv = din("w_qkv", [2, D, 9216])
    w_o = din("w_o", [2, 1024, D])
    w_in = din("w_in", [2, D, D])
    lam_re = din("lam_re", [2, 128, 64])
    lam_im = din("lam_im", [2, 128, 64])
    log_dt = din("log_dt", [2, 128])
    b_re = din("b_re", [2, 128, 64, 16])
    b_im = din("b_im", [2, 128, 64, 16])
    c_re = din("c_re", [2, 128, 16, 64])
    c_im = din("c_im", [2, 128, 16, 64])
    d_skip = din("d_skip", [2, D])
    w_glu = din("w_glu", [2, D, 2 * D])
    w_up = din("w_up", [4, D, 2 * DFF])
    conv_w = din("conv_w", [4, 3, 2 * DFF])
    conv_b = din("conv_b", [4, 2 * DFF])
    w_down = din("w_down", [4, DFF, D])
    c_ident = din("c_ident", [128, 128])
    c_masks = din("c_masks", [128, 7, 128])
    c_rope = din("c_rope", [128, 17, 32])
    c_mbq = din("c_mbq", [128, 7, 128])
    c_gmask = din("c_gmask", [128, 8])

    y_p = dout("y_p", [T, D])
    y_s = dout("y_s", [NS, D])
    kvp = [dout("kvp%d" % g, [2, min(WIN[g], T), 2048]) for g in range(3)]
    kvs = [dout("kvs%d" % g, [2, NS, 2048]) for g in range(3)]
    ssm_p = dout("ssm_p", [2, 128 * 64 * 2])
    ssm_s = dout("ssm_s", [2, NS, 128 * 64 * 2])
    conv_p = dout("conv_p", [4, 2, 2 * DFF])
    conv_s = dout("conv_s", [4, NS, 2, 2 * DFF])

    IK = "ExternalOutput" if dbg else "Internal"
    xT_s = kb.dram("xT_s", [KC, 128, TC], F32, "Internal")
    qT_s = kb.dram("qT_s", [3, 8, 128, T], BF16, IK)
    kT_s = kb.dram("kT_s", [3, 8, 128, T], BF16, IK)
    v_s = kb.dram("v_s", [3, T, 1024], BF16, IK)
    mT_s = kb.dram("mT_s", [8, 128, TC], BF16, IK)
    dbg_x1 = kb.dram("dbg_x1", [KC, 128, TC], F32, "ExternalOutput") if dbg else None
    dbg_m = kb.dram("dbg_m", [KC, 128, TC], F32, "ExternalOutput") if dbg else None
    uT_s = kb.dram("uT_s", [KC, 128, TC], F32, "Internal")
    zT_s = kb.dram("zT_s", [KC, 128, TC], BF16, "Internal")

    ident_f = kb.sb("ident_f", [128, 128], F32)
    ident_b = kb.sb("ident_b", [128, 128], BF16)
    ones_b = kb.sb("ones_b", [128, 128], BF16)
    rope = kb.sb("rope", [128, 17, 32], F32)
    gsc = kb.sb("gsc", [128, KC, 16], F32)
    wring_t = kb.sb("wring", [128, NSLOT, SLOT], BF16)
    wr = WRing(kb, wring_t)
    sqT = kb.sb("sqT", [128, 24, NS], BF16)
    skT = kb.sb("skT", [128, 24, NS], BF16)
    svT = kb.sb("svT", [128, 24, NS], F32)
    epsb = kb.sb("epsb", [128, 1], F32)

    pbank = [kb.ps("pb%d" % i, [128, 512], F32) for i in range(8)]
    prr = [0]

    def psum(lo=0, hi=6):
        i = lo + prr[0] % (hi - lo)
        prr[0] += 1
        return pbank[i], ("ps", i)

    dma("sp", ident_f[:], c_ident[:, :], writes=["ident_f"])
    dma("pool", ident_b[:], c_ident[:, :], writes=["ident_b"])
    dma("sp", rope[:], c_rope[:, :, :], writes=["rope"])
    op("dve", lambda e: e.memset(ones_b[:], 1.0), writes=["ones_b"])
    op("dve", lambda e: e.memset(epsb[:], EPS), writes=["epsb"])

    es0 = contextlib.ExitStack()
    gtmp = kb.sb("gtmp", [16, D], F32, es0)
    dma("sp", gtmp[:], norm_g[:, :], writes=["gtmp"])
    for kc in range(KC):
        pb, pk = psum()
        op("pe", lambda e: e.transpose(pb[:, 0:16], gtmp[0:16, kc * 128:(kc + 1) * 128], ident_f[0:16, 0:16]),
           reads=["gtmp", "ident_f"], writes=[pk])
        op("act", lambda e: e.copy(gsc[:, kc, :], pb[:, 0:16]), reads=[pk], writes=["gsc"])
    kb.barrier()
    es0.close()

    def gs(i, j, kc):
        return gsc[:, kc, 4 * i + j:4 * i + j + 1]

    def plan_qkv(li):
        for ct in range(36):
            src = w_qkv[li, :, ct * 256:(ct + 1) * 256].rearrange("(k p) c -> p k c", p=128)
            wr.add(("qkv", li, ct), [(0, [KC, 256], src)])

    def plan_win(li):
        for dc in range(KC):
            src = w_in[li, :, dc * 128:(dc + 1) * 128].rearrange("(k p) c -> p k c", p=128)
            wr.add(("win", li, dc), [(0, [KC, 128], src)])

    def plan_tile(i):
        li = i // 2
        if i % 2 == 0:
            for dc in range(KC):
                src = w_o[li, :, dc * 128:(dc + 1) * 128].rearrange("(k p) c -> p k c", p=128)
                wr.add(("wo", li, dc), [(0, [8, 128], src)])
        else:
            for dc in range(KC):
                sv = w_glu[li, :, dc * 128:(dc + 1) * 128].rearrange("(k p) c -> p k c", p=128)
                sg = w_glu[li, :, D + dc * 128:D + (dc + 1) * 128].rearrange("(k p) c -> p k c", p=128)
                wr.add(("glu", li, dc), [(0, [KC, 128], sv), (KC * 128, [KC, 128], sg)])
        for j in range(NFC):
            sg = w_up[i, :, j * 128:(j + 1) * 128].rearrange("(k p) c -> p k c", p=128)
            sv = w_up[i, :, DFF + j * 128:DFF + (j + 1) * 128].rearrange("(k p) c -> p k c", p=128)
            wr.add(("up", i, j), [(0, [KC, 128], sg), (KC * 128, [KC, 128], sv)])
        for dc in range(KC):
            src = w_down[i, :, dc * 128:(dc + 1) * 128].rearrange("(j p) c -> p j c", p=128)
            wr.add(("down", i, dc), [(0, [NFC, 128], src)])
        if i + 1 < nlayers:
            if (i + 1) % 2 == 0:
                plan_qkv((i + 1) // 2)
            else:
                plan_win((i + 1) // 2)

    for t in range(NT):
        plan_qkv(0)
    for i in range(nlayers):
        for t in range(NT):
            plan_tile(i)

    xT = kb.sb("xT", [128, KC, TWS], F32)
    mT = kb.sb("mT", [128, KC, TWS], F32)
    hT = kb.sb("hT", [128, KC, TWS], BF16)
    aT = kb.sb("aT", [128, NFC, TWS], BF16)
    sq = [kb.sb("sq%d" % i, [128, TW], BF16) for i in range(2)]
    rstd = kb.sb("rstd", [128, TWS], F32)
    rtmp = kb.sb("rtmp", [128, TWS], F32)
    cgt = [kb.sb("cg%d" % i, [128, TW], F32) for i in range(2)]
    cvt = [kb.sb("cv%d" % i, [128, TW], F32) for i in range(2)]
    sgt = [kb.sb("sg%d" % i, [128, TW], F32) for i in range(2)]
    cw = kb.sb("cw", [128, 4, NUC], F32)
    uprev = kb.sb("uprev", [128, NUC, 2], F32)
    cst = kb.sb("cst", [128, NUC, 2, NS], F32)
    csout = kb.sb("csout", [128, NUC, 2, NS], F32)
    vtmp = kb.sb("vtmp", [NUC, 128], F32)
    vtmp2 = kb.sb("vtmp2", [NUC, 128], F32)
    rr = {"sq": 0, "cg": 0, "stg": 0}

    mT_flat = mT[:].rearrange("p a b -> p (a b)")
    aT_flat = aT[:].rearrange("p a b -> p (a b)")

    def segs_of(t):
        s = [(0, TW, 1)]
        if t == NT - 1:
            s.append((TW, NS, NS))
        return s

    def fm_load(dst_ap, dst_key, dram_row):
        dma("sp", vtmp[:], dram_row.rearrange("(c p) -> c p", p=128), writes=["vtmp"])
        pb, pk = psum()
        op("pe", lambda e: e.transpose(pb[:, 0:NUC], vtmp[:, :], ident_f[0:NUC, 0:NUC]),
           reads=["vtmp", "ident_f"], writes=[pk])
        op("act", lambda e: e.copy(dst_ap, pb[:, 0:NUC]), reads=[pk], writes=[dst_key])

    def fm_store(src_ap, src_key, dram_row):
        op("act", lambda e: e.copy(rtmp[:, 0:NUC], src_ap), reads=(src_key if isinstance(src_key, list) else [src_key]), writes=["rtmp"])
        pb, pk = psum()
        op("pe", lambda e: e.transpose(pb[0:NUC, 0:128], rtmp[:, 0:NUC], ident_f[:, :]),
           reads=["rtmp", "ident_f"], writes=[pk])
        op("act", lambda e: e.copy(vtmp2[:, :], pb[0:NUC, 0:128]), reads=[pk], writes=["vtmp2"])
        dma("sp", dram_row.rearrange("(c p) -> c p", p=128), vtmp2[:], reads=["vtmp2"])

    def stats_begin():
        return psum(7, 8)

    def stats_add(pst, src_ap, src_key, c0, n, first, last, defer=None):
        pb, pk = pst
        s = sq[rr["sq"] % 2]
        sk = "sq%d" % (rr["sq"] % 2)
        rr["sq"] += 1
        op("act", lambda e: e.activation(out=s[:, 0:n], in_=src_ap, func=AF.Square), reads=[src_key], writes=[sk])

        def mm():
            op("pe", lambda e: e.matmul(pb[:, 0:n], ones_b[:, :], s[:, 0:n], start=first, stop=last),
               reads=[sk, "ones_b"], writes=[pk], signal=True)
        if defer is None:
            mm()
        else:
            defer.append(mm)

    def stats_end(pst, c0, n):
        pb, pk = pst
        op("act", lambda e: e.activation(out=rtmp[:, c0:c0 + n], in_=pb[:, 0:n], func=AF.Sqrt, bias=epsb[:, 0:1], scale=1.0 / D),
           reads=[pk, "epsb"], writes=["rtmp"])
        op("dve", lambda e: e.reciprocal(rstd[:, c0:c0 + n], rtmp[:, c0:c0 + n]), reads=["rtmp"], writes=["rstd"])

    def prenorm(i, j, segs):
        for (c0, n, S) in segs:
            pst = stats_begin()
            for kc in range(KC):
                stats_add(pst, xT[:, kc, c0:c0 + n], "xT", c0, n, kc == 0, kc == KC - 1)
            stats_end(pst, c0, n)
            for kc in range(KC):
                op("dve", lambda e: e.scalar_tensor_tensor(out=hT[:, kc, c0:c0 + n], in0=xT[:, kc, c0:c0 + n],
                                                           scalar=gs(i, j, kc), in1=rstd[:, c0:c0 + n],
                                                           op0=ALU.mult, op1=ALU.mult),
                   reads=["xT", "rstd", "gsc"], writes=["hT"])

    def postnorm_residual(i, j, segs):
        for (c0, n, S) in segs:
            for kc in range(KC):
                op("dve", lambda e: e.scalar_tensor_tensor(out=mT[:, kc, c0:c0 + n], in0=mT[:, kc, c0:c0 + n],
                                                           scalar=gs(i, j, kc), in1=rstd[:, c0:c0 + n],
                                                           op0=ALU.mult, op1=ALU.mult),
                   reads=["mT", "rstd", "gsc"], writes=["mT"])
                op("dve", lambda e: e.tensor_tensor(out=xT[:, kc, c0:c0 + n], in0=xT[:, kc, c0:c0 + n],
                                                    in1=mT[:, kc, c0:c0 + n], op=ALU.add),
                   reads=["mT", "xT"], writes=["xT"])

    def xT_store(t, segs):
        g0 = t * TW
        dma("sp", xT_s[:, :, g0:g0 + TW].rearrange("k p t -> p k t"), xT[:, :, 0:TW], reads=["xT"], writes=["xT_s%d" % t])
        if len(segs) > 1:
            dma("sp", xT_s[:, :, T:TC].rearrange("k p t -> p k t"), xT[:, :, TW:TWS], reads=["xT"], writes=["xT_ss"])

    def xT_load(t, segs):
        g0 = t * TW
        dma("sp", xT[:, :, 0:TW], xT_s[:, :, g0:g0 + TW].rearrange("k p t -> p k t"), reads=["xT_s%d" % t], writes=["xT"])
        if len(segs) > 1:
            dma("sp", xT[:, :, TW:TWS], xT_s[:, :, T:TC].rearrange("k p t -> p k t"), reads=["xT_ss"], writes=["xT"])

    stg_f = mT_flat
    stg_b = aT_flat
    rt = kb.sb("ropetmp", [128, 4, 2, 16], F32)

    def qkv_phase(li, t, segs):
        blocks = [(b, 128) for b in range(4)]
        if len(segs) > 1:
            blocks.append((4, NS))
        ffk = ["cg0", "cg1", "cv0", "cv1", "sg0", "sg1"]
        for eng_ in ("act", "dve", "pe", "sp"):
            kb.fence(eng_, writes=ffk)
        backlog = []
        stf_t = [cgt[0], cgt[1], cvt[0], cvt[1]]
        stb_t = [sgt[0][:, :].bitcast(BF16), sgt[1][:, :].bitcast(BF16)]
        for ct in range(36):
            wv, wk = wr.get(("qkv", li, ct))
            w3 = wv[:, 0:KC * 256].rearrange("p (k c) -> p k c", k=KC)
            s = ct // 12
            g = (ct % 12) // 4
            hp = ct % 4
            for (bl, m) in blocks:
                gb = t * 4 + bl if bl < 4 else 16
                while len(backlog) >= (1 if (s == 2 and bl < 4) else 2):
                    backlog.pop(0)()
                pb, pk = psum()
                for kc in range(KC):
                    lhs = hT[:, kc, bl * 128:bl * 128 + m] if bl < 4 else hT[:, kc, TW:TWS]
                    op("pe", lambda e: e.matmul(pb[0:m, 0:256], lhs, w3[:, kc, :], start=(kc == 0), stop=(kc == KC - 1)),
                       reads=["hT", wk], writes=[pk], signal=(kc == KC - 1))
                slot = rr["stg"] % 4
                rr["stg"] += 1
                sf = stf_t[slot][:, 0:256]
                sfk = ("sf", slot)
                op("act", lambda e: e.copy(sf[0:m, :], pb[0:m, 0:256]), reads=[pk], writes=[sfk])
                sf3 = sf.rearrange("p (h d) -> p h d", h=2)
                if s < 2:
                    cosb = rope[0:m, gb, 0:16].unsqueeze(1).broadcast_to([m, 2, 16])
                    sinb = rope[0:m, gb, 16:32].unsqueeze(1).broadcast_to([m, 2, 16])
                    x1 = sf3[0:m, :, 0:16]
                    x2 = sf3[0:m, :, 16:32]
                    t1, t2, t3, t4 = (rt[0:m, k, :, :] for k in range(4))
                    op("dve", lambda e: e.tensor_tensor(out=t1, in0=x1, in1=cosb, op=ALU.mult), reads=[sfk, "rope"], writes=["rt1"])
                    op("dve", lambda e: e.tensor_tensor(out=t2, in0=x2, in1=sinb, op=ALU.mult), reads=[sfk, "rope"], writes=["rt2"])
                    op("dve", lambda e: e.tensor_tensor(out=t3, in0=x2, in1=cosb, op=ALU.mult), reads=[sfk, "rope"], writes=["rt3"])
                    op("dve", lambda e: e.tensor_tensor(out=t4, in0=x1, in1=sinb, op=ALU.mult), reads=[sfk, "rope"], writes=["rt4"])
                    op("dve", lambda e: e.tensor_tensor(out=x1, in0=t1, in1=t2, op=ALU.subtract), reads=["rt1", "rt2", "rt3", "rt4"], writes=[sfk])
                    op("dve", lambda e: e.tensor_tensor(out=x2, in0=t3, in1=t4, op=ALU.add), reads=["rt3", "rt4"], writes=[sfk])
                if s >= 1:
                    half = (s - 1) * 1024 + hp * 256
                    if bl < 4:
                        keep = min(WIN[g], T)
                        row0 = gb * 128 - (T - keep)
                        if row0 >= 0:
                            dma("sp", kvp[g][li, row0:row0 + 128, half:half + 256], sf[:, :], reads=[sfk])
                    else:
                        dma("sp", kvs[g][li, :, half:half + 256], sf[0:m, :], reads=[sfk])
                sb_ = stb_t[0][:, slot * 256:(slot + 1) * 256]
                sbk = ("sb", slot)
                tbk = ("tb", slot)
                if not (bl == 4 and s == 2):
                    op("act", lambda e: e.copy(sb_[0:m, :], sf[0:m, :]), reads=[sfk], writes=[sbk])
                if s == 2 and bl < 4:
                    dma("sp", v_s[g, gb * 128:(gb + 1) * 128, hp * 256:(hp + 1) * 256], sb_[:, :], reads=[sbk], writes=["v_s"])
                    continue
                def make_back(s=s, g=g, hp=hp, bl=bl, m=m, gb=gb, slot=slot, sf=sf, sfk=sfk, sb_=sb_, sbk=sbk, tbk=tbk):
                    def back():
                        if s == 2:
                            pt, ptk = psum()
                            for hh in range(2):
                                op("pe", lambda e: e.transpose(pt[:, hh * NS:(hh + 1) * NS], sf[0:m, hh * 128:(hh + 1) * 128], ident_f[0:m, 0:m]),
                                   reads=[sfk, "ident_f"], writes=[ptk], signal=(hh == 1))
                            op("dve", lambda e: e.tensor_copy(out=svT[:, g * 8 + 2 * hp:g * 8 + 2 * hp + 2, :],
                                                              in_=pt[:, 0:2 * NS].rearrange("p (h s) -> p h s", h=2)),
                               reads=[ptk], writes=["svT"])
                            return
                        pt, ptk = psum()
                        ptb = pt[:, 0:256].bitcast(BF16)
                        for hh in range(2):
                            op("pe", lambda e: e.transpose(ptb[:, hh * 128:hh * 128 + m], sb_[0:m, hh * 128:(hh + 1) * 128], ident_b[0:m, 0:m]),
                               reads=[sbk, "ident_b"], writes=[ptk], signal=(hh == 1))
                        if bl < 4:
                            tb = stb_t[1][:, slot * 256:(slot + 1) * 256]
                            op("dve", lambda e: e.tensor_copy(out=tb, in_=ptb[:, 0:256]), reads=[ptk], writes=[tbk])
                            dst = (qT_s if s == 0 else kT_s)[g, 2 * hp:2 * hp + 2, :, gb * 128:(gb + 1) * 128].rearrange("h p t -> p h t")
                            dma("sp", dst, tb.rearrange("p (h t) -> p h t", h=2), reads=[tbk], writes=["qT_s" if s == 0 else "kT_s"])
                        else:
                            dstt = sqT if s == 0 else skT
                            op("dve", lambda e: e.tensor_copy(out=dstt[:, g * 8 + 2 * hp:g * 8 + 2 * hp + 2, :],
                                                              in_=ptb[:, 0:256].rearrange("p (h t) -> p h t", h=2)[:, :, 0:NS]),
                               reads=[ptk], writes=["sqT" if s == 0 else "skT"])
                    return back
                backlog.append(make_back())
        while backlog:
            backlog.pop(0)()

    _qkv_inner = qkv_phase

    def qkv_phase(li, t, segs):
        _qkv_inner(li, t, segs)
        kb.poison(["cg0", "cg1", "cv0", "cv1", "sg0", "sg1"])

    def attn_phase(li):
        kb.barrier()
        es = contextlib.ExitStack()
        aTf = aT[:].rearrange("p a b -> p (a b)")
        hTf = hT[:].rearrange("p a b -> p (a b)")
        xTb = xT[:].rearrange("p a b -> p (a b)").bitcast(BF16)
        mTf = mT[:].rearrange("p a b -> p (a b)")
        QN = 3 * T
        qTt = [aTf[:, i * QN:(i + 1) * QN].rearrange("p (g t) -> p g t", g=3) for i in range(2)]
        kTt = [aTf[:, 2 * QN:3 * QN].rearrange("p (g t) -> p g t", g=3), hTf[:, 0:QN].rearrange("p (g t) -> p g t", g=3)]
        vt = [xTb[:, i * QN:(i + 1) * QN].rearrange("p (g b d) -> p g b d", g=3, b=16) for i in range(2)]
        pT = [hTf[:, QN + i * 512:QN + (i + 1) * 512] for i in range(3)]
        mh = [xTb[:, 2 * QN + i * TC:2 * QN + (i + 1) * TC] for i in range(2)]
        mbq = aTf[:, 3 * QN:3 * QN + 1792].bitcast(F32).rearrange("p (m k) -> p m k", m=7)
        smk = mTf[:, 0:2048]
        junk = mTf[:, 2048:3072].bitcast(BF16)
        acc2 = mTf[:, 4096:4224]
        D3 = mTf[:, 3072:3456]
        cbc = mTf[:, 3456:3840]
        acc = mTf[:, 3840:4096]
        ones_f = kb.sb("ones_f", [128, 128], F32, es)
        masks = kb.sb("masks", [128, 7, 128], BF16, es)
        dma("pool", masks[:], c_masks[:, :, :], writes=["masks"])
        l0 = kb.sb("l0", [128, 16, 3], F32, es)
        colm = kb.sb("colm", [128, 8, 3], F32, es)
        dma("sp", mbq, c_mbq[:, :, :], writes=["mbq"])
        op("dve", lambda e: e.memset(ones_f[:], 1.0), writes=["ones_f"])

        def load_head(h):
            b = h % 2
            for g in range(3):
                dma("sp", qTt[b][:, g, :], qT_s[g, h, :, :], reads=["qT_s"], writes=[("qTt", b, g)])
                dma("sp", kTt[b][:, g, :], kT_s[g, h, :, :], reads=["kT_s"], writes=[("kTt", b, g)])
                dma("sp", vt[b][:, g, :, :], v_s[g, :, h * 128:(h + 1) * 128].rearrange("(b p) d -> p b d", p=128),
                    reads=["v_s"], writes=[("vt", b, g)])

        smk2 = [mTf[:, 0:2048], mTf[:, 4224:6272]]
        D32 = [mTf[:, 3072:3456], mTf[:, 6272:6656]]
        cbc2 = [mTf[:, 3456:3840], mTf[:, 6656:7040]]
        acc_2 = [mTf[:, 3840:4096], mTf[:, 7040:7296]]
        acc22 = [mTf[:, 4096:4224], mTf[:, 7296:7424]]
        colm2t = kb.sb("colm2", [128, 2, 8, 3], F32, es)
        pst = {"pidx": 0}

        def iteration(h, qb, sl):
            b = h % 2
            smk, D3, cbc, acc, acc2 = smk2[sl], D32[sl], cbc2[sl], acc_2[sl], acc22[sl]
            colm = colm2t[:, sl, :, :]
            K = lambda name: (name, sl)
            groups = []
            for g in range(3):
                nb = WIN[g] // 128
                lst = []
                for kbk in range(max(0, qb - nb), qb + 1):
                    db = qb - kbk
                    if g == 0:
                        mi = 0 if db == 0 else 1
                    elif g == 1:
                        mi = 2 if db == 0 else (4 if db == 4 else 3)
                    else:
                        mi = 5 if db == 0 else 6
                    lst.append((kbk, mi))
                groups.append(lst)
            po, pok = psum(4, 5) if sl == 0 else psum(6, 7)
            pc, pck = psum(5, 6) if sl == 0 else psum(7, 8)
            qsl = qTt[b][:, :, qb * 128:(qb + 1) * 128]
            for g in range(3):
                lst = groups[g]
                nk = len(lst) * 128
                for c4 in range(0, len(lst), 4):
                    ch = lst[c4:c4 + 4]
                    ps_, psk = psum(2, 4)
                    k0 = ch[0][0]
                    op("pe", lambda e: e.matmul(ps_[:, 0:len(ch) * 128], qsl[:, g, :], kTt[b][:, g, k0 * 128:(k0 + len(ch)) * 128],
                                                start=True, stop=True),
                       reads=[("qTt", b, g), ("kTt", b, g)], writes=[psk])
                    for ci, (kbk, mi) in enumerate(ch):
                        op("dve", lambda e: e.tensor_tensor(out=smk[:, (c4 + ci) * 128:(c4 + ci + 1) * 128], in0=ps_[:, ci * 128:(ci + 1) * 128],
                                                            in1=mbq[:, mi, :], op=ALU.add),
                           reads=[psk, "mbq"], writes=[("smk", sl, c4 + ci)])
                    yield
                smks = [("smk", sl, k_) for k_ in range(len(lst))]
                op("dve", lambda e: e.tensor_reduce(out=colm[:, 0, g:g + 1], in_=smk[:, 0:nk], axis=AX.X, op=ALU.max),
                   reads=smks, writes=[("c0", sl, g)])
                op("dve", lambda e: e.tensor_scalar(out=colm[:, 1, g:g + 1], in0=colm[:, 0, g:g + 1], scalar1=-SCALE, scalar2=None, op0=ALU.mult),
                   reads=[("c0", sl, g)], writes=[("c1", sl, g)])
                op("act", lambda e: e.activation(out=junk[:, 0:nk], in_=smk[:, 0:nk], func=AF.Exp, bias=colm[:, 1, g:g + 1], scale=SCALE,
                                                 accum_out=colm[:, 2, g:g + 1]),
                   reads=smks + [("c1", sl, g)], writes=[("c2", sl, g)])
                yield
                done = 0
                for c4 in range(0, len(lst), 4):
                    ch = lst[c4:c4 + 4]
                    pb, pk = psum(0, 2)
                    for ci, (kbk, mi) in enumerate(ch):
                        op("pe", lambda e: e.matmul(pb[:, ci * 128:(ci + 1) * 128], kTt[b][:, g, kbk * 128:(kbk + 1) * 128], qsl[:, g, :],
                                                    start=True, stop=True),
                           reads=[("kTt", b, g), ("qTt", b, g)], writes=[pk], signal=(ci == len(ch) - 1))
                    p_ = pT[pst["pidx"] % 3]
                    pk_ = "pT%d" % (pst["pidx"] % 3)
                    pst["pidx"] += 1
                    nc_ = len(ch) * 128
                    op("act", lambda e: e.activation(out=p_[:, 0:nc_], in_=pb[:, 0:nc_], func=AF.Exp, scale=SCALE), reads=[pk], writes=[pk_])
                    yield
                    for ci, (kbk, mi) in enumerate(ch):
                        op("pool", lambda e: e.tensor_tensor(out=p_[:, ci * 128:(ci + 1) * 128], in0=p_[:, ci * 128:(ci + 1) * 128],
                                                             in1=masks[:, mi, :], op=ALU.mult),
                           reads=[pk_, "masks"], writes=[pk_])
                    for ci, (kbk, mi) in enumerate(ch):
                        op("pe", lambda e: e.matmul(po[:, g * 128:(g + 1) * 128], vt[b][:, g, kbk, :], p_[:, ci * 128:(ci + 1) * 128],
                                                    start=(done == 0), stop=(done == len(lst) - 1)),
                           reads=[("vt", b, g), pk_], writes=[pok], signal=True)
                        done += 1
                    yield
            c1s = [("c1", sl, g_) for g_ in range(3)]
            c2s = [("c2", sl, g_) for g_ in range(3)]
            op("act", lambda e: e.activation(out=colm[:, 3, :], in_=colm[:, 1, :], func=AF.Exp, scale=-1.0), reads=c1s, writes=[K("c3")])
            op("dve", lambda e: e.tensor_tensor(out=colm[:, 4, :], in0=colm[:, 2, :], in1=colm[:, 3, :], op=ALU.mult), reads=c2s + [K("c3")], writes=[K("c4")])
            yield
            op("dve", lambda e: e.tensor_reduce(out=colm[:, 7, 0:1], in_=colm[:, 4, :], axis=AX.X, op=ALU.add), reads=[K("c4")], writes=[K("c7")])
            if h == 0:
                op("dve", lambda e: e.tensor_copy(out=l0[:, qb, :], in_=colm[:, 2, :]), reads=c2s, writes=[("l0", qb)])
            yield
            op("dve", lambda e: e.tensor_scalar(out=colm[:, 5, :], in0=l0[:, qb, :], scalar1=colm[:, 7, 0:1], scalar2=None, op0=ALU.mult),
               reads=[K("c7"), ("l0", qb)], writes=[K("c5")])
            yield
            op("dve", lambda e: e.reciprocal(colm[:, 5, :], colm[:, 5, :]), reads=[K("c5")], writes=[K("c5")])
            yield
            op("dve", lambda e: e.tensor_tensor(out=colm[:, 6, :], in0=colm[:, 2, :], in1=colm[:, 5, :], op=ALU.mult), reads=c2s + [K("c5")], writes=[K("c6")])
            yield
            for g in range(3):
                op("dve", lambda e: e.tensor_scalar(out=D3[:, g * 128:(g + 1) * 128], in0=ident_f[:, :], scalar1=colm[:, 6, g:g + 1], scalar2=None, op0=ALU.mult),
                   reads=[K("c6"), "ident_f"], writes=[("D3", sl, g)])
            op("pe", lambda e: e.matmul(pc[:, 0:384], ones_f[:, :], D3[:, :], start=True, stop=True), reads=["ones_f"] + [("D3", sl, g_) for g_ in range(3)], writes=[pck])
            op("act", lambda e: e.copy(cbc[:, :], pc[:, 0:384]), reads=[pck], writes=[K("cbc")])
            yield
            op("dve", lambda e: e.tensor_tensor(out=acc[:, 0:128], in0=po[:, 0:128], in1=cbc[:, 0:128], op=ALU.mult), reads=[pok, K("cbc")], writes=[K("acc0")])
            op("dve", lambda e: e.tensor_tensor(out=acc[:, 128:256], in0=po[:, 128:256], in1=cbc[:, 128:256], op=ALU.mult), reads=[pok, K("cbc")], writes=[K("acc1")])
            op("dve", lambda e: e.tensor_tensor(out=acc2, in0=po[:, 256:384], in1=cbc[:, 256:384], op=ALU.mult), reads=[pok, K("cbc")], writes=[K("acc2")])
            yield
            op("dve", lambda e: e.tensor_tensor(out=acc[:, 0:128], in0=acc[:, 0:128], in1=acc[:, 128:256], op=ALU.add), reads=[K("acc0"), K("acc1")], writes=[K("acc0")])
            yield
            op("dve", lambda e: e.tensor_tensor(out=mh[b][:, qb * 128:(qb + 1) * 128], in0=acc[:, 0:128], in1=acc2, op=ALU.add),
               reads=[K("acc0"), K("acc2")], writes=[("mh", b, qb)])

        load_head(0)
        for h in range(8):
            if h + 1 < 8:
                load_head(h + 1)
            b = h % 2
            for qb0 in range(0, 16, 2):
                gens = [iteration(h, qb0, 0), iteration(h, qb0 + 1, 1)]
                alive = [True, True]
                while any(alive):
                    for gi in range(2):
                        if alive[gi]:
                            try:
                                next(gens[gi])
                            except StopIteration:
                                alive[gi] = False
            dma("sp", mT_s[h, :, 0:T], mh[b][:, 0:T], reads=[("mh", b, q_) for q_ in range(16)], writes=["mT_s"])

        kb.barrier()
        cache = [mTf[:, i * 2048:(i + 1) * 2048] for i in range(2)]
        cb16 = [mTf[:, 4096 + i * 512:4096 + (i + 1) * 512].bitcast(BF16) for i in range(2)]
        vkeep = [mTf[:, 5120 + g * 512:5120 + (g + 1) * 512].bitcast(BF16) for g in range(3)]
        ckT = kb.sb("ckT", [128, 8, 128], BF16, es)
        qk = kb.sb("qk", [128, 24 * NS], F32, es)
        qkr = kb.sb("qkr", [1, 24 * NS], F32, es)
        srow = aTf[0:1, 0:2064].bitcast(F32).rearrange("p (h k) -> p h k", h=8)
        prow = aTf[0:1, 2064:2064 + 6192].bitcast(F32).rearrange("p (g h k) -> p g h k", g=3, h=8)
        rw = kb.sb("rw", [1, 12, 24], F32, es)
        one1 = kb.sb("one1", [1, 1], F32, es)
        pSk = kb.sb("pSk", [128, 3, 8], BF16, es)
        pnb = kb.sb("pnb", [128, 3, 8], F32, es)
        og = kb.sb("og", [128, 3, 8], F32, es)
        cbs = kb.sb("cbs", [128, 3, 8], F32, es)
        so = kb.sb("so", [128, 8], F32, es)
        sob = kb.sb("sob", [128, 8, NS], BF16, es)
        op("dve", lambda e: e.memset(one1[:], 1.0), writes=["one1"])
        op("dve", lambda e: e.tensor_tensor(out=qk[:, :], in0=sqT[:].rearrange("p a s -> p (a s)"), in1=skT[:].rearrange("p a s -> p (a s)"), op=ALU.mult),
           reads=["sqT", "skT"], writes=["qk"])
        pb, pk = psum(0, 2)
        op("pe", lambda e: e.matmul(pb[0:1, 0:24 * NS], ones_f[:, 0:1], qk[:, :], start=True, stop=True), reads=["ones_f", "qk"], writes=[pk])
        op("act", lambda e: e.copy(qkr[:, :], pb[0:1, 0:24 * NS]), reads=[pk], writes=["qkr"])
        qkr3 = qkr[:].rearrange("p (a s) -> p a s", s=NS)
        rwm, rwmm, rwl, rwe, rwt, rwc = (rw[:, k, :].rearrange("p (g h) -> p g h", g=3) for k in range(6))
        for s_ in range(NS):
            pov, povk = psum(4, 5)
            for g in range(3):
                cbuf = cache[g % 2]
                ck = "cache%d" % (g % 2)
                src = ckv[g][li, s_, :, :].rearrange("(j d) c -> j d c", d=DIL[g])[:, 0, :]
                dma("sp", cbuf[:, :], src, writes=[ck])
                c16 = cb16[g % 2]
                c16k = "cb16_%d" % (g % 2)
                op("pool", lambda e: e.tensor_copy(out=c16[:, 0:1024], in_=cbuf[:, 0:1024]), reads=[ck], writes=[c16k])
                op("act", lambda e: e.copy(vkeep[g][:, :], cbuf[:, 1024:2048]), reads=[ck], writes=["vk%d" % g])
                for h in range(8):
                    pt, ptk = psum(0, 2)
                    ptb = pt[:, 0:64].bitcast(BF16)
                    op("pe", lambda e: e.transpose(ptb[:, 0:128], c16[:, h * 128:(h + 1) * 128], ident_b[:, :]),
                       reads=[c16k, "ident_b"], writes=[ptk])
                    op("dve", lambda e: e.tensor_copy(out=ckT[:, h, :], in_=ptb[:, 0:128]), reads=[ptk], writes=["ckT"])
                for hq in range(2):
                    pr, prk = psum(2, 4)
                    for hh in range(4):
                        h = hq * 4 + hh
                        op("pe", lambda e: e.matmul(pr[0:1, hh * 128:(hh + 1) * 128], sqT[:, g * 8 + h, s_:s_ + 1], ckT[:, h, :], start=True, stop=True),
                           reads=["sqT", "ckT"], writes=[prk], signal=(hh == 3))
                    op("act", lambda e: e.copy(srow[0:1, hq * 4:(hq + 1) * 4, 0:128], pr[0:1, :].rearrange("p (h k) -> p h k", h=4)),
                       reads=[prk], writes=["srow"])
                op("dve", lambda e: e.tensor_copy(out=srow[0:1, :, 128:129], in_=qkr3[0:1, g * 8:(g + 1) * 8, s_:s_ + 1]), reads=["qkr"], writes=["srow"])
                op("dve", lambda e: e.tensor_reduce(out=rwm[0:1, g, :], in_=srow[0:1, :, :], axis=AX.X, op=ALU.max), reads=["srow"], writes=["rw"])
                op("dve", lambda e: e.tensor_tensor(out=srow[0:1, :, :], in0=srow[0:1, :, :],
                                                    in1=rwm[0:1, g, :].unsqueeze(2).broadcast_to([1, 8, 129]), op=ALU.subtract),
                   reads=["srow", "rw"], writes=["srow"])
                op("act", lambda e: e.activation(out=prow[0:1, g, :, :], in_=srow[0:1, :, :], func=AF.Exp, scale=SCALE), reads=["srow"], writes=["prow"])
                op("dve", lambda e: e.tensor_reduce(out=rwl[0:1, g, :], in_=prow[0:1, g, :, :], axis=AX.X, op=ALU.add), reads=["prow"], writes=["rw"])
                pp, ppk = psum(2, 4)
                for h in range(8):
                    op("pe", lambda e: e.matmul(pp[:, h:h + 1], prow[0:1, g, h, 0:128], one1[0:1, 0:1], start=True, stop=True),
                       reads=["prow", "one1"], writes=[ppk], signal=(h == 7))
                op("dve", lambda e: e.tensor_copy(out=pSk[:, g, :], in_=pp[:, 0:8]), reads=[ppk], writes=["pSk"])
                pn_, pnk = psum(2, 4)
                op("pe", lambda e: e.matmul(pn_[:, 0:8], ones_f[0:1, :], prow[0:1, g, :, 128], start=True, stop=True), reads=["ones_f", "prow"], writes=[pnk])
                op("dve", lambda e: e.tensor_copy(out=pnb[:, g, :], in_=pn_[:, 0:8]), reads=[pnk], writes=["pnb"])
                for h in range(8):
                    op("pe", lambda e: e.matmul(pov[:, g * 8 + h:g * 8 + h + 1], vkeep[g][:, h * 128:(h + 1) * 128], pSk[:, g, h:h + 1], start=True, stop=True),
                       reads=["vk%d" % g, "pSk"], writes=[povk], signal=(h == 7))
            op("dve", lambda e: e.tensor_tensor(out=og[:, :, :], in0=svT[:, :, s_].rearrange("p (g h) -> p g h", g=3), in1=pnb[:, :, :], op=ALU.mult),
               reads=["svT", "pnb"], writes=["og"])
            op("dve", lambda e: e.tensor_tensor(out=og[:, :, :], in0=og[:, :, :], in1=pov[:, 0:24].rearrange("p (g h) -> p g h", g=3), op=ALU.add),
               reads=["og", povk], writes=["og"])
            op("dve", lambda e: e.tensor_scalar(out=rwmm[0:1, :, :], in0=rwm[0:1, :, :], scalar1=SCALE, scalar2=None, op0=ALU.mult), reads=["rw"], writes=["rw"])
            op("act", lambda e: e.activation(out=rwe[0:1, :, :], in_=rwmm[0:1, :, :], func=AF.Exp), reads=["rw"], writes=["rw"])
            op("dve", lambda e: e.tensor_tensor(out=rwe[0:1, :, :], in0=rwe[0:1, :, :], in1=rwl[0:1, :, :], op=ALU.mult), reads=["rw"], writes=["rw"])
            op("dve", lambda e: e.tensor_tensor(out=rwt[0:1, 0, :], in0=rwe[0:1, 0, :], in1=rwe[0:1, 1, :], op=ALU.add), reads=["rw"], writes=["rw"])
            op("dve", lambda e: e.tensor_tensor(out=rwt[0:1, 0, :], in0=rwt[0:1, 0, :], in1=rwe[0:1, 2, :], op=ALU.add), reads=["rw"], writes=["rw"])
            op("dve", lambda e: e.tensor_tensor(out=rwc[0:1, :, :], in0=rwt[0:1, 0, :].unsqueeze(1).broadcast_to([1, 3, 8]),
                                                in1=rwl[0:1, :, 0:1].broadcast_to([1, 3, 8]), op=ALU.mult), reads=["rw"], writes=["rw"])
            op("dve", lambda e: e.reciprocal(rwc[0:1, :, :], rwc[0:1, :, :]), reads=["rw"], writes=["rw"])
            op("dve", lambda e: e.tensor_tensor(out=rwc[0:1, :, :], in0=rwc[0:1, :, :], in1=rwe[0:1, :, :], op=ALU.mult), reads=["rw"], writes=["rw"])
            pcb, pcbk = psum(2, 4)
            op("pe", lambda e: e.matmul(pcb[:, 0:24], ones_f[0:1, :], rw[0:1, 5, :], start=True, stop=True), reads=["ones_f", "rw"], writes=[pcbk])
            op("dve", lambda e: e.tensor_tensor(out=og[:, :, :], in0=og[:, :, :], in1=pcb[:, 0:24].rearrange("p (g h) -> p g h", g=3), op=ALU.mult),
               reads=["og", pcbk], writes=["og"])
            op("dve", lambda e: e.tensor_tensor(out=so[:, :], in0=og[:, 0, :], in1=og[:, 1, :], op=ALU.add), reads=["og"], writes=["so"])
            op("dve", lambda e: e.tensor_tensor(out=sob[:, :, s_], in0=so[:, :], in1=og[:, 2, :], op=ALU.add), reads=["so", "og"], writes=["sob"])
        with nc.allow_non_contiguous_dma(reason="tiny sample columns"):
            dma("sp", mT_s[:, :, T:TC].rearrange("h p s -> p h s"), sob[:, :, :], reads=["sob"], writes=["mT_s"])
        kb.barrier()
        es.close()

    LCH = 32
    NCH = T // LCH

    def win_phase(li, t, segs):
        g0 = t * TW
        for dc in range(KC):
            wv, wk = wr.get(("win", li, dc))
            w3 = wv[:, 0:KC * 128].rearrange("p (k c) -> p k c", k=KC)
            for (c0, n, S) in segs:
                pb, pk = psum()
                for kc in range(KC):
                    op("pe", lambda e: e.matmul(pb[:, 0:n], w3[:, kc, :], hT[:, kc, c0:c0 + n], start=(kc == 0), stop=(kc == KC - 1)),
                       reads=["hT", wk], writes=[pk], signal=(kc == KC - 1))
                r = rr["cg"] % 2
                rr["cg"] += 1
                op("act", lambda e: e.copy(cgt[r][:, 0:n], pb[:, 0:n]), reads=[pk], writes=["cg%d" % r])
                gc = g0 if S == 1 else T
                if S == 1:
                    dma("sp", uT_s[dc, :, gc:gc + n], cgt[r][:, 0:n], reads=["cg%d" % r], writes=["uT_s"])
                else:
                    with nc.allow_non_contiguous_dma(reason="tiny sample columns"):
                        dma("sp", uT_s[dc, :, gc:gc + n], cgt[r][:, 0:n], reads=["cg%d" % r], writes=["uT_s"])

    def mixer_out_ssm(li, t, segs):
        g0 = t * TW
        minT = aT[:, 0:KC, :]
        dma("sp", minT[:, :, 0:TW], zT_s[:, :, g0:g0 + TW].rearrange("k p t -> p k t"), reads=["zT_s"], writes=["aT"])
        if len(segs) > 1:
            with nc.allow_non_contiguous_dma(reason="tiny sample columns"):
                dma("sp", minT[:, :, TW:TWS], zT_s[:, :, T:TC].rearrange("k p t -> p k t"), reads=["zT_s"], writes=["aT"])
        psts = [stats_begin() if si == 0 else psum(6, 7) for si in range(len(segs))]
        for dc in range(KC):
            wv, wk = wr.get(("glu", li, dc))
            wval = wv[:, 0:KC * 128].rearrange("p (k c) -> p k c", k=KC)
            wgat = wv[:, KC * 128:2 * KC * 128].rearrange("p (k c) -> p k c", k=KC)
            for si, (c0, n, S) in enumerate(segs):
                pv, pvk = psum()
                pg, pgk = psum()
                for kc in range(KC):
                    op("pe", lambda e: e.matmul(pv[:, 0:n], wval[:, kc, :], minT[:, kc, c0:c0 + n], start=(kc == 0), stop=(kc == KC - 1)),
                       reads=["aT", wk], writes=[pvk], signal=(kc == KC - 1))
                for kc in range(KC):
                    op("pe", lambda e: e.matmul(pg[:, 0:n], wgat[:, kc, :], minT[:, kc, c0:c0 + n], start=(kc == 0), stop=(kc == KC - 1)),
                       reads=["aT", wk], writes=[pgk], signal=(kc == KC - 1))
                r = rr["cg"] % 2
                rr["cg"] += 1
                op("act", lambda e: e.activation(out=sgt[r][:, 0:n], in_=pg[:, 0:n], func=AF.Sigmoid), reads=[pgk], writes=["sg%d" % r])
                op("dve", lambda e: e.tensor_tensor(out=mT[:, dc, c0:c0 + n], in0=pv[:, 0:n], in1=sgt[r][:, 0:n], op=ALU.mult),
                   reads=[pvk, "sg%d" % r], writes=["mT"])
                stats_add(psts[si], mT[:, dc, c0:c0 + n], "mT", c0, n, dc == 0, dc == KC - 1)
        for si, (c0, n, S) in enumerate(segs):
            stats_end(psts[si], c0, n)

    def ssm_phase(li):
        kb.barrier()
        es = contextlib.ExitStack()
        aTf = aT[:].rearrange("p a b -> p (a b)")
        hTf = hT[:].rearrange("p a b -> p (a b)")
        xTf = xT[:].rearrange("p a b -> p (a b)")
        mTf = mT[:].rearrange("p a b -> p (a b)")
        Wb = aTf[:, 0:16384].rearrange("p (k g m) -> p k g m", k=KC, g=8)
        npi_t = aTf[:, 16384:16384 + 4096].bitcast(F32).rearrange("p (a i) -> p a i", i=LCH)
        Wc = xTf[:, 0:8192].bitcast(BF16).rearrange("p (k j r m) -> p k j r m", k=KC, j=4, r=2)
        pr_parts = [cgt[0], cgt[1], cvt[0], cvt[1]]
        pi_parts = [sgt[0], sgt[1], rstd, rtmp]

        def tab(parts, pair):
            return parts[pair // 16][:, (pair % 16) * LCH:(pair % 16 + 1) * LCH]
        small = hTf[:, 0:3840].bitcast(F32).rearrange("p (k w) -> p k w", w=64)
        smallB = hTf[0:64, 3840:3840 + 4096].bitcast(F32).rearrange("p (k w) -> p k w", w=128)
        dsk = kb.sb("dsk", [128, KC], F32, es)
        gmask = kb.sb("gmask", [128, 8], F32, es)
        nat = kb.sb("nat", [128, 128], F32, es)
        dma("sp", gmask[:], c_gmask[:, :], writes=["gmask"])
        dma("sp", nat[0:KC, :], d_skip[li, :].rearrange("(k p) -> k p", p=128), writes=["nat"])
        pb, pk = psum(5, 8)
        op("pe", lambda e: e.transpose(pb[:, 0:KC], nat[0:KC, :], ident_f[0:KC, 0:KC]), reads=["nat", "ident_f"], writes=[pk])
        op("act", lambda e: e.copy(dsk[:, :], pb[:, 0:KC]), reads=[pk], writes=["dsk"])

        def abar_chain(P, W, sm, lam_r_ap, lam_i_ap, ldt_ap, key):
            S_ = lambda k: sm[0:P, k, 0:W]
            o = lambda fn, **kw: op("dve", fn, reads=[key], writes=[key])
            o(lambda e: e.tensor_scalar(out=S_(0), in0=lam_r_ap, scalar1=-1e-4, scalar2=None, op0=ALU.min))
            o(lambda e: e.tensor_copy(out=S_(1), in_=lam_i_ap))
            op("act", lambda e: e.activation(out=S_(2), in_=ldt_ap, func=AF.Exp), reads=[key], writes=[key])
            o(lambda e: e.tensor_tensor(out=S_(3), in0=S_(1), in1=S_(2), op=ALU.mult))
            o(lambda e: e.tensor_scalar(out=S_(3), in0=S_(3), scalar1=1.0 / 16.0, scalar2=None, op0=ALU.mult))
            o(lambda e: e.tensor_tensor(out=S_(4), in0=S_(3), in1=S_(3), op=ALU.mult))
            o(lambda e: e.tensor_scalar(out=S_(5), in0=S_(4), scalar1=1.0 / 362880.0, scalar2=None, op0=ALU.mult))
            for cf in (-1.0 / 5040.0, 1.0 / 120.0, -1.0 / 6.0):
                o(lambda e: e.scalar_tensor_tensor(out=S_(5), in0=S_(5), scalar=cf, in1=S_(4), op0=ALU.add, op1=ALU.mult))
            o(lambda e: e.scalar_tensor_tensor(out=S_(5), in0=S_(5), scalar=1.0, in1=S_(3), op0=ALU.add, op1=ALU.mult))
            o(lambda e: e.tensor_scalar(out=S_(6), in0=S_(4), scalar1=-1.0 / 3628800.0, scalar2=None, op0=ALU.mult))
            for cf in (1.0 / 40320.0, -1.0 / 720.0, 1.0 / 24.0, -0.5):
                o(lambda e: e.scalar_tensor_tensor(out=S_(6), in0=S_(6), scalar=cf, in1=S_(4), op0=ALU.add, op1=ALU.mult))
            o(lambda e: e.tensor_scalar(out=S_(6), in0=S_(6), scalar1=1.0, scalar2=None, op0=ALU.add))
            for _ in range(4):
                o(lambda e: e.tensor_tensor(out=S_(7), in0=S_(5), in1=S_(6), op=ALU.mult))
                o(lambda e: e.tensor_tensor(out=S_(11), in0=S_(5), in1=S_(5), op=ALU.mult))
                o(lambda e: e.tensor_scalar(out=S_(6), in0=S_(11), scalar1=-2.0, scalar2=1.0, op0=ALU.mult, op1=ALU.add))
                o(lambda e: e.tensor_scalar(out=S_(5), in0=S_(7), scalar1=2.0, scalar2=None, op0=ALU.mult))
            o(lambda e: e.tensor_tensor(out=S_(11), in0=S_(0), in1=S_(2), op=ALU.mult))
            op("act", lambda e: e.activation(out=S_(8), in_=S_(11), func=AF.Exp), reads=[key], writes=[key])
            o(lambda e: e.tensor_tensor(out=S_(9), in0=S_(8), in1=S_(6), op=ALU.mult))
            o(lambda e: e.tensor_tensor(out=S_(10), in0=S_(8), in1=S_(5), op=ALU.mult))
            return S_(9), S_(10), S_(0), S_(1)

        def load_A(dst, src2d):
            dma("sp", nat[0:64, :], src2d.rearrange("(g s) p -> g (s p)", s=2), writes=["nat"])
            pb, pk = psum(5, 8)
            op("pe", lambda e: e.transpose(pb[:, 0:64], nat[0:64, :], ident_f[0:64, 0:64]), reads=["nat", "ident_f"], writes=[pk])
            op("act", lambda e: e.copy(dst, pb[:, 0:64]), reads=[pk], writes=["small"])
        A_ = lambda k: small[:, k, :]
        load_A(A_(12), lam_re[li, :, :])
        load_A(A_(13), lam_im[li, :, :])
        dma("sp", nat[0:64, 0:2], log_dt[li, :].rearrange("(g s) -> g s", s=2), writes=["nat"])
        op("dve", lambda e: e.tensor_copy(out=nat[0:64, 64:128].rearrange("p (s q) -> p s q", s=2)[:, :, :] if False else smallB[0:64, 15, :].rearrange("p (s q) -> p s q", s=2),
                                          in_=nat[0:64, 0:2].unsqueeze(2).broadcast_to([64, 2, 64])), reads=["nat"], writes=["smallB"])
        pb, pk = psum(5, 8)
        op("pe", lambda e: e.transpose(pb[:, 0:64], smallB[0:64, 15, :], ident_f[0:64, 0:64]), reads=["smallB", "ident_f"], writes=[pk])
        op("act", lambda e: e.copy(A_(14), pb[:, 0:64]), reads=[pk], writes=["small"])
        arA, aiA, _, _ = abar_chain(128, 64, small, A_(12), A_(13), A_(14), "small")
        for pair0 in range(0, 64, 16):
            pass
        prv = lambda i: [p_[:, :].rearrange("p (a i) -> p a i", i=LCH)[:, :, i] for p_ in pr_parts]
        piv = lambda i: [p_[:, 0:512].rearrange("p (a i) -> p a i", i=LCH)[:, :, i] for p_ in pi_parts]
        tkeys = ["cg0", "cg1", "cv0", "cv1", "sg0", "sg1", "rstd", "rtmp", "npi"]
        for q in range(4):
            op("dve", lambda e: e.tensor_copy(out=prv(0)[q], in_=arA[:, q * 16:(q + 1) * 16]), reads=["small"], writes=tkeys)
            op("dve", lambda e: e.tensor_copy(out=piv(0)[q], in_=aiA[:, q * 16:(q + 1) * 16]), reads=["small"], writes=tkeys)
        for i in range(1, LCH):
            for q in range(4):
                a_r = arA[:, q * 16:(q + 1) * 16]
                a_i = aiA[:, q * 16:(q + 1) * 16]
                t1, t2 = small[:, 15, 0:16], small[:, 16, 0:16]
                op("dve", lambda e: e.tensor_tensor(out=t1, in0=prv(i - 1)[q], in1=a_r, op=ALU.mult), reads=tkeys + ["small"], writes=["small"])
                op("dve", lambda e: e.tensor_tensor(out=t2, in0=piv(i - 1)[q], in1=a_i, op=ALU.mult), reads=tkeys + ["small"], writes=["small"])
                op("dve", lambda e: e.tensor_tensor(out=prv(i)[q], in0=t1, in1=t2, op=ALU.subtract), reads=["small"], writes=tkeys)
                op("dve", lambda e: e.tensor_tensor(out=t1, in0=prv(i - 1)[q], in1=a_i, op=ALU.mult), reads=tkeys + ["small"], writes=["small"])
                op("dve", lambda e: e.tensor_tensor(out=t2, in0=piv(i - 1)[q], in1=a_r, op=ALU.mult), reads=tkeys + ["small"], writes=["small"])
                op("dve", lambda e: e.tensor_tensor(out=piv(i)[q], in0=t1, in1=t2, op=ALU.add), reads=["small"], writes=tkeys)
        for q in range(4):
            op("dve", lambda e: e.tensor_scalar(out=npi_t[:, q * 16:(q + 1) * 16, :], in0=pi_parts[q][:, 0:512].rearrange("p (a i) -> p a i", i=LCH),
                                                scalar1=-1.0, scalar2=None, op0=ALU.mult), reads=tkeys, writes=tkeys)

        A8 = kb.sb("A8", [128, 3, 64, 8], F32, es)
        Ar_all = [p_[:, 0:512].rearrange("p (a i) -> p a i", i=LCH)[:, :, LCH - 1] for p_ in pr_parts]
        Ai_all = [p_[:, 0:512].rearrange("p (a i) -> p a i", i=LCH)[:, :, LCH - 1] for p_ in pi_parts]
        for q in range(4):
            qs = slice(q * 16, (q + 1) * 16)
            op("dve", lambda e: e.tensor_copy(out=A8[:, 0, qs, 0], in_=Ar_all[q]), reads=tkeys, writes=["A8"])
            op("dve", lambda e: e.tensor_copy(out=A8[:, 1, qs, 0], in_=Ai_all[q]), reads=tkeys, writes=["A8"])
            for j in range(1, 8):
                t1, t2 = small[:, 15, 0:16], small[:, 16, 0:16]
                op("dve", lambda e: e.tensor_tensor(out=t1, in0=A8[:, 0, qs, j - 1], in1=Ar_all[q], op=ALU.mult), reads=tkeys + ["A8", "small"], writes=["small"])
                op("dve", lambda e: e.tensor_tensor(out=t2, in0=A8[:, 1, qs, j - 1], in1=Ai_all[q], op=ALU.mult), reads=tkeys + ["A8", "small"], writes=["small"])
                op("dve", lambda e: e.tensor_tensor(out=A8[:, 0, qs, j], in0=t1, in1=t2, op=ALU.subtract), reads=["small"], writes=["A8"])
                op("dve", lambda e: e.tensor_tensor(out=t1, in0=A8[:, 0, qs, j - 1], in1=Ai_all[q], op=ALU.mult), reads=tkeys + ["A8", "small"], writes=["small"])
                op("dve", lambda e: e.tensor_tensor(out=t2, in0=A8[:, 1, qs, j - 1], in1=Ar_all[q], op=ALU.mult), reads=tkeys + ["A8", "small"], writes=["small"])
                op("dve", lambda e: e.tensor_tensor(out=A8[:, 1, qs, j], in0=t1, in1=t2, op=ALU.add), reads=["small"], writes=["A8"])
        op("dve", lambda e: e.tensor_scalar(out=A8[:, 2, :, :], in0=A8[:, 1, :, :], scalar1=-1.0, scalar2=None, op0=ALU.mult), reads=["A8"], writes=["A8"])

        B_ = lambda k: smallB[0:64, k, :]
        for (dst, src) in ((B_(12), lam_re), (B_(13), lam_im)):
            dma("sp", nat[:, 0:64], src[li, :, :], writes=["nat"])
            pb, pk = psum(5, 8)
            op("pe", lambda e: e.transpose(pb[0:64, 0:128], nat[:, 0:64], ident_f[:, :]), reads=["nat", "ident_f"], writes=[pk])
            op("act", lambda e: e.copy(dst, pb[0:64, 0:128]), reads=[pk], writes=["smallB"])
        dma("sp", B_(14), log_dt[li:li + 1, :].partition_broadcast(64).rearrange("p a g -> p (a g)") if False else log_dt[li:li + 1, :].broadcast_to([64, 128]), writes=["smallB"])
        arB, aiB, lrB, liB = abar_chain(64, 128, smallB, B_(12), B_(13), B_(14), "smallB")
        ob = lambda fn: op("dve", fn, reads=["smallB"], writes=["smallB"])
        ob(lambda e: e.tensor_scalar(out=B_(11), in0=arB, scalar1=-1.0, scalar2=None, op0=ALU.add))
        ob(lambda e: e.tensor_tensor(out=B_(7), in0=lrB, in1=lrB, op=ALU.mult))
        ob(lambda e: e.tensor_tensor(out=B_(2), in0=liB, in1=liB, op=ALU.mult))
        ob(lambda e: e.tensor_tensor(out=B_(7), in0=B_(7), in1=B_(2), op=ALU.add))
        ob(lambda e: e.reciprocal(B_(7), B_(7)))
        ob(lambda e: e.tensor_tensor(out=B_(3), in0=B_(11), in1=lrB, op=ALU.mult))
        ob(lambda e: e.tensor_tensor(out=B_(2), in0=aiB, in1=liB, op=ALU.mult))
        ob(lambda e: e.tensor_tensor(out=B_(3), in0=B_(3), in1=B_(2), op=ALU.add))
        ob(lambda e: e.tensor_tensor(out=B_(3), in0=B_(3), in1=B_(7), op=ALU.mult))
        ob(lambda e: e.tensor_tensor(out=B_(4), in0=aiB, in1=lrB, op=ALU.mult))
        ob(lambda e: e.tensor_tensor(out=B_(2), in0=B_(11), in1=liB, op=ALU.mult))
        ob(lambda e: e.tensor_tensor(out=B_(4), in0=B_(4), in1=B_(2), op=ALU.subtract))
        ob(lambda e: e.tensor_tensor(out=B_(4), in0=B_(4), in1=B_(7), op=ALU.mult))
        bre = mTf[0:64, 0:2048].rearrange("p (g c) -> p g c", c=16)
        bim = mTf[0:64, 2048:4096].rearrange("p (g c) -> p g c", c=16)
        bbr = mTf[0:64, 4096:6144].rearrange("p (g c) -> p g c", c=16)
        bbi = mTf[0:64, 6144:8192].rearrange("p (g c) -> p g c", c=16)
        with nc.allow_non_contiguous_dma(reason="b tensors 64B runs"):
            dma("sp", bre, b_re[li, :, :, :].rearrange("g p c -> p g c"), writes=["bb"])
            dma("sp", bim, b_im[li, :, :, :].rearrange("g p c -> p g c"), writes=["bb"])
        cr = B_(3).unsqueeze(2).broadcast_to([64, 128, 16])
        ci = B_(4).unsqueeze(2).broadcast_to([64, 128, 16])
        o2 = lambda fn: op("dve", fn, reads=["bb", "smallB"], writes=["bb"])
        o2(lambda e: e.tensor_tensor(out=bbr, in0=bre, in1=cr, op=ALU.mult))
        o2(lambda e: e.tensor_tensor(out=bbi, in0=bim, in1=ci, op=ALU.mult))
        o2(lambda e: e.tensor_tensor(out=bbr, in0=bbr, in1=bbi, op=ALU.subtract))
        o2(lambda e: e.tensor_tensor(out=bbi, in0=bre, in1=ci, op=ALU.mult))
        o2(lambda e: e.tensor_tensor(out=bre, in0=bim, in1=cr, op=ALU.mult))
        o2(lambda e: e.tensor_tensor(out=bbi, in0=bbi, in1=bre, op=ALU.add))
        for kc in range(KC):
            for r_, src in ((0, bbr), (1, bbi)):
                pb, pk = psum(5, 8)
                op("pe", lambda e: e.transpose(pb[:, 0:64], src[:, kc * 8:(kc + 1) * 8, :].rearrange("p g c -> p (g c)"), ident_f[0:64, 0:64]),
                   reads=["bb", "ident_f"], writes=[pk])
                for gl in range(8):
                    op("act" if gl % 2 else "dve",
                       (lambda e: e.activation(out=Wb[:, kc, gl, r_ * 64:(r_ + 1) * 64], in_=pb[:, 0:64], func=AF.Identity, scale=gmask[:, gl:gl + 1])) if gl % 2 else
                       (lambda e: e.tensor_scalar(out=Wb[:, kc, gl, r_ * 64:(r_ + 1) * 64], in0=pb[:, 0:64], scalar1=gmask[:, gl:gl + 1], scalar2=None, op0=ALU.mult)),
                       reads=[pk, "gmask"], writes=["Wb"])
        kb.barrier()
        Cn = [hTf[0:64, r_ * 4096:(r_ + 1) * 4096].bitcast(F32).rearrange("p (s c q) -> p s c q", s=2, c=16) for r_ in range(2)]
        dma("sp", Cn[0], c_re[li, :, :, :].rearrange("(g s) c q -> g s c q", s=2), writes=["Cn"])
        dma("sp", Cn[1], c_im[li, :, :, :].rearrange("(g s) c q -> g s c q", s=2), writes=["Cn"])
        op("dve", lambda e: e.memset(xTf[:, 0:8192], 0.0), writes=["Wc"])
        for r_ in range(2):
            for c in range(16):
                pb, pk = psum(5, 8)
                op("dve", lambda e: e.tensor_copy(out=nat[0:64, :].rearrange("p (s q) -> p s q", s=2), in_=Cn[r_][:, :, c, :]), reads=["Cn"], writes=["nat"])
                op("pe", lambda e: e.transpose(pb[:, 0:64], nat[0:64, :], ident_f[0:64, 0:64]), reads=["nat", "ident_f"], writes=[pk])
                for s in range(2):
                    for j4 in range(4):
                        src = pb[s * 64:(s + 1) * 64, 0:64].rearrange("p (k j) -> p k j", j=4)[:, :, j4]
                        dst = Wc[s * 64:(s + 1) * 64, :, j4, r_, 32 * j4 + 16 * s + c]
                        sc = 1.0 if r_ == 0 else -1.0
                        if (s + j4) % 2:
                            op("act", lambda e: e.mul(dst, src, sc), reads=[pk], writes=["Wc"])
                        else:
                            op("dve", lambda e: e.tensor_scalar(out=dst, in0=src, scalar1=sc, scalar2=None, op0=ALU.mult), reads=[pk], writes=["Wc"])
        kb.barrier()
        XR = [mTf[:, (2 * q) * TC:(2 * q + 1) * TC] for q in range(2)]
        XI = [mTf[:, (2 * q + 1) * TC:(2 * q + 2) * TC] for q in range(2)]
        ya = kb.sb("ya", [128, TW], F32, es)
        yb2 = aTf[:, 20480:20480 + 1024].bitcast(F32)
        Xs = kb.sb("Xs", [128, 4, NCH], F32, es)
        h0t = kb.sb("h0t", [128, 64, NS, 2], F32, es)
        h0 = h0t[:]
        sto = kb.sb("sto", [128, 64, 2], F32, es)
        stos = kb.sb("stos", [128, NS, 64, 2], F32, es)
        ubf = hTf[:, 0:TC]
        xbr = hTf[:, TC:2 * TC]
        xbi = hTf[:, 2 * TC:3 * TC]
        zst = hTf[:, 3 * TC:4 * TC]
        for s_ in range(NS):
            with nc.allow_non_contiguous_dma(reason="state 8B runs"):
                dma("sp", h0[:, :, s_, :], st_ssm[li, s_, :].rearrange("(a p r) -> p a r", p=128, r=2), writes=["h0"])
        coltiles = [(tq * TW, TW) for tq in range(NT)] + [(T, NS)]
        NBK = T // 4
        XAR = [x[:, 0:T].rearrange("p (r m) -> p r m", m=NBK) for x in XR]
        XAI = [x[:, 0:T].rearrange("p (r m) -> p r m", m=NBK) for x in XI]
        X3R = [x[:, 3, :].rearrange("p (n i) -> p i n", i=8) for x in XAR]
        X3I = [x[:, 3, :].rearrange("p (n i) -> p i n", i=8) for x in XAI]
        STT = lambda o_, a_, sc_, b_: (lambda e: e.scalar_tensor_tensor(out=o_, in0=a_, scalar=sc_, in1=b_, op0=ALU.mult, op1=ALU.add))
        for kc in range(KC):
            dma("pool", ubf[:, :], uT_s[kc, :, :], reads=["uT_s"], writes=["ubf"])
            ybanks = [(pbank[k], ("ps", k)) for k in range(5)]
            for jp in range(2):
                pairs = [kc * 4 + 2 * jp + q for q in range(2)]
                tabs = []
                for q in range(2):
                    pair = pairs[q]
                    j4 = 2 * jp + q
                    PR, PI, NPI = tab(pr_parts, pair), tab(pi_parts, pair), npi_t[:, pair, :]
                    tabs.append((PR, PI, NPI))
                    for (c0, n) in coltiles:
                        for r_, dstx, dk in ((0, XR[q], ("xr", q)), (1, XI[q], ("xi", q))):
                            pb, pk = psum(5, 8)
                            for s in range(2):
                                op("pe", lambda e: e.matmul(pb[s * 64:(s + 1) * 64, 0:n], Wb[:, kc, 2 * j4 + s, r_ * 64:(r_ + 1) * 64], ubf[:, c0:c0 + n],
                                                            start=True, stop=True), reads=["Wb", "ubf"], writes=[pk], signal=(s == 1))
                            if n == TW:
                                m0 = c0 // 4
                                dview = dstx[:, 0:T].rearrange("p (r m) -> p r m", m=T // 4)[:, :, m0:m0 + TW // 4]
                                op("act", lambda e: e.copy(dview, pb[:, 0:TW].rearrange("p (m r) -> p r m", r=4)), reads=[pk], writes=[dk])
                            else:
                                op("act", lambda e: e.copy(dstx[:, c0:c0 + n], pb[:, 0:n]), reads=[pk], writes=[dk])
                allk = [("xr", 0), ("xi", 0), ("xr", 1), ("xi", 1), "Xs"] + tkeys
                kb.fence("dve", reads=allk, writes=allk)
                ox = kb.opx
                l2 = [0, 0]
                l4 = [0, 0]

                def cplx_step(dR, dI, sR, sI, ti, l2, l4):
                    c1 = [0, 0]
                    c3 = [0, 0]
                    for q in range(2):
                        c1[q] = ox("dve", STT(dR[q], sR[q], tabs[q][0][:, ti:ti + 1], dR[q]), after=[l2[q], l4[q]])
                    for q in range(2):
                        c3[q] = ox("dve", STT(dI[q], sI[q], tabs[q][0][:, ti:ti + 1], dI[q]), after=[l2[q], l4[q]])
                    n2 = [0, 0]
                    n4 = [0, 0]
                    for q in range(2):
                        n2[q] = ox("dve", STT(dR[q], sI[q], tabs[q][2][:, ti:ti + 1], dR[q]), after=[c1[q]])
                    for q in range(2):
                        n4[q] = ox("dve", STT(dI[q], sR[q], tabs[q][1][:, ti:ti + 1], dI[q]), after=[c3[q]])
                    return n2, n4
                for r in range(1, 4):
                    l2, l4 = cplx_step([x[:, r, :] for x in XAR], [x[:, r, :] for x in XAI], [x[:, r - 1, :] for x in XAR], [x[:, r - 1, :] for x in XAI], 0, l2, l4)
                for i in range(1, 8):
                    l2, l4 = cplx_step([x[:, i, :] for x in X3R], [x[:, i, :] for x in X3I], [x[:, i - 1, :] for x in X3R], [x[:, i - 1, :] for x in X3I], 3, l2, l4)
                XRs = [Xs[:, 2 * q, :] for q in range(2)]
                XIs = [Xs[:, 2 * q + 1, :] for q in range(2)]
                e2 = [0, 0]
                e4 = [0, 0]
                for q in range(2):
                    e2[q] = ox("dve", lambda e: e.tensor_copy(out=XRs[q], in_=X3R[q][:, 7, :]), after=[l2[q], l4[q]])
                    e4[q] = ox("dve", lambda e: e.tensor_copy(out=XIs[q], in_=X3I[q][:, 7, :]), after=[l2[q], l4[q]])
                X8R = [x.rearrange("p (m j) -> p m j", j=8) for x in XRs]
                X8I = [x.rearrange("p (m j) -> p m j", j=8) for x in XIs]
                A8q = [(A8[:, 0, pairs[q], :], A8[:, 1, pairs[q], :], A8[:, 2, pairs[q], :]) for q in range(2)]

                def cstep(dstR, dstI, srcR, srcI, coef, dR, dI):
                    c1 = [0, 0]
                    c3 = [0, 0]
                    o2 = [0, 0]
                    o4 = [0, 0]
                    for q in range(2):
                        c1[q] = ox("dve", STT(dstR[q], srcR[q], coef[q][0], dstR[q]), after=[dR[q], dI[q]])
                    for q in range(2):
                        c3[q] = ox("dve", STT(dstI[q], srcI[q], coef[q][0], dstI[q]), after=[dR[q], dI[q]])
                    for q in range(2):
                        o2[q] = ox("dve", STT(dstR[q], srcI[q], coef[q][2], dstR[q]), after=[c1[q]])
                    for q in range(2):
                        o4[q] = ox("dve", STT(dstI[q], srcR[q], coef[q][1], dstI[q]), after=[c3[q]])
                    return o2, o4
                for j in range(1, 8):
                    e2, e4 = cstep([x[:, :, j] for x in X8R], [x[:, :, j] for x in X8I], [x[:, :, j - 1] for x in X8R], [x[:, :, j - 1] for x in X8I],
                                   [(A8q[q][0][:, 0:1], A8q[q][1][:, 0:1], A8q[q][2][:, 0:1]) for q in range(2)], e2, e4)
                for m_ in range(1, 8):
                    e2, e4 = cstep([x[:, m_, 7:8] for x in X8R], [x[:, m_, 7:8] for x in X8I], [x[:, m_ - 1, 7:8] for x in X8R], [x[:, m_ - 1, 7:8] for x in X8I],
                                   [(A8q[q][0][:, 7:8], A8q[q][1][:, 7:8], A8q[q][2][:, 7:8]) for q in range(2)], e2, e4)
                f2, f4 = e2, e4
                for j in range(7):
                    g2, g4 = cstep([x[:, 1:8, j] for x in X8R], [x[:, 1:8, j] for x in X8I], [x[:, 0:7, 7] for x in X8R], [x[:, 0:7, 7] for x in X8I],
                                   [(A8q[q][0][:, j:j + 1], A8q[q][1][:, j:j + 1], A8q[q][2][:, j:j + 1]) for q in range(2)], e2, e4)
                    f2 = [max(f2[q], g2[q]) for q in range(2)]
                    f4 = [max(f4[q], g4[q]) for q in range(2)]
                e2, e4 = f2, f4
                f2, f4 = list(e2), list(e4)
                for i in range(8):
                    g2, g4 = cplx_step([x[:, i, 1:NCH] for x in X3R], [x[:, i, 1:NCH] for x in X3I], [x[:, 0:NCH - 1] for x in XRs], [x[:, 0:NCH - 1] for x in XIs],
                                       4 * i + 3, e2, e4)
                    f2 = [max(f2[q], g2[q]) for q in range(2)]
                    f4 = [max(f4[q], g4[q]) for q in range(2)]
                for r in range(3):
                    cplx_step([x[:, r, 1:NBK] for x in XAR], [x[:, r, 1:NBK] for x in XAI], [x[:, 3, 0:NBK - 1] for x in XAR], [x[:, 3, 0:NBK - 1] for x in XAI],
                              r, f2, f4)
                kb.mark("dve", writes=[("xr", 0), ("xi", 0), ("xr", 1), ("xi", 1), "Xs"], reads=tkeys)
                for q in range(2):
                    pair = pairs[q]
                    j4 = 2 * jp + q
                    PR, PI, NPI = tabs[q]
                    ar, ai, nai = PR[:, 0:1], PI[:, 0:1], NPI[:, 0:1]
                    xr, xi = XR[q], XI[q]
                    xk, ik = ("xr", q), ("xi", q)
                    sc = lambda fn, rd, wrt: op("dve", fn, reads=rd + tkeys, writes=wrt)
                    xs_r, xs_i = xr[:, T:TC], xi[:, T:TC]
                    sc(STT(xs_r, h0[:, pair, :, 0], ar, xs_r), ["h0", xk], [xk])
                    sc(STT(xs_i, h0[:, pair, :, 1], ar, xs_i), ["h0", ik], [ik])
                    sc(STT(xs_r, h0[:, pair, :, 1], nai, xs_r), ["h0", xk], [xk])
                    sc(STT(xs_i, h0[:, pair, :, 0], ai, xs_i), ["h0", ik], [ik])
                    op("act", lambda e: e.copy(sto[:, pair, 0:1], xr[:, T - 1:T]), reads=[xk], writes=["sto"])
                    op("act", lambda e: e.copy(sto[:, pair, 1:2], xi[:, T - 1:T]), reads=[ik], writes=["sto"])
                    op("act", lambda e: e.copy(stos[:, :, pair, 0], xr[:, T:TC]), reads=[xk], writes=["stos"])
                    op("act", lambda e: e.copy(stos[:, :, pair, 1], xi[:, T:TC]), reads=[ik], writes=["stos"])
                    op("act", lambda e: e.copy(xbr[:, 0:T].rearrange("p (m r) -> p m r", r=4), xr[:, 0:T].rearrange("p (r m) -> p m r", m=T // 4)), reads=[xk], writes=["xbr"])
                    op("act", lambda e: e.copy(xbr[:, T:TC], xr[:, T:TC]), reads=[xk], writes=["xbr"])
                    op("pool", lambda e: e.tensor_copy(out=xbi[:, 0:T].rearrange("p (m r) -> p m r", r=4), in_=xi[:, 0:T].rearrange("p (r m) -> p m r", m=T // 4)), reads=[ik], writes=["xbi"])
                    op("pool", lambda e: e.tensor_copy(out=xbi[:, T:TC], in_=xi[:, T:TC]), reads=[ik], writes=["xbi"])
                    for ti, (c0, n) in enumerate(coltiles):
                        yb, ybk = ybanks[ti]
                        op("pe", lambda e: e.matmul(yb[:, 0:n], Wc[:, kc, j4, 0, :], xbr[:, c0:c0 + n], start=(j4 == 0), stop=False),
                           reads=["Wc", "xbr"], writes=[ybk], signal=False)
                        op("pe", lambda e: e.matmul(yb[:, 0:n], Wc[:, kc, j4, 1, :], xbi[:, c0:c0 + n], start=False, stop=(j4 == 3)),
                           reads=["Wc", "xbi"], writes=[ybk], signal=True)
            for ti, (c0, n) in enumerate(coltiles):
                yb, ybk = ybanks[ti]
                y_, w_ = ya[:, 0:n], yb2[:, 0:n]
                if n == TW:
                    dma("sp", y_, uT_s[kc, :, c0:c0 + n], reads=["uT_s"], writes=["ya"])
                else:
                    with nc.allow_non_contiguous_dma(reason="tiny sample columns"):
                        dma("sp", y_, uT_s[kc, :, c0:c0 + n], reads=["uT_s"], writes=["ya"])
                op("dve", lambda e: e.scalar_tensor_tensor(out=y_, in0=y_, scalar=dsk[:, kc:kc + 1], in1=yb[:, 0:n], op0=ALU.mult, op1=ALU.add),
                   reads=["ya", "dsk", ybk], writes=["ya"])
                op("dve", lambda e: e.tensor_tensor(out=w_, in0=y_, in1=y_, op=ALU.mult), reads=["ya"], writes=["yb2"])
                op("dve", lambda e: e.tensor_scalar(out=w_, in0=w_, scalar1=0.044715, scalar2=1.0, op0=ALU.mult, op1=ALU.add), reads=["yb2"], writes=["yb2"])
                op("dve", lambda e: e.tensor_tensor(out=w_, in0=w_, in1=y_, op=ALU.mult), reads=["yb2", "ya"], writes=["yb2"])
                op("act", lambda e: e.activation(out=w_, in_=w_, func=AF.Sigmoid, scale=GELU_C), reads=["yb2"], writes=["yb2"])
                op("dve", lambda e: e.tensor_tensor(out=zst[:, c0:c0 + n], in0=y_, in1=w_, op=ALU.mult), reads=["ya", "yb2"], writes=["zst"])
            dma("sp", zT_s[kc, :, :], zst[:, :], reads=["zst"], writes=["zT_s"])
        with nc.allow_non_contiguous_dma(reason="state 8B runs"):
            dma("sp", ssm_p[li, :].rearrange("(a p r) -> p a r", p=128, r=2), sto[:, :, :], reads=["sto"])
            for s_ in range(NS):
                dma("sp", ssm_s[li, s_, :].rearrange("(a p r) -> p a r", p=128, r=2), stos[:, s_, :, :], reads=["stos"])
        kb.barrier()
        es.close()

    def tile_tail(i_next, t, segs):
        prenorm(i_next, 0, segs)
        if i_next % 2 == 0:
            qkv_phase(i_next // 2, t, segs)
        else:
            win_phase(i_next // 2, t, segs)

    def mixer_out_attn(li, t, segs):
        g0 = t * TW
        minT = aT[:, 0:8, :]
        dma("sp", minT[:, :, 0:TW], mT_s[:, :, g0:g0 + TW].rearrange("h p t -> p h t"), reads=["mT_s"], writes=["aT"])
        if len(segs) > 1:
            with nc.allow_non_contiguous_dma(reason="tiny sample columns"):
                dma("sp", minT[:, :, TW:TWS], mT_s[:, :, T:TC].rearrange("h p t -> p h t"), reads=["mT_s"], writes=["aT"])
        psts = [stats_begin() if si == 0 else psum(6, 7) for si in range(len(segs))]
        dq = []
        for dc in range(KC):
            wv, wk = wr.get(("wo", li, dc))
            w3 = wv[:, 0:8 * 128].rearrange("p (k c) -> p k c", k=8)
            for si, (c0, n, S) in enumerate(segs):
                pb, pk = psum()
                for hh in range(8):
                    op("pe", lambda e: e.matmul(pb[:, 0:n], w3[:, hh, :], minT[:, hh, c0:c0 + n], start=(hh == 0), stop=(hh == 7)),
                       reads=["aT", wk], writes=[pk], signal=(hh == 7))
                while dq:
                    dq.pop(0)()
                op("act", lambda e: e.copy(mT[:, dc, c0:c0 + n], pb[:, 0:n]), reads=[pk], writes=["mT"])
                stats_add(psts[si], pb[:, 0:n], pk, c0, n, dc == 0, dc == KC - 1, defer=dq)
        while dq:
            dq.pop(0)()
        for si, (c0, n, S) in enumerate(segs):
            stats_end(psts[si], c0, n)

    def ffn(i, t, segs):
        for j in range(NFC):
            wv, wk = wr.get(("up", i, j))
            wg = wv[:, 0:KC * 128].rearrange("p (k c) -> p k c", k=KC)
            wvv = wv[:, KC * 128:2 * KC * 128].rearrange("p (k c) -> p k c", k=KC)
            for (c0, n, S) in segs:
                pg, pgk = psum()
                pv, pvk = psum()
                for kc in range(KC):
                    op("pe", lambda e: e.matmul(pg[:, 0:n], wg[:, kc, :], hT[:, kc, c0:c0 + n], start=(kc == 0), stop=(kc == KC - 1)),
                       reads=["hT", wk], writes=[pgk], signal=(kc == KC - 1))
                for kc in range(KC):
                    op("pe", lambda e: e.matmul(pv[:, 0:n], wvv[:, kc, :], hT[:, kc, c0:c0 + n], start=(kc == 0), stop=(kc == KC - 1)),
                       reads=["hT", wk], writes=[pvk], signal=(kc == KC - 1))
                r = rr["cg"] % 2
                rr["cg"] += 1
                halves = []
                for (pp, ppk, jj, ct_, ck_) in ((pg, pgk, j, cgt[r], "cg%d" % r), (pv, pvk, NFC + j, cvt[r], "cv%d" % r)):
                    if S == 1:
                        halo, hk = uprev[:, jj, :], ("uprev", jj)
                    else:
                        halo, hk = cst[:, jj, :, :].rearrange("p a b -> p (a b)"), "cst"
                    halves.append((pp, ppk, jj, ct_, ck_, halo, hk, cw[:, 0, jj:jj + 1], cw[:, 1, jj:jj + 1], cw[:, 2, jj:jj + 1], cw[:, 3, jj:jj + 1]))
                m2 = min(2 * S, n)
                for (pp, ppk, jj, ct_, ck_, halo, hk, w0, w1, w2, bb) in halves:
                    op("act", lambda e: e.activation(out=ct_[:, 0:n], in_=pp[:, 0:n], func=AF.Identity, bias=bb, scale=w2),
                       reads=[ppk, "cw"], writes=[ck_])
                if n > S:
                    for (pp, ppk, jj, ct_, ck_, halo, hk, w0, w1, w2, bb) in halves:
                        op("dve", lambda e: e.scalar_tensor_tensor(out=ct_[:, S:n], in0=pp[:, 0:n - S], scalar=w1, in1=ct_[:, S:n],
                                                                   op0=ALU.mult, op1=ALU.add), reads=[ppk, "cw", ck_], writes=[ck_])
                if n > 2 * S:
                    for (pp, ppk, jj, ct_, ck_, halo, hk, w0, w1, w2, bb) in halves:
                        op("dve", lambda e: e.scalar_tensor_tensor(out=ct_[:, 2 * S:n], in0=pp[:, 0:n - 2 * S], scalar=w0, in1=ct_[:, 2 * S:n],
                                                                   op0=ALU.mult, op1=ALU.add), reads=[ppk, "cw", ck_], writes=[ck_])
                for (pp, ppk, jj, ct_, ck_, halo, hk, w0, w1, w2, bb) in halves:
                    op("dve", lambda e: e.scalar_tensor_tensor(out=ct_[:, 0:S], in0=halo[:, S:2 * S], scalar=w1, in1=ct_[:, 0:S],
                                                               op0=ALU.mult, op1=ALU.add), reads=[hk, "cw", ck_], writes=[ck_])
                for (pp, ppk, jj, ct_, ck_, halo, hk, w0, w1, w2, bb) in halves:
                    op("dve", lambda e: e.scalar_tensor_tensor(out=ct_[:, 0:m2], in0=halo[:, 0:m2], scalar=w0, in1=ct_[:, 0:m2],
                                                               op0=ALU.mult, op1=ALU.add), reads=[hk, "cw", ck_], writes=[ck_])
                for (pp, ppk, jj, ct_, ck_, halo, hk, w0, w1, w2, bb) in halves:
                    if S == 1:
                        op("dve", lambda e: e.tensor_copy(out=uprev[:, jj, :], in_=pp[:, n - 2:n]), reads=[ppk], writes=[hk])
                    else:
                        op("dve", lambda e: e.tensor_copy(out=csout[:, jj, 0, :], in_=cst[:, jj, 1, :]), reads=["cst"], writes=["csout"])
                        op("dve", lambda e: e.tensor_copy(out=csout[:, jj, 1, :], in_=pp[:, 0:NS]), reads=[ppk], writes=["csout"])
                sg_ = sgt[r]
                op("act", lambda e: e.activation(out=sg_[:, 0:n], in_=cgt[r][:, 0:n], func=AF.Silu), reads=["cg%d" % r], writes=["sg%d" % r])
                op("dve", lambda e: e.tensor_tensor(out=aT[:, j, c0:c0 + n], in0=sg_[:, 0:n], in1=cvt[r][:, 0:n], op=ALU.mult),
                   reads=["sg%d" % r, "cv%d" % r], writes=["aT"])
        psts = [stats_begin() if si == 0 else psum(6, 7) for si in range(len(segs))]
        dq = []
        for dc in range(KC):
            wv, wk = wr.get(("down", i, dc))
            w3 = wv[:, 0:NFC * 128].rearrange("p (k c) -> p k c", k=NFC)
            for si, (c0, n, S) in enumerate(segs):
                pb, pk = psum()
                for j in range(NFC):
                    op("pe", lambda e: e.matmul(pb[:, 0:n], w3[:, j, :], aT[:, j, c0:c0 + n], start=(j == 0), stop=(j == NFC - 1)),
                       reads=["aT", wk], writes=[pk], signal=(j == NFC - 1))
                while dq:
                    dq.pop(0)()
                op("act", lambda e: e.copy(mT[:, dc, c0:c0 + n], pb[:, 0:n]), reads=[pk], writes=["mT"])
                stats_add(psts[si], pb[:, 0:n], pk, c0, n, dc == 0, dc == KC - 1, defer=dq)
        while dq:
            dq.pop(0)()
        for si, (c0, n, S) in enumerate(segs):
            stats_end(psts[si], c0, n)

    def out_y(t, segs):
        ytok = mT_flat
        for bl in range(4):
            for q4 in range(4):
                pb, pk = psum()
                for k4 in range(4):
                    kc = q4 * 4 + k4
                    op("pe", lambda e: e.transpose(pb[:, k4 * 128:(k4 + 1) * 128], xT[:, kc, bl * 128:(bl + 1) * 128], ident_f[:, :]),
                       reads=["xT", "ident_f"], writes=[pk], signal=(k4 == 3))
                op("act", lambda e: e.copy(ytok[:, (bl % 2) * D + q4 * 512:(bl % 2) * D + (q4 + 1) * 512], pb[:, :]), reads=[pk], writes=["mT"])
            dma("sp", y_p[t * TW + bl * 128:t * TW + (bl + 1) * 128, :], ytok[:, (bl % 2) * D:(bl % 2 + 1) * D], reads=["mT"])
        if len(segs) > 1:
            pb, pk = psum()
            for kc in range(KC):
                op("pe", lambda e: e.transpose(pb[0:NS, (kc % 4) * 128:(kc % 4 + 1) * 128], xT[:, kc, TW:TWS], ident_f[:, :]),
                   reads=["xT", "ident_f"], writes=[pk], signal=True)
                if kc % 4 == 3:
                    q4 = kc // 4
                    op("act", lambda e: e.copy(ytok[0:NS, q4 * 512:(q4 + 1) * 512], pb[0:NS, :]), reads=[pk], writes=["mT"])
            dma("sp", y_s[:, :], ytok[0:NS, 0:D], reads=["mT"])

    xtok = mT_flat
    for t in range(NT):
        segs = segs_of(t)
        for bl in range(4):
            xo = (bl % 2) * D
            dma("sp", xtok[:, xo:xo + D], x_p[t * TW + bl * 128:t * TW + (bl + 1) * 128, :], writes=["mT"])
            for q4 in range(4):
                pb, pk = psum()
                for k4 in range(4):
                    kc = q4 * 4 + k4
                    op("pe", lambda e: e.transpose(pb[:, k4 * 128:(k4 + 1) * 128], xtok[:, xo + kc * 128:xo + (kc + 1) * 128], ident_f[:, :]),
                       reads=["mT", "ident_f"], writes=[pk], signal=(k4 == 3))
                op("act", lambda e: e.copy(xT[:, q4 * 4:(q4 + 1) * 4, bl * 128:(bl + 1) * 128], pb[:, :].rearrange("p (k t) -> p k t", k=4)),
                   reads=[pk], writes=["xT"])
        if len(segs) > 1:
            dma("sp", xtok[0:NS, 0:D], x_s[:, :], writes=["mT"])
            pb, pk = psum()
            for kc in range(KC):
                op("pe", lambda e: e.transpose(pb[:, kc * NS:(kc + 1) * NS], xtok[0:NS, kc * 128:(kc + 1) * 128], ident_f[0:NS, 0:NS]),
                   reads=["mT", "ident_f"], writes=[pk], signal=(kc == KC - 1))
            op("act", lambda e: e.copy(xT[:, :, TW:TWS], pb[:, 0:KC * NS].rearrange("p (k s) -> p k s", k=KC)), reads=[pk], writes=["xT"])
        xT_store(t, segs)
        tile_tail(0, t, segs)

    for i in range(nlayers):
        li = i // 2
        if i % 2 == 0:
            attn_phase(li)
        else:
            ssm_phase(li)
        for k3 in range(3):
            fm_load(cw[:, k3, :], "cw", conv_w[i, k3, :])
        fm_load(cw[:, 3, :], "cw", conv_b[i, :])
        for s_ in range(NS):
            for r_ in range(2):
                fm_load(cst[:, :, r_, s_], "cst", st_conv[i, s_, r_, :])
        op("dve", lambda e: e.memset(uprev[:], 0.0), writes=[("uprev", jj_) for jj_ in range(NUC)])
        for t in range(NT):
            segs = segs_of(t)
            xT_load(t, segs)
            if i % 2 == 0:
                mixer_out_attn(li, t, segs)
            else:
                mixer_out_ssm(li, t, segs)
            if dbg and i == 0:
                dma("sp", dbg_m[:, :, t * TW:(t + 1) * TW].rearrange("k p t -> p k t"), mT[:, :, 0:TW], reads=["mT"])
                if len(segs) > 1:
                    dma("sp", dbg_m[:, :, T:TC].rearrange("k p t -> p k t"), mT[:, :, TW:TWS], reads=["mT"])
            postnorm_residual(i, 1, segs)
            if dbg and i == 0:
                dma("sp", dbg_x1[:, :, t * TW:(t + 1) * TW].rearrange("k p t -> p k t"), xT[:, :, 0:TW], reads=["xT"])
                if len(segs) > 1:
                    dma("sp", dbg_x1[:, :, T:TC].rearrange("k p t -> p k t"), xT[:, :, TW:TWS], reads=["xT"])
            prenorm(i, 2, segs)
            ffn(i, t, segs)
            postnorm_residual(i, 3, segs)
            if i + 1 < nlayers:
                xT_store(t, segs)
                tile_tail(i + 1, t, segs)
            else:
                out_y(t, segs)
        for r_ in range(2):
            fm_store(uprev[:, :, r_], [("uprev", jj_) for jj_ in range(NUC)], conv_p[i, r_, :])
            for s_ in range(NS):
                fm_store(csout[:, :, r_, s_], "csout", conv_s[i, s_, r_, :])

    def _unused():
        pass

    assert wr.consumed == len(wr.plan), (wr.consumed, len(wr.plan))
    kb.barrier()
    kb.es.close()
    return nc


def _consts():
    ident = np.eye(128, dtype=np.float32)
    k = np.arange(128)[:, None]
    q = np.arange(128)[None, :]
    m = np.zeros((128, 7, 128), np.float32)
    m[:, 0] = (q >= k)
    m[:, 1] = (q <= k)
    r4 = ((q - k) % 4 == 0)
    m[:, 2] = r4 & (q >= k)
    m[:, 3] = r4
    m[:, 4] = r4 & (q <= k)
    r16 = ((q - k) % 16 == 0)
    m[:, 5] = r16 & (q >= k)
    m[:, 6] = r16
    half = 16
    inv = (np.float32(500000.0) ** (-(np.arange(half, dtype=np.float32) * np.float32(2.0 / 32)))).astype(np.float32)
    pos = np.zeros((128, 17), np.float32)
    for b in range(16):
        pos[:, b] = b * 128 + np.arange(128)
    pos[:, 16] = PAST
    ang = (pos[:, :, None] * inv[None, None, :]).astype(np.float32)
    rope = np.concatenate([np.cos(ang), np.sin(ang)], axis=-1).astype(np.float32)
    mbq = np.where(m.transpose(2, 1, 0) > 0.5, 0.0, -30000.0).astype(np.float32)
    gmask = (np.arange(128)[:, None] // 16 == np.arange(8)[None, :]).astype(np.float32)
    return ident, m, rope, np.ascontiguousarray(mbq), gmask


_NC_CACHE = {}


def kernel(**inp):
    f = lambda a: np.ascontiguousarray(np.asarray(a, dtype=np.float32))
    ident, masks, rope, mbq, gmask = _consts()
    if "nc" not in _NC_CACHE:
        _NC_CACHE["nc"] = build(4)
    nc = _NC_CACHE["nc"]
    shared = {
        "norm_g": f(inp["norm_g"]).reshape(16, D), "w_qkv": f(inp["w_qkv"]), "w_o": f(inp["w_attn_o"]),
        "w_in": f(inp["w_ssm_in"]), "lam_re": f(inp["lambda_re"]), "lam_im": f(inp["lambda_im"]),
        "log_dt": f(inp["log_dt"]), "b_re": f(inp["b_re"]), "b_im": f(inp["b_im"]), "c_re": f(inp["c_re"]),
        "c_im": f(inp["c_im"]), "d_skip": f(inp["d_skip"]), "w_glu": f(inp["w_glu"]), "w_up": f(inp["w_up"]),
        "conv_w": f(inp["conv_w"]), "conv_b": f(inp["conv_b"]), "w_down": f(inp["w_down"]),
        "c_ident": ident, "c_masks": masks, "c_rope": rope, "c_mbq": mbq, "c_gmask": gmask,
    }
    xp = f(inp["x_prompt"])
    xs = f(inp["x_sample"]).reshape(8, D)
    ck = [f(inp["cache_kv_g0"]), f(inp["cache_kv_g1"]), f(inp["cache_kv_g2"])]
    ss = f(inp["state_ssm"])
    sc = f(inp["state_conv"])
    in_maps = []
    for c in range(NCORES):
        m = dict(shared)
        m["x_p"] = xp[c]
        sl = slice(2 * c, 2 * c + 2)
        m["x_s"] = np.ascontiguousarray(xs[sl])
        for g in range(3):
            m["ckv%d" % g] = np.ascontiguousarray(ck[g][:, sl].reshape(2, NS, WIN[g], 2048))
        m["st_ssm"] = np.ascontiguousarray(ss[:, sl].reshape(2, NS, 128 * 64 * 2))
        m["st_conv"] = np.ascontiguousarray(sc[:, sl])
        in_maps.append(m)
    res = run_bass_kernel_spmd(nc, in_maps, core_ids=list(range(NCORES)))
    R = res.results
    y_prompt = np.stack([R[c]["y_p"] for c in range(NCORES)], 0)
    y_sample = np.concatenate([R[c]["y_s"] for c in range(NCORES)], 0).reshape(8, 1, D)
    outs = [y_prompt, y_sample]
    for g in range(3):
        keep = min(WIN[g], T)
        outs.append(np.stack([R[c]["kvp%d" % g] for c in range(NCORES)], 1).reshape(2, 4, keep, 2, 8, 128))
        outs.append(np.concatenate([R[c]["kvs%d" % g] for c in range(NCORES)], 1).reshape(2, 8, 1, 2, 8, 128))
    outs.append(np.stack([R[c]["ssm_p"] for c in range(NCORES)], 1).reshape(2, 4, 128, 64, 2))
    outs.append(np.concatenate([R[c]["ssm_s"] for c in range(NCORES)], 1).reshape(2, 8, 128, 64, 2))
    outs.append(np.stack([R[c]["conv_p"] for c in range(NCORES)], 1).reshape(4, 4, 2, 2 * DFF))
    outs.append(np.concatenate([R[c]["conv_s"] for c in range(NCORES)], 1).reshape(4, 8, 2, 2 * DFF))
    return tuple(np.ascontiguousarray(o.astype(np.float32)) for o in outs)
```
